# Optimizing a Trainium2 kernel written in Bass

```python
import math
import jax, jax.numpy as jnp
from jax import lax
import numpy as np

D_MODEL = 1024
BATCH = 8
SEQ = 4096
DEPTH = 2

EPS = 1e-6
NEG_BIG = -1e30
CONV_A_WIDTH = D_MODEL
CONV_A_K = 3
ATTN_HEADS = 16
HEAD_DIM = 64
ATTN_WIDTH = ATTN_HEADS * HEAD_DIM
ROT_DIM = HEAD_DIM // 4
ROPE_THETA = 500000.0
DILATED_PATTERNS = ((128, 1), (512, 4), (2048, 16))
Q_BLOCK = 128
SSD_EXPAND = 2
SSD_WIDTH = SSD_EXPAND * D_MODEL
SSD_HEAD_DIM = 64
SSD_HEADS = SSD_WIDTH // SSD_HEAD_DIM
SSD_GROUPS = 4
SSD_STATE = 128
SSD_CONV_K = 5
SSD_CHUNK = 128
SSD_CONV_DIM = SSD_WIDTH + 2 * SSD_GROUPS * SSD_STATE
N_BRANCH = 3
D_FF = 4 * D_MODEL
IN_SIZES = (3 * CONV_A_WIDTH, 3 * ATTN_WIDTH, SSD_WIDTH, SSD_CONV_DIM, 2 * SSD_HEADS, N_BRANCH * D_MODEL)
D_IN_PROJ = 3 * CONV_A_WIDTH + 3 * ATTN_WIDTH + SSD_WIDTH + SSD_CONV_DIM + 2 * SSD_HEADS + N_BRANCH * D_MODEL

kernel_name = "hybrid_conv_dilattn_ssd_encoder"


def _split_points(sizes):
    return tuple(int(v) for v in np.cumsum(sizes)[:-1])


def rmsnorm(x, w):
    x32 = x.astype(jnp.float32)
    y = x32 * lax.rsqrt(jnp.mean(x32 * x32, axis=-1, keepdims=True) + EPS)
    return (y * w.astype(jnp.float32)).astype(x.dtype)


def depthwise_conv_centred(x, w):
    k, c = w.shape
    return lax.conv_general_dilated(
        x, w.astype(x.dtype).reshape(k, 1, c), window_strides=(1,), padding=[(k // 2, k // 2)],
        dimension_numbers=("NWC", "WIO", "NWC"), feature_group_count=c)


def short_conv_mixer(a_in, conv_w):
    b_gate, c_gate, xh = jnp.split(a_in, 3, axis=-1)
    return b_gate * depthwise_conv_centred(c_gate * xh, conv_w)


def partial_rotary(t, positions):
    half = ROT_DIM // 2
    inv_freq = jnp.power(ROPE_THETA, -jnp.arange(0, ROT_DIM, 2, dtype=jnp.float32) / ROT_DIM)
    ang = positions.astype(jnp.float32)[..., None] * inv_freq
    cos = jnp.cos(ang)[:, :, None, :]
    sin = jnp.sin(ang)[:, :, None, :]
    t1 = t[..., :half].astype(jnp.float32)
    t2 = t[..., half:ROT_DIM].astype(jnp.float32)
    rot = jnp.concatenate([t1 * cos - t2 * sin, t2 * cos + t1 * sin], axis=-1).astype(t.dtype)
    return jnp.concatenate([rot, t[..., ROT_DIM:]], axis=-1)


def banded_attention(q, k, v, half_window):
    n, length, dh = q.shape
    n_blk = -(-length // Q_BLOCK)
    l_pad = n_blk * Q_BLOCK
    span = Q_BLOCK + 2 * half_window
    qb = jnp.pad(q, ((0, 0), (0, l_pad - length), (0, 0))).reshape(n, n_blk, Q_BLOCK, dh)
    pad_k = ((0, 0), (half_window, half_window + l_pad - length), (0, 0))
    kp = jnp.pad(k, pad_k)
    vp = jnp.pad(v, pad_k)
    key_idx = jnp.arange(n_blk)[:, None] * Q_BLOCK + jnp.arange(span)[None, :]
    kb = kp[:, key_idx]
    vb = vp[:, key_idx]
    s = jnp.einsum("nbqd,nbkd->nbqk", qb, kb, preferred_element_type=jnp.float32) * (dh ** -0.5)
    rel = jnp.arange(span)[None, :] - jnp.arange(Q_BLOCK)[:, None]
    in_band = (rel >= 0) & (rel <= 2 * half_window)
    key_pos = key_idx - half_window
    key_ok = (key_pos >= 0) & (key_pos < length)
    mask = in_band[None, :, :] & key_ok[:, None, :]
    s = jnp.where(mask[None], s, NEG_BIG)
    m = jnp.max(s, axis=-1, keepdims=True)
    p = jnp.exp(s - m)
    den = jnp.sum(p, axis=-1, keepdims=True)
    o = jnp.einsum("nbqk,nbkd->nbqd", p, vb.astype(jnp.float32)) / den
    lse = (m + jnp.log(den))[..., 0]
    return o.reshape(n, l_pad, dh)[:, :length], lse.reshape(n, l_pad)[:, :length]


def dilated_attention(q, k, v):
    b, s, h, dh = q.shape
    outs, lses = [], []
    for window, dil in DILATED_PATTERNS:
        half = window // (2 * dil)
        length = s // dil

        def to_sub(t):
            t = t.reshape(b, length, dil, h, dh).transpose(0, 2, 3, 1, 4)
            return t.reshape(b * dil * h, length, dh)

        o, lse = banded_attention(to_sub(q), to_sub(k), to_sub(v), half)
        outs.append(o.reshape(b, dil, h, length, dh).transpose(0, 3, 1, 2, 4).reshape(b, s, h, dh))
        lses.append(lse.reshape(b, dil, h, length).transpose(0, 3, 1, 2).reshape(b, s, h))
    wts = jax.nn.softmax(jnp.stack(lses, axis=-1), axis=-1)
    o = wts[..., 0, None] * outs[0]
    for i in range(1, len(outs)):
        o = o + wts[..., i, None] * outs[i]
    return o.astype(q.dtype)


def ssd_scan(x, dt, a, bm, cm):
    bt, s, h, p = x.shape
    g = SSD_GROUPS
    r = h // g
    n = bm.shape[-1]
    t = SSD_CHUNK
    nc = s // t
    xd = (x * dt[..., None]).reshape(bt, nc, t, g, r, p)
    la = (dt * a).reshape(bt, nc, t, g, r)
    bc = bm.reshape(bt, nc, t, g, n)
    cc = cm.reshape(bt, nc, t, g, n)
    cum = jnp.cumsum(la, axis=2)
    tri = jnp.tril(jnp.ones((t, t), dtype=bool))[None, None, :, :, None, None]
    seg = cum[:, :, :, None] - cum[:, :, None, :]
    decay = jnp.exp(jnp.where(tri, seg, -jnp.inf))
    cb = jnp.einsum("bclgn,bcsgn->bclsg", cc, bc)
    y_diag = jnp.einsum("bclsgr,bcsgrp->bclgrp", cb[..., None] * decay, xd)
    xw = xd * jnp.exp(cum[:, :, -1:] - cum)[..., None]
    states = jnp.einsum("bclgn,bclgrp->bcgrpn", bc, xw)
    chunk_decay = jnp.exp(cum[:, :, -1])

    def step(hs, inp):
        st, dec = inp
        return hs * dec[..., None, None] + st, hs

    h0 = jnp.zeros((bt, g, r, p, n), dtype=x.dtype)
    _, h_in = lax.scan(step, h0, (jnp.moveaxis(states, 1, 0), jnp.moveaxis(chunk_decay, 1, 0)))
    h_in = jnp.moveaxis(h_in, 0, 1)
    y_off = jnp.einsum("bclgn,bcgrpn->bclgrp", cc, h_in) * jnp.exp(cum)[..., None]
    return (y_diag + y_off).reshape(bt, s, h, p)


def ssd_mixer(z, xbc, dt_raw, conv_w, conv_b, a_log, dt_bias, d_skip, norm_w):
    bt, s, _ = z.shape
    xbc = jax.nn.silu(depthwise_conv_centred(xbc, conv_w) + conv_b.astype(xbc.dtype))
    xs, bm, cm = jnp.split(xbc, _split_points((SSD_WIDTH, SSD_GROUPS * SSD_STATE, SSD_GROUPS * SSD_STATE)), axis=-1)
    xh = xs.astype(jnp.float32).reshape(bt, s, SSD_HEADS, SSD_HEAD_DIM)
    bm = bm.astype(jnp.float32).reshape(bt, s, SSD_GROUPS, SSD_STATE)
    cm = cm.astype(jnp.float32).reshape(bt, s, SSD_GROUPS, SSD_STATE)
    dt = jax.nn.softplus(dt_raw.astype(jnp.float32).reshape(bt, s, 2, SSD_HEADS) + dt_bias.astype(jnp.float32))
    a = -jnp.exp(a_log.astype(jnp.float32))
    y_fwd = ssd_scan(xh, dt[:, :, 0], a[0], bm, cm)
    flip = lambda u: jnp.flip(u, axis=1)
    y_bwd = flip(ssd_scan(flip(xh), flip(dt[:, :, 1]), a[1], flip(bm), flip(cm)))
    y = y_fwd + y_bwd + xh * d_skip.astype(jnp.float32)[:, None]
    y = y.reshape(bt, s, SSD_WIDTH) * jax.nn.silu(z.astype(jnp.float32))
    yg = y.reshape(bt, s, SSD_GROUPS, SSD_WIDTH // SSD_GROUPS)
    yg = yg * lax.rsqrt(jnp.mean(yg * yg, axis=-1, keepdims=True) + EPS)
    y = yg.reshape(bt, s, SSD_WIDTH) * norm_w.astype(jnp.float32)
    return y.astype(z.dtype)


def hybrid_layer(x, positions, mix_norm_w, w_in, conv_a_w, w_a_out, w_b_out, ssd_conv_w, ssd_conv_b,
                 ssd_a_log, ssd_dt_bias, ssd_d, ssd_norm_w, w_c_out, w_o, mlp_norm_w, w_ff1, w_ff2):
    b, s, _ = x.shape
    u = rmsnorm(x, mix_norm_w)
    proj = u @ w_in
    a_in, qkv, z, xbc, dt_raw, gate_logits = jnp.split(proj, _split_points(IN_SIZES), axis=-1)
    y_a = short_conv_mixer(a_in, conv_a_w) @ w_a_out
    q, k, v = jnp.split(qkv, 3, axis=-1)
    q = partial_rotary(q.reshape(b, s, ATTN_HEADS, HEAD_DIM), positions)
    k = partial_rotary(k.reshape(b, s, ATTN_HEADS, HEAD_DIM), positions)
    v = v.reshape(b, s, ATTN_HEADS, HEAD_DIM)
    y_b = dilated_attention(q, k, v).reshape(b, s, ATTN_WIDTH) @ w_b_out
    y_c = ssd_mixer(z, xbc, dt_raw, ssd_conv_w, ssd_conv_b, ssd_a_log, ssd_dt_bias, ssd_d, ssd_norm_w) @ w_c_out
    g = jax.nn.sigmoid(gate_logits.astype(jnp.float32)).astype(x.dtype)
    g_a, g_b, g_c = jnp.split(g, N_BRANCH, axis=-1)
    x = x + (g_a * y_a + g_b * y_b + g_c * y_c) @ w_o
    h = rmsnorm(x, mlp_norm_w)
    return x + jnp.square(jax.nn.relu(h @ w_ff1)) @ w_ff2


def setup_inputs(seed: int = 0) -> dict:
    key = jax.random.key(seed)
    ks = jax.random.split(key, 20)
    f32 = jnp.float32
    nrm = lambda k, shape, scale: jax.random.normal(k, shape, f32) * scale
    x = jax.random.normal(ks[0], (BATCH, SEQ, D_MODEL), f32)
    offsets = jax.random.randint(ks[1], (BATCH, 1), 0, 1024, dtype=jnp.int32)
    positions = offsets + jnp.arange(SEQ, dtype=jnp.int32)[None, :]
    dt_init = jnp.exp(jax.random.uniform(ks[9], (DEPTH, 2, SSD_HEADS), f32, math.log(1e-3), math.log(1e-1)))
    return {
        "x": x,
        "positions": positions,
        "mix_norm_w": 1.0 + nrm(ks[2], (DEPTH, D_MODEL), 0.02),
        "w_in": nrm(ks[3], (DEPTH, D_MODEL, D_IN_PROJ), D_MODEL ** -0.5),
        "conv_a_w": nrm(ks[4], (DEPTH, CONV_A_K, CONV_A_WIDTH), CONV_A_K ** -0.5),
        "w_a_out": nrm(ks[5], (DEPTH, CONV_A_WIDTH, D_MODEL), CONV_A_WIDTH ** -0.5),
        "w_b_out": nrm(ks[6], (DEPTH, ATTN_WIDTH, D_MODEL), ATTN_WIDTH ** -0.5),
        "ssd_conv_w": nrm(ks[7], (DEPTH, SSD_CONV_K, SSD_CONV_DIM), SSD_CONV_K ** -0.5),
        "ssd_conv_b": nrm(ks[8], (DEPTH, SSD_CONV_DIM), 0.02),
        "ssd_a_log": jnp.log(jax.random.uniform(ks[10], (DEPTH, 2, SSD_HEADS), f32, 1.0, 16.0)),
        "ssd_dt_bias": dt_init + jnp.log(-jnp.expm1(-dt_init)),
        "ssd_d": 1.0 + nrm(ks[11], (DEPTH, SSD_HEADS), 0.02),
        "ssd_norm_w": 1.0 + nrm(ks[12], (DEPTH, SSD_WIDTH), 0.02),
        "w_c_out": nrm(ks[13], (DEPTH, SSD_WIDTH, D_MODEL), SSD_WIDTH ** -0.5),
        "w_o": nrm(ks[14], (DEPTH, D_MODEL, D_MODEL), D_MODEL ** -0.5),
        "mlp_norm_w": 1.0 + nrm(ks[15], (DEPTH, D_MODEL), 0.02),
        "w_ff1": nrm(ks[16], (DEPTH, D_MODEL, D_FF), D_MODEL ** -0.5),
        "w_ff2": nrm(ks[17], (DEPTH, D_FF, D_MODEL), D_FF ** -0.5),
        "final_norm_w": 1.0 + nrm(ks[18], (D_MODEL,), 0.02),
    }


def reference(x, positions, mix_norm_w, w_in, conv_a_w, w_a_out, w_b_out, ssd_conv_w, ssd_conv_b,
              ssd_a_log, ssd_dt_bias, ssd_d, ssd_norm_w, w_c_out, w_o, mlp_norm_w, w_ff1, w_ff2, final_norm_w):
    for l in range(DEPTH):
        x = hybrid_layer(x, positions, mix_norm_w[l], w_in[l], conv_a_w[l], w_a_out[l], w_b_out[l],
                         ssd_conv_w[l], ssd_conv_b[l], ssd_a_log[l], ssd_dt_bias[l], ssd_d[l], ssd_norm_w[l],
                         w_c_out[l], w_o[l], mlp_norm_w[l], w_ff1[l], w_ff2[l])
    return rmsnorm(x, final_norm_w)
```

```python
import numpy as np
import ml_dtypes
import concourse.bass as bass
import concourse.mybir as mybir
from concourse.bass_utils import run_bass_kernel_spmd
from contextlib import ExitStack
from types import SimpleNamespace

F32 = mybir.dt.float32
BF16 = mybir.dt.bfloat16
I32 = mybir.dt.int32
AF = mybir.ActivationFunctionType
ALU = mybir.AluOpType
AX = mybir.AxisListType

ENG = ("pe", "act", "dve", "pool", "sp")
DMAQ = ("sp", "act", "pool")


class _Buf:
    __slots__ = ("w", "r")

    def __init__(self):
        self.w = None
        self.r = {}


class _TBuf:
    __slots__ = ("whole", "subs")

    def __init__(self):
        self.whole = _Buf()
        self.subs = {}


class Sched:
    NRING = 8
    SAME_ENGINE_SYNC = True

    def __init__(self, nc):
        self.nc = nc
        self.sems = []
        self.esem = {}
        for e in ENG:
            self.esem[e] = self._new_sem("s_" + e)
        self.ecnt = {e: 0 for e in ENG}
        self.ring = {q: [self._new_sem(f"d_{q}{i}") for i in range(self.NRING)] for q in DMAQ}
        self.dman = {q: 0 for q in DMAQ}
        self.known = {e: {} for e in ENG}
        self.streams = {e: [] for e in ENG}
        self.tb = {}
        self.n_wait = 0
        self.n_ins = 0

    def _new_sem(self, name):
        h = self.nc.alloc_semaphore(name=name)
        self.sems.append(h)
        return len(self.sems) - 1

    def _spec(self, a):
        if isinstance(a, tuple):
            t, key = a
        else:
            t, key = a, None
        name = t if isinstance(t, str) else t.name
        tb = self.tb.get(name)
        if tb is None:
            tb = self.tb[name] = _TBuf()
        return tb, key

    def _deps(self, eng, reads, writes):
        need = {}

        def add(ev):
            if ev is not None and need.get(ev[0], 0) < ev[1]:
                need[ev[0]] = ev[1]

        def addr(b):
            for s, v in b.r.items():
                if need.get(s, 0) < v:
                    need[s] = v

        for a in reads:
            tb, key = self._spec(a)
            add(tb.whole.w)
            if key is None:
                for sb in tb.subs.values():
                    add(sb.w)
            else:
                sb = tb.subs.get(key)
                if sb is not None:
                    add(sb.w)
        for a in writes:
            tb, key = self._spec(a)
            add(tb.whole.w)
            addr(tb.whole)
            if key is None:
                for sb in tb.subs.values():
                    add(sb.w)
                    addr(sb)
            else:
                sb = tb.subs.get(key)
                if sb is not None:
                    add(sb.w)
                    addr(sb)
        kn = self.known[eng]
        own = self.esem[eng]
        waits = []
        for s, v in need.items():
            if s == own and (eng == "pe" or not self.SAME_ENGINE_SYNC):
                continue
            if kn.get(s, 0) < v:
                kn[s] = v
                waits.append((s, v))
        return waits

    def _record(self, ev, reads, writes):
        s, v = ev
        for a in reads:
            tb, key = self._spec(a)
            if key is None:
                b = tb.whole
            else:
                b = tb.subs.get(key)
                if b is None:
                    b = tb.subs[key] = _Buf()
            if b.r.get(s, 0) < v:
                b.r[s] = v
        for a in writes:
            tb, key = self._spec(a)
            if key is None:
                tb.whole.w = ev
                tb.whole.r = {}
                tb.subs = {}
            else:
                b = tb.subs.get(key)
                if b is None:
                    b = tb.subs[key] = _Buf()
                b.w = ev
                b.r = {}

    def op(self, eng, emit, r=(), w=()):
        waits = self._deps(eng, r, w)
        self.ecnt[eng] += 1
        ev = (self.esem[eng], self.ecnt[eng])
        self._record(ev, r, w)
        self.streams[eng].append((waits, emit, ev[0], 1))
        self.n_wait += len(waits)
        self.n_ins += 1

    def dma(self, q, out, in_, r=(), w=(), **kw):
        waits = self._deps(q, r, w)
        i = self.dman[q]
        self.dman[q] += 1
        s = self.ring[q][i % self.NRING]
        rnd = i // self.NRING
        if rnd > 0:
            kn = self.known[q]
            if kn.get(s, 0) < 16 * rnd:
                kn[s] = 16 * rnd
                waits.append((s, 16 * rnd))
        ev = (s, 16 * (rnd + 1))
        self._record(ev, r, w)
        self.streams[q].append((waits, lambda e: e.dma_start(out=out, in_=in_, **kw), s, 16))
        self.n_wait += len(waits)
        self.n_ins += 1

    def barrier(self, skip_q=("pool",)):
        evs = []
        for e in ENG:
            if self.ecnt[e] > 0:
                evs.append((self.esem[e], self.ecnt[e]))
        for q in DMAQ:
            if q in skip_q:
                continue
            n = self.dman[q]
            for j in range(min(n, self.NRING)):
                last = ((n - 1 - j) // self.NRING) * self.NRING + j
                evs.append((self.ring[q][j], 16 * (last // self.NRING + 1)))
        for e in ENG:
            kn = self.known[e]
            waits = []
            for s, v in evs:
                if kn.get(s, 0) < v:
                    kn[s] = v
                    waits.append((s, v))
            if waits:
                self.streams[e].append((waits, None, None, 0))
                self.n_wait += len(waits)

    def flush(self):
        nc = self.nc
        sems = self.sems

        def run(name):
            items = self.streams[name]
            self.streams[name] = []

            def f(e):
                for waits, emit, s, inc in items:
                    for ws, wv in waits:
                        e.wait_ge(sems[ws], wv)
                    if emit is not None:
                        ins = emit(e)
                        ins.then_inc(sems[s], inc)

            return f

        with nc.Block() as block:
            block.tensor(run("pe"))
            block.scalar(run("act"))
            block.vector(run("dve"))
            block.gpsimd(run("pool"))
            block.sync(run("sp"))


S_ = 4096
D_ = 1024
NT_ = 32
L_ = 2
DIN_ = 14400
PI = 3.141592653589793


_UID = [0]


def _sbt(nc, name, shape, dt):
    _UID[0] += 1
    return nc.sbuf_tensor(f"{name}_u{_UID[0]}", shape, dt)


def _pst(nc, name, shape, dt):
    _UID[0] += 1
    return nc.psum_tensor(f"{name}_u{_UID[0]}", shape, dt)


class Ops:
    def __init__(self, S):
        self.S = S

    def mm(self, out, lhsT, rhs, start, stop, r, w):
        self.S.op("pe", lambda e: e.matmul(out, lhsT=lhsT, rhs=rhs, start=start, stop=stop), r, w)

    def tr(self, out, in_, ident, r, w):
        self.S.op("pe", lambda e: e.transpose(out=out, in_=in_, identity=ident), r, w)

    def act(self, out, in_, func, r, w, **kw):
        self.S.op("act", lambda e: e.activation(out=out, in_=in_, func=func, **kw), r, w)

    def tt(self, eng, out, in0, in1, op, r, w):
        self.S.op(eng, lambda e: e.tensor_tensor(out=out, in0=in0, in1=in1, op=op), r, w)

    def ts(self, eng, out, in0, s1, s2, op0, op1, r, w):
        if s2 is None:
            self.S.op(eng, lambda e: e.tensor_scalar(out=out, in0=in0, scalar1=s1, scalar2=None, op0=op0), r, w)
        else:
            self.S.op(eng, lambda e: e.tensor_scalar(out=out, in0=in0, scalar1=s1, scalar2=s2, op0=op0, op1=op1), r, w)

    def stt(self, eng, out, in0, scalar, in1, op0, op1, r, w):
        self.S.op(eng, lambda e: e.scalar_tensor_tensor(out=out, in0=in0, scalar=scalar, in1=in1, op0=op0, op1=op1), r, w)

    def copy(self, eng, out, in_, r, w):
        if eng == "act":
            self.S.op(eng, lambda e: e.copy(out=out, in_=in_), r, w)
        else:
            self.S.op(eng, lambda e: e.tensor_copy(out=out, in_=in_), r, w)

    def memset(self, eng, ap, val, w):
        self.S.op(eng, lambda e: e.memset(ap, val), (), w)

    def recip(self, out, in_, r, w):
        self.S.op("dve", lambda e: e.reciprocal(out=out, in_=in_), r, w)


def build(dbg=(), stop=None, nlayers=L_):
    nc = bass.Bass("TRN2", target_bir_lowering=False)

    def din(name, shape, dt=F32):
        return nc.dram_tensor(name, list(shape), dt, kind="ExternalInput").ap()

    def dscr(name, shape, dt):
        kind = "ExternalOutput" if name in dbg else "Internal"
        return nc.dram_tensor(name, list(shape), dt, kind=kind).ap()

    x_in = din("x", [S_, D_])
    pos_in = din("pos", [128, NT_], I32)
    mixw = din("mix_norm_w", [L_, D_])
    mlpw = din("mlp_norm_w", [L_, D_])
    finw = din("final_norm_w", [1, D_])
    w_in = din("w_in", [L_, D_, DIN_])
    w_a = din("w_a_out", [L_, D_, D_])
    w_b = din("w_b_out", [L_, D_, D_])
    w_c = din("w_c_out", [L_, 2 * D_, D_])
    w_o = din("w_o", [L_, D_, D_])
    w_1 = din("w_ff1", [L_, D_, 4 * D_])
    w_2 = din("w_ff2", [L_, 4 * D_, D_])
    cawT = din("conv_a_wT", [L_, 128, 8, 3])
    scwT = din("ssd_conv_wT", [L_, 128, 24, 5])
    scbT = din("ssd_conv_bT", [L_, 128, 24])
    scbR = din("ssd_conv_bR", [L_, 1, 3072])
    alog = din("ssd_a_log", [L_, 1, 64])
    dtb = din("ssd_dt_bias", [L_, 1, 64])
    sdd = din("ssd_d", [L_, 1, 32])
    snw = din("ssd_norm_w", [L_, 1, 2048])
    c_ident = din("c_ident", [128, 128], BF16)
    c_tri = din("c_tri", [128, 4, 128], BF16)
    c_trif = din("c_trif", [128, 5, 128], F32)
    c_invf = din("c_invf", [1, 8], F32)
    y_out = nc.dram_tensor("y", [S_, D_], F32, kind="ExternalOutput").ap()

    wb_in = [dscr(f"wb_in{l}", [D_, DIN_], BF16) for l in range(L_)]
    wb_a = [dscr(f"wb_a{l}", [D_, D_], BF16) for l in range(L_)]
    wb_b = [dscr(f"wb_b{l}", [D_, D_], BF16) for l in range(L_)]
    wb_c = [dscr(f"wb_c{l}", [2 * D_, D_], BF16) for l in range(L_)]
    wb_o = [dscr(f"wb_o{l}", [D_, D_], BF16) for l in range(L_)]
    wb_1 = [dscr(f"wb_1{l}", [D_, 4 * D_], BF16) for l in range(L_)]
    wb_2 = [dscr(f"wb_2{l}", [4 * D_, D_], BF16) for l in range(L_)]
    yAT = dscr("yAT", [D_, S_], BF16)
    gT = dscr("gT", [3 * D_, S_], BF16)
    qT = dscr("qT", [D_, S_], BF16)
    kT = dscr("kT", [D_, S_], BF16)
    v_d = dscr("v_d", [S_, D_], BF16)
    sz_d = dscr("sz_d", [S_, 2 * D_], BF16)
    xs_d = dscr("xs_d", [S_, 2 * D_], BF16)
    Bt_d = dscr("Bt_d", [S_, 512], BF16)
    BT_d = dscr("BT_d", [512, S_], BF16)
    CT_d = dscr("CT_d", [512, S_], BF16)
    dt_dbg = dscr("dt_dbg", [128, NT_, 64], F32)
    o_d = [dscr(f"o_d{p}", [S_, 16, 65], F32) for p in range(3)]
    hin_d = [dscr(f"hin_d{d}", [NT_, 128, 2048], BF16) for d in range(2)]
    ynT_d = dscr("ynT_d", [2 * D_, S_], BF16)
    xmid = dscr("xmid", [S_, D_], F32)
    xl = [dscr(f"xl{l}", [S_, D_], F32) for l in range(L_ - 1)]

    S = Sched(nc)
    O = Ops(S)

    with ExitStack() as gst:
        def gsb(name, shape, dt):
            return gst.enter_context(_sbt(nc, name, list(shape), dt))

        ident = gsb("ident", [128, 128], BF16)
        tri = gsb("tri", [128, 4, 128], BF16)
        trif = gsb("trif", [128, 5, 128], F32)
        cosT = gsb("cosT", [128, NT_, 8], F32)
        sinT = gsb("sinT", [128, NT_, 8], F32)
        dt_all = gsb("dt_all", [128, NT_, 64], F32)
        la_all = gsb("la_all", [128, NT_, 64], F32)
        S.dma("sp", ident[:], c_ident, r=[], w=[ident])
        S.dma("sp", tri[:], c_tri, r=[], w=[tri])
        S.dma("sp", trif[:], c_trif, r=[], w=[trif])

        def cast_w(src, dst, rows_per):
            R = src.shape[0]
            for r0 in range(0, R, rows_per):
                S.dma("pool", dst[r0:r0 + rows_per, :], src[r0:r0 + rows_per, :], r=[], w=[(dst, r0)])

        for l in range(nlayers):
            cast_w(w_in[l], wb_in[l], 64)
            cast_w(w_a[l], wb_a[l], 512)
            cast_w(w_b[l], wb_b[l], 512)
            cast_w(w_c[l], wb_c[l], 512)
            cast_w(w_o[l], wb_o[l], 512)
            cast_w(w_1[l], wb_1[l], 128)
            cast_w(w_2[l], wb_2[l], 512)

        with ExitStack() as st:
            def sb(name, shape, dt):
                return st.enter_context(_sbt(nc, name, list(shape), dt))
            posi = sb("posi", [128, NT_], I32)
            posf = sb("posf", [128, NT_], F32)
            invf = sb("invf", [128, 8], F32)
            ang = sb("ang", [128, NT_, 8], F32)
            a1 = sb("a1", [128, NT_, 8], F32)
            S.dma("sp", posi[:], pos_in, r=[], w=[posi])
            S.dma("sp", invf[:], c_invf.partition_broadcast(128), r=[], w=[invf])
            O.copy("dve", posf[:], posi[:], [posi], [posf])
            O.tt("dve", ang[:], posf[:].unsqueeze(2).to_broadcast([128, NT_, 8]),
                 invf[:].unsqueeze(1).to_broadcast([128, NT_, 8]), ALU.mult, [posf, invf], [ang])
            ki = sb("ki", [128, NT_, 8], I32)
            kf = sb("kf", [128, NT_, 8], F32)
            mm_ = sb("mm_", [128, NT_, 8], F32)
            for shift, dstT in ((0.0, sinT), (0.5 * PI, cosT)):
                O.ts("dve", a1[:], ang[:], shift, 1.0 / (2 * PI), ALU.add, ALU.mult, [ang], [a1])
                O.copy("dve", ki[:], a1[:], [a1], [ki])
                O.copy("dve", kf[:], ki[:], [ki], [kf])
                O.ts("dve", a1[:], ang[:], shift, None, ALU.add, None, [ang], [a1])
                O.stt("dve", a1[:], kf[:], -2 * PI, a1[:], ALU.mult, ALU.add, [kf, a1], [a1])
                O.ts("dve", mm_[:], a1[:], PI, 2 * PI, ALU.is_ge, ALU.mult, [a1], [mm_])
                O.tt("dve", a1[:], a1[:], mm_[:], ALU.subtract, [a1, mm_], [a1])
                O.ts("dve", mm_[:], a1[:], -PI, 2 * PI, ALU.is_lt, ALU.mult, [a1], [mm_])
                O.tt("dve", a1[:], a1[:], mm_[:], ALU.add, [a1, mm_], [a1])
                O.ts("dve", a1[:], a1[:], -PI, PI, ALU.max, ALU.min, [a1], [a1])
                O.act(dstT[:], a1[:], AF.Sin, [a1], [dstT])
            S.barrier()
            S.flush()

        for l in range(nlayers):
            x_src = x_in if l == 0 else xl[l - 1]
            x_dst = xl[l] if l < L_ - 1 else None
            C = SimpleNamespace(**locals())
            layer(C, l, x_src, x_dst, stop)
            if stop is not None and stop[0] == l:
                break
        S.barrier(skip_q=())
        S.flush()
    return nc


def layer(C, l, x_src, x_dst, stop):
    nc, S, O = C.nc, C.S, C.O
    with ExitStack() as lst:
        uT = lst.enter_context(_sbt(nc, "uT", [128, 8, S_], BF16))
        phase1(C, l, x_src, uT)
        if stop == (l, 1):
            return
        phase2(C, l, uT)
    S.barrier()
    S.flush()
    if stop == (l, 2):
        return
    phase3(C, l)
    if stop == (l, 3):
        return
    phase4(C, l)
    if stop == (l, 4):
        return
    phase5(C, l, x_src, x_dst)


def rms_tile(C, xt, ss, sq, nwt, ub, tag):
    O = C.O
    O.act(sq[:], xt[:], AF.Square, [xt], [sq, ss], accum_out=ss[:])
    O.act(ss[:], ss[:], AF.Sqrt, [ss], [ss], bias=1e-6, scale=1.0 / D_)
    O.recip(ss[:], ss[:], [ss], [ss])
    O.stt("dve", ub[:], xt[:], ss[:], nwt[:], ALU.mult, ALU.mult, [xt, ss, nwt], [ub])


def phase1(C, l, x_src, uT):
    nc, S, O = C.nc, C.S, C.O
    with ExitStack() as st:
        def sb(name, shape, dt):
            return st.enter_context(_sbt(nc, name, list(shape), dt))
        xt = [sb(f"p1x{i}", [128, D_], F32) for i in range(2)]
        sq = sb("p1sq", [128, D_], BF16)
        ss = [sb(f"p1ss{i}", [128, 1], F32) for i in range(2)]
        ub = [sb(f"p1ub{i}", [128, D_], BF16) for i in range(2)]
        nwt = sb("p1nw", [128, D_], F32)
        pT = [st.enter_context(_pst(nc, f"p1pT{i}", [128, 8, 128], BF16)) for i in range(2)]
        S.dma("sp", nwt[:], C.mixw[l:l + 1, :].partition_broadcast(128), r=[], w=[nwt])
        for t in range(NT_):
            b = t % 2
            S.dma("sp", xt[b][:], x_src[t * 128:(t + 1) * 128, :], r=[(x_src, t)], w=[xt[b]])
            rms_tile(C, xt[b], ss[b], sq, nwt, ub[b], "p1")
            for k in range(8):
                O.tr(pT[b][:, k, :], ub[b][:, k * 128:(k + 1) * 128], C.ident[:], [ub[b], C.ident], [pT[b]])
            O.copy("act" if t % 2 else "dve", uT[:, :, t * 128:(t + 1) * 128], pT[b][:], [pT[b]], [(uT, t)])
        S.barrier()
        S.flush()


def phase2(C, l, uT):
    nc, S, O = C.nc, C.S, C.O
    wsrc = C.wb_in[l].rearrange("(k p) n -> p k n", p=128)
    seq = []
    for g in range(2):
        seq += [(512 * g, 512), (1024 + 512 * g, 512), (2048 + 512 * g, 512)]
    seq += [(3072 + 512 * i, 512) for i in range(4)]
    seq += [(5120 + 512 * i, 512) for i in range(2)]
    seq += [(6144 + 512 * i, 512) for i in range(4)]
    seq += [(8192 + 512 * i, 512) for i in range(6)]
    seq += [(11264, 64)]
    seq += [(11328 + 512 * i, 512) for i in range(6)]
    with ExitStack() as st:
        wbuf = [st.enter_context(_sbt(nc, f"p2w{i}", [128, 8, 512], BF16)) for i in range(3)]
        PB = [st.enter_context(_pst(nc, f"p2ps{i}", [128, 512], F32)) for i in range(6)]
        PT = [st.enter_context(_pst(nc, f"p2pt{i}", [128, 4, 128], BF16)) for i in range(2)]
        state = {"issued": 0, "ps": 0, "done": 0}

        def prefetch():
            while state["issued"] < min(len(seq), state["done"] + 3):
                i = state["issued"]
                c0, ncw = seq[i]
                S.dma("sp", wbuf[i % 3][:, :, :ncw], wsrc[:, :, c0:c0 + ncw], r=[C.wb_in[l]], w=[wbuf[i % 3]])
                state["issued"] += 1

        def get_w(idx):
            prefetch()
            assert idx < state["issued"]
            return wbuf[idx % 3]

        def release(n):
            state["done"] = n
            prefetch()

        def PS():
            p = PB[state["ps"] % len(PB)]
            state["ps"] += 1
            return p

        def mm_feat(ps, wt, jj, tb):
            for k in range(8):
                O.mm(ps[:], wt[:, k, jj * 128:(jj + 1) * 128], uT[:, k, tb * 512:(tb + 1) * 512], k == 0, k == 7,
                     [wt, uT], [ps])

        def mm_tok(ps_ap, ps, wt, t, ncw):
            for k in range(8):
                O.mm(ps_ap, uT[:, k, t * 128:(t + 1) * 128], wt[:, k, :ncw], k == 0, k == 7, [wt, uT], [ps])

        wi = 0
        with ExitStack() as s2:
            def sb(name, shape, dt):
                return s2.enter_context(_sbt(nc, name, list(shape), dt))
            bT = sb("p2bT", [128, S_], F32)
            cst = [sb(f"p2cst{i}", [128, 512], F32) for i in range(2)]
            cx = sb("p2cx", [128, S_ + 2], F32)
            tcv = sb("p2tcv", [128, S_], F32)
            yAo = sb("p2yAo", [128, S_], BF16)
            wA = sb("p2wA", [128, 8, 3], F32)
            S.dma("sp", wA[:], C.cawT[l], r=[], w=[wA])
            O.memset("dve", cx[:, 0:1], 0.0, [(cx, "l")])
            O.memset("dve", cx[:, S_ + 1:S_ + 2], 0.0, [(cx, "r")])
            for g in range(2):
                release(wi)
                wt_b, wt_c, wt_x = get_w(wi), get_w(wi + 1), get_w(wi + 2)
                wi += 3
                for jj in range(4):
                    j = 4 * g + jj
                    for tb in range(8):
                        pb, pc, px = PS(), PS(), PS()
                        mm_feat(pb, wt_b, jj, tb)
                        mm_feat(pc, wt_c, jj, tb)
                        mm_feat(px, wt_x, jj, tb)
                        O.copy("act", bT[:, tb * 512:(tb + 1) * 512], pb[:], [pb], [(bT, tb)])
                        O.copy("act", cst[tb % 2][:], pc[:], [pc], [cst[tb % 2]])
                        O.tt("dve", cx[:, 1 + tb * 512:1 + (tb + 1) * 512], px[:], cst[tb % 2][:], ALU.mult,
                             [px, cst[tb % 2]], [(cx, tb)])
                    O.ts("dve", tcv[:], cx[:, 0:S_], wA[:, j, 0:1], None, ALU.mult, None, [cx, wA], [tcv])
                    O.stt("dve", tcv[:], cx[:, 1:S_ + 1], wA[:, j, 1:2], tcv[:], ALU.mult, ALU.add, [cx, wA, tcv], [tcv])
                    O.stt("dve", tcv[:], cx[:, 2:S_ + 2], wA[:, j, 2:3], tcv[:], ALU.mult, ALU.add, [cx, wA, tcv], [tcv])
                    O.tt("dve", yAo[:], tcv[:], bT[:], ALU.mult, [tcv, bT], [yAo])
                    S.dma("sp", C.yAT[j * 128:(j + 1) * 128, :], yAo[:], r=[yAo], w=[(C.yAT, j)])
            S.barrier()
            S.flush()
        with ExitStack() as s2:
            def sb(name, shape, dt):
                return s2.enter_context(_sbt(nc, name, list(shape), dt))
            qs = [sb(f"p2qs{i}", [128, 8, 64], F32) for i in range(2)]
            tmp = [sb(f"p2tmp{i}", [128, 4, 8, 8], F32) for i in range(2)]
            qr = [sb(f"p2qr{i}", [128, 512], BF16) for i in range(2)]
            stg = [sb(f"p2stg{i}", [128, 4, S_], BF16) for i in range(2)]
            for ci in range(4):
                release(wi)
                wt = get_w(wi)
                wi += 1
                sg = stg[ci % 2]
                for t in range(NT_):
                    b = t % 2
                    ps = PS()
                    mm_tok(ps[:], ps, wt, t, 512)
                    O.copy("act", qs[b][:].rearrange("p h d -> p (h d)"), ps[:], [ps], [qs[b]])
                    cs = C.cosT[:, t, :].unsqueeze(1).to_broadcast([128, 8, 8])
                    sn = C.sinT[:, t, :].unsqueeze(1).to_broadcast([128, 8, 8])
                    t1 = qs[b][:, :, 0:8]
                    t2 = qs[b][:, :, 8:16]
                    tm = tmp[b]
                    O.tt("dve", tm[:, 0], t1, cs, ALU.mult, [qs[b], C.cosT], [(tm, 0)])
                    O.tt("dve", tm[:, 1], t2, sn, ALU.mult, [qs[b], C.sinT], [(tm, 1)])
                    O.tt("dve", tm[:, 2], t2, cs, ALU.mult, [qs[b], C.cosT], [(tm, 2)])
                    O.tt("dve", tm[:, 3], t1, sn, ALU.mult, [qs[b], C.sinT], [(tm, 3)])
                    O.tt("dve", t1, tm[:, 0], tm[:, 1], ALU.subtract, [tm], [qs[b]])
                    O.tt("dve", t2, tm[:, 2], tm[:, 3], ALU.add, [tm], [qs[b]])
                    O.copy("dve", qr[b][:], qs[b][:].rearrange("p h d -> p (h d)"), [qs[b]], [qr[b]])
                    pt = PT[b]
                    for jj in range(4):
                        O.tr(pt[:, jj, :], qr[b][:, jj * 128:(jj + 1) * 128], C.ident[:], [qr[b], C.ident], [pt])
                    O.copy("act", sg[:, :, t * 128:(t + 1) * 128], pt[:], [pt], [(sg, t)])
                dst = C.qT if ci < 2 else C.kT
                for jj in range(4):
                    r0 = (ci % 2) * 512 + jj * 128
                    S.dma("sp", dst[r0:r0 + 128, :], sg[:, jj, :], r=[sg], w=[(dst, r0)])
            S.barrier()
            S.flush()
        with ExitStack() as s2:
            def sb(name, shape, dt):
                return s2.enter_context(_sbt(nc, name, list(shape), dt))
            vst = [sb(f"p2vst{i}", [128, 512], BF16) for i in range(4)]
            n = 0
            for ci in range(6):
                release(wi)
                wt = get_w(wi)
                wi += 1
                for t in range(NT_):
                    ps = PS()
                    mm_tok(ps[:], ps, wt, t, 512)
                    vs = vst[n % 4]
                    n += 1
                    if ci < 2:
                        O.copy("act", vs[:], ps[:], [ps], [vs])
                        S.dma("sp", C.v_d[t * 128:(t + 1) * 128, ci * 512:(ci + 1) * 512], vs[:], r=[vs], w=[(C.v_d, (t, ci))])
                    else:
                        O.act(vs[:], ps[:], AF.Silu, [ps], [vs])
                        c2 = ci - 2
                        S.dma("sp", C.sz_d[t * 128:(t + 1) * 128, c2 * 512:(c2 + 1) * 512], vs[:], r=[vs], w=[(C.sz_d, (t, c2))])
            S.barrier()
            S.flush()
        with ExitStack() as s2:
            def sb(name, shape, dt):
                return s2.enter_context(_sbt(nc, name, list(shape), dt))
            xin = sb("p2xin", [128, 4, S_ + 4], BF16)
            diag = sb("p2diag", [128, 4, 5, 128], BF16)
            stok = [sb(f"p2stok{i}", [128, 8, 512], BF16) for i in range(2)]
            sfeat = [sb(f"p2sfeat{i}", [128, S_], BF16) for i in range(2)]
            wS = sb("p2wS", [128, 24, 5], F32)
            bS = sb("p2bS", [128, 24], F32)
            brf = sb("p2brf", [1, 3072], F32)
            brow = sb("p2brow", [1, 3072], BF16)
            ones = sb("p2ones", [1, 128], BF16)
            S.dma("sp", wS[:], C.scwT[l], r=[], w=[wS])
            S.dma("sp", bS[:], C.scbT[l], r=[], w=[bS])
            S.dma("sp", brf[:], C.scbR[l], r=[], w=[brf])
            O.copy("dve", brow[:], brf[:], [brf], [brow])
            O.memset("dve", ones[:], 1.0, [ones])
            O.memset("dve", xin[:, :, 0:2], 0.0, [(xin, "l")])
            O.memset("dve", xin[:, :, S_ + 2:S_ + 4], 0.0, [(xin, "r")])
            nf = 0
            for ci in range(6):
                release(wi)
                wt = get_w(wi)
                wi += 1
                for jj in range(4):
                    J = 4 * ci + jj
                    for tb in range(8):
                        ps = PS()
                        mm_feat(ps, wt, jj, tb)
                        O.copy("act" if tb % 2 else "dve", xin[:, jj, 2 + tb * 512:2 + (tb + 1) * 512], ps[:], [ps], [(xin, (jj, tb))])
                    for k5 in range(5):
                        O.ts("dve", diag[:, jj, k5, :], C.ident[:], wS[:, J, k5:k5 + 1], None, ALU.mult, None,
                             [C.ident, wS], [(diag, (jj, k5))])
                if ci <= 4:
                    for t in range(NT_):
                        ps = PS()
                        for jj in range(4):
                            J = 4 * ci + jj
                            o_ap = ps[:, jj * 128:(jj + 1) * 128]
                            for k5 in range(5):
                                O.mm(o_ap, xin[:, jj, t * 128 + k5:t * 128 + k5 + 128], diag[:, jj, k5, :], k5 == 0, False,
                                     [xin, diag], [ps])
                            O.mm(o_ap, ones[0:1, :], brow[0:1, J * 128:(J + 1) * 128], False, True, [ones, brow], [ps])
                        sk = stok[(t // 8) % 2]
                        O.act(sk[:, t % 8, :], ps[:], AF.Silu, [ps], [(sk, t % 8)])
                        if t % 8 == 7:
                            t0 = t - 7
                            if ci < 4:
                                dst = C.xs_d[t0 * 128:(t0 + 8) * 128, ci * 512:(ci + 1) * 512]
                                key = (C.xs_d, (t0, ci))
                            else:
                                dst = C.Bt_d[t0 * 128:(t0 + 8) * 128, :]
                                key = (C.Bt_d, t0)
                            S.dma("sp", dst.rearrange("(t p) c -> p t c", p=128), sk[:], r=[sk], w=[key])
                if ci >= 4:
                    for jj in range(4):
                        J = 4 * ci + jj
                        sf = sfeat[nf % 2]
                        nf += 1
                        for tb in range(8):
                            ps = PS()
                            for k5 in range(5):
                                O.mm(ps[:], diag[:, jj, k5, :], xin[:, jj, tb * 512 + k5:tb * 512 + k5 + 512], k5 == 0, k5 == 4,
                                     [xin, diag], [ps])
                            O.act(sf[:, tb * 512:(tb + 1) * 512], ps[:], AF.Silu, [ps, bS], [(sf, tb)], bias=bS[:, J:J + 1])
                        dst = C.BT_d if ci == 4 else C.CT_d
                        S.dma("sp", dst[jj * 128:(jj + 1) * 128, :], sf[:], r=[sf], w=[(dst, jj)])
            S.barrier()
            S.flush()
        with ExitStack() as s2:
            def sb(name, shape, dt):
                return s2.enter_context(_sbt(nc, name, list(shape), dt))
            dtbb = sb("p2dtbb", [128, 64], F32)
            abc = sb("p2abc", [128, 64], F32)
            tdt = [sb(f"p2tdt{i}", [128, 64], F32) for i in range(2)]
            S.dma("sp", dtbb[:], C.dtb[l].partition_broadcast(128), r=[], w=[dtbb])
            S.dma("sp", abc[:], C.alog[l].partition_broadcast(128), r=[], w=[abc])
            O.act(abc[:], abc[:], AF.Exp, [abc], [abc])
            O.ts("dve", abc[:], abc[:], -1.0, None, ALU.mult, None, [abc], [abc])
            release(wi)
            wt = get_w(wi)
            wi += 1
            for t in range(NT_):
                ps = PS()
                mm_tok(ps[:, 0:64], ps, wt, t, 64)
                td = tdt[t % 2]
                O.tt("dve", td[:], ps[:, 0:64], dtbb[:], ALU.add, [ps, dtbb], [td])
                O.act(td[:], td[:], AF.Exp, [td], [td])
                O.act(C.dt_all[:, t, :], td[:], AF.Ln, [td], [(C.dt_all, t)], bias=1.0)
                O.tt("dve", C.la_all[:, t, :], C.dt_all[:, t, :], abc[:], ALU.mult, [(C.dt_all, t), abc], [(C.la_all, t)])
            if "dt_dbg" in C.dbg:
                S.dma("sp", C.dt_dbg, C.dt_all[:], r=[C.dt_all], w=[C.dt_dbg])
            S.barrier()
            S.flush()
        with ExitStack() as s2:
            def sb(name, shape, dt):
                return s2.enter_context(_sbt(nc, name, list(shape), dt))
            sfeat = [sb(f"p2gfeat{i}", [128, S_], BF16) for i in range(2)]
            nf = 0
            for ci in range(6):
                release(wi)
                wt = get_w(wi)
                wi += 1
                for jj in range(4):
                    G = 4 * ci + jj
                    sf = sfeat[nf % 2]
                    nf += 1
                    for tb in range(8):
                        ps = PS()
                        mm_feat(ps, wt, jj, tb)
                        O.act(sf[:, tb * 512:(tb + 1) * 512], ps[:], AF.Sigmoid, [ps], [(sf, tb)])
                    S.dma("sp", C.gT[G * 128:(G + 1) * 128, :], sf[:], r=[sf], w=[(C.gT, G)])
            S.barrier()
            S.flush()
        assert wi == len(seq)


def phase3(C, l):
    nc, S, O = C.nc, C.S, C.O
    pats = [(1, 33), (4, 9), (16, 3)]
    with ExitStack() as st:
        def sb(name, shape, dt):
            return st.enter_context(_sbt(nc, name, list(shape), dt))
        qTh = [sb(f"p3q{i}", [64, S_], BF16) for i in range(2)]
        kTp = [sb(f"p3k{i}", [64, S_ + 2048], BF16) for i in range(2)]
        vP = [[sb(f"p3v{pi}_{b}", [128, nb, dil, 65], BF16) for b in range(2)] for pi, (dil, nb) in enumerate(pats)]
        mask = sb("p3mask", [128, 2, 128], BF16)
        PTs = [sb(f"p3pt{i}", [128, 2, 2, 128], BF16) for i in range(2)]
        PTm = [sb(f"p3pm{i}", [128, 2, 2, 128], BF16) for i in range(2)]
        ost = [sb(f"p3ost{i}", [128, 32, 65], F32) for i in range(2)]
        PSs = [st.enter_context(_pst(nc, f"p3ps{i}", [128, 512], F32)) for i in range(3)]
        PSo = [st.enter_context(_pst(nc, f"p3po{i}", [128, 2, 65], F32)) for i in range(3)]
        O.copy("dve", mask[:, 0, :], C.tri[:, 1, :], [C.tri], [(mask, 0)])
        O.copy("dve", mask[:, 1, :], C.tri[:, 0, :], [C.tri], [(mask, 1)])
        for b in range(2):
            O.memset("dve", kTp[b][:, 0:1024], 0.0, [(kTp[b], "l")])
            O.memset("dve", kTp[b][:, 1024 + S_:2048 + S_], 0.0, [(kTp[b], "r")])
            for pi, (dil, nb) in enumerate(pats):
                v = vP[pi][b]
                O.memset("pool", v[:], 0.0, [v])
                O.memset("pool", v[:, :, :, 64:65], 1.0, [v])
                O.memset("pool", v[0:64, 0, :, 64:65], 0.0, [v])
                O.memset("pool", v[64:128, nb - 1, :, 64:65], 0.0, [v])
        n_s = 0
        n_o = 0
        n_ost = 0
        for h in range(16):
            b = h % 2
            S.dma("sp", qTh[b][:], C.qT[h * 64:(h + 1) * 64, :], r=[C.qT], w=[qTh[b]])
            S.dma("sp", kTp[b][:, 1024:1024 + S_], C.kT[h * 64:(h + 1) * 64, :], r=[C.kT], w=[(kTp[b], "m")])
            vsrc = C.v_d[:, h * 64:(h + 1) * 64]
            for pi, (dil, nb) in enumerate(pats):
                v = vP[pi][b]
                nin = nb - 2
                if dil == 1:
                    for j0 in range(0, nin, 8):
                        j1 = min(nin, j0 + 8)
                        src = vsrc[64 + j0 * 128:64 + j1 * 128, :]
                        S.dma("sp", v[:, 1 + j0:1 + j1, 0, 0:64], src.rearrange("(j i) d -> i j d", i=128),
                              r=[C.v_d], w=[(v, ("in", j0))])
                else:
                    for j0 in range(nin):
                        src = vsrc[64 * dil + j0 * 128 * dil:64 * dil + (j0 + 1) * 128 * dil, :]
                        S.dma("sp", v[:, 1 + j0, :, 0:64], src.rearrange("(i r) d -> i r d", r=dil),
                              r=[C.v_d], w=[(v, ("in", j0))])
                S.dma("sp", v[64:128, 0, :, 0:64], vsrc[0:64 * dil, :].rearrange("(i r) d -> i r d", r=dil),
                      r=[C.v_d], w=[(v, "first")])
                S.dma("sp", v[0:64, nb - 1, :, 0:64], vsrc[S_ - 64 * dil:S_, :].rearrange("(i r) d -> i r d", r=dil),
                      r=[C.v_d], w=[(v, "last")])
            for pi, (dil, nb) in enumerate(pats):
                v = vP[pi][b]
                nqb = nb - 1
                osb = ost[n_ost % 2]
                n_ost += 1
                osv = osb[:].rearrange("p (q r) e -> p q r e", r=dil)
                for r in range(dil):
                    for qp in range(nqb // 2):
                        ps = PSs[n_s % 3]
                        pt = PTs[n_s % 2]
                        pm = PTm[n_s % 2]
                        n_s += 1
                        psv = ps[:].rearrange("p (a b c) -> p a b c", a=2, b=2)
                        for qi in range(2):
                            qb = 2 * qp + qi
                            q0 = 128 * dil * qb + r
                            qsl = qTh[b][:, q0:q0 + 127 * dil + 1:dil]
                            for ab in range(2):
                                k0 = 1024 - 64 * dil + 128 * dil * (qb + ab) + r
                                ksl = kTp[b][:, k0:k0 + 127 * dil + 1:dil]
                                O.mm(psv[:, qi, ab, :], ksl, qsl, True, True, [kTp[b], qTh[b]], [ps])
                        O.act(pt[:].rearrange("p a b c -> p (a b c)"), ps[:], AF.Exp, [ps], [pt], scale=0.125)
                        O.tt("dve", pm[:], pt[:], mask[:].unsqueeze(1).to_broadcast([128, 2, 2, 128]), ALU.mult,
                             [pt, mask], [pm])
                        po = PSo[n_o % 3]
                        n_o += 1
                        for qi in range(2):
                            qb = 2 * qp + qi
                            for ab in range(2):
                                O.mm(po[:, qi, :], pm[:, qi, ab, :], v[:, qb + ab, r, :], ab == 0, ab == 1, [pm, v], [po])
                        O.copy("act" if n_o % 2 else "dve", osv[:, 2 * qp:2 * qp + 2, r, :], po[:], [po], [(osb, (r, qp))])
                if dil == 1:
                    S.dma("sp", C.o_d[pi][:, h, :].rearrange("(q i) e -> i q e", i=128), osb[:], r=[osb],
                          w=[(C.o_d[pi], h)])
                else:
                    for q in range(nqb):
                        S.dma("sp", C.o_d[pi][q * 128 * dil:(q + 1) * 128 * dil, h, :].rearrange("(i r) e -> i r e", r=dil),
                              osv[:, q, :, :], r=[osb], w=[(C.o_d[pi], (h, q))])
        S.barrier()
        S.flush()


def phase4(C, l):
    nc, S, O = C.nc, C.S, C.O
    tri, trif = C.tri, C.trif
    with ExitStack() as st:
        def sb(name, shape, dt):
            return st.enter_context(_sbt(nc, name, list(shape), dt))
        H = [sb(f"p4H{d}", [128, 2048], F32) for d in range(2)]
        hbf = [[sb(f"p4hbf{d}_{i}", [128, 2048], BF16) for i in range(2)] for d in range(2)]
        xs_t = [sb(f"p4xs{i}", [128, 2048], BF16) for i in range(4)]
        Bt_t = [sb(f"p4Bt{i}", [128, 512], BF16) for i in range(4)]
        ew = [sb(f"p4ew{i}", [128, 2, 32], F32) for i in range(2)]
        dtw = [sb(f"p4dtw{i}", [128, 32], F32) for i in range(2)]
        xw = [sb(f"p4xw{i}", [128, 2048], BF16) for i in range(2)]
        tmpH = sb("p4tmpH", [128, 2048], F32)
        PSw = [st.enter_context(_pst(nc, f"p4psw{i}", [128, 2, 32], F32)) for i in range(2)]
        PSs = [st.enter_context(_pst(nc, f"p4pss{i}", [128, 512], F32)) for i in range(4)]
        for d in range(2):
            O.memset("dve", H[d][:], 0.0, [H[d]])
            O.memset("pool", hbf[d][0][:], 0.0, [hbf[d][0]])
        n = 0
        for i in range(NT_):
            for d in range(2):
                c = i if d == 0 else NT_ - 1 - i
                xt = xs_t[n % 4]
                bt = Bt_t[n % 4]
                e_ = ew[n % 2]
                dw = dtw[n % 2]
                xw_ = xw[n % 2]
                pw = PSw[n % 2]
                n += 1
                S.dma("sp", xt[:], C.xs_d[c * 128:(c + 1) * 128, :], r=[C.xs_d], w=[xt])
                S.dma("sp", bt[:], C.Bt_d[c * 128:(c + 1) * 128, :], r=[C.Bt_d], w=[bt])
                la_c = C.la_all[:, c, d * 32:(d + 1) * 32]
                dt_c = C.dt_all[:, c, d * 32:(d + 1) * 32]
                O.mm(pw[:, 0, :], trif[:, 1 if d == 0 else 0, :], la_c, True, True, [trif, C.la_all], [pw])
                O.mm(pw[:, 1, :], trif[:, 2, :], la_c, True, True, [trif, C.la_all], [pw])
                O.act(e_[:], pw[:], AF.Exp, [pw], [e_])
                O.tt("dve", dw[:], dt_c, e_[:, 0, :], ALU.mult, [C.dt_all, e_], [dw])
                O.tt("pool", xw_[:].rearrange("p (h d) -> p h d", d=64), xt[:].rearrange("p (h d) -> p h d", d=64),
                     dw[:].unsqueeze(2).to_broadcast([128, 32, 64]), ALU.mult, [xt, dw], [xw_])
                for g in range(4):
                    O.mm(PSs[g][:], bt[:, g * 128:(g + 1) * 128], xw_[:, g * 512:(g + 1) * 512], True, True, [bt, xw_], [PSs[g]])
                hb = hbf[d][i % 2]
                S.dma("sp", C.hin_d[d][c], hb[:], r=[hb], w=[(C.hin_d[d], c)])
                O.tt("dve", tmpH[:].rearrange("p (h d) -> p h d", d=64), H[d][:].rearrange("p (h d) -> p h d", d=64),
                     e_[:, 1, :].unsqueeze(2).to_broadcast([128, 32, 64]), ALU.mult, [H[d], e_], [tmpH])
                for g in range(4):
                    O.tt("dve", H[d][:, g * 512:(g + 1) * 512], tmpH[:, g * 512:(g + 1) * 512], PSs[g][:], ALU.add,
                         [tmpH, PSs[g]], [(H[d], g)])
                O.copy("act", hbf[d][(i + 1) % 2][:], H[d][:], [H[d]], [hbf[d][(i + 1) % 2]])
        S.barrier()
        S.flush()
    with ExitStack() as st:
        def sb(name, shape, dt):
            return st.enter_context(_sbt(nc, name, list(shape), dt))
        NB = 2
        xs_t = [sb(f"p4xs{i}", [128, 2048], BF16) for i in range(NB)]
        BT_t = [sb(f"p4BT{i}", [128, 4, 128], BF16) for i in range(NB)]
        CT_t = [sb(f"p4CT{i}", [128, 4, 128], BF16) for i in range(NB)]
        sz_t = [sb(f"p4sz{i}", [128, 2048], BF16) for i in range(NB)]
        hh_t = [[sb(f"p4hh{d}_{i}", [128, 2048], BF16) for i in range(NB)] for d in range(2)]
        cbm = [sb(f"p4cbm{d}", [128, 4, 128], BF16) for d in range(2)]
        rseg = [sb(f"p4rseg{d}", [128, 32, 128], BF16) for d in range(2)]
        xd = [sb(f"p4xd{d}", [128, 2048], BF16) for d in range(2)]
        ecum = [sb(f"p4ecum{d}", [128, 32], F32) for d in range(2)]
        eseg = [sb(f"p4eseg{i}", [128, 4, 128], BF16) for i in range(2)]
        MT = [sb(f"p4MT{i}", [128, 4, 128], BF16) for i in range(2)]
        tt_ = [sb(f"p4t{d}", [128, 2048], F32) for d in range(2)]
        yy = sb("p4y", [128, 2048], F32)
        ynf = sb("p4ynf", [128, 2048], BF16)
        ssg = sb("p4ssg", [128, 4], F32)
        sqj = sb("p4sqj", [128, 512], BF16)
        ynT_st = [sb(f"p4ynT{i}", [128, 16, 128], BF16) for i in range(2)]
        nwb = sb("p4nwb", [128, 2048], F32)
        Dbc = sb("p4Dbc", [128, 32], F32)
        PScb = st.enter_context(_pst(nc, "p4pscb", [128, 512], F32))
        PSy = [st.enter_context(_pst(nc, f"p4psy{i}", [128, 512], F32)) for i in range(4)]
        PSseg = st.enter_context(_pst(nc, "p4psseg", [128, 512], F32))
        PSo = st.enter_context(_pst(nc, "p4pso", [128, 512], F32))
        PSt = st.enter_context(_pst(nc, "p4pst", [128, 512], F32))
        S.dma("sp", nwb[:], C.snw[l].partition_broadcast(128), r=[], w=[nwb])
        S.dma("sp", Dbc[:], C.sdd[l].partition_broadcast(128), r=[], w=[Dbc])

        def loads(c):
            b = c % NB
            S.dma("sp", xs_t[b][:], C.xs_d[c * 128:(c + 1) * 128, :], r=[C.xs_d], w=[xs_t[b]])
            S.dma("sp", BT_t[b][:], C.BT_d[:, c * 128:(c + 1) * 128].rearrange("(g n) t -> n g t", n=128), r=[C.BT_d], w=[BT_t[b]])
            S.dma("sp", CT_t[b][:], C.CT_d[:, c * 128:(c + 1) * 128].rearrange("(g n) t -> n g t", n=128), r=[C.CT_d], w=[CT_t[b]])
            S.dma("sp", sz_t[b][:], C.sz_d[c * 128:(c + 1) * 128, :], r=[C.sz_d], w=[sz_t[b]])
            for d in range(2):
                S.dma("sp", hh_t[d][b][:], C.hin_d[d][c], r=[C.hin_d[d]], w=[hh_t[d][b]])

        loads(0)
        nm = 0
        for c in range(NT_):
            b = c % NB
            if c + 1 < NT_:
                loads(c + 1)
            xt, BT, CT, sz = xs_t[b], BT_t[b], CT_t[b], sz_t[b]
            pcb = PScb[:].rearrange("p (g t) -> p g t", g=4)
            for g in range(4):
                O.mm(pcb[:, g, :], BT[:, g, :], CT[:, g, :], True, True, [BT, CT], [PScb])
            O.tt("dve", cbm[0][:], pcb, tri[:, 0, :].unsqueeze(1).to_broadcast([128, 4, 128]), ALU.mult, [PScb, tri], [cbm[0]])
            O.tt("dve", cbm[1][:], pcb, tri[:, 1, :].unsqueeze(1).to_broadcast([128, 4, 128]), ALU.mult, [PScb, tri], [cbm[1]])
            pse = PSt[:, 0:64].rearrange("p (d h) -> p d h", d=2)
            for d in range(2):
                la_c = C.la_all[:, c, d * 32:(d + 1) * 32]
                O.mm(pse[:, d, :], trif[:, 3 + d, :], la_c, True, True, [trif, C.la_all], [PSt])
            for d in range(2):
                O.act(ecum[d][:], pse[:, d, :], AF.Exp, [PSt], [ecum[d]])
            for d in range(2):
                la_c = C.la_all[:, c, d * 32:(d + 1) * 32]
                dt_c = C.dt_all[:, c, d * 32:(d + 1) * 32]
                O.tt("pool", rseg[d][:], la_c.unsqueeze(2).to_broadcast([128, 32, 128]),
                     tri[:, d, :].unsqueeze(1).to_broadcast([128, 32, 128]), ALU.mult, [C.la_all, tri], [rseg[d]])
                O.tt("pool", xd[d][:].rearrange("p (h e) -> p h e", e=64), xt[:].rearrange("p (h e) -> p h e", e=64),
                     dt_c.unsqueeze(2).to_broadcast([128, 32, 64]), ALU.mult, [xt, C.dt_all], [xd[d]])
                A_d = tri[:, 3 - d, :]
                for hq in range(8):
                    es = eseg[nm % 2]
                    mt = MT[nm % 2]
                    nm += 1
                    O.mm(PSseg[:], A_d, rseg[d][:, hq * 4:(hq + 1) * 4, :].rearrange("p h t -> p (h t)"), True, True,
                         [tri, rseg[d]], [PSseg])
                    O.act(es[:].rearrange("p h t -> p (h t)"), PSseg[:], AF.Exp, [PSseg], [es])
                    O.tt("dve" if hq % 2 else "pool", mt[:], es[:],
                         cbm[d][:, hq // 2, :].unsqueeze(1).to_broadcast([128, 4, 128]), ALU.mult, [es, cbm[d]], [mt])
                    for hh in range(4):
                        h = hq * 4 + hh
                        O.mm(PSy[h // 8][:, (h % 8) * 64:(h % 8 + 1) * 64], mt[:, hh, :], xd[d][:, h * 64:(h + 1) * 64],
                             d == 0 and h % 8 == 0, d == 1 and h % 8 == 7, [mt, xd[d]], [PSy[h // 8]])
                hd = hh_t[d][b]
                for g in range(4):
                    O.mm(PSo[:], CT[:, g, :], hd[:, g * 512:(g + 1) * 512], True, True, [CT, hd], [PSo])
                    O.tt("dve", tt_[d][:, g * 512:(g + 1) * 512].rearrange("p (h e) -> p h e", e=64),
                         PSo[:].rearrange("p (h e) -> p h e", e=64),
                         ecum[d][:, g * 8:(g + 1) * 8].unsqueeze(2).to_broadcast([128, 8, 64]), ALU.mult,
                         [PSo, ecum[d]], [(tt_[d], g)])
            O.tt("pool", tt_[0][:], tt_[0][:], tt_[1][:], ALU.add, [tt_[0], tt_[1]], [tt_[0]])
            O.tt("pool", tt_[1][:].rearrange("p (h e) -> p h e", e=64), xt[:].rearrange("p (h e) -> p h e", e=64),
                 Dbc[:].unsqueeze(2).to_broadcast([128, 32, 64]), ALU.mult, [xt, Dbc, tt_[1]], [tt_[1]])
            O.tt("pool", tt_[0][:], tt_[0][:], tt_[1][:], ALU.add, [tt_[0], tt_[1]], [tt_[0]])
            for g in range(4):
                O.tt("dve", yy[:, g * 512:(g + 1) * 512], PSy[g][:], tt_[0][:, g * 512:(g + 1) * 512], ALU.add,
                     [PSy[g], tt_[0]], [(yy, g)])
            O.tt("dve", yy[:], yy[:], sz[:], ALU.mult, [yy, sz], [yy])
            for g in range(4):
                O.act(sqj[:], yy[:, g * 512:(g + 1) * 512], AF.Square, [yy], [sqj, (ssg, g)], accum_out=ssg[:, g:g + 1])
            O.act(ssg[:], ssg[:], AF.Sqrt, [ssg], [ssg], bias=1e-6, scale=1.0 / 512)
            O.recip(ssg[:], ssg[:], [ssg], [ssg])
            for g in range(4):
                O.stt("dve", ynf[:, g * 512:(g + 1) * 512], yy[:, g * 512:(g + 1) * 512], ssg[:, g:g + 1],
                      nwb[:, g * 512:(g + 1) * 512], ALU.mult, ALU.mult, [yy, ssg, nwb], [(ynf, g)])
            ptb = PSt[:, 0:256].bitcast(BF16).rearrange("p (a t) -> p a t", a=4)
            ys = ynT_st[c % 2]
            for k4 in range(4):
                for a in range(4):
                    kk = k4 * 4 + a
                    O.tr(ptb[:, a, :], ynf[:, kk * 128:(kk + 1) * 128], C.ident[:], [ynf, C.ident], [PSt])
                O.copy("act", ys[:, k4 * 4:(k4 + 1) * 4, :], ptb, [PSt], [(ys, k4)])
            S.dma("sp", C.ynT_d[:, c * 128:(c + 1) * 128].rearrange("(k p) t -> p k t", p=128), ys[:], r=[ys],
                  w=[(C.ynT_d, c)])
        S.barrier()
        S.flush()


def phase5(C, l, x_src, x_dst):
    nc, S, O = C.nc, C.S, C.O
    with ExitStack() as st:
        def sb(name, shape, dt):
            return st.enter_context(_sbt(nc, name, list(shape), dt))
        wa = sb("p5wa", [128, 8, D_], BF16)
        wb_ = sb("p5wb", [128, 8, D_], BF16)
        wc = sb("p5wc", [128, 16, D_], BF16)
        wo = sb("p5wo", [128, 8, D_], BF16)
        yA_b = sb("p5yA", [128, 8, 512], BF16)
        yn_b = sb("p5yn", [128, 16, 512], BF16)
        oT_b = sb("p5oT", [128, 8, 512], BF16)
        g_b = sb("p5g", [128, 24, 512], BF16)
        mT = sb("p5mT", [128, 8, 512], BF16)
        ot = [sb(f"p5ot{p}", [128, 16, 65], F32) for p in range(3)]
        rden = sb("p5rden", [128, 16], F32)
        ob = sb("p5ob", [128, 16, 64], BF16)
        xt = sb("p5xt", [128, D_], F32)
        xo = sb("p5xo", [128, D_], F32)
        t1 = sb("p5t1", [128, 512], F32)
        t2 = sb("p5t2", [128, 512], F32)
        PB = [st.enter_context(_pst(nc, f"p5ps{i}", [128, 512], F32)) for i in range(6)]
        PT = st.enter_context(_pst(nc, "p5pt", [128, 8, 128], BF16))
        S.dma("sp", wa[:], C.wb_a[l].rearrange("(k p) n -> p k n", p=128), r=[C.wb_a[l]], w=[wa])
        S.dma("sp", wb_[:], C.wb_b[l].rearrange("(k p) n -> p k n", p=128), r=[C.wb_b[l]], w=[wb_])
        S.dma("sp", wc[:], C.wb_c[l].rearrange("(k p) n -> p k n", p=128), r=[C.wb_c[l]], w=[wc])
        S.dma("sp", wo[:], C.wb_o[l].rearrange("(k p) n -> p k n", p=128), r=[C.wb_o[l]], w=[wo])
        nps = 0
        for tb in range(8):
            tsl = slice(tb * 512, (tb + 1) * 512)
            S.dma("sp", yA_b[:], C.yAT[:, tsl].rearrange("(k p) t -> p k t", p=128), r=[C.yAT], w=[yA_b])
            S.dma("sp", yn_b[:], C.ynT_d[:, tsl].rearrange("(k p) t -> p k t", p=128), r=[C.ynT_d], w=[yn_b])
            S.dma("sp", g_b[:], C.gT[:, tsl].rearrange("(k p) t -> p k t", p=128), r=[C.gT], w=[g_b])
            for tt in range(4):
                t = tb * 4 + tt
                for p in range(3):
                    S.dma("sp", ot[p][:], C.o_d[p][t * 128:(t + 1) * 128], r=[C.o_d[p]], w=[ot[p]])
                O.tt("dve", ot[0][:], ot[0][:], ot[1][:], ALU.add, [ot[0], ot[1]], [ot[0]])
                O.tt("dve", ot[0][:], ot[0][:], ot[2][:], ALU.add, [ot[0], ot[2]], [ot[0]])
                O.recip(rden[:].unsqueeze(2), ot[0][:, :, 64:65], [ot[0]], [rden])
                O.tt("dve", ob[:], ot[0][:, :, 0:64], rden[:].unsqueeze(2).to_broadcast([128, 16, 64]), ALU.mult,
                     [ot[0], rden], [ob])
                obf = ob[:].rearrange("p h d -> p (h d)")
                for k in range(8):
                    O.tr(PT[:, k, :], obf[:, k * 128:(k + 1) * 128], C.ident[:], [ob, C.ident], [PT])
                O.copy("act", oT_b[:, :, tt * 128:(tt + 1) * 128], PT[:], [PT], [(oT_b, tt)])
            for cc in range(8):
                pa, pb, pc = PB[nps % 6], PB[(nps + 1) % 6], PB[(nps + 2) % 6]
                nps += 3
                for k in range(8):
                    O.mm(pa[:], wa[:, k, cc * 128:(cc + 1) * 128], yA_b[:, k, :], k == 0, k == 7, [wa, yA_b], [pa])
                for k in range(8):
                    O.mm(pb[:], wb_[:, k, cc * 128:(cc + 1) * 128], oT_b[:, k, :], k == 0, k == 7, [wb_, oT_b], [pb])
                for k in range(16):
                    O.mm(pc[:], wc[:, k, cc * 128:(cc + 1) * 128], yn_b[:, k, :], k == 0, k == 15, [wc, yn_b], [pc])
                O.tt("dve", t1[:], pa[:], g_b[:, cc, :], ALU.mult, [pa, g_b], [t1])
                O.tt("dve", t2[:], pb[:], g_b[:, 8 + cc, :], ALU.mult, [pb, g_b], [t2])
                O.tt("pool", t1[:], t1[:], t2[:], ALU.add, [t1, t2], [t1])
                O.tt("dve", t2[:], pc[:], g_b[:, 16 + cc, :], ALU.mult, [pc, g_b], [t2])
                O.tt("pool", mT[:, cc, :], t1[:], t2[:], ALU.add, [t1, t2], [(mT, cc)])
            for tt in range(4):
                t = tb * 4 + tt
                S.dma("sp", xt[:], x_src[t * 128:(t + 1) * 128, :], r=[(x_src, t)], w=[xt])
                for hf in range(2):
                    ps = PB[nps % 6]
                    nps += 1
                    for k in range(8):
                        O.mm(ps[:], mT[:, k, tt * 128:(tt + 1) * 128], wo[:, k, hf * 512:(hf + 1) * 512], k == 0, k == 7,
                             [mT, wo], [ps])
                    O.tt("dve", xo[:, hf * 512:(hf + 1) * 512], ps[:], xt[:, hf * 512:(hf + 1) * 512], ALU.add,
                         [ps, xt], [(xo, hf)])
                S.dma("sp", C.xmid[t * 128:(t + 1) * 128, :], xo[:], r=[xo], w=[(C.xmid, t)])
        S.barrier()
        S.flush()
    if "xmid" in C.dbg and C.stop == (l, 5):
        return
    with ExitStack() as st:
        def sb(name, shape, dt):
            return st.enter_context(_sbt(nc, name, list(shape), dt))
        w2 = sb("p5w2", [128, 32, D_], BF16)
        w1b = [sb(f"p5w1_{i}", [128, 8, 512], BF16) for i in range(3)]
        xts = [sb(f"p5x{i}", [128, D_], F32) for i in range(4)]
        hT = sb("p5hT", [128, 8, 512], BF16)
        h1T = sb("p5h1T", [128, 32, 512], BF16)
        ub = [sb(f"p5ub{i}", [128, D_], BF16) for i in range(2)]
        sq = sb("p5sq", [128, D_], BF16)
        ss = [sb(f"p5ss{i}", [128, 1], F32) for i in range(2)]
        rr = [sb(f"p5r{i}", [128, 512], F32) for i in range(2)]
        xo = [sb(f"p5xo{i}", [128, D_], F32) for i in range(2)]
        nwt = sb("p5nw", [128, D_], F32)
        fnw = sb("p5fnw", [128, D_], F32)
        PB = [st.enter_context(_pst(nc, f"p5bps{i}", [128, 512], F32)) for i in range(6)]
        PT = [st.enter_context(_pst(nc, f"p5bpt{i}", [128, 8, 128], BF16)) for i in range(2)]
        S.dma("sp", w2[:], C.wb_2[l].rearrange("(k p) n -> p k n", p=128), r=[C.wb_2[l]], w=[w2])
        S.dma("sp", nwt[:], C.mlpw[l:l + 1, :].partition_broadcast(128), r=[], w=[nwt])
        S.dma("sp", fnw[:], C.finw.partition_broadcast(128), r=[], w=[fnw])
        w1src = C.wb_1[l].rearrange("(k p) n -> p k n", p=128)
        nps = 0
        nw1 = 0
        for tb in range(8):
            for tt in range(4):
                t = tb * 4 + tt
                S.dma("sp", xts[tt][:], C.xmid[t * 128:(t + 1) * 128, :], r=[(C.xmid, t)], w=[xts[tt]])
                rms_tile(C, xts[tt], ss[tt % 2], sq, nwt, ub[tt % 2], "p5")
                for k in range(8):
                    O.tr(PT[tt % 2][:, k, :], ub[tt % 2][:, k * 128:(k + 1) * 128], C.ident[:], [ub[tt % 2], C.ident], [PT[tt % 2]])
                O.copy("act", hT[:, :, tt * 128:(tt + 1) * 128], PT[tt % 2][:], [PT[tt % 2]], [(hT, tt)])
            for f4 in range(8):
                w1t = w1b[nw1 % 3]
                nw1 += 1
                S.dma("sp", w1t[:], w1src[:, :, f4 * 512:(f4 + 1) * 512], r=[C.wb_1[l]], w=[w1t])
                for fj in range(4):
                    fc = f4 * 4 + fj
                    ps = PB[nps % 6]
                    r_ = rr[nps % 2]
                    nps += 1
                    for k in range(8):
                        O.mm(ps[:], w1t[:, k, fj * 128:(fj + 1) * 128], hT[:, k, :], k == 0, k == 7, [w1t, hT], [ps])
                    O.act(r_[:], ps[:], AF.Relu, [ps], [r_])
                    O.tt("pool" if fc % 2 else "dve", h1T[:, fc, :], r_[:], r_[:], ALU.mult, [r_], [(h1T, fc)])
            for tt in range(4):
                t = tb * 4 + tt
                xo_ = xo[tt % 2]
                for hf in range(2):
                    ps = PB[nps % 6]
                    nps += 1
                    for fc in range(32):
                        O.mm(ps[:], h1T[:, fc, tt * 128:(tt + 1) * 128], w2[:, fc, hf * 512:(hf + 1) * 512], fc == 0, fc == 31,
                             [h1T, w2], [ps])
                    O.tt("dve", xo_[:, hf * 512:(hf + 1) * 512], ps[:], xts[tt][:, hf * 512:(hf + 1) * 512], ALU.add,
                         [ps, xts[tt]], [(xo_, hf)])
                if x_dst is not None:
                    S.dma("sp", x_dst[t * 128:(t + 1) * 128, :], xo_[:], r=[xo_], w=[(x_dst, t)])
                else:
                    s_ = ss[tt % 2]
                    O.act(sq[:], xo_[:], AF.Square, [xo_], [sq, s_], accum_out=s_[:])
                    O.act(s_[:], s_[:], AF.Sqrt, [s_], [s_], bias=1e-6, scale=1.0 / D_)
                    O.recip(s_[:], s_[:], [s_], [s_])
                    O.stt("dve", xo_[:], xo_[:], s_[:], fnw[:], ALU.mult, ALU.mult, [xo_, s_, fnw], [xo_])
                    S.dma("sp", C.y_out[t * 128:(t + 1) * 128, :], xo_[:], r=[xo_], w=[(C.y_out, t)])
        S.barrier()
        S.flush()


def host_consts():
    bf = ml_dtypes.bfloat16
    p = np.arange(128)[:, None]
    f = np.arange(128)[None, :]
    tri = np.stack([(p <= f), (p >= f), (p < f), (p > f)], axis=1).astype(np.float32)
    trif = np.stack([(p < f), (p > f), np.ones((128, 128), bool), (p <= f), (p >= f)], axis=1).astype(np.float32)
    invf = (500000.0 ** (-np.arange(0, 16, 2, dtype=np.float32) / 16.0)).astype(np.float32)[None, :]
    return {"c_ident": np.eye(128, dtype=np.float32).astype(bf), "c_tri": tri.astype(bf), "c_trif": trif, "c_invf": invf}


def prep_inputs(inp):
    f32 = np.float32
    sh = dict(host_consts())
    for k in ("mix_norm_w", "mlp_norm_w", "w_in", "w_a_out", "w_b_out", "w_c_out", "w_o", "w_ff1", "w_ff2"):
        sh[k] = np.ascontiguousarray(inp[k], dtype=f32)
    sh["final_norm_w"] = np.ascontiguousarray(inp["final_norm_w"], dtype=f32).reshape(1, D_)
    sh["conv_a_wT"] = np.ascontiguousarray(inp["conv_a_w"].reshape(L_, 3, 8, 128).transpose(0, 3, 2, 1), dtype=f32)
    sh["ssd_conv_wT"] = np.ascontiguousarray(inp["ssd_conv_w"].reshape(L_, 5, 24, 128).transpose(0, 3, 2, 1), dtype=f32)
    sh["ssd_conv_bT"] = np.ascontiguousarray(inp["ssd_conv_b"].reshape(L_, 24, 128).transpose(0, 2, 1), dtype=f32)
    sh["ssd_conv_bR"] = np.ascontiguousarray(inp["ssd_conv_b"].reshape(L_, 1, 3072), dtype=f32)
    sh["ssd_a_log"] = np.ascontiguousarray(inp["ssd_a_log"].reshape(L_, 1, 64), dtype=f32)
    sh["ssd_dt_bias"] = np.ascontiguousarray(inp["ssd_dt_bias"].reshape(L_, 1, 64), dtype=f32)
    sh["ssd_d"] = np.ascontiguousarray(inp["ssd_d"].reshape(L_, 1, 32), dtype=f32)
    sh["ssd_norm_w"] = np.ascontiguousarray(inp["ssd_norm_w"].reshape(L_, 1, 2048), dtype=f32)
    per = []
    for b in range(inp["x"].shape[0]):
        d = dict(sh)
        d["x"] = np.ascontiguousarray(inp["x"][b], dtype=f32)
        d["pos"] = np.ascontiguousarray(inp["positions"][b].reshape(NT_, 128).T, dtype=np.int32)
        per.append(d)
    return per


_NC_CACHE = {}


def kernel(**inputs):
    per = prep_inputs(inputs)
    if "nc" not in _NC_CACHE:
        _NC_CACHE["nc"] = build()
    nc = _NC_CACHE["nc"]
    res = run_bass_kernel_spmd(nc, per, core_ids=list(range(len(per))))
    return np.stack([np.asarray(r["y"], dtype=np.float32) for r in res.results], axis=0)
```

```python
import numpy as np
import ml_dtypes
import concourse.bass as bass
import concourse.mybir as mybir
from concourse.bass_utils import run_bass_kernel_spmd
from contextlib import ExitStack
from types import SimpleNamespace

F32 = mybir.dt.float32
BF16 = mybir.dt.bfloat16
I32 = mybir.dt.int32
AF = mybir.ActivationFunctionType
ALU = mybir.AluOpType
AX = mybir.AxisListType

ENG = ("pe", "act", "dve", "pool", "sp")
DMAQ = ("sp", "act", "pool")


class _Buf:
    __slots__ = ("w", "r")

    def __init__(self):
        self.w = None
        self.r = {}


class _TBuf:
    __slots__ = ("whole", "subs")

    def __init__(self):
        self.whole = _Buf()
        self.subs = {}


class Sched:
    NRING = 8
    SAME_ENGINE_SYNC = ("act", "dve", "pool")

    def __init__(self, nc):
        self.nc = nc
        self.sems = []
        self.esem = {}
        for e in ENG:
            self.esem[e] = self._new_sem("s_" + e)
        self.ecnt = {e: 0 for e in ENG}
        self.ring = {q: [self._new_sem(f"d_{q}{i}") for i in range(self.NRING)] for q in DMAQ}
        self.dman = {q: 0 for q in DMAQ}
        self.known = {e: {} for e in ENG}
        self.streams = {e: [] for e in ENG}
        self.tb = {}
        self.n_wait = 0
        self.n_ins = 0

    def _new_sem(self, name):
        h = self.nc.alloc_semaphore(name=name)
        self.sems.append(h)
        return len(self.sems) - 1

    def _spec(self, a):
        if isinstance(a, tuple):
            t, key = a
        else:
            t, key = a, None
        name = t if isinstance(t, str) else t.name
        tb = self.tb.get(name)
        if tb is None:
            tb = self.tb[name] = _TBuf()
        return tb, key

    def _deps(self, eng, reads, writes):
        need = {}

        def add(ev):
            if ev is not None and need.get(ev[0], 0) < ev[1]:
                need[ev[0]] = ev[1]

        def addr(b):
            for s, v in b.r.items():
                if need.get(s, 0) < v:
                    need[s] = v

        for a in reads:
            tb, key = self._spec(a)
            add(tb.whole.w)
            if key is None:
                for sb in tb.subs.values():
                    add(sb.w)
            else:
                sb = tb.subs.get(key)
                if sb is not None:
                    add(sb.w)
        for a in writes:
            tb, key = self._spec(a)
            add(tb.whole.w)
            addr(tb.whole)
            if key is None:
                for sb in tb.subs.values():
                    add(sb.w)
                    addr(sb)
            else:
                sb = tb.subs.get(key)
                if sb is not None:
                    add(sb.w)
                    addr(sb)
        kn = self.known[eng]
        own = self.esem[eng]
        waits = []
        for s, v in need.items():
            if s == own and eng not in self.SAME_ENGINE_SYNC:
                continue
            if kn.get(s, 0) < v:
                kn[s] = v
                waits.append((s, v))
        return waits

    def _record(self, ev, reads, writes):
        s, v = ev
        for a in reads:
            tb, key = self._spec(a)
            if key is None:
                b = tb.whole
            else:
                b = tb.subs.get(key)
                if b is None:
                    b = tb.subs[key] = _Buf()
            if b.r.get(s, 0) < v:
                b.r[s] = v
        for a in writes:
            tb, key = self._spec(a)
            if key is None:
                tb.whole.w = ev
                tb.whole.r = {}
                tb.subs = {}
            else:
                b = tb.subs.get(key)
                if b is None:
                    b = tb.subs[key] = _Buf()
                b.w = ev
                b.r = {}

    def op(self, eng, emit, r=(), w=()):
        waits = self._deps(eng, r, w)
        self.ecnt[eng] += 1
        ev = (self.esem[eng], self.ecnt[eng])
        self._record(ev, r, w)
        self.streams[eng].append((waits, emit, ev[0], 1))
        self.n_wait += len(waits)
        self.n_ins += 1

    def dma(self, q, out, in_, r=(), w=(), **kw):
        waits = self._deps(q, r, w)
        i = self.dman[q]
        self.dman[q] += 1
        s = self.ring[q][i % self.NRING]
        rnd = i // self.NRING
        if rnd > 0:
            kn = self.known[q]
            if kn.get(s, 0) < 16 * rnd:
                kn[s] = 16 * rnd
                waits.append((s, 16 * rnd))
        ev = (s, 16 * (rnd + 1))
        self._record(ev, r, w)
        self.streams[q].append((waits, lambda e: e.dma_start(out=out, in_=in_, **kw), s, 16))
        self.n_wait += len(waits)
        self.n_ins += 1

    def barrier(self, skip_q=("pool",)):
        evs = []
        for e in ENG:
            if self.ecnt[e] > 0:
                evs.append((self.esem[e], self.ecnt[e]))
        for q in DMAQ:
            if q in skip_q:
                continue
            n = self.dman[q]
            for j in range(min(n, self.NRING)):
                last = ((n - 1 - j) // self.NRING) * self.NRING + j
                evs.append((self.ring[q][j], 16 * (last // self.NRING + 1)))
        for e in ENG:
            kn = self.known[e]
            waits = []
            for s, v in evs:
                if kn.get(s, 0) < v:
                    kn[s] = v
                    waits.append((s, v))
            if waits:
                self.streams[e].append((waits, None, None, 0))
                self.n_wait += len(waits)

    def flush(self):
        nc = self.nc
        sems = self.sems

        def run(name):
            items = self.streams[name]
            self.streams[name] = []

            def f(e):
                for waits, emit, s, inc in items:
                    for ws, wv in waits:
                        e.wait_ge(sems[ws], wv)
                    if emit is not None:
                        ins = emit(e)
                        ins.then_inc(sems[s], inc)

            return f

        with nc.Block() as block:
            block.tensor(run("pe"))
            block.scalar(run("act"))
            block.vector(run("dve"))
            block.gpsimd(run("pool"))
            block.sync(run("sp"))


S_ = 4096
D_ = 1024
NT_ = 32
L_ = 2
DIN_ = 14400
PI = 3.141592653589793


_UID = [0]


def _sbt(nc, name, shape, dt):
    _UID[0] += 1
    return nc.sbuf_tensor(f"{name}_u{_UID[0]}", shape, dt)


def _pst(nc, name, shape, dt):
    _UID[0] += 1
    return nc.psum_tensor(f"{name}_u{_UID[0]}", shape, dt)


class Ops:
    def __init__(self, S):
        self.S = S

    def mm(self, out, lhsT, rhs, start, stop, r, w):
        self.S.op("pe", lambda e: e.matmul(out, lhsT=lhsT, rhs=rhs, start=start, stop=stop), r, w)

    def tr(self, out, in_, ident, r, w):
        self.S.op("pe", lambda e: e.transpose(out=out, in_=in_, identity=ident), r, w)

    def act(self, out, in_, func, r, w, **kw):
        self.S.op("act", lambda e: e.activation(out=out, in_=in_, func=func, **kw), r, w)

    def tt(self, eng, out, in0, in1, op, r, w):
        self.S.op(eng, lambda e: e.tensor_tensor(out=out, in0=in0, in1=in1, op=op), r, w)

    def ts(self, eng, out, in0, s1, s2, op0, op1, r, w):
        if s2 is None:
            self.S.op(eng, lambda e: e.tensor_scalar(out=out, in0=in0, scalar1=s1, scalar2=None, op0=op0), r, w)
        else:
            self.S.op(eng, lambda e: e.tensor_scalar(out=out, in0=in0, scalar1=s1, scalar2=s2, op0=op0, op1=op1), r, w)

    def stt(self, eng, out, in0, scalar, in1, op0, op1, r, w):
        self.S.op(eng, lambda e: e.scalar_tensor_tensor(out=out, in0=in0, scalar=scalar, in1=in1, op0=op0, op1=op1), r, w)

    def copy(self, eng, out, in_, r, w):
        if eng == "act":
            self.S.op(eng, lambda e: e.copy(out=out, in_=in_), r, w)
        else:
            self.S.op(eng, lambda e: e.tensor_copy(out=out, in_=in_), r, w)

    def memset(self, eng, ap, val, w):
        self.S.op(eng, lambda e: e.memset(ap, val), (), w)

    def recip(self, out, in_, r, w):
        self.S.op("dve", lambda e: e.reciprocal(out=out, in_=in_), r, w)


def build(dbg=(), stop=None, nlayers=L_):
    nc = bass.Bass("TRN2", target_bir_lowering=False)

    def din(name, shape, dt=F32):
        return nc.dram_tensor(name, list(shape), dt, kind="ExternalInput").ap()

    def dscr(name, shape, dt):
        kind = "ExternalOutput" if name in dbg else "Internal"
        return nc.dram_tensor(name, list(shape), dt, kind=kind).ap()

    x_in = din("x", [S_, D_])
    pos_in = din("pos", [128, NT_], I32)
    mixw = din("mix_norm_w", [L_, D_])
    mlpw = din("mlp_norm_w", [L_, D_])
    finw = din("final_norm_w", [1, D_])
    w_in = din("w_in", [L_, D_, DIN_])
    w_a = din("w_a_out", [L_, D_, D_])
    w_b = din("w_b_out", [L_, D_, D_])
    w_c = din("w_c_out", [L_, 2 * D_, D_])
    w_o = din("w_o", [L_, D_, D_])
    w_1 = din("w_ff1", [L_, D_, 4 * D_])
    w_2 = din("w_ff2", [L_, 4 * D_, D_])
    cawT = din("conv_a_wT", [L_, 128, 8, 3])
    scwT = din("ssd_conv_wT", [L_, 128, 24, 5])
    scbT = din("ssd_conv_bT", [L_, 128, 24])
    scbR = din("ssd_conv_bR", [L_, 1, 3072])
    alog = din("ssd_a_log", [L_, 1, 64])
    dtb = din("ssd_dt_bias", [L_, 1, 64])
    sdd = din("ssd_d", [L_, 1, 32])
    snw = din("ssd_norm_w", [L_, 1, 2048])
    c_ident = din("c_ident", [128, 128], BF16)
    c_tri = din("c_tri", [128, 4, 128], BF16)
    c_trif = din("c_trif", [128, 5, 128], F32)
    c_invf = din("c_invf", [1, 8], F32)
    y_out = nc.dram_tensor("y", [S_, D_], F32, kind="ExternalOutput").ap()

    wb_in = [dscr(f"wb_in{l}", [D_, DIN_], BF16) for l in range(L_)]
    wb_a = [dscr(f"wb_a{l}", [D_, D_], BF16) for l in range(L_)]
    wb_b = [dscr(f"wb_b{l}", [D_, D_], BF16) for l in range(L_)]
    wb_c = [dscr(f"wb_c{l}", [2 * D_, D_], BF16) for l in range(L_)]
    wb_o = [dscr(f"wb_o{l}", [D_, D_], BF16) for l in range(L_)]
    wb_1 = [dscr(f"wb_1{l}", [D_, 4 * D_], BF16) for l in range(L_)]
    wb_2 = [dscr(f"wb_2{l}", [4 * D_, D_], BF16) for l in range(L_)]
    yAT = dscr("yAT", [D_, S_], BF16)
    gT = dscr("gT", [3 * D_, S_], BF16)
    qT = dscr("qT", [D_, S_], BF16)
    kT = dscr("kT", [D_, S_], BF16)
    v_d = dscr("v_d", [S_, D_], BF16)
    sz_d = dscr("sz_d", [S_, 2 * D_], BF16)
    xs_d = dscr("xs_d", [S_, 2 * D_], BF16)
    Bt_d = dscr("Bt_d", [S_, 512], BF16)
    BT_d = dscr("BT_d", [512, S_], BF16)
    CT_d = dscr("CT_d", [512, S_], BF16)
    dt_dbg = dscr("dt_dbg", [128, NT_, 64], F32)
    o_d = [dscr(f"o_d{p}", [S_, 16, 65], F32) for p in range(3)]
    hin_d = [dscr(f"hin_d{d}", [NT_, 128, 2048], BF16) for d in range(2)]
    ynT_d = dscr("ynT_d", [2 * D_, S_], BF16)
    xmid = dscr("xmid", [S_, D_], F32)
    xl = [dscr(f"xl{l}", [S_, D_], F32) for l in range(L_ - 1)]

    S = Sched(nc)
    O = Ops(S)

    with ExitStack() as gst:
        def gsb(name, shape, dt):
            return gst.enter_context(_sbt(nc, name, list(shape), dt))

        ident = gsb("ident", [128, 128], BF16)
        tri = gsb("tri", [128, 4, 128], BF16)
        trif = gsb("trif", [128, 5, 128], F32)
        cosT = gsb("cosT", [128, NT_, 8], F32)
        sinT = gsb("sinT", [128, NT_, 8], F32)
        dt_all = gsb("dt_all", [128, NT_, 64], F32)
        la_all = gsb("la_all", [128, NT_, 64], F32)
        S.dma("sp", ident[:], c_ident, r=[], w=[ident])
        S.dma("sp", tri[:], c_tri, r=[], w=[tri])
        S.dma("sp", trif[:], c_trif, r=[], w=[trif])

        def cast_w(src, dst, rows_per):
            R = src.shape[0]
            for r0 in range(0, R, rows_per):
                S.dma("pool", dst[r0:r0 + rows_per, :], src[r0:r0 + rows_per, :], r=[], w=[(dst, r0)])

        for l in range(nlayers):
            cast_w(w_in[l], wb_in[l], 64)
            cast_w(w_a[l], wb_a[l], 512)
            cast_w(w_b[l], wb_b[l], 512)
            cast_w(w_c[l], wb_c[l], 512)
            cast_w(w_o[l], wb_o[l], 512)
            cast_w(w_1[l], wb_1[l], 128)
            cast_w(w_2[l], wb_2[l], 512)

        with ExitStack() as st:
            def sb(name, shape, dt):
                return st.enter_context(_sbt(nc, name, list(shape), dt))
            posi = sb("posi", [128, NT_], I32)
            posf = sb("posf", [128, NT_], F32)
            invf = sb("invf", [128, 8], F32)
            ang = sb("ang", [128, NT_, 8], F32)
            a1 = sb("a1", [128, NT_, 8], F32)
            S.dma("sp", posi[:], pos_in, r=[], w=[posi])
            S.dma("sp", invf[:], c_invf.partition_broadcast(128), r=[], w=[invf])
            O.copy("dve", posf[:], posi[:], [posi], [posf])
            O.tt("dve", ang[:], posf[:].unsqueeze(2).to_broadcast([128, NT_, 8]),
                 invf[:].unsqueeze(1).to_broadcast([128, NT_, 8]), ALU.mult, [posf, invf], [ang])
            ki = sb("ki", [128, NT_, 8], I32)
            kf = sb("kf", [128, NT_, 8], F32)
            mm_ = sb("mm_", [128, NT_, 8], F32)
            for shift, dstT in ((0.0, sinT), (0.5 * PI, cosT)):
                O.ts("dve", a1[:], ang[:], shift, 1.0 / (2 * PI), ALU.add, ALU.mult, [ang], [a1])
                O.copy("dve", ki[:], a1[:], [a1], [ki])
                O.copy("dve", kf[:], ki[:], [ki], [kf])
                O.ts("dve", a1[:], ang[:], shift, None, ALU.add, None, [ang], [a1])
                O.stt("dve", a1[:], kf[:], -2 * PI, a1[:], ALU.mult, ALU.add, [kf, a1], [a1])
                O.ts("dve", mm_[:], a1[:], PI, 2 * PI, ALU.is_ge, ALU.mult, [a1], [mm_])
                O.tt("dve", a1[:], a1[:], mm_[:], ALU.subtract, [a1, mm_], [a1])
                O.ts("dve", mm_[:], a1[:], -PI, 2 * PI, ALU.is_lt, ALU.mult, [a1], [mm_])
                O.tt("dve", a1[:], a1[:], mm_[:], ALU.add, [a1, mm_], [a1])
                O.ts("dve", a1[:], a1[:], -PI, PI, ALU.max, ALU.min, [a1], [a1])
                O.act(dstT[:], a1[:], AF.Sin, [a1], [dstT])
            S.barrier()
            S.flush()

        for l in range(nlayers):
            x_src = x_in if l == 0 else xl[l - 1]
            x_dst = xl[l] if l < L_ - 1 else None
            C = SimpleNamespace(**locals())
            layer(C, l, x_src, x_dst, stop)
            if stop is not None and stop[0] == l:
                break
        S.barrier(skip_q=())
        S.flush()
    return nc


def layer(C, l, x_src, x_dst, stop):
    nc, S, O = C.nc, C.S, C.O
    with ExitStack() as lst:
        uT = lst.enter_context(_sbt(nc, "uT", [128, 8, S_], BF16))
        phase1(C, l, x_src, uT)
        if stop == (l, 1):
            return
        phase2(C, l, uT)
    S.barrier()
    S.flush()
    if stop == (l, 2):
        return
    phase3(C, l)
    if stop == (l, 3):
        return
    phase4(C, l)
    if stop == (l, 4):
        return
    phase5(C, l, x_src, x_dst)


def rms_tile(C, xt, ss, sq, nwt, ub, tag):
    O = C.O
    O.act(sq[:], xt[:], AF.Square, [xt], [sq, ss], accum_out=ss[:])
    O.act(ss[:], ss[:], AF.Sqrt, [ss], [ss], bias=1e-6, scale=1.0 / D_)
    O.recip(ss[:], ss[:], [ss], [ss])
    O.stt("dve", ub[:], xt[:], ss[:], nwt[:], ALU.mult, ALU.mult, [xt, ss, nwt], [ub])


def phase1(C, l, x_src, uT):
    nc, S, O = C.nc, C.S, C.O
    with ExitStack() as st:
        def sb(name, shape, dt):
            return st.enter_context(_sbt(nc, name, list(shape), dt))
        xt = [sb(f"p1x{i}", [128, D_], F32) for i in range(2)]
        sq = sb("p1sq", [128, D_], BF16)
        ss = [sb(f"p1ss{i}", [128, 1], F32) for i in range(2)]
        ub = [sb(f"p1ub{i}", [128, D_], BF16) for i in range(2)]
        nwt = sb("p1nw", [128, D_], F32)
        pT = [st.enter_context(_pst(nc, f"p1pT{i}", [128, 8, 128], BF16)) for i in range(2)]
        S.dma("sp", nwt[:], C.mixw[l:l + 1, :].partition_broadcast(128), r=[], w=[nwt])
        for t in range(NT_):
            b = t % 2
            S.dma("sp", xt[b][:], x_src[t * 128:(t + 1) * 128, :], r=[(x_src, t)], w=[xt[b]])
            rms_tile(C, xt[b], ss[b], sq, nwt, ub[b], "p1")
            for k in range(8):
                O.tr(pT[b][:, k, :], ub[b][:, k * 128:(k + 1) * 128], C.ident[:], [ub[b], C.ident], [pT[b]])
            O.copy("act" if t % 2 else "dve", uT[:, :, t * 128:(t + 1) * 128], pT[b][:], [pT[b]], [(uT, t)])
        S.barrier()
        S.flush()


def phase2(C, l, uT):
    nc, S, O = C.nc, C.S, C.O
    wsrc = C.wb_in[l].rearrange("(k p) n -> p k n", p=128)
    seq = []
    for g in range(2):
        seq += [(512 * g, 512), (1024 + 512 * g, 512), (2048 + 512 * g, 512)]
    seq += [(3072 + 512 * i, 512) for i in range(4)]
    seq += [(5120 + 512 * i, 512) for i in range(2)]
    seq += [(6144 + 512 * i, 512) for i in range(4)]
    seq += [(8192 + 512 * i, 512) for i in range(6)]
    seq += [(11264, 64)]
    seq += [(11328 + 512 * i, 512) for i in range(6)]
    with ExitStack() as st:
        wbuf = [st.enter_context(_sbt(nc, f"p2w{i}", [128, 8, 512], BF16)) for i in range(3)]
        PB = [st.enter_context(_pst(nc, f"p2ps{i}", [128, 512], F32)) for i in range(6)]
        PT = [st.enter_context(_pst(nc, f"p2pt{i}", [128, 4, 128], BF16)) for i in range(2)]
        state = {"issued": 0, "ps": 0, "done": 0}

        def prefetch():
            while state["issued"] < min(len(seq), state["done"] + 3):
                i = state["issued"]
                c0, ncw = seq[i]
                S.dma("sp", wbuf[i % 3][:, :, :ncw], wsrc[:, :, c0:c0 + ncw], r=[C.wb_in[l]], w=[wbuf[i % 3]])
                state["issued"] += 1

        def get_w(idx):
            prefetch()
            assert idx < state["issued"]
            return wbuf[idx % 3]

        def release(n):
            state["done"] = n
            prefetch()

        def PS():
            p = PB[state["ps"] % len(PB)]
            state["ps"] += 1
            return p

        def mm_feat(ps, wt, jj, tb):
            for k in range(8):
                O.mm(ps[:], wt[:, k, jj * 128:(jj + 1) * 128], uT[:, k, tb * 512:(tb + 1) * 512], k == 0, k == 7,
                     [wt, uT], [ps])

        def mm_tok(ps_ap, ps, wt, t, ncw):
            for k in range(8):
                O.mm(ps_ap, uT[:, k, t * 128:(t + 1) * 128], wt[:, k, :ncw], k == 0, k == 7, [wt, uT], [ps])

        wi = 0
        with ExitStack() as s2:
            def sb(name, shape, dt):
                return s2.enter_context(_sbt(nc, name, list(shape), dt))
            bT = sb("p2bT", [128, S_], F32)
            cst = [sb(f"p2cst{i}", [128, 512], F32) for i in range(2)]
            cx = sb("p2cx", [128, S_ + 2], F32)
            tcv = sb("p2tcv", [128, S_], F32)
            yAo = sb("p2yAo", [128, S_], BF16)
            wA = sb("p2wA", [128, 8, 3], F32)
            S.dma("sp", wA[:], C.cawT[l], r=[], w=[wA])
            O.memset("dve", cx[:, 0:1], 0.0, [(cx, "l")])
            O.memset("dve", cx[:, S_ + 1:S_ + 2], 0.0, [(cx, "r")])
            for g in range(2):
                release(wi)
                wt_b, wt_c, wt_x = get_w(wi), get_w(wi + 1), get_w(wi + 2)
                wi += 3
                for jj in range(4):
                    j = 4 * g + jj
                    for tb in range(8):
                        pb, pc, px = PS(), PS(), PS()
                        mm_feat(pb, wt_b, jj, tb)
                        mm_feat(pc, wt_c, jj, tb)
                        mm_feat(px, wt_x, jj, tb)
                        O.copy("act", bT[:, tb * 512:(tb + 1) * 512], pb[:], [pb], [(bT, tb)])
                        O.copy("act", cst[tb % 2][:], pc[:], [pc], [cst[tb % 2]])
                        O.tt("dve", cx[:, 1 + tb * 512:1 + (tb + 1) * 512], px[:], cst[tb % 2][:], ALU.mult,
                             [px, cst[tb % 2]], [(cx, tb)])
                    O.ts("dve", tcv[:], cx[:, 0:S_], wA[:, j, 0:1], None, ALU.mult, None, [cx, wA], [tcv])
                    O.stt("dve", tcv[:], cx[:, 1:S_ + 1], wA[:, j, 1:2], tcv[:], ALU.mult, ALU.add, [cx, wA, tcv], [tcv])
                    O.stt("dve", tcv[:], cx[:, 2:S_ + 2], wA[:, j, 2:3], tcv[:], ALU.mult, ALU.add, [cx, wA, tcv], [tcv])
                    O.tt("dve", yAo[:], tcv[:], bT[:], ALU.mult, [tcv, bT], [yAo])
                    S.dma("sp", C.yAT[j * 128:(j + 1) * 128, :], yAo[:], r=[yAo], w=[(C.yAT, j)])
            S.barrier()
            S.flush()
        with ExitStack() as s2:
            def sb(name, shape, dt):
                return s2.enter_context(_sbt(nc, name, list(shape), dt))
            qs = [sb(f"p2qs{i}", [128, 8, 64], F32) for i in range(2)]
            tmp = [sb(f"p2tmp{i}", [128, 4, 8, 8], F32) for i in range(2)]
            qr = [sb(f"p2qr{i}", [128, 512], BF16) for i in range(2)]
            stg = [sb(f"p2stg{i}", [128, 4, S_], BF16) for i in range(2)]
            for ci in range(4):
                release(wi)
                wt = get_w(wi)
                wi += 1
                sg = stg[ci % 2]
                for t in range(NT_):
                    b = t % 2
                    ps = PS()
                    mm_tok(ps[:], ps, wt, t, 512)
                    O.copy("act", qs[b][:].rearrange("p h d -> p (h d)"), ps[:], [ps], [qs[b]])
                    cs = C.cosT[:, t, :].unsqueeze(1).to_broadcast([128, 8, 8])
                    sn = C.sinT[:, t, :].unsqueeze(1).to_broadcast([128, 8, 8])
                    t1 = qs[b][:, :, 0:8]
                    t2 = qs[b][:, :, 8:16]
                    tm = tmp[b]
                    O.tt("dve", tm[:, 0], t1, cs, ALU.mult, [qs[b], C.cosT], [(tm, 0)])
                    O.tt("dve", tm[:, 1], t2, sn, ALU.mult, [qs[b], C.sinT], [(tm, 1)])
                    O.tt("dve", tm[:, 2], t2, cs, ALU.mult, [qs[b], C.cosT], [(tm, 2)])
                    O.tt("dve", tm[:, 3], t1, sn, ALU.mult, [qs[b], C.sinT], [(tm, 3)])
                    O.tt("dve", t1, tm[:, 0], tm[:, 1], ALU.subtract, [tm], [qs[b]])
                    O.tt("dve", t2, tm[:, 2], tm[:, 3], ALU.add, [tm], [qs[b]])
                    O.copy("dve", qr[b][:], qs[b][:].rearrange("p h d -> p (h d)"), [qs[b]], [qr[b]])
                    pt = PT[b]
                    for jj in range(4):
                        O.tr(pt[:, jj, :], qr[b][:, jj * 128:(jj + 1) * 128], C.ident[:], [qr[b], C.ident], [pt])
                    O.copy("act", sg[:, :, t * 128:(t + 1) * 128], pt[:], [pt], [(sg, t)])
                dst = C.qT if ci < 2 else C.kT
                for jj in range(4):
                    r0 = (ci % 2) * 512 + jj * 128
                    S.dma("sp", dst[r0:r0 + 128, :], sg[:, jj, :], r=[sg], w=[(dst, r0)])
            S.barrier()
            S.flush()
        with ExitStack() as s2:
            def sb(name, shape, dt):
                return s2.enter_context(_sbt(nc, name, list(shape), dt))
            vst = [sb(f"p2vst{i}", [128, 512], BF16) for i in range(4)]
            n = 0
            for ci in range(6):
                release(wi)
                wt = get_w(wi)
                wi += 1
                for t in range(NT_):
                    ps = PS()
                    mm_tok(ps[:], ps, wt, t, 512)
                    vs = vst[n % 4]
                    n += 1
                    if ci < 2:
                        O.copy("act", vs[:], ps[:], [ps], [vs])
                        S.dma("sp", C.v_d[t * 128:(t + 1) * 128, ci * 512:(ci + 1) * 512], vs[:], r=[vs], w=[(C.v_d, (t, ci))])
                    else:
                        O.act(vs[:], ps[:], AF.Silu, [ps], [vs])
                        c2 = ci - 2
                        S.dma("sp", C.sz_d[t * 128:(t + 1) * 128, c2 * 512:(c2 + 1) * 512], vs[:], r=[vs], w=[(C.sz_d, (t, c2))])
            S.barrier()
            S.flush()
        with ExitStack() as s2:
            def sb(name, shape, dt):
                return s2.enter_context(_sbt(nc, name, list(shape), dt))
            xin = sb("p2xin", [128, 4, S_ + 4], BF16)
            diag = sb("p2diag", [128, 4, 5, 128], BF16)
            stok = [sb(f"p2stok{i}", [128, 8, 512], BF16) for i in range(2)]
            sfeat = [sb(f"p2sfeat{i}", [128, S_], BF16) for i in range(2)]
            wS = sb("p2wS", [128, 24, 5], F32)
            bS = sb("p2bS", [128, 24], F32)
            brf = sb("p2brf", [1, 3072], F32)
            brow = sb("p2brow", [1, 3072], BF16)
            ones = sb("p2ones", [1, 128], BF16)
            S.dma("sp", wS[:], C.scwT[l], r=[], w=[wS])
            S.dma("sp", bS[:], C.scbT[l], r=[], w=[bS])
            S.dma("sp", brf[:], C.scbR[l], r=[], w=[brf])
            O.copy("dve", brow[:], brf[:], [brf], [brow])
            O.memset("dve", ones[:], 1.0, [ones])
            O.memset("dve", xin[:, :, 0:2], 0.0, [(xin, "l")])
            O.memset("dve", xin[:, :, S_ + 2:S_ + 4], 0.0, [(xin, "r")])
            nf = 0
            for ci in range(6):
                release(wi)
                wt = get_w(wi)
                wi += 1
                for jj in range(4):
                    J = 4 * ci + jj
                    for tb in range(8):
                        ps = PS()
                        mm_feat(ps, wt, jj, tb)
                        O.copy("act" if tb % 2 else "dve", xin[:, jj, 2 + tb * 512:2 + (tb + 1) * 512], ps[:], [ps], [(xin, (jj, tb))])
                    for k5 in range(5):
                        O.ts("dve", diag[:, jj, k5, :], C.ident[:], wS[:, J, k5:k5 + 1], None, ALU.mult, None,
                             [C.ident, wS], [(diag, (jj, k5))])
                if ci <= 4:
                    for t in range(NT_):
                        ps = PS()
                        for jj in range(4):
                            J = 4 * ci + jj
                            o_ap = ps[:, jj * 128:(jj + 1) * 128]
                            for k5 in range(5):
                                O.mm(o_ap, xin[:, jj, t * 128 + k5:t * 128 + k5 + 128], diag[:, jj, k5, :], k5 == 0, False,
                                     [xin, diag], [ps])
                            O.mm(o_ap, ones[0:1, :], brow[0:1, J * 128:(J + 1) * 128], False, True, [ones, brow], [ps])
                        sk = stok[(t // 8) % 2]
                        O.act(sk[:, t % 8, :], ps[:], AF.Silu, [ps], [(sk, t % 8)])
                        if t % 8 == 7:
                            t0 = t - 7
                            if ci < 4:
                                dst = C.xs_d[t0 * 128:(t0 + 8) * 128, ci * 512:(ci + 1) * 512]
                                key = (C.xs_d, (t0, ci))
                            else:
                                dst = C.Bt_d[t0 * 128:(t0 + 8) * 128, :]
                                key = (C.Bt_d, t0)
                            S.dma("sp", dst.rearrange("(t p) c -> p t c", p=128), sk[:], r=[sk], w=[key])
                if ci >= 4:
                    for jj in range(4):
                        J = 4 * ci + jj
                        sf = sfeat[nf % 2]
                        nf += 1
                        for tb in range(8):
                            ps = PS()
                            for k5 in range(5):
                                O.mm(ps[:], diag[:, jj, k5, :], xin[:, jj, tb * 512 + k5:tb * 512 + k5 + 512], k5 == 0, k5 == 4,
                                     [xin, diag], [ps])
                            O.act(sf[:, tb * 512:(tb + 1) * 512], ps[:], AF.Silu, [ps, bS], [(sf, tb)], bias=bS[:, J:J + 1])
                        dst = C.BT_d if ci == 4 else C.CT_d
                        S.dma("sp", dst[jj * 128:(jj + 1) * 128, :], sf[:], r=[sf], w=[(dst, jj)])
            S.barrier()
            S.flush()
        with ExitStack() as s2:
            def sb(name, shape, dt):
                return s2.enter_context(_sbt(nc, name, list(shape), dt))
            dtbb = sb("p2dtbb", [128, 64], F32)
            abc = sb("p2abc", [128, 64], F32)
            tdt = [sb(f"p2tdt{i}", [128, 64], F32) for i in range(2)]
            S.dma("sp", dtbb[:], C.dtb[l].partition_broadcast(128), r=[], w=[dtbb])
            S.dma("sp", abc[:], C.alog[l].partition_broadcast(128), r=[], w=[abc])
            O.act(abc[:], abc[:], AF.Exp, [abc], [abc])
            O.ts("dve", abc[:], abc[:], -1.0, None, ALU.mult, None, [abc], [abc])
            release(wi)
            wt = get_w(wi)
            wi += 1
            for t in range(NT_):
                ps = PS()
                mm_tok(ps[:, 0:64], ps, wt, t, 64)
                td = tdt[t % 2]
                O.tt("dve", td[:], ps[:, 0:64], dtbb[:], ALU.add, [ps, dtbb], [td])
                O.act(td[:], td[:], AF.Exp, [td], [td])
                O.act(C.dt_all[:, t, :], td[:], AF.Ln, [td], [(C.dt_all, t)], bias=1.0)
                O.tt("dve", C.la_all[:, t, :], C.dt_all[:, t, :], abc[:], ALU.mult, [(C.dt_all, t), abc], [(C.la_all, t)])
            if "dt_dbg" in C.dbg:
                S.dma("sp", C.dt_dbg, C.dt_all[:], r=[C.dt_all], w=[C.dt_dbg])
            S.barrier()
            S.flush()
        with ExitStack() as s2:
            def sb(name, shape, dt):
                return s2.enter_context(_sbt(nc, name, list(shape), dt))
            sfeat = [sb(f"p2gfeat{i}", [128, S_], BF16) for i in range(2)]
            nf = 0
            for ci in range(6):
                release(wi)
                wt = get_w(wi)
                wi += 1
                for jj in range(4):
                    G = 4 * ci + jj
                    sf = sfeat[nf % 2]
                    nf += 1
                    for tb in range(8):
                        ps = PS()
                        mm_feat(ps, wt, jj, tb)
                        O.act(sf[:, tb * 512:(tb + 1) * 512], ps[:], AF.Sigmoid, [ps], [(sf, tb)])
                    S.dma("sp", C.gT[G * 128:(G + 1) * 128, :], sf[:], r=[sf], w=[(C.gT, G)])
            S.barrier()
            S.flush()
        assert wi == len(seq)


def phase3(C, l):
    nc, S, O = C.nc, C.S, C.O
    pats = [(1, 33), (4, 9), (16, 3)]
    with ExitStack() as st:
        def sb(name, shape, dt):
            return st.enter_context(_sbt(nc, name, list(shape), dt))
        qTh = [sb(f"p3q{i}", [64, S_], BF16) for i in range(2)]
        kTp = [sb(f"p3k{i}", [64, S_ + 2048], BF16) for i in range(2)]
        vP = [[sb(f"p3v{pi}_{b}", [128, nb, dil, 65], BF16) for b in range(2)] for pi, (dil, nb) in enumerate(pats)]
        mask = sb("p3mask", [128, 2, 128], BF16)
        PTs = [sb(f"p3pt{i}", [128, 2, 2, 128], BF16) for i in range(2)]
        PTm = [sb(f"p3pm{i}", [128, 2, 2, 128], BF16) for i in range(2)]
        ost = [sb(f"p3ost{i}", [128, 32, 65], F32) for i in range(2)]
        PSs = [st.enter_context(_pst(nc, f"p3ps{i}", [128, 512], F32)) for i in range(3)]
        PSo = [st.enter_context(_pst(nc, f"p3po{i}", [128, 2, 65], F32)) for i in range(3)]
        O.copy("dve", mask[:, 0, :], C.tri[:, 1, :], [C.tri], [(mask, 0)])
        O.copy("dve", mask[:, 1, :], C.tri[:, 0, :], [C.tri], [(mask, 1)])
        for b in range(2):
            O.memset("dve", kTp[b][:, 0:1024], 0.0, [(kTp[b], "l")])
            O.memset("dve", kTp[b][:, 1024 + S_:2048 + S_], 0.0, [(kTp[b], "r")])
            for pi, (dil, nb) in enumerate(pats):
                v = vP[pi][b]
                O.memset("pool", v[:], 0.0, [v])
                O.memset("pool", v[:, :, :, 64:65], 1.0, [v])
                O.memset("pool", v[0:64, 0, :, 64:65], 0.0, [v])
                O.memset("pool", v[64:128, nb - 1, :, 64:65], 0.0, [v])
        def loads(h):
            b = h % 2
            S.dma("sp", qTh[b][:], C.qT[h * 64:(h + 1) * 64, :], r=[C.qT], w=[qTh[b]])
            S.dma("sp", kTp[b][:, 1024:1024 + S_], C.kT[h * 64:(h + 1) * 64, :], r=[C.kT], w=[(kTp[b], "m")])
            vsrc = C.v_d[:, h * 64:(h + 1) * 64]
            for pi, (dil, nb) in enumerate(pats):
                v = vP[pi][b]
                nin = nb - 2
                if dil == 1:
                    for j0 in range(0, nin, 8):
                        j1 = min(nin, j0 + 8)
                        src = vsrc[64 + j0 * 128:64 + j1 * 128, :]
                        S.dma("sp", v[:, 1 + j0:1 + j1, 0, 0:64], src.rearrange("(j i) d -> i j d", i=128),
                              r=[C.v_d], w=[(v, ("in", j0))])
                else:
                    for j0 in range(nin):
                        src = vsrc[64 * dil + j0 * 128 * dil:64 * dil + (j0 + 1) * 128 * dil, :]
                        S.dma("sp", v[:, 1 + j0, :, 0:64], src.rearrange("(i r) d -> i r d", r=dil),
                              r=[C.v_d], w=[(v, ("in", j0))])
                S.dma("sp", v[64:128, 0, :, 0:64], vsrc[0:64 * dil, :].rearrange("(i r) d -> i r d", r=dil),
                      r=[C.v_d], w=[(v, "first")])
                S.dma("sp", v[0:64, nb - 1, :, 0:64], vsrc[S_ - 64 * dil:S_, :].rearrange("(i r) d -> i r d", r=dil),
                      r=[C.v_d], w=[(v, "last")])

        def store(h, pi, osb):
            dil, nb = pats[pi]
            nqb = nb - 1
            osv = osb[:].rearrange("p (q r) e -> p q r e", r=dil)
            if dil == 1:
                S.dma("sp", C.o_d[pi][:, h, :].rearrange("(q i) e -> i q e", i=128), osb[:], r=[osb],
                      w=[(C.o_d[pi], h)])
            else:
                for q in range(nqb):
                    S.dma("sp", C.o_d[pi][q * 128 * dil:(q + 1) * 128 * dil, h, :].rearrange("(i r) e -> i r e", r=dil),
                          osv[:, q, :, :], r=[osb], w=[(C.o_d[pi], (h, q))])

        pairs = []
        n_ost = 0
        for h in range(16):
            first = True
            for pi, (dil, nb) in enumerate(pats):
                osb = ost[n_ost % 2]
                n_ost += 1
                nqb = nb - 1
                lst = [(r, qp) for r in range(dil) for qp in range(nqb // 2)]
                for idx, (r, qp) in enumerate(lst):
                    pairs.append(dict(h=h, pi=pi, dil=dil, r=r, qp=qp, osb=osb, pre=None,
                                      post=(h, pi, osb) if idx == len(lst) - 1 else None))
        npairs_h = len(pairs) // 16
        for h in range(16):
            if h == 0:
                pairs[0]["pre"] = 0
            if h + 1 < 16:
                pairs[h * npairs_h + 4]["pre"] = h + 1
        NPS = 3

        def stA(i):
            p = pairs[i]
            b = p["h"] % 2
            dil, r, qp = p["dil"], p["r"], p["qp"]
            ps = PSs[i % NPS]
            psv = ps[:].rearrange("p (a b c) -> p a b c", a=2, b=2)
            for qi in range(2):
                qb = 2 * qp + qi
                q0 = 128 * dil * qb + r
                qsl = qTh[b][:, q0:q0 + 127 * dil + 1:dil]
                for ab in range(2):
                    k0 = 1024 - 64 * dil + 128 * dil * (qb + ab) + r
                    ksl = kTp[b][:, k0:k0 + 127 * dil + 1:dil]
                    O.mm(psv[:, qi, ab, :], ksl, qsl, True, True, [kTp[b], qTh[b]], [ps])

        def stB(i):
            ps = PSs[i % NPS]
            pt = PTs[i % 2]
            pm = PTm[i % 2]
            O.act(pt[:].rearrange("p a b c -> p (a b c)"), ps[:], AF.Exp, [ps], [pt], scale=0.125)
            O.tt("dve", pm[:], pt[:], mask[:].unsqueeze(1).to_broadcast([128, 2, 2, 128]), ALU.mult, [pt, mask], [pm])

        def stC(i):
            p = pairs[i]
            b = p["h"] % 2
            dil, r, qp, pi = p["dil"], p["r"], p["qp"], p["pi"]
            v = vP[pi][b]
            pm = PTm[i % 2]
            po = PSo[i % 3]
            osb = p["osb"]
            osv = osb[:].rearrange("p (q r) e -> p q r e", r=dil)
            for qi in range(2):
                qb = 2 * qp + qi
                for ab in range(2):
                    O.mm(po[:, qi, :], pm[:, qi, ab, :], v[:, qb + ab, r, :], ab == 0, ab == 1, [pm, v], [po])
            O.copy("act" if i % 2 else "dve", osv[:, 2 * qp:2 * qp + 2, r, :], po[:], [po], [(osb, (r, qp))])
            if p["post"] is not None:
                store(*p["post"])

        n = len(pairs)
        for i in range(n + 2):
            if i < n:
                if pairs[i]["pre"] is not None:
                    loads(pairs[i]["pre"])
                stA(i)
            if 0 <= i - 1 < n:
                stB(i - 1)
            if 0 <= i - 2 < n:
                stC(i - 2)
        S.barrier()
        S.flush()


def phase4(C, l):
    nc, S, O = C.nc, C.S, C.O
    tri, trif = C.tri, C.trif
    with ExitStack() as st:
        def sb(name, shape, dt):
            return st.enter_context(_sbt(nc, name, list(shape), dt))
        H = [sb(f"p4H{d}", [128, 2048], F32) for d in range(2)]
        hbf = [[sb(f"p4hbf{d}_{i}", [128, 2048], BF16) for i in range(2)] for d in range(2)]
        xs_t = [sb(f"p4xs{i}", [128, 2048], BF16) for i in range(4)]
        Bt_t = [sb(f"p4Bt{i}", [128, 512], BF16) for i in range(4)]
        ew = [sb(f"p4ew{i}", [128, 2, 32], F32) for i in range(2)]
        dtw = [sb(f"p4dtw{i}", [128, 32], F32) for i in range(2)]
        xw = [sb(f"p4xw{i}", [128, 2048], BF16) for i in range(2)]
        tmpH = sb("p4tmpH", [128, 2048], F32)
        PSw = [st.enter_context(_pst(nc, f"p4psw{i}", [128, 2, 32], F32)) for i in range(2)]
        PSs = [st.enter_context(_pst(nc, f"p4pss{i}", [128, 512], F32)) for i in range(4)]
        for d in range(2):
            O.memset("dve", H[d][:], 0.0, [H[d]])
            O.memset("pool", hbf[d][0][:], 0.0, [hbf[d][0]])
        n = 0
        for i in range(NT_):
            for d in range(2):
                c = i if d == 0 else NT_ - 1 - i
                xt = xs_t[n % 4]
                bt = Bt_t[n % 4]
                e_ = ew[n % 2]
                dw = dtw[n % 2]
                xw_ = xw[n % 2]
                pw = PSw[n % 2]
                n += 1
                S.dma("sp", xt[:], C.xs_d[c * 128:(c + 1) * 128, :], r=[C.xs_d], w=[xt])
                S.dma("sp", bt[:], C.Bt_d[c * 128:(c + 1) * 128, :], r=[C.Bt_d], w=[bt])
                la_c = C.la_all[:, c, d * 32:(d + 1) * 32]
                dt_c = C.dt_all[:, c, d * 32:(d + 1) * 32]
                O.mm(pw[:, 0, :], trif[:, 1 if d == 0 else 0, :], la_c, True, True, [trif, C.la_all], [pw])
                O.mm(pw[:, 1, :], trif[:, 2, :], la_c, True, True, [trif, C.la_all], [pw])
                O.act(e_[:], pw[:], AF.Exp, [pw], [e_])
                O.tt("dve", dw[:], dt_c, e_[:, 0, :], ALU.mult, [C.dt_all, e_], [dw])
                O.tt("pool", xw_[:].rearrange("p (h d) -> p h d", d=64), xt[:].rearrange("p (h d) -> p h d", d=64),
                     dw[:].unsqueeze(2).to_broadcast([128, 32, 64]), ALU.mult, [xt, dw], [xw_])
                for g in range(4):
                    O.mm(PSs[g][:], bt[:, g * 128:(g + 1) * 128], xw_[:, g * 512:(g + 1) * 512], True, True, [bt, xw_], [PSs[g]])
                hb = hbf[d][i % 2]
                S.dma("sp", C.hin_d[d][c], hb[:], r=[hb], w=[(C.hin_d[d], c)])
                O.tt("dve", tmpH[:].rearrange("p (h d) -> p h d", d=64), H[d][:].rearrange("p (h d) -> p h d", d=64),
                     e_[:, 1, :].unsqueeze(2).to_broadcast([128, 32, 64]), ALU.mult, [H[d], e_], [tmpH])
                for g in range(4):
                    O.tt("dve", H[d][:, g * 512:(g + 1) * 512], tmpH[:, g * 512:(g + 1) * 512], PSs[g][:], ALU.add,
                         [tmpH, PSs[g]], [(H[d], g)])
                O.copy("act", hbf[d][(i + 1) % 2][:], H[d][:], [H[d]], [hbf[d][(i + 1) % 2]])
        S.barrier()
        S.flush()
    with ExitStack() as st:
        def sb(name, shape, dt):
            return st.enter_context(_sbt(nc, name, list(shape), dt))
        NB = 3
        xs_t = [sb(f"p4xs{i}", [128, 2048], BF16) for i in range(NB)]
        BT_t = [sb(f"p4BT{i}", [128, 4, 128], BF16) for i in range(NB)]
        CT_t = [sb(f"p4CT{i}", [128, 4, 128], BF16) for i in range(NB)]
        sz_t = [sb(f"p4sz{i}", [128, 2048], BF16) for i in range(NB)]
        hh_t = [[sb(f"p4hh{d}_{i}", [128, 2048], BF16) for i in range(NB)] for d in range(2)]
        cbm = [[sb(f"p4cbm{d}_{i}", [128, 4, 128], BF16) for i in range(2)] for d in range(2)]
        ecum = [[sb(f"p4ecum{d}_{i}", [128, 32], F32) for i in range(2)] for d in range(2)]
        rseg = [[sb(f"p4rseg{d}_{i}", [128, 8, 128], BF16) for i in range(2)] for d in range(2)]
        xd = [[sb(f"p4xd{d}_{i}", [128, 512], BF16) for i in range(4)] for d in range(2)]
        eseg = [sb(f"p4eseg{i}", [128, 4, 128], BF16) for i in range(8)]
        MT = [sb(f"p4MT{i}", [128, 4, 128], BF16) for i in range(8)]
        tt_ = [[sb(f"p4t{d}_{i}", [128, 512], F32) for i in range(2)] for d in range(2)]
        xsD = [sb(f"p4xsD{i}", [128, 512], F32) for i in range(2)]
        yy = [sb(f"p4y{i}", [128, 512], F32) for i in range(3)]
        ynf = [sb(f"p4ynf{i}", [128, 512], BF16) for i in range(2)]
        ssg = [sb(f"p4ssg{i}", [128, 1], F32) for i in range(2)]
        sqj = sb("p4sqj", [128, 512], BF16)
        ynT_st = [sb(f"p4ynT{i}", [128, 16, 128], BF16) for i in range(3)]
        nwb = sb("p4nwb", [128, 2048], F32)
        Dbc = sb("p4Dbc", [128, 32], F32)
        PScb = st.enter_context(_pst(nc, "p4pscb", [128, 512], F32))
        PSy = [st.enter_context(_pst(nc, f"p4psy{i}", [128, 512], F32)) for i in range(2)]
        PSseg = [st.enter_context(_pst(nc, f"p4psseg{i}", [128, 512], F32)) for i in range(2)]
        PSo = [st.enter_context(_pst(nc, f"p4pso{i}", [128, 512], F32)) for i in range(2)]
        PSt = st.enter_context(_pst(nc, "p4pst", [128, 512], F32))
        S.dma("sp", nwb[:], C.snw[l].partition_broadcast(128), r=[], w=[nwb])
        S.dma("sp", Dbc[:], C.sdd[l].partition_broadcast(128), r=[], w=[Dbc])

        def loads(c):
            b = c % NB
            S.dma("sp", xs_t[b][:], C.xs_d[c * 128:(c + 1) * 128, :], r=[C.xs_d], w=[xs_t[b]])
            S.dma("sp", BT_t[b][:], C.BT_d[:, c * 128:(c + 1) * 128].rearrange("(g n) t -> n g t", n=128), r=[C.BT_d], w=[BT_t[b]])
            S.dma("sp", CT_t[b][:], C.CT_d[:, c * 128:(c + 1) * 128].rearrange("(g n) t -> n g t", n=128), r=[C.CT_d], w=[CT_t[b]])
            S.dma("sp", sz_t[b][:], C.sz_d[c * 128:(c + 1) * 128, :], r=[C.sz_d], w=[sz_t[b]])
            for d in range(2):
                S.dma("sp", hh_t[d][b][:], C.hin_d[d][c], r=[C.hin_d[d]], w=[hh_t[d][b]])

        units = [(d, q) for d in range(2) for q in range(2)]

        def s1(k):
            c, g = divmod(k, 4)
            b, cp = c % NB, c % 2
            xt, BT, CT = xs_t[b], BT_t[b], CT_t[b]
            if g == 0:
                pcb = PScb[:].rearrange("p (g t) -> p g t", g=4)
                for g2 in range(4):
                    O.mm(pcb[:, g2, :], BT[:, g2, :], CT[:, g2, :], True, True, [BT, CT], [PScb])
                for d in range(2):
                    O.tt("dve", cbm[d][cp][:], pcb, tri[:, d, :].unsqueeze(1).to_broadcast([128, 4, 128]), ALU.mult,
                         [PScb, tri], [cbm[d][cp]])
                pse = PScb[:, 0:64].rearrange("p (d h) -> p d h", d=2)
                for d in range(2):
                    la_c = C.la_all[:, c, d * 32:(d + 1) * 32]
                    O.mm(pse[:, d, :], trif[:, 3 + d, :], la_c, True, True, [trif, C.la_all], [PScb])
                for d in range(2):
                    O.act(ecum[d][cp][:], pse[:, d, :], AF.Exp, [PScb], [ecum[d][cp]])
            for d in range(2):
                la_g = C.la_all[:, c, d * 32 + g * 8:d * 32 + (g + 1) * 8]
                dt_g = C.dt_all[:, c, d * 32 + g * 8:d * 32 + (g + 1) * 8]
                O.tt("dve", rseg[d][k % 2][:], la_g.unsqueeze(2).to_broadcast([128, 8, 128]),
                     tri[:, d, :].unsqueeze(1).to_broadcast([128, 8, 128]), ALU.mult, [C.la_all, tri], [rseg[d][k % 2]])
                O.tt("pool", xd[d][k % 4][:].rearrange("p (h e) -> p h e", e=64),
                     xt[:, g * 512:(g + 1) * 512].rearrange("p (h e) -> p h e", e=64),
                     dt_g.unsqueeze(2).to_broadcast([128, 8, 64]), ALU.mult, [xt, C.dt_all], [xd[d][k % 4]])

        def s2(k):
            for u, (d, q) in enumerate(units):
                n = k * 4 + u
                pss = PSseg[n % 2]
                es = eseg[n % 8]
                O.mm(pss[:], tri[:, 3 - d, :], rseg[d][k % 2][:, q * 4:(q + 1) * 4, :].rearrange("p h t -> p (h t)"), True, True,
                     [tri, rseg[d][k % 2]], [pss])
                O.act(es[:].rearrange("p h t -> p (h t)"), pss[:], AF.Exp, [pss], [es])

        def s3(k):
            c, g = divmod(k, 4)
            cp = c % 2
            for u, (d, q) in enumerate(units):
                n = k * 4 + u
                O.tt("dve" if u % 2 else "pool", MT[n % 8][:], eseg[n % 8][:],
                     cbm[d][cp][:, g, :].unsqueeze(1).to_broadcast([128, 4, 128]), ALU.mult, [eseg[n % 8], cbm[d][cp]], [MT[n % 8]])

        def s4(k):
            for u, (d, q) in enumerate(units):
                n = k * 4 + u
                mt = MT[n % 8]
                for hh in range(4):
                    h8 = q * 4 + hh
                    O.mm(PSy[k % 2][:, h8 * 64:(h8 + 1) * 64], mt[:, hh, :], xd[d][k % 4][:, h8 * 64:(h8 + 1) * 64],
                         u == 0 and hh == 0, u == 3 and hh == 3, [mt, xd[d][k % 4]], [PSy[k % 2]])

        def s5(k):
            c, g = divmod(k, 4)
            b, cp, kp = c % NB, c % 2, k % 2
            xt, CT, sz = xs_t[b], CT_t[b], sz_t[b]
            for d in range(2):
                hd = hh_t[d][b]
                O.mm(PSo[d][:], CT[:, g, :], hd[:, g * 512:(g + 1) * 512], True, True, [CT, hd], [PSo[d]])
            O.tt("pool", xsD[kp][:].rearrange("p (h e) -> p h e", e=64),
                 xt[:, g * 512:(g + 1) * 512].rearrange("p (h e) -> p h e", e=64),
                 Dbc[:, g * 8:(g + 1) * 8].unsqueeze(2).to_broadcast([128, 8, 64]), ALU.mult, [xt, Dbc], [xsD[kp]])
            for d in range(2):
                O.tt("dve", tt_[d][kp][:].rearrange("p (h e) -> p h e", e=64), PSo[d][:].rearrange("p (h e) -> p h e", e=64),
                     ecum[d][cp][:, g * 8:(g + 1) * 8].unsqueeze(2).to_broadcast([128, 8, 64]), ALU.mult,
                     [PSo[d], ecum[d][cp]], [tt_[d][kp]])
            O.tt("pool", tt_[0][kp][:], tt_[0][kp][:], tt_[1][kp][:], ALU.add, [tt_[0][kp], tt_[1][kp]], [tt_[0][kp]])
            O.tt("pool", tt_[0][kp][:], tt_[0][kp][:], xsD[kp][:], ALU.add, [tt_[0][kp], xsD[kp]], [tt_[0][kp]])
            y_ = yy[k % 3]
            O.tt("dve", y_[:], PSy[kp][:], tt_[0][kp][:], ALU.add, [PSy[kp], tt_[0][kp]], [y_])
            O.tt("pool", y_[:], y_[:], sz[:, g * 512:(g + 1) * 512], ALU.mult, [y_, sz], [y_])

        def s6(k):
            c, g = divmod(k, 4)
            y_ = yy[k % 3]
            s_ = ssg[k % 2]
            O.act(sqj[:], y_[:], AF.Square, [y_], [sqj, s_], accum_out=s_[:])
            O.act(s_[:], s_[:], AF.Sqrt, [s_], [s_], bias=1e-6, scale=1.0 / 512)
            O.recip(s_[:], s_[:], [s_], [s_])
            O.stt("dve", ynf[k % 2][:], y_[:], s_[:], nwb[:, g * 512:(g + 1) * 512], ALU.mult, ALU.mult, [y_, s_, nwb], [ynf[k % 2]])

        def s7(k):
            c, g = divmod(k, 4)
            ptb = PSt[:, 0:256].bitcast(BF16).rearrange("p (a t) -> p a t", a=4)
            ys = ynT_st[c % 3]
            yn_ = ynf[k % 2]
            for a in range(4):
                O.tr(ptb[:, a, :], yn_[:, a * 128:(a + 1) * 128], C.ident[:], [yn_, C.ident], [PSt])
            O.copy("act", ys[:, g * 4:(g + 1) * 4, :], ptb, [PSt], [(ys, g)])
            if g == 3:
                S.dma("sp", C.ynT_d[:, c * 128:(c + 1) * 128].rearrange("(k p) t -> p k t", p=128), ys[:], r=[ys],
                      w=[(C.ynT_d, c)])

        stages = [s1, s2, s3, s4, s5, s6, s7]
        NK = NT_ * 4
        for c in range(NB):
            loads(c)
        for it in range(NK + len(stages) - 1):
            if it >= 8 and it % 4 == 0:
                cn = (it - 8) // 4 + NB
                if cn < NT_:
                    loads(cn)
            for si in range(len(stages) - 1, -1, -1):
                k = it - si
                if 0 <= k < NK:
                    stages[si](k)
        S.barrier()
        S.flush()


def phase5(C, l, x_src, x_dst):
    nc, S, O = C.nc, C.S, C.O
    with ExitStack() as st:
        def sb(name, shape, dt):
            return st.enter_context(_sbt(nc, name, list(shape), dt))
        wa = sb("p5wa", [128, 8, D_], BF16)
        wb_ = sb("p5wb", [128, 8, D_], BF16)
        wc = sb("p5wc", [128, 16, D_], BF16)
        wo = sb("p5wo", [128, 8, D_], BF16)
        yA_b = sb("p5yA", [128, 8, 512], BF16)
        yn_b = sb("p5yn", [128, 16, 512], BF16)
        oT_b = sb("p5oT", [128, 8, 512], BF16)
        g_b = sb("p5g", [128, 24, 512], BF16)
        mT = sb("p5mT", [128, 8, 512], BF16)
        ot = [sb(f"p5ot{p}", [128, 16, 65], F32) for p in range(3)]
        rden = sb("p5rden", [128, 16], F32)
        ob = sb("p5ob", [128, 16, 64], BF16)
        xt = sb("p5xt", [128, D_], F32)
        xo = sb("p5xo", [128, D_], F32)
        t1 = sb("p5t1", [128, 512], F32)
        t2 = sb("p5t2", [128, 512], F32)
        PB = [st.enter_context(_pst(nc, f"p5ps{i}", [128, 512], F32)) for i in range(6)]
        PT = st.enter_context(_pst(nc, "p5pt", [128, 8, 128], BF16))
        S.dma("sp", wa[:], C.wb_a[l].rearrange("(k p) n -> p k n", p=128), r=[C.wb_a[l]], w=[wa])
        S.dma("sp", wb_[:], C.wb_b[l].rearrange("(k p) n -> p k n", p=128), r=[C.wb_b[l]], w=[wb_])
        S.dma("sp", wc[:], C.wb_c[l].rearrange("(k p) n -> p k n", p=128), r=[C.wb_c[l]], w=[wc])
        S.dma("sp", wo[:], C.wb_o[l].rearrange("(k p) n -> p k n", p=128), r=[C.wb_o[l]], w=[wo])
        nps = 0
        for tb in range(8):
            tsl = slice(tb * 512, (tb + 1) * 512)
            S.dma("sp", yA_b[:], C.yAT[:, tsl].rearrange("(k p) t -> p k t", p=128), r=[C.yAT], w=[yA_b])
            S.dma("sp", yn_b[:], C.ynT_d[:, tsl].rearrange("(k p) t -> p k t", p=128), r=[C.ynT_d], w=[yn_b])
            S.dma("sp", g_b[:], C.gT[:, tsl].rearrange("(k p) t -> p k t", p=128), r=[C.gT], w=[g_b])
            for tt in range(4):
                t = tb * 4 + tt
                for p in range(3):
                    S.dma("sp", ot[p][:], C.o_d[p][t * 128:(t + 1) * 128], r=[C.o_d[p]], w=[ot[p]])
                O.tt("dve", ot[0][:], ot[0][:], ot[1][:], ALU.add, [ot[0], ot[1]], [ot[0]])
                O.tt("dve", ot[0][:], ot[0][:], ot[2][:], ALU.add, [ot[0], ot[2]], [ot[0]])
                O.recip(rden[:].unsqueeze(2), ot[0][:, :, 64:65], [ot[0]], [rden])
                O.tt("dve", ob[:], ot[0][:, :, 0:64], rden[:].unsqueeze(2).to_broadcast([128, 16, 64]), ALU.mult,
                     [ot[0], rden], [ob])
                obf = ob[:].rearrange("p h d -> p (h d)")
                for k in range(8):
                    O.tr(PT[:, k, :], obf[:, k * 128:(k + 1) * 128], C.ident[:], [ob, C.ident], [PT])
                O.copy("act", oT_b[:, :, tt * 128:(tt + 1) * 128], PT[:], [PT], [(oT_b, tt)])
            for cc in range(8):
                pa, pb, pc = PB[nps % 6], PB[(nps + 1) % 6], PB[(nps + 2) % 6]
                nps += 3
                for k in range(8):
                    O.mm(pa[:], wa[:, k, cc * 128:(cc + 1) * 128], yA_b[:, k, :], k == 0, k == 7, [wa, yA_b], [pa])
                for k in range(8):
                    O.mm(pb[:], wb_[:, k, cc * 128:(cc + 1) * 128], oT_b[:, k, :], k == 0, k == 7, [wb_, oT_b], [pb])
                for k in range(16):
                    O.mm(pc[:], wc[:, k, cc * 128:(cc + 1) * 128], yn_b[:, k, :], k == 0, k == 15, [wc, yn_b], [pc])
                O.tt("dve", t1[:], pa[:], g_b[:, cc, :], ALU.mult, [pa, g_b], [t1])
                O.tt("dve", t2[:], pb[:], g_b[:, 8 + cc, :], ALU.mult, [pb, g_b], [t2])
                O.tt("pool", t1[:], t1[:], t2[:], ALU.add, [t1, t2], [t1])
                O.tt("dve", t2[:], pc[:], g_b[:, 16 + cc, :], ALU.mult, [pc, g_b], [t2])
                O.tt("pool", mT[:, cc, :], t1[:], t2[:], ALU.add, [t1, t2], [(mT, cc)])
            for tt in range(4):
                t = tb * 4 + tt
                S.dma("sp", xt[:], x_src[t * 128:(t + 1) * 128, :], r=[(x_src, t)], w=[xt])
                for hf in range(2):
                    ps = PB[nps % 6]
                    nps += 1
                    for k in range(8):
                        O.mm(ps[:], mT[:, k, tt * 128:(tt + 1) * 128], wo[:, k, hf * 512:(hf + 1) * 512], k == 0, k == 7,
                             [mT, wo], [ps])
                    O.tt("dve", xo[:, hf * 512:(hf + 1) * 512], ps[:], xt[:, hf * 512:(hf + 1) * 512], ALU.add,
                         [ps, xt], [(xo, hf)])
                S.dma("sp", C.xmid[t * 128:(t + 1) * 128, :], xo[:], r=[xo], w=[(C.xmid, t)])
        S.barrier()
        S.flush()
    if "xmid" in C.dbg and C.stop == (l, 5):
        return
    with ExitStack() as st:
        def sb(name, shape, dt):
            return st.enter_context(_sbt(nc, name, list(shape), dt))
        w2 = sb("p5w2", [128, 32, D_], BF16)
        w1b = [sb(f"p5w1_{i}", [128, 8, 512], BF16) for i in range(3)]
        xts = [sb(f"p5x{i}", [128, D_], F32) for i in range(4)]
        hT = sb("p5hT", [128, 8, 512], BF16)
        h1T = sb("p5h1T", [128, 32, 512], BF16)
        ub = [sb(f"p5ub{i}", [128, D_], BF16) for i in range(2)]
        sq = sb("p5sq", [128, D_], BF16)
        ss = [sb(f"p5ss{i}", [128, 1], F32) for i in range(2)]
        rr = [sb(f"p5r{i}", [128, 512], F32) for i in range(2)]
        xo = [sb(f"p5xo{i}", [128, D_], F32) for i in range(2)]
        nwt = sb("p5nw", [128, D_], F32)
        fnw = sb("p5fnw", [128, D_], F32)
        PB = [st.enter_context(_pst(nc, f"p5bps{i}", [128, 512], F32)) for i in range(6)]
        PT = [st.enter_context(_pst(nc, f"p5bpt{i}", [128, 8, 128], BF16)) for i in range(2)]
        S.dma("sp", w2[:], C.wb_2[l].rearrange("(k p) n -> p k n", p=128), r=[C.wb_2[l]], w=[w2])
        S.dma("sp", nwt[:], C.mlpw[l:l + 1, :].partition_broadcast(128), r=[], w=[nwt])
        S.dma("sp", fnw[:], C.finw.partition_broadcast(128), r=[], w=[fnw])
        w1src = C.wb_1[l].rearrange("(k p) n -> p k n", p=128)
        nps = 0
        nw1 = 0
        for tb in range(8):
            for tt in range(4):
                t = tb * 4 + tt
                S.dma("sp", xts[tt][:], C.xmid[t * 128:(t + 1) * 128, :], r=[(C.xmid, t)], w=[xts[tt]])
                rms_tile(C, xts[tt], ss[tt % 2], sq, nwt, ub[tt % 2], "p5")
                for k in range(8):
                    O.tr(PT[tt % 2][:, k, :], ub[tt % 2][:, k * 128:(k + 1) * 128], C.ident[:], [ub[tt % 2], C.ident], [PT[tt % 2]])
                O.copy("act", hT[:, :, tt * 128:(tt + 1) * 128], PT[tt % 2][:], [PT[tt % 2]], [(hT, tt)])
            for f4 in range(8):
                w1t = w1b[nw1 % 3]
                nw1 += 1
                S.dma("sp", w1t[:], w1src[:, :, f4 * 512:(f4 + 1) * 512], r=[C.wb_1[l]], w=[w1t])
                for fj in range(4):
                    fc = f4 * 4 + fj
                    ps = PB[nps % 6]
                    r_ = rr[nps % 2]
                    nps += 1
                    for k in range(8):
                        O.mm(ps[:], w1t[:, k, fj * 128:(fj + 1) * 128], hT[:, k, :], k == 0, k == 7, [w1t, hT], [ps])
                    O.act(r_[:], ps[:], AF.Relu, [ps], [r_])
                    O.tt("pool" if fc % 2 else "dve", h1T[:, fc, :], r_[:], r_[:], ALU.mult, [r_], [(h1T, fc)])
            for tt in range(4):
                t = tb * 4 + tt
                xo_ = xo[tt % 2]
                for hf in range(2):
                    ps = PB[nps % 6]
                    nps += 1
                    for fc in range(32):
                        O.mm(ps[:], h1T[:, fc, tt * 128:(tt + 1) * 128], w2[:, fc, hf * 512:(hf + 1) * 512], fc == 0, fc == 31,
                             [h1T, w2], [ps])
                    O.tt("dve", xo_[:, hf * 512:(hf + 1) * 512], ps[:], xts[tt][:, hf * 512:(hf + 1) * 512], ALU.add,
                         [ps, xts[tt]], [(xo_, hf)])
                if x_dst is not None:
                    S.dma("sp", x_dst[t * 128:(t + 1) * 128, :], xo_[:], r=[xo_], w=[(x_dst, t)])
                else:
                    s_ = ss[tt % 2]
                    O.act(sq[:], xo_[:], AF.Square, [xo_], [sq, s_], accum_out=s_[:])
                    O.act(s_[:], s_[:], AF.Sqrt, [s_], [s_], bias=1e-6, scale=1.0 / D_)
                    O.recip(s_[:], s_[:], [s_], [s_])
                    O.stt("dve", xo_[:], xo_[:], s_[:], fnw[:], ALU.mult, ALU.mult, [xo_, s_, fnw], [xo_])
                    S.dma("sp", C.y_out[t * 128:(t + 1) * 128, :], xo_[:], r=[xo_], w=[(C.y_out, t)])
        S.barrier()
        S.flush()


def host_consts():
    bf = ml_dtypes.bfloat16
    p = np.arange(128)[:, None]
    f = np.arange(128)[None, :]
    tri = np.stack([(p <= f), (p >= f), (p < f), (p > f)], axis=1).astype(np.float32)
    trif = np.stack([(p < f), (p > f), np.ones((128, 128), bool), (p <= f), (p >= f)], axis=1).astype(np.float32)
    invf = (500000.0 ** (-np.arange(0, 16, 2, dtype=np.float32) / 16.0)).astype(np.float32)[None, :]
    return {"c_ident": np.eye(128, dtype=np.float32).astype(bf), "c_tri": tri.astype(bf), "c_trif": trif, "c_invf": invf}


def prep_inputs(inp):
    f32 = np.float32
    sh = dict(host_consts())
    for k in ("mix_norm_w", "mlp_norm_w", "w_in", "w_a_out", "w_b_out", "w_c_out", "w_o", "w_ff1", "w_ff2"):
        sh[k] = np.ascontiguousarray(inp[k], dtype=f32)
    sh["final_norm_w"] = np.ascontiguousarray(inp["final_norm_w"], dtype=f32).reshape(1, D_)
    sh["conv_a_wT"] = np.ascontiguousarray(inp["conv_a_w"].reshape(L_, 3, 8, 128).transpose(0, 3, 2, 1), dtype=f32)
    sh["ssd_conv_wT"] = np.ascontiguousarray(inp["ssd_conv_w"].reshape(L_, 5, 24, 128).transpose(0, 3, 2, 1), dtype=f32)
    sh["ssd_conv_bT"] = np.ascontiguousarray(inp["ssd_conv_b"].reshape(L_, 24, 128).transpose(0, 2, 1), dtype=f32)
    sh["ssd_conv_bR"] = np.ascontiguousarray(inp["ssd_conv_b"].reshape(L_, 1, 3072), dtype=f32)
    sh["ssd_a_log"] = np.ascontiguousarray(inp["ssd_a_log"].reshape(L_, 1, 64), dtype=f32)
    sh["ssd_dt_bias"] = np.ascontiguousarray(inp["ssd_dt_bias"].reshape(L_, 1, 64), dtype=f32)
    sh["ssd_d"] = np.ascontiguousarray(inp["ssd_d"].reshape(L_, 1, 32), dtype=f32)
    sh["ssd_norm_w"] = np.ascontiguousarray(inp["ssd_norm_w"].reshape(L_, 1, 2048), dtype=f32)
    per = []
    for b in range(inp["x"].shape[0]):
        d = dict(sh)
        d["x"] = np.ascontiguousarray(inp["x"][b], dtype=f32)
        d["pos"] = np.ascontiguousarray(inp["positions"][b].reshape(NT_, 128).T, dtype=np.int32)
        per.append(d)
    return per


_NC_CACHE = {}


def kernel(**inputs):
    per = prep_inputs(inputs)
    if "nc" not in _NC_CACHE:
        _NC_CACHE["nc"] = build()
    nc = _NC_CACHE["nc"]
    res = run_bass_kernel_spmd(nc, per, core_ids=list(range(len(per))))
    return np.stack([np.asarray(r["y"], dtype=np.float32) for r in res.results], axis=0)
```

```python
import numpy as np
import ml_dtypes
import concourse.bass as bass
import concourse.mybir as mybir
from concourse.bass_utils import run_bass_kernel_spmd
from contextlib import ExitStack
from types import SimpleNamespace

F32 = mybir.dt.float32
BF16 = mybir.dt.bfloat16
I32 = mybir.dt.int32
AF = mybir.ActivationFunctionType
ALU = mybir.AluOpType
AX = mybir.AxisListType

ENG = ("pe", "act", "dve", "pool", "sp")
DMAQ = ("sp", "act", "pool")


class _Buf:
    __slots__ = ("w", "r")

    def __init__(self):
        self.w = None
        self.r = {}


class _TBuf:
    __slots__ = ("whole", "subs")

    def __init__(self):
        self.whole = _Buf()
        self.subs = {}


class Sched:
    NRING = 8
    SAME_ENGINE_SYNC = ("act", "dve", "pool")

    def __init__(self, nc):
        self.nc = nc
        self.sems = []
        self.esem = {}
        for e in ENG:
            self.esem[e] = self._new_sem("s_" + e)
        self.ecnt = {e: 0 for e in ENG}
        self.ring = {q: [self._new_sem(f"d_{q}{i}") for i in range(self.NRING)] for q in DMAQ}
        self.dman = {q: 0 for q in DMAQ}
        self.known = {e: {} for e in ENG}
        self.streams = {e: [] for e in ENG}
        self.tb = {}
        self.n_wait = 0
        self.n_ins = 0

    def _new_sem(self, name):
        h = self.nc.alloc_semaphore(name=name)
        self.sems.append(h)
        return len(self.sems) - 1

    def _spec(self, a):
        if isinstance(a, tuple):
            t, key = a
        else:
            t, key = a, None
        name = t if isinstance(t, str) else t.name
        tb = self.tb.get(name)
        if tb is None:
            tb = self.tb[name] = _TBuf()
        return tb, key

    def _deps(self, eng, reads, writes):
        need = {}

        def add(ev):
            if ev is not None and need.get(ev[0], 0) < ev[1]:
                need[ev[0]] = ev[1]

        def addr(b):
            for s, v in b.r.items():
                if need.get(s, 0) < v:
                    need[s] = v

        for a in reads:
            tb, key = self._spec(a)
            add(tb.whole.w)
            if key is None:
                for sb in tb.subs.values():
                    add(sb.w)
            else:
                sb = tb.subs.get(key)
                if sb is not None:
                    add(sb.w)
        for a in writes:
            tb, key = self._spec(a)
            add(tb.whole.w)
            addr(tb.whole)
            if key is None:
                for sb in tb.subs.values():
                    add(sb.w)
                    addr(sb)
            else:
                sb = tb.subs.get(key)
                if sb is not None:
                    add(sb.w)
                    addr(sb)
        kn = self.known[eng]
        own = self.esem[eng]
        waits = []
        for s, v in need.items():
            if s == own and eng not in self.SAME_ENGINE_SYNC:
                continue
            if kn.get(s, 0) < v:
                kn[s] = v
                waits.append((s, v))
        return waits

    def _record(self, ev, reads, writes):
        s, v = ev
        for a in reads:
            tb, key = self._spec(a)
            if key is None:
                b = tb.whole
            else:
                b = tb.subs.get(key)
                if b is None:
                    b = tb.subs[key] = _Buf()
            if b.r.get(s, 0) < v:
                b.r[s] = v
        for a in writes:
            tb, key = self._spec(a)
            if key is None:
                tb.whole.w = ev
                tb.whole.r = {}
                tb.subs = {}
            else:
                b = tb.subs.get(key)
                if b is None:
                    b = tb.subs[key] = _Buf()
                b.w = ev
                b.r = {}

    def op(self, eng, emit, r=(), w=()):
        waits = self._deps(eng, r, w)
        self.ecnt[eng] += 1
        ev = (self.esem[eng], self.ecnt[eng])
        self._record(ev, r, w)
        self.streams[eng].append((waits, emit, ev[0], 1))
        self.n_wait += len(waits)
        self.n_ins += 1

    def dma(self, q, out, in_, r=(), w=(), **kw):
        waits = self._deps(q, r, w)
        i = self.dman[q]
        self.dman[q] += 1
        s = self.ring[q][i % self.NRING]
        rnd = i // self.NRING
        if rnd > 0:
            kn = self.known[q]
            if kn.get(s, 0) < 16 * rnd:
                kn[s] = 16 * rnd
                waits.append((s, 16 * rnd))
        ev = (s, 16 * (rnd + 1))
        self._record(ev, r, w)
        self.streams[q].append((waits, lambda e: e.dma_start(out=out, in_=in_, **kw), s, 16))
        self.n_wait += len(waits)
        self.n_ins += 1

    def barrier(self, skip_q=()):
        evs = []
        for e in ENG:
            if self.ecnt[e] > 0:
                evs.append((self.esem[e], self.ecnt[e]))
        for q in DMAQ:
            if q in skip_q:
                continue
            n = self.dman[q]
            for j in range(min(n, self.NRING)):
                last = ((n - 1 - j) // self.NRING) * self.NRING + j
                evs.append((self.ring[q][j], 16 * (last // self.NRING + 1)))
        for e in ENG:
            kn = self.known[e]
            waits = []
            for s, v in evs:
                if kn.get(s, 0) < v:
                    kn[s] = v
                    waits.append((s, v))
            if waits:
                self.streams[e].append((waits, None, None, 0))
                self.n_wait += len(waits)

    def flush(self):
        nc = self.nc
        sems = self.sems

        def run(name):
            items = self.streams[name]
            self.streams[name] = []

            def f(e):
                for waits, emit, s, inc in items:
                    for ws, wv in waits:
                        e.wait_ge(sems[ws], wv)
                    if emit is not None:
                        ins = emit(e)
                        ins.then_inc(sems[s], inc)

            return f

        with nc.Block() as block:
            block.tensor(run("pe"))
            block.scalar(run("act"))
            block.vector(run("dve"))
            block.gpsimd(run("pool"))
            block.sync(run("sp"))


S_ = 4096
D_ = 1024
NT_ = 32
L_ = 2
DIN_ = 14400
PI = 3.141592653589793


_UID = [0]


def _sbt(nc, name, shape, dt):
    _UID[0] += 1
    return nc.sbuf_tensor(f"{name}_u{_UID[0]}", shape, dt)


def _pst(nc, name, shape, dt):
    _UID[0] += 1
    return nc.psum_tensor(f"{name}_u{_UID[0]}", shape, dt)


class Ops:
    def __init__(self, S):
        self.S = S

    def mm(self, out, lhsT, rhs, start, stop, r, w):
        self.S.op("pe", lambda e: e.matmul(out, lhsT=lhsT, rhs=rhs, start=start, stop=stop), r, w)

    def tr(self, out, in_, ident, r, w):
        self.S.op("pe", lambda e: e.transpose(out=out, in_=in_, identity=ident), r, w)

    def act(self, out, in_, func, r, w, **kw):
        self.S.op("act", lambda e: e.activation(out=out, in_=in_, func=func, **kw), r, w)

    def tt(self, eng, out, in0, in1, op, r, w):
        self.S.op(eng, lambda e: e.tensor_tensor(out=out, in0=in0, in1=in1, op=op), r, w)

    def ts(self, eng, out, in0, s1, s2, op0, op1, r, w):
        if s2 is None:
            self.S.op(eng, lambda e: e.tensor_scalar(out=out, in0=in0, scalar1=s1, scalar2=None, op0=op0), r, w)
        else:
            self.S.op(eng, lambda e: e.tensor_scalar(out=out, in0=in0, scalar1=s1, scalar2=s2, op0=op0, op1=op1), r, w)

    def stt(self, eng, out, in0, scalar, in1, op0, op1, r, w):
        self.S.op(eng, lambda e: e.scalar_tensor_tensor(out=out, in0=in0, scalar=scalar, in1=in1, op0=op0, op1=op1), r, w)

    def copy(self, eng, out, in_, r, w):
        if eng == "act":
            self.S.op(eng, lambda e: e.copy(out=out, in_=in_), r, w)
        else:
            self.S.op(eng, lambda e: e.tensor_copy(out=out, in_=in_), r, w)

    def memset(self, eng, ap, val, w):
        self.S.op(eng, lambda e: e.memset(ap, val), (), w)

    def recip(self, out, in_, r, w):
        self.S.op("dve", lambda e: e.reciprocal(out=out, in_=in_), r, w)


def build(dbg=(), stop=None, nlayers=L_):
    nc = bass.Bass("TRN2", target_bir_lowering=False)

    def din(name, shape, dt=F32):
        return nc.dram_tensor(name, list(shape), dt, kind="ExternalInput").ap()

    def dscr(name, shape, dt):
        kind = "ExternalOutput" if name in dbg else "Internal"
        return nc.dram_tensor(name, list(shape), dt, kind=kind).ap()

    x_in = din("x", [S_, D_])
    pos_in = din("pos", [128, NT_], I32)
    mixw = din("mix_norm_w", [L_, D_])
    mlpw = din("mlp_norm_w", [L_, D_])
    finw = din("final_norm_w", [1, D_])
    w_in = din("w_in", [L_, D_, DIN_])
    w_a = din("w_a_out", [L_, D_, D_])
    w_b = din("w_b_out", [L_, D_, D_])
    w_c = din("w_c_out", [L_, 2 * D_, D_])
    w_o = din("w_o", [L_, D_, D_])
    w_1 = din("w_ff1", [L_, D_, 4 * D_])
    w_2 = din("w_ff2", [L_, 4 * D_, D_])
    cawT = din("conv_a_wT", [L_, 128, 8, 3])
    scwT = din("ssd_conv_wT", [L_, 128, 24, 5])
    scbT = din("ssd_conv_bT", [L_, 128, 24])
    scbR = din("ssd_conv_bR", [L_, 1, 3072])
    alog = din("ssd_a_log", [L_, 1, 64])
    dtb = din("ssd_dt_bias", [L_, 1, 64])
    sdd = din("ssd_d", [L_, 1, 32])
    snw = din("ssd_norm_w", [L_, 1, 2048])
    c_ident = din("c_ident", [128, 128], BF16)
    c_tri = din("c_tri", [128, 4, 128], BF16)
    c_trif = din("c_trif", [128, 5, 128], F32)
    c_invf = din("c_invf", [1, 8], F32)
    y_out = nc.dram_tensor("y", [S_, D_], F32, kind="ExternalOutput").ap()

    wb_in = [dscr(f"wb_in{l}", [D_, DIN_], BF16) for l in range(L_)]
    wb_a = [dscr(f"wb_a{l}", [D_, D_], BF16) for l in range(L_)]
    wb_b = [dscr(f"wb_b{l}", [D_, D_], BF16) for l in range(L_)]
    wb_c = [dscr(f"wb_c{l}", [2 * D_, D_], BF16) for l in range(L_)]
    wb_o = [dscr(f"wb_o{l}", [D_, D_], BF16) for l in range(L_)]
    wb_1 = [dscr(f"wb_1{l}", [D_, 4 * D_], BF16) for l in range(L_)]
    wb_2 = [dscr(f"wb_2{l}", [4 * D_, D_], BF16) for l in range(L_)]
    yAT = dscr("yAT", [D_, S_], BF16)
    gT = dscr("gT", [3 * D_, S_], BF16)
    qT = dscr("qT", [D_, S_], BF16)
    kT = dscr("kT", [D_, S_], BF16)
    v_d = dscr("v_d", [S_, D_], BF16)
    sz_d = dscr("sz_d", [S_, 2 * D_], BF16)
    xs_d = dscr("xs_d", [S_, 2 * D_], BF16)
    Bt_d = dscr("Bt_d", [S_, 512], BF16)
    BT_d = dscr("BT_d", [512, S_], BF16)
    CT_d = dscr("CT_d", [512, S_], BF16)
    dt_dbg = dscr("dt_dbg", [128, NT_, 64], F32)
    o_d = [dscr(f"o_d{p}", [S_, 16, 65], F32) for p in range(3)]
    hin_d = [dscr(f"hin_d{d}", [NT_, 128, 2048], BF16) for d in range(2)]
    ynT_d = dscr("ynT_d", [2 * D_, S_], BF16)
    xmid = dscr("xmid", [S_, D_], F32)
    xl = [dscr(f"xl{l}", [S_, D_], F32) for l in range(L_ - 1)]

    S = Sched(nc)
    O = Ops(S)

    with ExitStack() as gst:
        def gsb(name, shape, dt):
            return gst.enter_context(_sbt(nc, name, list(shape), dt))

        ident = gsb("ident", [128, 128], BF16)
        tri = gsb("tri", [128, 4, 128], BF16)
        trif = gsb("trif", [128, 5, 128], F32)
        cosT = gsb("cosT", [128, NT_, 8], F32)
        sinT = gsb("sinT", [128, NT_, 8], F32)
        dt_all = gsb("dt_all", [128, NT_, 64], F32)
        la_all = gsb("la_all", [128, NT_, 64], F32)
        S.dma("sp", ident[:], c_ident, r=[], w=[ident])
        S.dma("sp", tri[:], c_tri, r=[], w=[tri])
        S.dma("sp", trif[:], c_trif, r=[], w=[trif])

        def cast_w(src, dst, rows_per):
            R = src.shape[0]
            for r0 in range(0, R, rows_per):
                S.dma("pool", dst[r0:r0 + rows_per, :], src[r0:r0 + rows_per, :], r=[], w=[(dst, r0)])

        with ExitStack() as st:
            def sb(name, shape, dt):
                return st.enter_context(_sbt(nc, name, list(shape), dt))
            posi = sb("posi", [128, NT_], I32)
            posf = sb("posf", [128, NT_], F32)
            invf = sb("invf", [128, 8], F32)
            ang = sb("ang", [128, NT_, 8], F32)
            a1 = sb("a1", [128, NT_, 8], F32)
            S.dma("sp", posi[:], pos_in, r=[], w=[posi])
            S.dma("sp", invf[:], c_invf.partition_broadcast(128), r=[], w=[invf])
            O.copy("dve", posf[:], posi[:], [posi], [posf])
            O.tt("dve", ang[:], posf[:].unsqueeze(2).to_broadcast([128, NT_, 8]),
                 invf[:].unsqueeze(1).to_broadcast([128, NT_, 8]), ALU.mult, [posf, invf], [ang])
            ki = sb("ki", [128, NT_, 8], I32)
            kf = sb("kf", [128, NT_, 8], F32)
            mm_ = sb("mm_", [128, NT_, 8], F32)
            for shift, dstT in ((0.0, sinT), (0.5 * PI, cosT)):
                O.ts("dve", a1[:], ang[:], shift, 1.0 / (2 * PI), ALU.add, ALU.mult, [ang], [a1])
                O.copy("dve", ki[:], a1[:], [a1], [ki])
                O.copy("dve", kf[:], ki[:], [ki], [kf])
                O.ts("dve", a1[:], ang[:], shift, None, ALU.add, None, [ang], [a1])
                O.stt("dve", a1[:], kf[:], -2 * PI, a1[:], ALU.mult, ALU.add, [kf, a1], [a1])
                O.ts("dve", mm_[:], a1[:], PI, 2 * PI, ALU.is_ge, ALU.mult, [a1], [mm_])
                O.tt("dve", a1[:], a1[:], mm_[:], ALU.subtract, [a1, mm_], [a1])
                O.ts("dve", mm_[:], a1[:], -PI, 2 * PI, ALU.is_lt, ALU.mult, [a1], [mm_])
                O.tt("dve", a1[:], a1[:], mm_[:], ALU.add, [a1, mm_], [a1])
                O.ts("dve", a1[:], a1[:], -PI, PI, ALU.max, ALU.min, [a1], [a1])
                O.act(dstT[:], a1[:], AF.Sin, [a1], [dstT])
            S.barrier()
            S.flush()

        for l in range(nlayers):
            x_src = x_in if l == 0 else xl[l - 1]
            x_dst = xl[l] if l < L_ - 1 else None
            C = SimpleNamespace(**locals())
            layer(C, l, x_src, x_dst, stop)
            if stop is not None and stop[0] == l:
                break
        S.barrier(skip_q=())
        S.flush()
    return nc


def layer(C, l, x_src, x_dst, stop):
    nc, S, O = C.nc, C.S, C.O
    with ExitStack() as lst:
        uT = lst.enter_context(_sbt(nc, "uT", [128, 8, S_], BF16))
        phase1(C, l, x_src, uT)
        if stop == (l, 1):
            return
        phase2(C, l, uT)
    S.barrier()
    S.flush()
    if stop == (l, 2):
        return
    phase3(C, l)
    if stop == (l, 3):
        return
    phase4(C, l)
    if stop == (l, 4):
        return
    phase5(C, l, x_src, x_dst)


def rms_tile(C, xt, ss, sq, nwt, ub, tag):
    O = C.O
    O.act(sq[:], xt[:], AF.Square, [xt], [sq, ss], accum_out=ss[:])
    O.act(ss[:], ss[:], AF.Sqrt, [ss], [ss], bias=1e-6, scale=1.0 / D_)
    O.recip(ss[:], ss[:], [ss], [ss])
    O.stt("dve", ub[:], xt[:], ss[:], nwt[:], ALU.mult, ALU.mult, [xt, ss, nwt], [ub])


def phase1(C, l, x_src, uT):
    nc, S, O = C.nc, C.S, C.O
    with ExitStack() as st:
        def sb(name, shape, dt):
            return st.enter_context(_sbt(nc, name, list(shape), dt))
        xt = [sb(f"p1x{i}", [128, D_], F32) for i in range(2)]
        sq = sb("p1sq", [128, D_], BF16)
        ss = [sb(f"p1ss{i}", [128, 1], F32) for i in range(2)]
        ub = [sb(f"p1ub{i}", [128, D_], BF16) for i in range(2)]
        nwt = sb("p1nw", [128, D_], F32)
        pT = [st.enter_context(_pst(nc, f"p1pT{i}", [128, 8, 128], BF16)) for i in range(2)]
        S.dma("sp", nwt[:], C.mixw[l:l + 1, :].partition_broadcast(128), r=[], w=[nwt])
        for t in range(NT_):
            b = t % 2
            S.dma("sp", xt[b][:], x_src[t * 128:(t + 1) * 128, :], r=[(x_src, t)], w=[xt[b]])
            rms_tile(C, xt[b], ss[b], sq, nwt, ub[b], "p1")
            for k in range(8):
                O.tr(pT[b][:, k, :], ub[b][:, k * 128:(k + 1) * 128], C.ident[:], [ub[b], C.ident], [pT[b]])
            O.copy("act" if t % 2 else "dve", uT[:, :, t * 128:(t + 1) * 128], pT[b][:], [pT[b]], [(uT, t)])
        S.barrier()
        S.flush()


def phase2(C, l, uT):
    nc, S, O = C.nc, C.S, C.O
    wsrc = C.w_in[l].rearrange("(k p) n -> p k n", p=128)
    seq = []
    for g in range(2):
        seq += [(512 * g, 512), (1024 + 512 * g, 512), (2048 + 512 * g, 512)]
    seq += [(3072 + 512 * i, 512) for i in range(4)]
    seq += [(5120 + 512 * i, 512) for i in range(2)]
    seq += [(6144 + 512 * i, 512) for i in range(4)]
    seq += [(8192 + 512 * i, 512) for i in range(6)]
    seq += [(11264, 64)]
    seq += [(11328 + 512 * i, 512) for i in range(6)]
    with ExitStack() as st:
        wbuf = [st.enter_context(_sbt(nc, f"p2w{i}", [128, 8, 512], BF16)) for i in range(3)]
        PB = [st.enter_context(_pst(nc, f"p2ps{i}", [128, 512], F32)) for i in range(6)]
        PT = [st.enter_context(_pst(nc, f"p2pt{i}", [128, 4, 128], BF16)) for i in range(2)]
        state = {"issued": 0, "ps": 0, "done": 0}

        def prefetch():
            while state["issued"] < min(len(seq), state["done"] + 3):
                i = state["issued"]
                c0, ncw = seq[i]
                S.dma("pool", wbuf[i % 3][:, :, :ncw], wsrc[:, :, c0:c0 + ncw], r=[], w=[wbuf[i % 3]])
                state["issued"] += 1

        def get_w(idx):
            prefetch()
            assert idx < state["issued"]
            return wbuf[idx % 3]

        def release(n):
            state["done"] = n
            prefetch()

        def PS():
            p = PB[state["ps"] % len(PB)]
            state["ps"] += 1
            return p

        def mm_feat(ps, wt, jj, tb):
            for k in range(8):
                O.mm(ps[:], wt[:, k, jj * 128:(jj + 1) * 128], uT[:, k, tb * 512:(tb + 1) * 512], k == 0, k == 7,
                     [wt, uT], [ps])

        def mm_tok(ps_ap, ps, wt, t, ncw):
            for k in range(8):
                O.mm(ps_ap, uT[:, k, t * 128:(t + 1) * 128], wt[:, k, :ncw], k == 0, k == 7, [wt, uT], [ps])

        wi = 0
        with ExitStack() as s2:
            def sb(name, shape, dt):
                return s2.enter_context(_sbt(nc, name, list(shape), dt))
            bT = [sb(f"p2bT{i}", [128, S_], F32) for i in range(2)]
            cst = [sb(f"p2cst{i}", [128, 512], F32) for i in range(2)]
            cx = [sb(f"p2cx{i}", [128, S_ + 2], BF16) for i in range(2)]
            yAo = [sb(f"p2yAo{i}", [128, S_], BF16) for i in range(2)]
            dgA = [sb(f"p2dgA{i}", [128, 3, 128], BF16) for i in range(2)]
            wA = sb("p2wA", [128, 8, 3], F32)
            S.dma("sp", wA[:], C.cawT[l], r=[], w=[wA])
            for i in range(2):
                O.memset("dve", cx[i][:, 0:1], 0.0, [(cx[i], "l")])
                O.memset("dve", cx[i][:, S_ + 1:S_ + 2], 0.0, [(cx[i], "r")])

            def convA(j):
                p = j % 2
                for tb in range(8):
                    ps = PS()
                    for k3 in range(3):
                        O.mm(ps[:], dgA[p][:, k3, :], cx[p][:, tb * 512 + k3:tb * 512 + k3 + 512], k3 == 0, k3 == 2,
                             [dgA[p], cx[p]], [ps])
                    O.tt("dve", yAo[p][:, tb * 512:(tb + 1) * 512], ps[:], bT[p][:, tb * 512:(tb + 1) * 512], ALU.mult,
                         [ps, bT[p]], [(yAo[p], tb)])
                S.dma("sp", C.yAT[j * 128:(j + 1) * 128, :], yAo[p][:], r=[yAo[p]], w=[(C.yAT, j)])

            for g in range(2):
                release(wi)
                wt_b, wt_c, wt_x = get_w(wi), get_w(wi + 1), get_w(wi + 2)
                wi += 3
                for jj in range(4):
                    j = 4 * g + jj
                    p = j % 2
                    for k3 in range(3):
                        O.ts("dve", dgA[p][:, k3, :], C.ident[:], wA[:, j, k3:k3 + 1], None, ALU.mult, None,
                             [C.ident, wA], [(dgA[p], k3)])
                    for tb in range(8):
                        pb, pc, px = PS(), PS(), PS()
                        mm_feat(pb, wt_b, jj, tb)
                        mm_feat(pc, wt_c, jj, tb)
                        mm_feat(px, wt_x, jj, tb)
                        O.copy("act", bT[p][:, tb * 512:(tb + 1) * 512], pb[:], [pb], [(bT[p], tb)])
                        O.copy("act", cst[tb % 2][:], pc[:], [pc], [cst[tb % 2]])
                        O.tt("dve", cx[p][:, 1 + tb * 512:1 + (tb + 1) * 512], px[:], cst[tb % 2][:], ALU.mult,
                             [px, cst[tb % 2]], [(cx[p], tb)])
                        if tb == 1 and j > 0:
                            convA(j - 1)
            convA(7)
            S.barrier()
            S.flush()
        with ExitStack() as s2:
            def sb(name, shape, dt):
                return s2.enter_context(_sbt(nc, name, list(shape), dt))
            qs = [sb(f"p2qs{i}", [128, 8, 64], F32) for i in range(3)]
            tmp = [sb(f"p2tmp{i}", [128, 4, 8, 8], F32) for i in range(3)]
            qr = [sb(f"p2qr{i}", [128, 512], BF16) for i in range(3)]
            stg = [sb(f"p2stg{i}", [128, 4, S_], BF16) for i in range(2)]
            for ci in range(4):
                release(wi)
                wt = get_w(wi)
                wi += 1
                sg = stg[ci % 2]
                def qk_front(t):
                    b = t % 3
                    ps = PS()
                    mm_tok(ps[:], ps, wt, t, 512)
                    O.copy("act", qs[b][:].rearrange("p h d -> p (h d)"), ps[:], [ps], [qs[b]])
                    cs = C.cosT[:, t, :].unsqueeze(1).to_broadcast([128, 8, 8])
                    sn = C.sinT[:, t, :].unsqueeze(1).to_broadcast([128, 8, 8])
                    t1 = qs[b][:, :, 0:8]
                    t2 = qs[b][:, :, 8:16]
                    tm = tmp[b]
                    O.tt("dve", tm[:, 0], t1, cs, ALU.mult, [qs[b], C.cosT], [(tm, 0)])
                    O.tt("dve", tm[:, 1], t2, sn, ALU.mult, [qs[b], C.sinT], [(tm, 1)])
                    O.tt("dve", tm[:, 2], t2, cs, ALU.mult, [qs[b], C.cosT], [(tm, 2)])
                    O.tt("dve", tm[:, 3], t1, sn, ALU.mult, [qs[b], C.sinT], [(tm, 3)])
                    O.tt("dve", t1, tm[:, 0], tm[:, 1], ALU.subtract, [(tm, 0), (tm, 1)], [(qs[b], "a")])
                    O.tt("dve", t2, tm[:, 2], tm[:, 3], ALU.add, [(tm, 2), (tm, 3)], [(qs[b], "b")])
                    O.copy("act", qr[b][:], qs[b][:].rearrange("p h d -> p (h d)"), [qs[b]], [qr[b]])

                def qk_back(t):
                    b = t % 3
                    pt = PT[t % 2]
                    for jj in range(4):
                        O.tr(pt[:, jj, :], qr[b][:, jj * 128:(jj + 1) * 128], C.ident[:], [qr[b], C.ident], [pt])
                    O.copy("act", sg[:, :, t * 128:(t + 1) * 128], pt[:], [pt], [(sg, t)])

                for t in range(NT_ + 2):
                    if t < NT_:
                        qk_front(t)
                    if 0 <= t - 2 < NT_:
                        qk_back(t - 2)
                dst = C.qT if ci < 2 else C.kT
                for jj in range(4):
                    r0 = (ci % 2) * 512 + jj * 128
                    S.dma("sp", dst[r0:r0 + 128, :], sg[:, jj, :], r=[sg], w=[(dst, r0)])
            S.barrier()
            S.flush()
        with ExitStack() as s2:
            def sb(name, shape, dt):
                return s2.enter_context(_sbt(nc, name, list(shape), dt))
            vst = [sb(f"p2vst{i}", [128, 512], BF16) for i in range(4)]
            n = 0
            for ci in range(6):
                release(wi)
                wt = get_w(wi)
                wi += 1
                for t in range(NT_):
                    ps = PS()
                    mm_tok(ps[:], ps, wt, t, 512)
                    vs = vst[n % 4]
                    n += 1
                    if ci < 2:
                        O.copy("act", vs[:], ps[:], [ps], [vs])
                        S.dma("sp", C.v_d[t * 128:(t + 1) * 128, ci * 512:(ci + 1) * 512], vs[:], r=[vs], w=[(C.v_d, (t, ci))])
                    else:
                        O.act(vs[:], ps[:], AF.Silu, [ps], [vs])
                        c2 = ci - 2
                        S.dma("sp", C.sz_d[t * 128:(t + 1) * 128, c2 * 512:(c2 + 1) * 512], vs[:], r=[vs], w=[(C.sz_d, (t, c2))])
            S.barrier()
            S.flush()
        with ExitStack() as s2:
            def sb(name, shape, dt):
                return s2.enter_context(_sbt(nc, name, list(shape), dt))
            xin = sb("p2xin", [128, 4, S_ + 4], BF16)
            diag = sb("p2diag", [128, 4, 5, 128], BF16)
            stok = [sb(f"p2stok{i}", [128, 8, 512], BF16) for i in range(2)]
            sfeat = [sb(f"p2sfeat{i}", [128, S_], BF16) for i in range(2)]
            wS = sb("p2wS", [128, 24, 5], F32)
            bS = sb("p2bS", [128, 24], F32)
            brf = sb("p2brf", [1, 3072], F32)
            brow = sb("p2brow", [1, 3072], BF16)
            ones = sb("p2ones", [1, 128], BF16)
            S.dma("sp", wS[:], C.scwT[l], r=[], w=[wS])
            S.dma("sp", bS[:], C.scbT[l], r=[], w=[bS])
            S.dma("sp", brf[:], C.scbR[l], r=[], w=[brf])
            O.copy("dve", brow[:], brf[:], [brf], [brow])
            O.memset("dve", ones[:], 1.0, [ones])
            O.memset("dve", xin[:, :, 0:2], 0.0, [(xin, "l")])
            O.memset("dve", xin[:, :, S_ + 2:S_ + 4], 0.0, [(xin, "r")])
            nf = 0
            for ci in range(6):
                release(wi)
                wt = get_w(wi)
                wi += 1
                for jj in range(4):
                    J = 4 * ci + jj
                    for tb in range(8):
                        ps = PS()
                        mm_feat(ps, wt, jj, tb)
                        O.copy("act" if tb % 2 else "dve", xin[:, jj, 2 + tb * 512:2 + (tb + 1) * 512], ps[:], [ps], [(xin, (jj, tb))])
                    for k5 in range(5):
                        O.ts("dve", diag[:, jj, k5, :], C.ident[:], wS[:, J, k5:k5 + 1], None, ALU.mult, None,
                             [C.ident, wS], [(diag, (jj, k5))])
                if ci <= 4:
                    for t in range(NT_):
                        ps = PS()
                        for jj in range(4):
                            J = 4 * ci + jj
                            o_ap = ps[:, jj * 128:(jj + 1) * 128]
                            for k5 in range(5):
                                O.mm(o_ap, xin[:, jj, t * 128 + k5:t * 128 + k5 + 128], diag[:, jj, k5, :], k5 == 0, False,
                                     [xin, diag], [ps])
                            O.mm(o_ap, ones[0:1, :], brow[0:1, J * 128:(J + 1) * 128], False, True, [ones, brow], [ps])
                        sk = stok[(t // 8) % 2]
                        O.act(sk[:, t % 8, :], ps[:], AF.Silu, [ps], [(sk, t % 8)])
                        if t % 8 == 7:
                            t0 = t - 7
                            if ci < 4:
                                dst = C.xs_d[t0 * 128:(t0 + 8) * 128, ci * 512:(ci + 1) * 512]
                                key = (C.xs_d, (t0, ci))
                            else:
                                dst = C.Bt_d[t0 * 128:(t0 + 8) * 128, :]
                                key = (C.Bt_d, t0)
                            S.dma("sp", dst.rearrange("(t p) c -> p t c", p=128), sk[:], r=[sk], w=[key])
                if ci >= 4:
                    for jj in range(4):
                        J = 4 * ci + jj
                        sf = sfeat[nf % 2]
                        nf += 1
                        for tb in range(8):
                            ps = PS()
                            for k5 in range(5):
                                O.mm(ps[:], diag[:, jj, k5, :], xin[:, jj, tb * 512 + k5:tb * 512 + k5 + 512], k5 == 0, k5 == 4,
                                     [xin, diag], [ps])
                            O.act(sf[:, tb * 512:(tb + 1) * 512], ps[:], AF.Silu, [ps, bS], [(sf, tb)], bias=bS[:, J:J + 1])
                        dst = C.BT_d if ci == 4 else C.CT_d
                        S.dma("sp", dst[jj * 128:(jj + 1) * 128, :], sf[:], r=[sf], w=[(dst, jj)])
            S.barrier()
            S.flush()
        with ExitStack() as s2:
            def sb(name, shape, dt):
                return s2.enter_context(_sbt(nc, name, list(shape), dt))
            dtbb = sb("p2dtbb", [128, 64], F32)
            abc = sb("p2abc", [128, 64], F32)
            tdt = [sb(f"p2tdt{i}", [128, 64], F32) for i in range(2)]
            S.dma("sp", dtbb[:], C.dtb[l].partition_broadcast(128), r=[], w=[dtbb])
            S.dma("sp", abc[:], C.alog[l].partition_broadcast(128), r=[], w=[abc])
            O.act(abc[:], abc[:], AF.Exp, [abc], [abc])
            O.ts("dve", abc[:], abc[:], -1.0, None, ALU.mult, None, [abc], [abc])
            release(wi)
            wt = get_w(wi)
            wi += 1
            for t in range(NT_):
                ps = PS()
                mm_tok(ps[:, 0:64], ps, wt, t, 64)
                td = tdt[t % 2]
                O.tt("dve", td[:], ps[:, 0:64], dtbb[:], ALU.add, [ps, dtbb], [td])
                O.act(td[:], td[:], AF.Exp, [td], [td])
                O.act(C.dt_all[:, t, :], td[:], AF.Ln, [td], [(C.dt_all, t)], bias=1.0)
                O.tt("dve", C.la_all[:, t, :], C.dt_all[:, t, :], abc[:], ALU.mult, [(C.dt_all, t), abc], [(C.la_all, t)])
            if "dt_dbg" in C.dbg:
                S.dma("sp", C.dt_dbg, C.dt_all[:], r=[C.dt_all], w=[C.dt_dbg])
            S.barrier()
            S.flush()
        with ExitStack() as s2:
            def sb(name, shape, dt):
                return s2.enter_context(_sbt(nc, name, list(shape), dt))
            sfeat = [sb(f"p2gfeat{i}", [128, S_], BF16) for i in range(2)]
            nf = 0
            for ci in range(6):
                release(wi)
                wt = get_w(wi)
                wi += 1
                for jj in range(4):
                    G = 4 * ci + jj
                    sf = sfeat[nf % 2]
                    nf += 1
                    for tb in range(8):
                        ps = PS()
                        mm_feat(ps, wt, jj, tb)
                        O.act(sf[:, tb * 512:(tb + 1) * 512], ps[:], AF.Sigmoid, [ps], [(sf, tb)])
                    S.dma("sp", C.gT[G * 128:(G + 1) * 128, :], sf[:], r=[sf], w=[(C.gT, G)])
            S.barrier()
            S.flush()
        assert wi == len(seq)


def phase3(C, l):
    nc, S, O = C.nc, C.S, C.O
    C.cast_w(C.w_1[l], C.wb_1[l], 128)
    pats = [(1, 33), (4, 9), (16, 3)]
    with ExitStack() as st:
        def sb(name, shape, dt):
            return st.enter_context(_sbt(nc, name, list(shape), dt))
        qTh = [sb(f"p3q{i}", [64, S_], BF16) for i in range(2)]
        kTp = [sb(f"p3k{i}", [64, S_ + 2048], BF16) for i in range(2)]
        vP = [[sb(f"p3v{pi}_{b}", [128, nb, dil, 65], BF16) for b in range(2)] for pi, (dil, nb) in enumerate(pats)]
        mask = sb("p3mask", [128, 2, 128], BF16)
        PTs = [sb(f"p3pt{i}", [128, 2, 2, 128], BF16) for i in range(2)]
        PTm = [sb(f"p3pm{i}", [128, 2, 2, 128], BF16) for i in range(2)]
        ost = [sb(f"p3ost{i}", [128, 32, 65], F32) for i in range(2)]
        PSs = [st.enter_context(_pst(nc, f"p3ps{i}", [128, 512], F32)) for i in range(3)]
        PSo = [st.enter_context(_pst(nc, f"p3po{i}", [128, 2, 65], F32)) for i in range(3)]
        O.copy("dve", mask[:, 0, :], C.tri[:, 1, :], [C.tri], [(mask, 0)])
        O.copy("dve", mask[:, 1, :], C.tri[:, 0, :], [C.tri], [(mask, 1)])
        for b in range(2):
            O.memset("dve", kTp[b][:, 0:1024], 0.0, [(kTp[b], "l")])
            O.memset("dve", kTp[b][:, 1024 + S_:2048 + S_], 0.0, [(kTp[b], "r")])
            for pi, (dil, nb) in enumerate(pats):
                v = vP[pi][b]
                O.memset("pool", v[:], 0.0, [v])
                O.memset("pool", v[:, :, :, 64:65], 1.0, [v])
                O.memset("pool", v[0:64, 0, :, 64:65], 0.0, [v])
                O.memset("pool", v[64:128, nb - 1, :, 64:65], 0.0, [v])
        def loads(h):
            b = h % 2
            S.dma("sp", qTh[b][:], C.qT[h * 64:(h + 1) * 64, :], r=[C.qT], w=[qTh[b]])
            S.dma("sp", kTp[b][:, 1024:1024 + S_], C.kT[h * 64:(h + 1) * 64, :], r=[C.kT], w=[(kTp[b], "m")])
            vsrc = C.v_d[:, h * 64:(h + 1) * 64]
            for pi, (dil, nb) in enumerate(pats):
                v = vP[pi][b]
                nin = nb - 2
                if dil == 1:
                    for j0 in range(0, nin, 8):
                        j1 = min(nin, j0 + 8)
                        src = vsrc[64 + j0 * 128:64 + j1 * 128, :]
                        S.dma("sp", v[:, 1 + j0:1 + j1, 0, 0:64], src.rearrange("(j i) d -> i j d", i=128),
                              r=[C.v_d], w=[(v, ("in", j0))])
                else:
                    for j0 in range(nin):
                        src = vsrc[64 * dil + j0 * 128 * dil:64 * dil + (j0 + 1) * 128 * dil, :]
                        S.dma("sp", v[:, 1 + j0, :, 0:64], src.rearrange("(i r) d -> i r d", r=dil),
                              r=[C.v_d], w=[(v, ("in", j0))])
                S.dma("sp", v[64:128, 0, :, 0:64], vsrc[0:64 * dil, :].rearrange("(i r) d -> i r d", r=dil),
                      r=[C.v_d], w=[(v, "first")])
                S.dma("sp", v[0:64, nb - 1, :, 0:64], vsrc[S_ - 64 * dil:S_, :].rearrange("(i r) d -> i r d", r=dil),
                      r=[C.v_d], w=[(v, "last")])

        def store(h, pi, osb):
            dil, nb = pats[pi]
            nqb = nb - 1
            osv = osb[:].rearrange("p (q r) e -> p q r e", r=dil)
            if dil == 1:
                S.dma("sp", C.o_d[pi][:, h, :].rearrange("(q i) e -> i q e", i=128), osb[:], r=[osb],
                      w=[(C.o_d[pi], h)])
            else:
                for q in range(nqb):
                    S.dma("sp", C.o_d[pi][q * 128 * dil:(q + 1) * 128 * dil, h, :].rearrange("(i r) e -> i r e", r=dil),
                          osv[:, q, :, :], r=[osb], w=[(C.o_d[pi], (h, q))])

        pairs = []
        n_ost = 0
        for h in range(16):
            first = True
            for pi, (dil, nb) in enumerate(pats):
                osb = ost[n_ost % 2]
                n_ost += 1
                nqb = nb - 1
                lst = [(r, qp) for r in range(dil) for qp in range(nqb // 2)]
                for idx, (r, qp) in enumerate(lst):
                    pairs.append(dict(h=h, pi=pi, dil=dil, r=r, qp=qp, osb=osb, pre=None,
                                      post=(h, pi, osb) if idx == len(lst) - 1 else None))
        npairs_h = len(pairs) // 16
        for h in range(16):
            if h == 0:
                pairs[0]["pre"] = 0
            if h + 1 < 16:
                pairs[h * npairs_h + 4]["pre"] = h + 1
        NPS = 3

        def stA(i):
            p = pairs[i]
            b = p["h"] % 2
            dil, r, qp = p["dil"], p["r"], p["qp"]
            ps = PSs[i % NPS]
            psv = ps[:].rearrange("p (a b c) -> p a b c", a=2, b=2)
            for qi in range(2):
                qb = 2 * qp + qi
                q0 = 128 * dil * qb + r
                qsl = qTh[b][:, q0:q0 + 127 * dil + 1:dil]
                for ab in range(2):
                    k0 = 1024 - 64 * dil + 128 * dil * (qb + ab) + r
                    ksl = kTp[b][:, k0:k0 + 127 * dil + 1:dil]
                    O.mm(psv[:, qi, ab, :], ksl, qsl, True, True, [kTp[b], qTh[b]], [ps])

        def stB(i):
            ps = PSs[i % NPS]
            pt = PTs[i % 2]
            pm = PTm[i % 2]
            O.act(pt[:].rearrange("p a b c -> p (a b c)"), ps[:], AF.Exp, [ps], [pt], scale=0.125)
            O.tt("dve", pm[:], pt[:], mask[:].unsqueeze(1).to_broadcast([128, 2, 2, 128]), ALU.mult, [pt, mask], [pm])

        def stC(i):
            p = pairs[i]
            b = p["h"] % 2
            dil, r, qp, pi = p["dil"], p["r"], p["qp"], p["pi"]
            v = vP[pi][b]
            pm = PTm[i % 2]
            po = PSo[i % 3]
            osb = p["osb"]
            osv = osb[:].rearrange("p (q r) e -> p q r e", r=dil)
            for qi in range(2):
                qb = 2 * qp + qi
                for ab in range(2):
                    O.mm(po[:, qi, :], pm[:, qi, ab, :], v[:, qb + ab, r, :], ab == 0, ab == 1, [pm, v], [po])
            O.copy("act" if i % 2 else "dve", osv[:, 2 * qp:2 * qp + 2, r, :], po[:], [po], [(osb, (r, qp))])
            if p["post"] is not None:
                store(*p["post"])

        n = len(pairs)
        for i in range(n + 2):
            if i < n:
                if pairs[i]["pre"] is not None:
                    loads(pairs[i]["pre"])
                stA(i)
            if 0 <= i - 1 < n:
                stB(i - 1)
            if 0 <= i - 2 < n:
                stC(i - 2)
        S.barrier()
        S.flush()


def phase4(C, l):
    nc, S, O = C.nc, C.S, C.O
    tri, trif = C.tri, C.trif
    with ExitStack() as st:
        def sb(name, shape, dt):
            return st.enter_context(_sbt(nc, name, list(shape), dt))
        H = [sb(f"p4H{d}", [128, 2048], F32) for d in range(2)]
        hbf = [[sb(f"p4hbf{d}_{i}", [128, 2048], BF16) for i in range(2)] for d in range(2)]
        xs_t = [sb(f"p4xs{i}", [128, 2048], BF16) for i in range(4)]
        Bt_t = [sb(f"p4Bt{i}", [128, 512], BF16) for i in range(4)]
        ew = [sb(f"p4ew{i}", [128, 2, 32], F32) for i in range(2)]
        dtw = [sb(f"p4dtw{i}", [128, 32], F32) for i in range(2)]
        xw = [sb(f"p4xw{i}", [128, 2048], BF16) for i in range(2)]
        tmpH = sb("p4tmpH", [128, 2048], F32)
        PSw = [st.enter_context(_pst(nc, f"p4psw{i}", [128, 2, 32], F32)) for i in range(2)]
        PSs = [st.enter_context(_pst(nc, f"p4pss{i}", [128, 512], F32)) for i in range(4)]
        for d in range(2):
            O.memset("dve", H[d][:], 0.0, [H[d]])
            O.memset("pool", hbf[d][0][:], 0.0, [hbf[d][0]])
        n = 0
        for i in range(NT_):
            for d in range(2):
                c = i if d == 0 else NT_ - 1 - i
                xt = xs_t[n % 4]
                bt = Bt_t[n % 4]
                e_ = ew[n % 2]
                dw = dtw[n % 2]
                xw_ = xw[n % 2]
                pw = PSw[n % 2]
                n += 1
                S.dma("sp", xt[:], C.xs_d[c * 128:(c + 1) * 128, :], r=[C.xs_d], w=[xt])
                S.dma("sp", bt[:], C.Bt_d[c * 128:(c + 1) * 128, :], r=[C.Bt_d], w=[bt])
                la_c = C.la_all[:, c, d * 32:(d + 1) * 32]
                dt_c = C.dt_all[:, c, d * 32:(d + 1) * 32]
                O.mm(pw[:, 0, :], trif[:, 1 if d == 0 else 0, :], la_c, True, True, [trif, C.la_all], [pw])
                O.mm(pw[:, 1, :], trif[:, 2, :], la_c, True, True, [trif, C.la_all], [pw])
                O.act(e_[:], pw[:], AF.Exp, [pw], [e_])
                O.tt("dve", dw[:], dt_c, e_[:, 0, :], ALU.mult, [C.dt_all, e_], [dw])
                O.tt("pool", xw_[:].rearrange("p (h d) -> p h d", d=64), xt[:].rearrange("p (h d) -> p h d", d=64),
                     dw[:].unsqueeze(2).to_broadcast([128, 32, 64]), ALU.mult, [xt, dw], [xw_])
                for g in range(4):
                    O.mm(PSs[g][:], bt[:, g * 128:(g + 1) * 128], xw_[:, g * 512:(g + 1) * 512], True, True, [bt, xw_], [PSs[g]])
                hb = hbf[d][i % 2]
                S.dma("sp", C.hin_d[d][c], hb[:], r=[hb], w=[(C.hin_d[d], c)])
                O.tt("dve", tmpH[:].rearrange("p (h d) -> p h d", d=64), H[d][:].rearrange("p (h d) -> p h d", d=64),
                     e_[:, 1, :].unsqueeze(2).to_broadcast([128, 32, 64]), ALU.mult, [H[d], e_], [tmpH])
                for g in range(4):
                    O.tt("dve", H[d][:, g * 512:(g + 1) * 512], tmpH[:, g * 512:(g + 1) * 512], PSs[g][:], ALU.add,
                         [tmpH, PSs[g]], [(H[d], g)])
                O.copy("act", hbf[d][(i + 1) % 2][:], H[d][:], [H[d]], [hbf[d][(i + 1) % 2]])
        S.barrier()
        S.flush()
    with ExitStack() as st:
        def sb(name, shape, dt):
            return st.enter_context(_sbt(nc, name, list(shape), dt))
        NB = 3
        xs_t = [sb(f"p4xs{i}", [128, 2048], BF16) for i in range(NB)]
        BT_t = [sb(f"p4BT{i}", [128, 4, 128], BF16) for i in range(NB)]
        CT_t = [sb(f"p4CT{i}", [128, 4, 128], BF16) for i in range(NB)]
        sz_t = [sb(f"p4sz{i}", [128, 2048], BF16) for i in range(NB)]
        hh_t = [[sb(f"p4hh{d}_{i}", [128, 2048], BF16) for i in range(NB)] for d in range(2)]
        cbm = [[sb(f"p4cbm{d}_{i}", [128, 4, 128], BF16) for i in range(2)] for d in range(2)]
        ecum = [[sb(f"p4ecum{d}_{i}", [128, 32], F32) for i in range(2)] for d in range(2)]
        rseg = [[sb(f"p4rseg{d}_{i}", [128, 8, 128], BF16) for i in range(2)] for d in range(2)]
        xd = [[sb(f"p4xd{d}_{i}", [128, 512], BF16) for i in range(4)] for d in range(2)]
        eseg = [sb(f"p4eseg{i}", [128, 4, 128], BF16) for i in range(8)]
        MT = [sb(f"p4MT{i}", [128, 4, 128], BF16) for i in range(8)]
        tt_ = [[sb(f"p4t{d}_{i}", [128, 512], F32) for i in range(2)] for d in range(2)]
        xsD = [sb(f"p4xsD{i}", [128, 512], F32) for i in range(2)]
        yy = [sb(f"p4y{i}", [128, 512], F32) for i in range(3)]
        ynf = [sb(f"p4ynf{i}", [128, 512], BF16) for i in range(2)]
        ssg = [sb(f"p4ssg{i}", [128, 1], F32) for i in range(2)]
        sqj = sb("p4sqj", [128, 512], BF16)
        ynT_st = [sb(f"p4ynT{i}", [128, 16, 128], BF16) for i in range(3)]
        nwb = sb("p4nwb", [128, 2048], F32)
        Dbc = sb("p4Dbc", [128, 32], F32)
        PScb = st.enter_context(_pst(nc, "p4pscb", [128, 512], F32))
        PSy = [st.enter_context(_pst(nc, f"p4psy{i}", [128, 512], F32)) for i in range(2)]
        PSseg = [st.enter_context(_pst(nc, f"p4psseg{i}", [128, 512], F32)) for i in range(2)]
        PSo = [st.enter_context(_pst(nc, f"p4pso{i}", [128, 512], F32)) for i in range(2)]
        PSt = st.enter_context(_pst(nc, "p4pst", [128, 512], F32))
        S.dma("sp", nwb[:], C.snw[l].partition_broadcast(128), r=[], w=[nwb])
        S.dma("sp", Dbc[:], C.sdd[l].partition_broadcast(128), r=[], w=[Dbc])

        def loads(c):
            b = c % NB
            S.dma("sp", xs_t[b][:], C.xs_d[c * 128:(c + 1) * 128, :], r=[C.xs_d], w=[xs_t[b]])
            S.dma("sp", BT_t[b][:], C.BT_d[:, c * 128:(c + 1) * 128].rearrange("(g n) t -> n g t", n=128), r=[C.BT_d], w=[BT_t[b]])
            S.dma("sp", CT_t[b][:], C.CT_d[:, c * 128:(c + 1) * 128].rearrange("(g n) t -> n g t", n=128), r=[C.CT_d], w=[CT_t[b]])
            S.dma("sp", sz_t[b][:], C.sz_d[c * 128:(c + 1) * 128, :], r=[C.sz_d], w=[sz_t[b]])
            for d in range(2):
                S.dma("sp", hh_t[d][b][:], C.hin_d[d][c], r=[C.hin_d[d]], w=[hh_t[d][b]])

        units = [(d, q) for d in range(2) for q in range(2)]

        def s1(k):
            c, g = divmod(k, 4)
            b, cp = c % NB, c % 2
            xt, BT, CT = xs_t[b], BT_t[b], CT_t[b]
            if g == 0:
                pcb = PScb[:].rearrange("p (g t) -> p g t", g=4)
                for g2 in range(4):
                    O.mm(pcb[:, g2, :], BT[:, g2, :], CT[:, g2, :], True, True, [BT, CT], [PScb])
                for d in range(2):
                    O.tt("dve", cbm[d][cp][:], pcb, tri[:, d, :].unsqueeze(1).to_broadcast([128, 4, 128]), ALU.mult,
                         [PScb, tri], [cbm[d][cp]])
                pse = PScb[:, 0:64].rearrange("p (d h) -> p d h", d=2)
                for d in range(2):
                    la_c = C.la_all[:, c, d * 32:(d + 1) * 32]
                    O.mm(pse[:, d, :], trif[:, 3 + d, :], la_c, True, True, [trif, C.la_all], [PScb])
                for d in range(2):
                    O.act(ecum[d][cp][:], pse[:, d, :], AF.Exp, [PScb], [ecum[d][cp]])
            for d in range(2):
                la_g = C.la_all[:, c, d * 32 + g * 8:d * 32 + (g + 1) * 8]
                dt_g = C.dt_all[:, c, d * 32 + g * 8:d * 32 + (g + 1) * 8]
                O.tt("dve", rseg[d][k % 2][:], la_g.unsqueeze(2).to_broadcast([128, 8, 128]),
                     tri[:, d, :].unsqueeze(1).to_broadcast([128, 8, 128]), ALU.mult, [C.la_all, tri], [rseg[d][k % 2]])
                O.tt("pool", xd[d][k % 4][:].rearrange("p (h e) -> p h e", e=64),
                     xt[:, g * 512:(g + 1) * 512].rearrange("p (h e) -> p h e", e=64),
                     dt_g.unsqueeze(2).to_broadcast([128, 8, 64]), ALU.mult, [xt, C.dt_all], [xd[d][k % 4]])

        def s2(k):
            for u, (d, q) in enumerate(units):
                n = k * 4 + u
                pss = PSseg[n % 2]
                es = eseg[n % 8]
                O.mm(pss[:], tri[:, 3 - d, :], rseg[d][k % 2][:, q * 4:(q + 1) * 4, :].rearrange("p h t -> p (h t)"), True, True,
                     [tri, rseg[d][k % 2]], [pss])
                O.act(es[:].rearrange("p h t -> p (h t)"), pss[:], AF.Exp, [pss], [es])

        def s3(k):
            c, g = divmod(k, 4)
            cp = c % 2
            for u, (d, q) in enumerate(units):
                n = k * 4 + u
                O.tt("dve" if u % 2 else "pool", MT[n % 8][:], eseg[n % 8][:],
                     cbm[d][cp][:, g, :].unsqueeze(1).to_broadcast([128, 4, 128]), ALU.mult, [eseg[n % 8], cbm[d][cp]], [MT[n % 8]])

        def s4(k):
            for u, (d, q) in enumerate(units):
                n = k * 4 + u
                mt = MT[n % 8]
                for hh in range(4):
                    h8 = q * 4 + hh
                    O.mm(PSy[k % 2][:, h8 * 64:(h8 + 1) * 64], mt[:, hh, :], xd[d][k % 4][:, h8 * 64:(h8 + 1) * 64],
                         u == 0 and hh == 0, u == 3 and hh == 3, [mt, xd[d][k % 4]], [PSy[k % 2]])

        def s5(k):
            c, g = divmod(k, 4)
            b, cp, kp = c % NB, c % 2, k % 2
            xt, CT, sz = xs_t[b], CT_t[b], sz_t[b]
            for d in range(2):
                hd = hh_t[d][b]
                O.mm(PSo[d][:], CT[:, g, :], hd[:, g * 512:(g + 1) * 512], True, True, [CT, hd], [PSo[d]])
            O.tt("pool", xsD[kp][:].rearrange("p (h e) -> p h e", e=64),
                 xt[:, g * 512:(g + 1) * 512].rearrange("p (h e) -> p h e", e=64),
                 Dbc[:, g * 8:(g + 1) * 8].unsqueeze(2).to_broadcast([128, 8, 64]), ALU.mult, [xt, Dbc], [xsD[kp]])
            for d in range(2):
                O.tt("dve", tt_[d][kp][:].rearrange("p (h e) -> p h e", e=64), PSo[d][:].rearrange("p (h e) -> p h e", e=64),
                     ecum[d][cp][:, g * 8:(g + 1) * 8].unsqueeze(2).to_broadcast([128, 8, 64]), ALU.mult,
                     [PSo[d], ecum[d][cp]], [tt_[d][kp]])
            O.tt("pool", tt_[0][kp][:], tt_[0][kp][:], tt_[1][kp][:], ALU.add, [tt_[0][kp], tt_[1][kp]], [tt_[0][kp]])
            O.tt("dve", tt_[0][kp][:], tt_[0][kp][:], xsD[kp][:], ALU.add, [tt_[0][kp], xsD[kp]], [tt_[0][kp]])
            y_ = yy[k % 3]
            O.tt("dve", y_[:], PSy[kp][:], tt_[0][kp][:], ALU.add, [PSy[kp], tt_[0][kp]], [y_])
            O.tt("pool", y_[:], y_[:], sz[:, g * 512:(g + 1) * 512], ALU.mult, [y_, sz], [y_])

        def s6(k):
            c, g = divmod(k, 4)
            y_ = yy[k % 3]
            s_ = ssg[k % 2]
            O.act(sqj[:], y_[:], AF.Square, [y_], [sqj, s_], accum_out=s_[:])
            O.act(s_[:], s_[:], AF.Sqrt, [s_], [s_], bias=1e-6, scale=1.0 / 512)
            O.recip(s_[:], s_[:], [s_], [s_])
            O.stt("dve", ynf[k % 2][:], y_[:], s_[:], nwb[:, g * 512:(g + 1) * 512], ALU.mult, ALU.mult, [y_, s_, nwb], [ynf[k % 2]])

        def s7(k):
            c, g = divmod(k, 4)
            ptb = PSt[:, 0:256].bitcast(BF16).rearrange("p (a t) -> p a t", a=4)
            ys = ynT_st[c % 3]
            yn_ = ynf[k % 2]
            for a in range(4):
                O.tr(ptb[:, a, :], yn_[:, a * 128:(a + 1) * 128], C.ident[:], [yn_, C.ident], [PSt])
            O.copy("act", ys[:, g * 4:(g + 1) * 4, :], ptb, [PSt], [(ys, g)])
            if g == 3:
                S.dma("sp", C.ynT_d[:, c * 128:(c + 1) * 128].rearrange("(k p) t -> p k t", p=128), ys[:], r=[ys],
                      w=[(C.ynT_d, c)])

        stages = [s1, s2, s3, s4, s5, s6, s7]
        NK = NT_ * 4
        for c in range(NB):
            loads(c)
        for it in range(NK + len(stages) - 1):
            if it >= 8 and it % 4 == 0:
                cn = (it - 8) // 4 + NB
                if cn < NT_:
                    loads(cn)
            for si in range(len(stages) - 1, -1, -1):
                k = it - si
                if 0 <= k < NK:
                    stages[si](k)
        S.barrier()
        S.flush()


def phase5(C, l, x_src, x_dst):
    nc, S, O = C.nc, C.S, C.O
    with ExitStack() as st:
        def sb(name, shape, dt):
            return st.enter_context(_sbt(nc, name, list(shape), dt))
        TB = 256
        NTB = S_ // TB
        wa = sb("p5wa", [128, 8, D_], BF16)
        wb_ = sb("p5wb", [128, 8, D_], BF16)
        wc = sb("p5wc", [128, 16, D_], BF16)
        wo = sb("p5wo", [128, 8, D_], BF16)
        yA_b = [sb(f"p5yA{i}", [128, 8, TB], BF16) for i in range(2)]
        yn_b = [sb(f"p5yn{i}", [128, 16, TB], BF16) for i in range(2)]
        oT_b = [sb(f"p5oT{i}", [128, 8, TB], BF16) for i in range(2)]
        g_b = [sb(f"p5g{i}", [128, 24, TB], BF16) for i in range(2)]
        mT = [sb(f"p5mT{i}", [128, 8, TB], BF16) for i in range(1)] * 2
        ot = [[sb(f"p5ot{p}_{i}", [128, 16, 65], F32) for p in range(3)] for i in range(1)] * 2
        rden = [sb(f"p5rden{i}", [128, 16], F32) for i in range(2)]
        ob = [sb(f"p5ob{i}", [128, 16, 64], BF16) for i in range(2)]
        xt = [sb(f"p5xt{i}", [128, D_], F32) for i in range(2)]
        xo = [sb(f"p5xo{i}", [128, D_], F32) for i in range(1)] * 2
        t1 = [sb(f"p5t1_{i}", [128, TB], F32) for i in range(2)]
        t2 = [sb(f"p5t2_{i}", [128, TB], F32) for i in range(2)]
        PB = [st.enter_context(_pst(nc, f"p5ps{i}", [128, 512], F32)) for i in range(6)]
        PT = [st.enter_context(_pst(nc, f"p5pt{i}", [128, 8, 128], BF16)) for i in range(2)]
        S.dma("pool", wa[:], C.w_a[l].rearrange("(k p) n -> p k n", p=128), r=[], w=[wa])
        S.dma("pool", wb_[:], C.w_b[l].rearrange("(k p) n -> p k n", p=128), r=[], w=[wb_])
        S.dma("pool", wc[:], C.w_c[l].rearrange("(k p) n -> p k n", p=128), r=[], w=[wc])
        S.dma("pool", wo[:], C.w_o[l].rearrange("(k p) n -> p k n", p=128), r=[], w=[wo])
        cnt = {"ps": 0, "o": 0}

        def PS():
            p = PB[cnt["ps"] % 6]
            cnt["ps"] += 1
            return p

        def loads(tb):
            b = tb % 2
            tsl = slice(tb * TB, (tb + 1) * TB)
            S.dma("sp", yA_b[b][:], C.yAT[:, tsl].rearrange("(k p) t -> p k t", p=128), r=[C.yAT], w=[yA_b[b]])
            S.dma("sp", yn_b[b][:], C.ynT_d[:, tsl].rearrange("(k p) t -> p k t", p=128), r=[C.ynT_d], w=[yn_b[b]])
            S.dma("sp", g_b[b][:], C.gT[:, tsl].rearrange("(k p) t -> p k t", p=128), r=[C.gT], w=[g_b[b]])

        def combine(tb):
            b = tb % 2
            for tt in range(TB // 128):
                t = tb * (TB // 128) + tt
                n = cnt["o"]
                cnt["o"] += 1
                o3 = ot[n % 2]
                for p in range(3):
                    S.dma("sp", o3[p][:], C.o_d[p][t * 128:(t + 1) * 128], r=[C.o_d[p]], w=[o3[p]])
                O.tt("dve", o3[0][:], o3[0][:], o3[1][:], ALU.add, [o3[0], o3[1]], [o3[0]])
                O.tt("pool", o3[0][:], o3[0][:], o3[2][:], ALU.add, [o3[0], o3[2]], [o3[0]])
                O.recip(rden[n % 2][:].unsqueeze(2), o3[0][:, :, 64:65], [o3[0]], [rden[n % 2]])
                O.tt("dve", ob[n % 2][:], o3[0][:, :, 0:64], rden[n % 2][:].unsqueeze(2).to_broadcast([128, 16, 64]), ALU.mult,
                     [o3[0], rden[n % 2]], [ob[n % 2]])
                obf = ob[n % 2][:].rearrange("p h d -> p (h d)")
                for k in range(8):
                    O.tr(PT[n % 2][:, k, :], obf[:, k * 128:(k + 1) * 128], C.ident[:], [ob[n % 2], C.ident], [PT[n % 2]])
                O.copy("act", oT_b[b][:, :, tt * 128:(tt + 1) * 128], PT[n % 2][:], [PT[n % 2]], [(oT_b[b], tt)])

        loads(0)
        combine(0)
        nx = 0
        for tb in range(NTB):
            b = tb % 2
            if tb + 1 < NTB:
                loads(tb + 1)
            for cc in range(8):
                if cc == 2 and tb + 1 < NTB:
                    combine(tb + 1)
                pa, pb, pc = PS(), PS(), PS()
                for k in range(8):
                    O.mm(pa[:, :TB], wa[:, k, cc * 128:(cc + 1) * 128], yA_b[b][:, k, :], k == 0, k == 7, [wa, yA_b[b]], [pa])
                for k in range(8):
                    O.mm(pb[:, :TB], wb_[:, k, cc * 128:(cc + 1) * 128], oT_b[b][:, k, :], k == 0, k == 7, [wb_, oT_b[b]], [pb])
                for k in range(16):
                    O.mm(pc[:, :TB], wc[:, k, cc * 128:(cc + 1) * 128], yn_b[b][:, k, :], k == 0, k == 15, [wc, yn_b[b]], [pc])
                a1, a2 = t1[cc % 2], t2[cc % 2]
                O.tt("dve", a1[:], pa[:, :TB], g_b[b][:, cc, :], ALU.mult, [pa, g_b[b]], [a1])
                O.tt("dve", a2[:], pb[:, :TB], g_b[b][:, 8 + cc, :], ALU.mult, [pb, g_b[b]], [a2])
                O.tt("pool", a1[:], a1[:], a2[:], ALU.add, [a1, a2], [a1])
                O.tt("dve", a2[:], pc[:, :TB], g_b[b][:, 16 + cc, :], ALU.mult, [pc, g_b[b]], [a2])
                O.tt("pool", mT[b][:, cc, :], a1[:], a2[:], ALU.add, [a1, a2], [(mT[b], cc)])
            for tt in range(TB // 128):
                t = tb * (TB // 128) + tt
                xt_, xo_ = xt[nx % 2], xo[nx % 2]
                nx += 1
                S.dma("sp", xt_[:], x_src[t * 128:(t + 1) * 128, :], r=[(x_src, t)], w=[xt_])
                for hf in range(2):
                    ps = PS()
                    for k in range(8):
                        O.mm(ps[:], mT[b][:, k, tt * 128:(tt + 1) * 128], wo[:, k, hf * 512:(hf + 1) * 512], k == 0, k == 7,
                             [mT[b], wo], [ps])
                    O.tt("dve", xo_[:, hf * 512:(hf + 1) * 512], ps[:], xt_[:, hf * 512:(hf + 1) * 512], ALU.add,
                         [ps, xt_], [(xo_, hf)])
                S.dma("sp", C.xmid[t * 128:(t + 1) * 128, :], xo_[:], r=[xo_], w=[(C.xmid, t)])
        S.barrier()
        S.flush()
    if "xmid" in C.dbg and C.stop == (l, 5):
        return
    with ExitStack() as st:
        def sb(name, shape, dt):
            return st.enter_context(_sbt(nc, name, list(shape), dt))
        w2 = sb("p5w2", [128, 32, D_], BF16)
        w1b = [sb(f"p5w1_{i}", [128, 8, 512], BF16) for i in range(3)]
        xts = [sb(f"p5x{i}", [128, D_], F32) for i in range(4)]
        hT = sb("p5hT", [128, 8, 512], BF16)
        h1T = sb("p5h1T", [128, 32, 512], BF16)
        ub = [sb(f"p5ub{i}", [128, D_], BF16) for i in range(2)]
        sq = sb("p5sq", [128, D_], BF16)
        ss = [sb(f"p5ss{i}", [128, 1], F32) for i in range(2)]
        rr = [sb(f"p5r{i}", [128, 512], F32) for i in range(2)]
        xo = [sb(f"p5xo{i}", [128, D_], F32) for i in range(2)]
        nwt = sb("p5nw", [128, D_], F32)
        fnw = sb("p5fnw", [128, D_], F32)
        PB = [st.enter_context(_pst(nc, f"p5bps{i}", [128, 512], F32)) for i in range(6)]
        PT = [st.enter_context(_pst(nc, f"p5bpt{i}", [128, 8, 128], BF16)) for i in range(2)]
        w2src = C.w_2[l].rearrange("(k p) n -> p k n", p=128)
        for k8 in range(4):
            S.dma("pool", w2[:, k8 * 8:(k8 + 1) * 8, :], w2src[:, k8 * 8:(k8 + 1) * 8, :], r=[], w=[(w2, k8)])
        S.dma("sp", nwt[:], C.mlpw[l:l + 1, :].partition_broadcast(128), r=[], w=[nwt])
        S.dma("sp", fnw[:], C.finw.partition_broadcast(128), r=[], w=[fnw])
        w1src = C.wb_1[l].rearrange("(k p) n -> p k n", p=128)
        nps = 0
        nw1 = 0
        for tb in range(8):
            for tt in range(4):
                t = tb * 4 + tt
                S.dma("sp", xts[tt][:], C.xmid[t * 128:(t + 1) * 128, :], r=[(C.xmid, t)], w=[xts[tt]])
                rms_tile(C, xts[tt], ss[tt % 2], sq, nwt, ub[tt % 2], "p5")
                for k in range(8):
                    O.tr(PT[tt % 2][:, k, :], ub[tt % 2][:, k * 128:(k + 1) * 128], C.ident[:], [ub[tt % 2], C.ident], [PT[tt % 2]])
                O.copy("act", hT[:, :, tt * 128:(tt + 1) * 128], PT[tt % 2][:], [PT[tt % 2]], [(hT, tt)])
            for f4 in range(8):
                w1t = w1b[nw1 % 3]
                nw1 += 1
                S.dma("sp", w1t[:], w1src[:, :, f4 * 512:(f4 + 1) * 512], r=[C.wb_1[l]], w=[w1t])
                for fj in range(4):
                    fc = f4 * 4 + fj
                    ps = PB[nps % 6]
                    r_ = rr[nps % 2]
                    nps += 1
                    for k in range(8):
                        O.mm(ps[:], w1t[:, k, fj * 128:(fj + 1) * 128], hT[:, k, :], k == 0, k == 7, [w1t, hT], [ps])
                    O.act(r_[:], ps[:], AF.Relu, [ps], [r_])
                    O.tt("pool" if fc % 2 else "dve", h1T[:, fc, :], r_[:], r_[:], ALU.mult, [r_], [(h1T, fc)])
            for tt in range(4):
                t = tb * 4 + tt
                xo_ = xo[tt % 2]
                for hf in range(2):
                    ps = PB[nps % 6]
                    nps += 1
                    for fc in range(32):
                        O.mm(ps[:], h1T[:, fc, tt * 128:(tt + 1) * 128], w2[:, fc, hf * 512:(hf + 1) * 512], fc == 0, fc == 31,
                             [h1T, w2], [ps])
                    O.tt("dve", xo_[:, hf * 512:(hf + 1) * 512], ps[:], xts[tt][:, hf * 512:(hf + 1) * 512], ALU.add,
                         [ps, xts[tt]], [(xo_, hf)])
                if x_dst is not None:
                    S.dma("sp", x_dst[t * 128:(t + 1) * 128, :], xo_[:], r=[xo_], w=[(x_dst, t)])
                else:
                    s_ = ss[tt % 2]
                    O.act(sq[:], xo_[:], AF.Square, [xo_], [sq, s_], accum_out=s_[:])
                    O.act(s_[:], s_[:], AF.Sqrt, [s_], [s_], bias=1e-6, scale=1.0 / D_)
                    O.recip(s_[:], s_[:], [s_], [s_])
                    O.stt("dve", xo_[:], xo_[:], s_[:], fnw[:], ALU.mult, ALU.mult, [xo_, s_, fnw], [xo_])
                    S.dma("sp", C.y_out[t * 128:(t + 1) * 128, :], xo_[:], r=[xo_], w=[(C.y_out, t)])
        S.barrier()
        S.flush()


def host_consts():
    bf = ml_dtypes.bfloat16
    p = np.arange(128)[:, None]
    f = np.arange(128)[None, :]
    tri = np.stack([(p <= f), (p >= f), (p < f), (p > f)], axis=1).astype(np.float32)
    trif = np.stack([(p < f), (p > f), np.ones((128, 128), bool), (p <= f), (p >= f)], axis=1).astype(np.float32)
    invf = (500000.0 ** (-np.arange(0, 16, 2, dtype=np.float32) / 16.0)).astype(np.float32)[None, :]
    return {"c_ident": np.eye(128, dtype=np.float32).astype(bf), "c_tri": tri.astype(bf), "c_trif": trif, "c_invf": invf}


def prep_inputs(inp):
    f32 = np.float32
    sh = dict(host_consts())
    for k in ("mix_norm_w", "mlp_norm_w", "w_in", "w_a_out", "w_b_out", "w_c_out", "w_o", "w_ff1", "w_ff2"):
        sh[k] = np.ascontiguousarray(inp[k], dtype=f32)
    sh["final_norm_w"] = np.ascontiguousarray(inp["final_norm_w"], dtype=f32).reshape(1, D_)
    sh["conv_a_wT"] = np.ascontiguousarray(inp["conv_a_w"].reshape(L_, 3, 8, 128).transpose(0, 3, 2, 1), dtype=f32)
    sh["ssd_conv_wT"] = np.ascontiguousarray(inp["ssd_conv_w"].reshape(L_, 5, 24, 128).transpose(0, 3, 2, 1), dtype=f32)
    sh["ssd_conv_bT"] = np.ascontiguousarray(inp["ssd_conv_b"].reshape(L_, 24, 128).transpose(0, 2, 1), dtype=f32)
    sh["ssd_conv_bR"] = np.ascontiguousarray(inp["ssd_conv_b"].reshape(L_, 1, 3072), dtype=f32)
    sh["ssd_a_log"] = np.ascontiguousarray(inp["ssd_a_log"].reshape(L_, 1, 64), dtype=f32)
    sh["ssd_dt_bias"] = np.ascontiguousarray(inp["ssd_dt_bias"].reshape(L_, 1, 64), dtype=f32)
    sh["ssd_d"] = np.ascontiguousarray(inp["ssd_d"].reshape(L_, 1, 32), dtype=f32)
    sh["ssd_norm_w"] = np.ascontiguousarray(inp["ssd_norm_w"].reshape(L_, 1, 2048), dtype=f32)
    per = []
    for b in range(inp["x"].shape[0]):
        d = dict(sh)
        d["x"] = np.ascontiguousarray(inp["x"][b], dtype=f32)
        d["pos"] = np.ascontiguousarray(inp["positions"][b].reshape(NT_, 128).T, dtype=np.int32)
        per.append(d)
    return per


_NC_CACHE = {}


def kernel(**inputs):
    per = prep_inputs(inputs)
    if "nc" not in _NC_CACHE:
        _NC_CACHE["nc"] = build()
    nc = _NC_CACHE["nc"]
    res = run_bass_kernel_spmd(nc, per, core_ids=list(range(len(per))))
    return np.stack([np.asarray(r["y"], dtype=np.float32) for r in res.results], axis=0)
```

```python
import numpy as np
import ml_dtypes
import concourse.bass as bass
import concourse.mybir as mybir
from concourse.bass_utils import run_bass_kernel_spmd
from contextlib import ExitStack
from types import SimpleNamespace

F32 = mybir.dt.float32
BF16 = mybir.dt.bfloat16
I32 = mybir.dt.int32
AF = mybir.ActivationFunctionType
ALU = mybir.AluOpType
AX = mybir.AxisListType

ENG = ("pe", "act", "dve", "pool", "sp")
DMAQ = ("sp", "act", "pool")


class _Buf:
    __slots__ = ("w", "r")

    def __init__(self):
        self.w = None
        self.r = {}


class _TBuf:
    __slots__ = ("whole", "subs")

    def __init__(self):
        self.whole = _Buf()
        self.subs = {}


class Sched:
    NRING = 8
    SAME_ENGINE_SYNC = ("act", "dve", "pool")

    def __init__(self, nc):
        self.nc = nc
        self.sems = []
        self.esem = {}
        for e in ENG:
            self.esem[e] = self._new_sem("s_" + e)
        self.ecnt = {e: 0 for e in ENG}
        self.ring = {q: [self._new_sem(f"d_{q}{i}") for i in range(self.NRING)] for q in DMAQ}
        self.dman = {q: 0 for q in DMAQ}
        self.known = {e: {} for e in ENG}
        self.streams = {e: [] for e in ENG}
        self.tb = {}
        self.n_wait = 0
        self.n_ins = 0

    def _new_sem(self, name):
        h = self.nc.alloc_semaphore(name=name)
        self.sems.append(h)
        return len(self.sems) - 1

    def _spec(self, a):
        if isinstance(a, tuple):
            t, key = a
        else:
            t, key = a, None
        name = t if isinstance(t, str) else t.name
        tb = self.tb.get(name)
        if tb is None:
            tb = self.tb[name] = _TBuf()
        return tb, key

    def _deps(self, eng, reads, writes):
        need = {}

        def add(ev):
            if ev is not None and need.get(ev[0], 0) < ev[1]:
                need[ev[0]] = ev[1]

        def addr(b):
            for s, v in b.r.items():
                if need.get(s, 0) < v:
                    need[s] = v

        for a in reads:
            tb, key = self._spec(a)
            add(tb.whole.w)
            if key is None:
                for sb in tb.subs.values():
                    add(sb.w)
            else:
                sb = tb.subs.get(key)
                if sb is not None:
                    add(sb.w)
        for a in writes:
            tb, key = self._spec(a)
            add(tb.whole.w)
            addr(tb.whole)
            if key is None:
                for sb in tb.subs.values():
                    add(sb.w)
                    addr(sb)
            else:
                sb = tb.subs.get(key)
                if sb is not None:
                    add(sb.w)
                    addr(sb)
        kn = self.known[eng]
        own = self.esem[eng]
        waits = []
        for s, v in need.items():
            if s == own and eng not in self.SAME_ENGINE_SYNC:
                continue
            if kn.get(s, 0) < v:
                kn[s] = v
                waits.append((s, v))
        return waits

    def _record(self, ev, reads, writes):
        s, v = ev
        for a in reads:
            tb, key = self._spec(a)
            if key is None:
                b = tb.whole
            else:
                b = tb.subs.get(key)
                if b is None:
                    b = tb.subs[key] = _Buf()
            if b.r.get(s, 0) < v:
                b.r[s] = v
        for a in writes:
            tb, key = self._spec(a)
            if key is None:
                tb.whole.w = ev
                tb.whole.r = {}
                tb.subs = {}
            else:
                b = tb.subs.get(key)
                if b is None:
                    b = tb.subs[key] = _Buf()
                b.w = ev
                b.r = {}

    def op(self, eng, emit, r=(), w=()):
        waits = self._deps(eng, r, w)
        self.ecnt[eng] += 1
        ev = (self.esem[eng], self.ecnt[eng])
        self._record(ev, r, w)
        self.streams[eng].append((waits, emit, ev[0], 1))
        self.n_wait += len(waits)
        self.n_ins += 1

    def dma(self, q, out, in_, r=(), w=(), **kw):
        waits = self._deps(q, r, w)
        i = self.dman[q]
        self.dman[q] += 1
        s = self.ring[q][i % self.NRING]
        rnd = i // self.NRING
        if rnd > 0:
            kn = self.known[q]
            if kn.get(s, 0) < 16 * rnd:
                kn[s] = 16 * rnd
                waits.append((s, 16 * rnd))
        ev = (s, 16 * (rnd + 1))
        self._record(ev, r, w)
        self.streams[q].append((waits, lambda e: e.dma_start(out=out, in_=in_, **kw), s, 16))
        self.n_wait += len(waits)
        self.n_ins += 1

    def barrier(self, skip_q=()):
        evs = []
        for e in ENG:
            if self.ecnt[e] > 0:
                evs.append((self.esem[e], self.ecnt[e]))
        for q in DMAQ:
            if q in skip_q:
                continue
            n = self.dman[q]
            for j in range(min(n, self.NRING)):
                last = ((n - 1 - j) // self.NRING) * self.NRING + j
                evs.append((self.ring[q][j], 16 * (last // self.NRING + 1)))
        for e in ENG:
            kn = self.known[e]
            waits = []
            for s, v in evs:
                if kn.get(s, 0) < v:
                    kn[s] = v
                    waits.append((s, v))
            if waits:
                self.streams[e].append((waits, None, None, 0))
                self.n_wait += len(waits)

    def flush(self):
        nc = self.nc
        sems = self.sems

        def run(name):
            items = self.streams[name]
            self.streams[name] = []

            def f(e):
                for waits, emit, s, inc in items:
                    for ws, wv in waits:
                        e.wait_ge(sems[ws], wv)
                    if emit is not None:
                        ins = emit(e)
                        ins.then_inc(sems[s], inc)

            return f

        with nc.Block() as block:
            block.tensor(run("pe"))
            block.scalar(run("act"))
            block.vector(run("dve"))
            block.gpsimd(run("pool"))
            block.sync(run("sp"))


S_ = 4096
D_ = 1024
NT_ = 32
L_ = 2
DIN_ = 14400
PI = 3.141592653589793


_UID = [0]


def _sbt(nc, name, shape, dt):
    _UID[0] += 1
    return nc.sbuf_tensor(f"{name}_u{_UID[0]}", shape, dt)


def _pst(nc, name, shape, dt):
    _UID[0] += 1
    return nc.psum_tensor(f"{name}_u{_UID[0]}", shape, dt)


class Ops:
    def __init__(self, S):
        self.S = S

    def mm(self, out, lhsT, rhs, start, stop, r, w):
        self.S.op("pe", lambda e: e.matmul(out, lhsT=lhsT, rhs=rhs, start=start, stop=stop), r, w)

    def tr(self, out, in_, ident, r, w):
        self.S.op("pe", lambda e: e.transpose(out=out, in_=in_, identity=ident), r, w)

    def act(self, out, in_, func, r, w, **kw):
        self.S.op("act", lambda e: e.activation(out=out, in_=in_, func=func, **kw), r, w)

    def tt(self, eng, out, in0, in1, op, r, w):
        self.S.op(eng, lambda e: e.tensor_tensor(out=out, in0=in0, in1=in1, op=op), r, w)

    def ts(self, eng, out, in0, s1, s2, op0, op1, r, w):
        if s2 is None:
            self.S.op(eng, lambda e: e.tensor_scalar(out=out, in0=in0, scalar1=s1, scalar2=None, op0=op0), r, w)
        else:
            self.S.op(eng, lambda e: e.tensor_scalar(out=out, in0=in0, scalar1=s1, scalar2=s2, op0=op0, op1=op1), r, w)

    def stt(self, eng, out, in0, scalar, in1, op0, op1, r, w):
        self.S.op(eng, lambda e: e.scalar_tensor_tensor(out=out, in0=in0, scalar=scalar, in1=in1, op0=op0, op1=op1), r, w)

    def copy(self, eng, out, in_, r, w):
        if eng == "act":
            self.S.op(eng, lambda e: e.copy(out=out, in_=in_), r, w)
        else:
            self.S.op(eng, lambda e: e.tensor_copy(out=out, in_=in_), r, w)

    def memset(self, eng, ap, val, w):
        self.S.op(eng, lambda e: e.memset(ap, val), (), w)

    def recip(self, out, in_, r, w):
        self.S.op("dve", lambda e: e.reciprocal(out=out, in_=in_), r, w)


def build(dbg=(), stop=None, nlayers=L_):
    nc = bass.Bass("TRN2", target_bir_lowering=False)

    def din(name, shape, dt=F32):
        return nc.dram_tensor(name, list(shape), dt, kind="ExternalInput").ap()

    def dscr(name, shape, dt):
        kind = "ExternalOutput" if name in dbg else "Internal"
        return nc.dram_tensor(name, list(shape), dt, kind=kind).ap()

    x_in = din("x", [S_, D_])
    pos_in = din("pos", [128, NT_], I32)
    mixw = din("mix_norm_w", [L_, D_])
    mlpw = din("mlp_norm_w", [L_, D_])
    finw = din("final_norm_w", [1, D_])
    w_in = din("w_in", [L_, D_, DIN_])
    w_a = din("w_a_out", [L_, D_, D_])
    w_b = din("w_b_out", [L_, D_, D_])
    w_c = din("w_c_out", [L_, 2 * D_, D_])
    w_o = din("w_o", [L_, D_, D_])
    w_1 = din("w_ff1", [L_, D_, 4 * D_])
    w_2 = din("w_ff2", [L_, 4 * D_, D_])
    cawT = din("conv_a_wT", [L_, 128, 8, 3])
    scwT = din("ssd_conv_wT", [L_, 128, 24, 5])
    scbT = din("ssd_conv_bT", [L_, 128, 24])
    scbR = din("ssd_conv_bR", [L_, 1, 3072])
    alog = din("ssd_a_log", [L_, 1, 64])
    dtb = din("ssd_dt_bias", [L_, 1, 64])
    sdd = din("ssd_d", [L_, 1, 32])
    snw = din("ssd_norm_w", [L_, 1, 2048])
    c_ident = din("c_ident", [128, 128], BF16)
    c_tri = din("c_tri", [128, 4, 128], BF16)
    c_trif = din("c_trif", [128, 5, 128], F32)
    c_invf = din("c_invf", [1, 8], F32)
    y_out = nc.dram_tensor("y", [S_, D_], F32, kind="ExternalOutput").ap()

    wb_in = [dscr(f"wb_in{l}", [D_, DIN_], BF16) for l in range(L_)]
    wb_a = [dscr(f"wb_a{l}", [D_, D_], BF16) for l in range(L_)]
    wb_b = [dscr(f"wb_b{l}", [D_, D_], BF16) for l in range(L_)]
    wb_c = [dscr(f"wb_c{l}", [2 * D_, D_], BF16) for l in range(L_)]
    wb_o = [dscr(f"wb_o{l}", [D_, D_], BF16) for l in range(L_)]
    wb_1 = [dscr(f"wb_1{l}", [D_, 4 * D_], BF16) for l in range(L_)]
    wb_2 = [dscr(f"wb_2{l}", [4 * D_, D_], BF16) for l in range(L_)]
    yAT = dscr("yAT", [16, 128, 8, 256], BF16)
    gT = dscr("gT", [16, 128, 24, 256], BF16)
    qT = dscr("qT", [D_, S_], BF16)
    kT = dscr("kT", [D_, S_], BF16)
    v_d = dscr("v_d", [S_, D_], BF16)
    sz_d = dscr("sz_d", [S_, 2 * D_], BF16)
    xs_d = dscr("xs_d", [S_, 2 * D_], BF16)
    Bt_d = dscr("Bt_d", [S_, 512], BF16)
    BT_d = dscr("BT_d", [512, S_], BF16)
    CT_d = dscr("CT_d", [512, S_], BF16)
    dt_dbg = dscr("dt_dbg", [128, NT_, 64], F32)
    o_d = [dscr(f"o_d{p}", [S_, 16, 65], F32) for p in range(3)]
    hin_d = [dscr(f"hin_d{d}", [NT_, 128, 2048], BF16) for d in range(2)]
    ynT_d = dscr("ynT_d", [16, 128, 16, 256], BF16)
    xmid = dscr("xmid", [S_, D_], F32)
    xl = [dscr(f"xl{l}", [S_, D_], F32) for l in range(L_ - 1)]

    S = Sched(nc)
    O = Ops(S)

    with ExitStack() as gst:
        def gsb(name, shape, dt):
            return gst.enter_context(_sbt(nc, name, list(shape), dt))

        ident = gsb("ident", [128, 128], BF16)
        tri = gsb("tri", [128, 4, 128], BF16)
        trif = gsb("trif", [128, 5, 128], F32)
        cosT = gsb("cosT", [128, NT_, 8], F32)
        sinT = gsb("sinT", [128, NT_, 8], F32)
        dt_all = gsb("dt_all", [128, NT_, 64], F32)
        la_all = gsb("la_all", [128, NT_, 64], F32)
        S.dma("sp", ident[:], c_ident, r=[], w=[ident])
        S.dma("sp", tri[:], c_tri, r=[], w=[tri])
        S.dma("sp", trif[:], c_trif, r=[], w=[trif])

        def cast_w(src, dst, rows_per):
            R = src.shape[0]
            for r0 in range(0, R, rows_per):
                S.dma("pool", dst[r0:r0 + rows_per, :], src[r0:r0 + rows_per, :], r=[], w=[(dst, r0)])

        with ExitStack() as st:
            def sb(name, shape, dt):
                return st.enter_context(_sbt(nc, name, list(shape), dt))
            posi = sb("posi", [128, NT_], I32)
            posf = sb("posf", [128, NT_], F32)
            invf = sb("invf", [128, 8], F32)
            ang = sb("ang", [128, NT_, 8], F32)
            a1 = sb("a1", [128, NT_, 8], F32)
            S.dma("sp", posi[:], pos_in, r=[], w=[posi])
            S.dma("sp", invf[:], c_invf.partition_broadcast(128), r=[], w=[invf])
            O.copy("dve", posf[:], posi[:], [posi], [posf])
            O.tt("dve", ang[:], posf[:].unsqueeze(2).to_broadcast([128, NT_, 8]),
                 invf[:].unsqueeze(1).to_broadcast([128, NT_, 8]), ALU.mult, [posf, invf], [ang])
            ki = sb("ki", [128, NT_, 8], I32)
            kf = sb("kf", [128, NT_, 8], F32)
            mm_ = sb("mm_", [128, NT_, 8], F32)
            for shift, dstT in ((0.0, sinT), (0.5 * PI, cosT)):
                O.ts("dve", a1[:], ang[:], shift, 1.0 / (2 * PI), ALU.add, ALU.mult, [ang], [a1])
                O.copy("dve", ki[:], a1[:], [a1], [ki])
                O.copy("dve", kf[:], ki[:], [ki], [kf])
                O.ts("dve", a1[:], ang[:], shift, None, ALU.add, None, [ang], [a1])
                O.stt("dve", a1[:], kf[:], -2 * PI, a1[:], ALU.mult, ALU.add, [kf, a1], [a1])
                O.ts("dve", mm_[:], a1[:], PI, 2 * PI, ALU.is_ge, ALU.mult, [a1], [mm_])
                O.tt("dve", a1[:], a1[:], mm_[:], ALU.subtract, [a1, mm_], [a1])
                O.ts("dve", mm_[:], a1[:], -PI, 2 * PI, ALU.is_lt, ALU.mult, [a1], [mm_])
                O.tt("dve", a1[:], a1[:], mm_[:], ALU.add, [a1, mm_], [a1])
                O.ts("dve", a1[:], a1[:], -PI, PI, ALU.max, ALU.min, [a1], [a1])
                O.act(dstT[:], a1[:], AF.Sin, [a1], [dstT])
            S.barrier()
            S.flush()

        for l in range(nlayers):
            x_src = x_in if l == 0 else xl[l - 1]
            x_dst = xl[l] if l < L_ - 1 else None
            C = SimpleNamespace(**locals())
            layer(C, l, x_src, x_dst, stop)
            if stop is not None and stop[0] == l:
                break
        S.barrier(skip_q=())
        S.flush()
    return nc


def layer(C, l, x_src, x_dst, stop):
    nc, S, O = C.nc, C.S, C.O
    with ExitStack() as lst:
        uT = lst.enter_context(_sbt(nc, "uT", [128, 8, S_], BF16))
        phase1(C, l, x_src, uT)
        if stop == (l, 1):
            return
        phase2(C, l, uT)
    S.barrier()
    S.flush()
    if stop == (l, 2):
        return
    phase3(C, l)
    if stop == (l, 3):
        return
    phase4(C, l)
    if stop == (l, 4):
        return
    phase5(C, l, x_src, x_dst)


def rms_tile(C, xt, ss, sq, nwt, ub, tag):
    O = C.O
    O.act(sq[:], xt[:], AF.Square, [xt], [sq, ss], accum_out=ss[:])
    O.act(ss[:], ss[:], AF.Sqrt, [ss], [ss], bias=1e-6, scale=1.0 / D_)
    O.recip(ss[:], ss[:], [ss], [ss])
    O.stt("dve", ub[:], xt[:], ss[:], nwt[:], ALU.mult, ALU.mult, [xt, ss, nwt], [ub])


def phase1(C, l, x_src, uT):
    nc, S, O = C.nc, C.S, C.O
    with ExitStack() as st:
        def sb(name, shape, dt):
            return st.enter_context(_sbt(nc, name, list(shape), dt))
        xt = [sb(f"p1x{i}", [128, D_], F32) for i in range(2)]
        sq = sb("p1sq", [128, D_], BF16)
        ss = [sb(f"p1ss{i}", [128, 1], F32) for i in range(2)]
        ub = [sb(f"p1ub{i}", [128, D_], BF16) for i in range(2)]
        nwt = sb("p1nw", [128, D_], F32)
        pT = [st.enter_context(_pst(nc, f"p1pT{i}", [128, 8, 128], BF16)) for i in range(2)]
        S.dma("sp", nwt[:], C.mixw[l:l + 1, :].partition_broadcast(128), r=[], w=[nwt])
        for t in range(NT_):
            b = t % 2
            S.dma("sp", xt[b][:], x_src[t * 128:(t + 1) * 128, :], r=[(x_src, t)], w=[xt[b]])
            rms_tile(C, xt[b], ss[b], sq, nwt, ub[b], "p1")
            for k in range(8):
                O.tr(pT[b][:, k, :], ub[b][:, k * 128:(k + 1) * 128], C.ident[:], [ub[b], C.ident], [pT[b]])
            O.copy("act" if t % 2 else "dve", uT[:, :, t * 128:(t + 1) * 128], pT[b][:], [pT[b]], [(uT, t)])
        S.barrier()
        S.flush()


def phase2(C, l, uT):
    nc, S, O = C.nc, C.S, C.O
    wsrc = C.w_in[l].rearrange("(k p) n -> p k n", p=128)
    seq = []
    for g in range(2):
        seq += [(512 * g, 512), (1024 + 512 * g, 512), (2048 + 512 * g, 512)]
    seq += [(3072 + 512 * i, 512) for i in range(4)]
    seq += [(5120 + 512 * i, 512) for i in range(2)]
    seq += [(6144 + 512 * i, 512) for i in range(4)]
    seq += [(8192 + 512 * i, 512) for i in range(6)]
    seq += [(11264, 64)]
    seq += [(11328 + 512 * i, 512) for i in range(6)]
    with ExitStack() as st:
        wbuf = [st.enter_context(_sbt(nc, f"p2w{i}", [128, 8, 512], BF16)) for i in range(3)]
        PB = [st.enter_context(_pst(nc, f"p2ps{i}", [128, 512], F32)) for i in range(6)]
        PT = [st.enter_context(_pst(nc, f"p2pt{i}", [128, 4, 128], BF16)) for i in range(2)]
        state = {"issued": 0, "ps": 0, "done": 0}

        def prefetch():
            while state["issued"] < min(len(seq), state["done"] + 3):
                i = state["issued"]
                c0, ncw = seq[i]
                S.dma("pool", wbuf[i % 3][:, :, :ncw], wsrc[:, :, c0:c0 + ncw], r=[], w=[wbuf[i % 3]])
                state["issued"] += 1

        def get_w(idx):
            prefetch()
            assert idx < state["issued"]
            return wbuf[idx % 3]

        def release(n):
            state["done"] = n
            prefetch()

        def PS():
            p = PB[state["ps"] % len(PB)]
            state["ps"] += 1
            return p

        def mm_feat(ps, wt, jj, tb):
            for k in range(8):
                O.mm(ps[:], wt[:, k, jj * 128:(jj + 1) * 128], uT[:, k, tb * 512:(tb + 1) * 512], k == 0, k == 7,
                     [wt, uT], [ps])

        def mm_tok(ps_ap, ps, wt, t, ncw):
            for k in range(8):
                O.mm(ps_ap, uT[:, k, t * 128:(t + 1) * 128], wt[:, k, :ncw], k == 0, k == 7, [wt, uT], [ps])

        wi = 0
        with ExitStack() as s2:
            def sb(name, shape, dt):
                return s2.enter_context(_sbt(nc, name, list(shape), dt))
            bT = [sb(f"p2bT{i}", [128, S_], F32) for i in range(2)]
            cst = [sb(f"p2cst{i}", [128, 512], F32) for i in range(2)]
            cx = [sb(f"p2cx{i}", [128, S_ + 2], BF16) for i in range(2)]
            yAo = [sb(f"p2yAo{i}", [128, S_], BF16) for i in range(2)]
            dgA = [sb(f"p2dgA{i}", [128, 3, 128], BF16) for i in range(2)]
            wA = sb("p2wA", [128, 8, 3], F32)
            S.dma("sp", wA[:], C.cawT[l], r=[], w=[wA])
            for i in range(2):
                O.memset("dve", cx[i][:, 0:1], 0.0, [(cx[i], "l")])
                O.memset("dve", cx[i][:, S_ + 1:S_ + 2], 0.0, [(cx[i], "r")])

            def convA(j):
                p = j % 2
                for tb in range(8):
                    ps = PS()
                    for k3 in range(3):
                        O.mm(ps[:], dgA[p][:, k3, :], cx[p][:, tb * 512 + k3:tb * 512 + k3 + 512], k3 == 0, k3 == 2,
                             [dgA[p], cx[p]], [ps])
                    O.tt("dve", yAo[p][:, tb * 512:(tb + 1) * 512], ps[:], bT[p][:, tb * 512:(tb + 1) * 512], ALU.mult,
                         [ps, bT[p]], [(yAo[p], tb)])
                S.dma("sp", C.yAT[:, :, j, :].rearrange("tb p t -> p tb t"), yAo[p][:].rearrange("p (tb t) -> p tb t", t=256),
                      r=[yAo[p]], w=[(C.yAT, j)])

            for g in range(2):
                release(wi)
                wt_b, wt_c, wt_x = get_w(wi), get_w(wi + 1), get_w(wi + 2)
                wi += 3
                for jj in range(4):
                    j = 4 * g + jj
                    p = j % 2
                    for k3 in range(3):
                        O.ts("dve", dgA[p][:, k3, :], C.ident[:], wA[:, j, k3:k3 + 1], None, ALU.mult, None,
                             [C.ident, wA], [(dgA[p], k3)])
                    for tb in range(8):
                        pb, pc, px = PS(), PS(), PS()
                        mm_feat(pb, wt_b, jj, tb)
                        mm_feat(pc, wt_c, jj, tb)
                        mm_feat(px, wt_x, jj, tb)
                        O.copy("act", bT[p][:, tb * 512:(tb + 1) * 512], pb[:], [pb], [(bT[p], tb)])
                        O.copy("act", cst[tb % 2][:], pc[:], [pc], [cst[tb % 2]])
                        O.tt("dve", cx[p][:, 1 + tb * 512:1 + (tb + 1) * 512], px[:], cst[tb % 2][:], ALU.mult,
                             [px, cst[tb % 2]], [(cx[p], tb)])
                        if tb == 1 and j > 0:
                            convA(j - 1)
            convA(7)
            S.barrier()
            S.flush()
        with ExitStack() as s2:
            def sb(name, shape, dt):
                return s2.enter_context(_sbt(nc, name, list(shape), dt))
            qs = [sb(f"p2qs{i}", [128, 8, 64], F32) for i in range(3)]
            tmp = [sb(f"p2tmp{i}", [128, 4, 8, 8], F32) for i in range(3)]
            qr = [sb(f"p2qr{i}", [128, 512], BF16) for i in range(3)]
            stg = [sb(f"p2stg{i}", [128, 4, S_], BF16) for i in range(2)]
            for ci in range(4):
                release(wi)
                wt = get_w(wi)
                wi += 1
                sg = stg[ci % 2]
                def qk_front(t):
                    b = t % 3
                    ps = PS()
                    mm_tok(ps[:], ps, wt, t, 512)
                    O.copy("act", qs[b][:].rearrange("p h d -> p (h d)"), ps[:], [ps], [qs[b]])
                    cs = C.cosT[:, t, :].unsqueeze(1).to_broadcast([128, 8, 8])
                    sn = C.sinT[:, t, :].unsqueeze(1).to_broadcast([128, 8, 8])
                    t1 = qs[b][:, :, 0:8]
                    t2 = qs[b][:, :, 8:16]
                    tm = tmp[b]
                    O.tt("dve", tm[:, 0], t1, cs, ALU.mult, [qs[b], C.cosT], [(tm, 0)])
                    O.tt("dve", tm[:, 1], t2, sn, ALU.mult, [qs[b], C.sinT], [(tm, 1)])
                    O.tt("dve", tm[:, 2], t2, cs, ALU.mult, [qs[b], C.cosT], [(tm, 2)])
                    O.tt("dve", tm[:, 3], t1, sn, ALU.mult, [qs[b], C.sinT], [(tm, 3)])
                    O.tt("dve", t1, tm[:, 0], tm[:, 1], ALU.subtract, [(tm, 0), (tm, 1)], [(qs[b], "a")])
                    O.tt("dve", t2, tm[:, 2], tm[:, 3], ALU.add, [(tm, 2), (tm, 3)], [(qs[b], "b")])
                    O.copy("act", qr[b][:], qs[b][:].rearrange("p h d -> p (h d)"), [qs[b]], [qr[b]])

                def qk_back(t):
                    b = t % 3
                    pt = PT[t % 2]
                    for jj in range(4):
                        O.tr(pt[:, jj, :], qr[b][:, jj * 128:(jj + 1) * 128], C.ident[:], [qr[b], C.ident], [pt])
                    O.copy("act", sg[:, :, t * 128:(t + 1) * 128], pt[:], [pt], [(sg, t)])

                for t in range(NT_ + 2):
                    if t < NT_:
                        qk_front(t)
                    if 0 <= t - 2 < NT_:
                        qk_back(t - 2)
                dst = C.qT if ci < 2 else C.kT
                for jj in range(4):
                    r0 = (ci % 2) * 512 + jj * 128
                    S.dma("sp", dst[r0:r0 + 128, :], sg[:, jj, :], r=[sg], w=[(dst, r0)])
            S.barrier()
            S.flush()
        with ExitStack() as s2:
            def sb(name, shape, dt):
                return s2.enter_context(_sbt(nc, name, list(shape), dt))
            vst = [sb(f"p2vst{i}", [128, 512], BF16) for i in range(4)]
            n = 0
            for ci in range(6):
                release(wi)
                wt = get_w(wi)
                wi += 1
                for t in range(NT_):
                    ps = PS()
                    mm_tok(ps[:], ps, wt, t, 512)
                    vs = vst[n % 4]
                    n += 1
                    if ci < 2:
                        O.copy("act", vs[:], ps[:], [ps], [vs])
                        S.dma("sp", C.v_d[t * 128:(t + 1) * 128, ci * 512:(ci + 1) * 512], vs[:], r=[vs], w=[(C.v_d, (t, ci))])
                    else:
                        O.act(vs[:], ps[:], AF.Silu, [ps], [vs])
                        c2 = ci - 2
                        S.dma("sp", C.sz_d[t * 128:(t + 1) * 128, c2 * 512:(c2 + 1) * 512], vs[:], r=[vs], w=[(C.sz_d, (t, c2))])
            S.barrier()
            S.flush()
        with ExitStack() as s2:
            def sb(name, shape, dt):
                return s2.enter_context(_sbt(nc, name, list(shape), dt))
            xin = sb("p2xin", [128, 4, S_ + 4], BF16)
            diag = sb("p2diag", [128, 4, 5, 128], BF16)
            stok = [sb(f"p2stok{i}", [128, 8, 512], BF16) for i in range(2)]
            sfeat = [sb(f"p2sfeat{i}", [128, S_], BF16) for i in range(2)]
            wS = sb("p2wS", [128, 24, 5], F32)
            bS = sb("p2bS", [128, 24], F32)
            brf = sb("p2brf", [1, 3072], F32)
            brow = sb("p2brow", [1, 3072], BF16)
            ones = sb("p2ones", [1, 128], BF16)
            S.dma("sp", wS[:], C.scwT[l], r=[], w=[wS])
            S.dma("sp", bS[:], C.scbT[l], r=[], w=[bS])
            S.dma("sp", brf[:], C.scbR[l], r=[], w=[brf])
            O.copy("dve", brow[:], brf[:], [brf], [brow])
            O.memset("dve", ones[:], 1.0, [ones])
            O.memset("dve", xin[:, :, 0:2], 0.0, [(xin, "l")])
            O.memset("dve", xin[:, :, S_ + 2:S_ + 4], 0.0, [(xin, "r")])
            nf = 0
            for ci in range(6):
                release(wi)
                wt = get_w(wi)
                wi += 1
                for jj in range(4):
                    J = 4 * ci + jj
                    for tb in range(8):
                        ps = PS()
                        mm_feat(ps, wt, jj, tb)
                        O.copy("act" if tb % 2 else "dve", xin[:, jj, 2 + tb * 512:2 + (tb + 1) * 512], ps[:], [ps], [(xin, (jj, tb))])
                    for k5 in range(5):
                        O.ts("dve", diag[:, jj, k5, :], C.ident[:], wS[:, J, k5:k5 + 1], None, ALU.mult, None,
                             [C.ident, wS], [(diag, (jj, k5))])
                if ci <= 4:
                    for t in range(NT_):
                        ps = PS()
                        for jj in range(4):
                            J = 4 * ci + jj
                            o_ap = ps[:, jj * 128:(jj + 1) * 128]
                            for k5 in range(5):
                                O.mm(o_ap, xin[:, jj, t * 128 + k5:t * 128 + k5 + 128], diag[:, jj, k5, :], k5 == 0, False,
                                     [xin, diag], [ps])
                            O.mm(o_ap, ones[0:1, :], brow[0:1, J * 128:(J + 1) * 128], False, True, [ones, brow], [ps])
                        sk = stok[(t // 8) % 2]
                        O.act(sk[:, t % 8, :], ps[:], AF.Silu, [ps], [(sk, t % 8)])
                        if t % 8 == 7:
                            t0 = t - 7
                            if ci < 4:
                                dst = C.xs_d[t0 * 128:(t0 + 8) * 128, ci * 512:(ci + 1) * 512]
                                key = (C.xs_d, (t0, ci))
                            else:
                                dst = C.Bt_d[t0 * 128:(t0 + 8) * 128, :]
                                key = (C.Bt_d, t0)
                            S.dma("sp", dst.rearrange("(t p) c -> p t c", p=128), sk[:], r=[sk], w=[key])
                if ci >= 4:
                    for jj in range(4):
                        J = 4 * ci + jj
                        sf = sfeat[nf % 2]
                        nf += 1
                        for tb in range(8):
                            ps = PS()
                            for k5 in range(5):
                                O.mm(ps[:], diag[:, jj, k5, :], xin[:, jj, tb * 512 + k5:tb * 512 + k5 + 512], k5 == 0, k5 == 4,
                                     [xin, diag], [ps])
                            O.act(sf[:, tb * 512:(tb + 1) * 512], ps[:], AF.Silu, [ps, bS], [(sf, tb)], bias=bS[:, J:J + 1])
                        dst = C.BT_d if ci == 4 else C.CT_d
                        S.dma("sp", dst[jj * 128:(jj + 1) * 128, :], sf[:], r=[sf], w=[(dst, jj)])
            S.barrier()
            S.flush()
        with ExitStack() as s2:
            def sb(name, shape, dt):
                return s2.enter_context(_sbt(nc, name, list(shape), dt))
            dtbb = sb("p2dtbb", [128, 64], F32)
            abc = sb("p2abc", [128, 64], F32)
            tdt = [sb(f"p2tdt{i}", [128, 64], F32) for i in range(2)]
            S.dma("sp", dtbb[:], C.dtb[l].partition_broadcast(128), r=[], w=[dtbb])
            S.dma("sp", abc[:], C.alog[l].partition_broadcast(128), r=[], w=[abc])
            O.act(abc[:], abc[:], AF.Exp, [abc], [abc])
            O.ts("dve", abc[:], abc[:], -1.0, None, ALU.mult, None, [abc], [abc])
            release(wi)
            wt = get_w(wi)
            wi += 1
            for t in range(NT_):
                ps = PS()
                mm_tok(ps[:, 0:64], ps, wt, t, 64)
                td = tdt[t % 2]
                O.tt("dve", td[:], ps[:, 0:64], dtbb[:], ALU.add, [ps, dtbb], [td])
                O.act(td[:], td[:], AF.Exp, [td], [td])
                O.act(C.dt_all[:, t, :], td[:], AF.Ln, [td], [(C.dt_all, t)], bias=1.0)
                O.tt("dve", C.la_all[:, t, :], C.dt_all[:, t, :], abc[:], ALU.mult, [(C.dt_all, t), abc], [(C.la_all, t)])
            if "dt_dbg" in C.dbg:
                S.dma("sp", C.dt_dbg, C.dt_all[:], r=[C.dt_all], w=[C.dt_dbg])
            S.barrier()
            S.flush()
        with ExitStack() as s2:
            def sb(name, shape, dt):
                return s2.enter_context(_sbt(nc, name, list(shape), dt))
            sfeat = [sb(f"p2gfeat{i}", [128, S_], BF16) for i in range(2)]
            nf = 0
            for ci in range(6):
                release(wi)
                wt = get_w(wi)
                wi += 1
                for jj in range(4):
                    G = 4 * ci + jj
                    sf = sfeat[nf % 2]
                    nf += 1
                    for tb in range(8):
                        ps = PS()
                        mm_feat(ps, wt, jj, tb)
                        O.act(sf[:, tb * 512:(tb + 1) * 512], ps[:], AF.Sigmoid, [ps], [(sf, tb)])
                    S.dma("sp", C.gT[:, :, G, :].rearrange("tb p t -> p tb t"), sf[:].rearrange("p (tb t) -> p tb t", t=256),
                          r=[sf], w=[(C.gT, G)])
            S.barrier()
            S.flush()
        assert wi == len(seq)


def phase3(C, l):
    nc, S, O = C.nc, C.S, C.O
    C.cast_w(C.w_1[l], C.wb_1[l], 128)
    pats = [(1, 33), (4, 9), (16, 3)]
    with ExitStack() as st:
        def sb(name, shape, dt):
            return st.enter_context(_sbt(nc, name, list(shape), dt))
        qTh = [sb(f"p3q{i}", [64, S_], BF16) for i in range(2)]
        kTp = [sb(f"p3k{i}", [64, S_ + 2048], BF16) for i in range(2)]
        vP = [[sb(f"p3v{pi}_{b}", [128, nb, dil, 65], BF16) for b in range(2)] for pi, (dil, nb) in enumerate(pats)]
        mask = sb("p3mask", [128, 2, 128], BF16)
        PTs = [sb(f"p3pt{i}", [128, 2, 2, 128], BF16) for i in range(2)]
        PTm = [sb(f"p3pm{i}", [128, 2, 2, 128], BF16) for i in range(2)]
        ost = [sb(f"p3ost{i}", [128, 32, 65], F32) for i in range(2)]
        PSs = [st.enter_context(_pst(nc, f"p3ps{i}", [128, 512], F32)) for i in range(3)]
        PSo = [st.enter_context(_pst(nc, f"p3po{i}", [128, 2, 65], F32)) for i in range(3)]
        O.copy("dve", mask[:, 0, :], C.tri[:, 1, :], [C.tri], [(mask, 0)])
        O.copy("dve", mask[:, 1, :], C.tri[:, 0, :], [C.tri], [(mask, 1)])
        for b in range(2):
            O.memset("dve", kTp[b][:, 0:1024], 0.0, [(kTp[b], "l")])
            O.memset("dve", kTp[b][:, 1024 + S_:2048 + S_], 0.0, [(kTp[b], "r")])
            for pi, (dil, nb) in enumerate(pats):
                v = vP[pi][b]
                O.memset("pool", v[:], 0.0, [v])
                O.memset("pool", v[:, :, :, 64:65], 1.0, [v])
                O.memset("pool", v[0:64, 0, :, 64:65], 0.0, [v])
                O.memset("pool", v[64:128, nb - 1, :, 64:65], 0.0, [v])
        def loads(h):
            b = h % 2
            S.dma("sp", qTh[b][:], C.qT[h * 64:(h + 1) * 64, :], r=[C.qT], w=[qTh[b]])
            S.dma("sp", kTp[b][:, 1024:1024 + S_], C.kT[h * 64:(h + 1) * 64, :], r=[C.kT], w=[(kTp[b], "m")])
            vsrc = C.v_d[:, h * 64:(h + 1) * 64]
            for pi, (dil, nb) in enumerate(pats):
                v = vP[pi][b]
                nin = nb - 2
                if dil == 1:
                    for j0 in range(0, nin, 8):
                        j1 = min(nin, j0 + 8)
                        src = vsrc[64 + j0 * 128:64 + j1 * 128, :]
                        S.dma("sp", v[:, 1 + j0:1 + j1, 0, 0:64], src.rearrange("(j i) d -> i j d", i=128),
                              r=[C.v_d], w=[(v, ("in", j0))])
                else:
                    for j0 in range(nin):
                        src = vsrc[64 * dil + j0 * 128 * dil:64 * dil + (j0 + 1) * 128 * dil, :]
                        S.dma("sp", v[:, 1 + j0, :, 0:64], src.rearrange("(i r) d -> i r d", r=dil),
                              r=[C.v_d], w=[(v, ("in", j0))])
                S.dma("sp", v[64:128, 0, :, 0:64], vsrc[0:64 * dil, :].rearrange("(i r) d -> i r d", r=dil),
                      r=[C.v_d], w=[(v, "first")])
                S.dma("sp", v[0:64, nb - 1, :, 0:64], vsrc[S_ - 64 * dil:S_, :].rearrange("(i r) d -> i r d", r=dil),
                      r=[C.v_d], w=[(v, "last")])

        def store(h, pi, osb):
            dil, nb = pats[pi]
            nqb = nb - 1
            osv = osb[:].rearrange("p (q r) e -> p q r e", r=dil)
            if dil == 1:
                S.dma("sp", C.o_d[pi][:, h, :].rearrange("(q i) e -> i q e", i=128), osb[:], r=[osb],
                      w=[(C.o_d[pi], h)])
            else:
                for q in range(nqb):
                    S.dma("sp", C.o_d[pi][q * 128 * dil:(q + 1) * 128 * dil, h, :].rearrange("(i r) e -> i r e", r=dil),
                          osv[:, q, :, :], r=[osb], w=[(C.o_d[pi], (h, q))])

        pairs = []
        n_ost = 0
        for h in range(16):
            first = True
            for pi, (dil, nb) in enumerate(pats):
                osb = ost[n_ost % 2]
                n_ost += 1
                nqb = nb - 1
                lst = [(r, qp) for r in range(dil) for qp in range(nqb // 2)]
                for idx, (r, qp) in enumerate(lst):
                    pairs.append(dict(h=h, pi=pi, dil=dil, r=r, qp=qp, osb=osb, pre=None,
                                      post=(h, pi, osb) if idx == len(lst) - 1 else None))
        npairs_h = len(pairs) // 16
        for h in range(16):
            if h == 0:
                pairs[0]["pre"] = 0
            if h + 1 < 16:
                pairs[h * npairs_h + 4]["pre"] = h + 1
        NPS = 3

        def stA(i):
            p = pairs[i]
            b = p["h"] % 2
            dil, r, qp = p["dil"], p["r"], p["qp"]
            ps = PSs[i % NPS]
            psv = ps[:].rearrange("p (a b c) -> p a b c", a=2, b=2)
            for qi in range(2):
                qb = 2 * qp + qi
                q0 = 128 * dil * qb + r
                qsl = qTh[b][:, q0:q0 + 127 * dil + 1:dil]
                for ab in range(2):
                    k0 = 1024 - 64 * dil + 128 * dil * (qb + ab) + r
                    ksl = kTp[b][:, k0:k0 + 127 * dil + 1:dil]
                    O.mm(psv[:, qi, ab, :], ksl, qsl, True, True, [kTp[b], qTh[b]], [ps])

        def stB(i):
            ps = PSs[i % NPS]
            pt = PTs[i % 2]
            pm = PTm[i % 2]
            O.act(pt[:].rearrange("p a b c -> p (a b c)"), ps[:], AF.Exp, [ps], [pt], scale=0.125)
            O.tt("dve", pm[:], pt[:], mask[:].unsqueeze(1).to_broadcast([128, 2, 2, 128]), ALU.mult, [pt, mask], [pm])

        def stC(i):
            p = pairs[i]
            b = p["h"] % 2
            dil, r, qp, pi = p["dil"], p["r"], p["qp"], p["pi"]
            v = vP[pi][b]
            pm = PTm[i % 2]
            po = PSo[i % 3]
            osb = p["osb"]
            osv = osb[:].rearrange("p (q r) e -> p q r e", r=dil)
            for qi in range(2):
                qb = 2 * qp + qi
                for ab in range(2):
                    O.mm(po[:, qi, :], pm[:, qi, ab, :], v[:, qb + ab, r, :], ab == 0, ab == 1, [pm, v], [po])
            O.copy("act" if i % 2 else "dve", osv[:, 2 * qp:2 * qp + 2, r, :], po[:], [po], [(osb, (r, qp))])
            if p["post"] is not None:
                store(*p["post"])

        n = len(pairs)
        for i in range(n + 2):
            if i < n:
                if pairs[i]["pre"] is not None:
                    loads(pairs[i]["pre"])
                stA(i)
            if 0 <= i - 1 < n:
                stB(i - 1)
            if 0 <= i - 2 < n:
                stC(i - 2)
        S.barrier()
        S.flush()


def phase4(C, l):
    nc, S, O = C.nc, C.S, C.O
    tri, trif = C.tri, C.trif
    with ExitStack() as st:
        def sb(name, shape, dt):
            return st.enter_context(_sbt(nc, name, list(shape), dt))
        H = [[sb(f"p4H{d}_{i}", [128, 2048], F32) for i in range(2)] for d in range(2)]
        tmpH = [sb(f"p4tmpH{d}", [128, 2048], F32) for d in range(2)]
        hbf = [[sb(f"p4hbf{d}_{i}", [128, 2048], BF16) for i in range(2)] for d in range(2)]
        xs_t = [sb(f"p4xs{i}", [128, 2048], BF16) for i in range(4)]
        Bt_t = [sb(f"p4Bt{i}", [128, 512], BF16) for i in range(4)]
        ew = [sb(f"p4ew{i}", [128, 2, 32], F32) for i in range(4)]
        dtw = [sb(f"p4dtw{i}", [128, 32], F32) for i in range(2)]
        xw = [sb(f"p4xw{i}", [128, 2048], BF16) for i in range(2)]
        st_sb = [sb(f"p4st{i}", [128, 2048], F32) for i in range(3)]
        PSw = [st.enter_context(_pst(nc, f"p4psw{i}", [128, 2, 32], F32)) for i in range(2)]
        PSs = [st.enter_context(_pst(nc, f"p4pss{i}", [128, 512], F32)) for i in range(4)]
        for d in range(2):
            O.memset("dve", H[d][0][:], 0.0, [H[d][0]])
            O.memset("pool", hbf[d][0][:], 0.0, [hbf[d][0]])

        def stA(n):
            i, d = divmod(n, 2)
            c = i if d == 0 else NT_ - 1 - i
            xt, bt, e_, dw, xw_, pw = xs_t[n % 4], Bt_t[n % 4], ew[n % 4], dtw[n % 2], xw[n % 2], PSw[n % 2]
            S.dma("sp", xt[:], C.xs_d[c * 128:(c + 1) * 128, :], r=[C.xs_d], w=[xt])
            S.dma("sp", bt[:], C.Bt_d[c * 128:(c + 1) * 128, :], r=[C.Bt_d], w=[bt])
            la_c = C.la_all[:, c, d * 32:(d + 1) * 32]
            dt_c = C.dt_all[:, c, d * 32:(d + 1) * 32]
            O.mm(pw[:, 0, :], trif[:, 1 if d == 0 else 0, :], la_c, True, True, [trif, C.la_all], [pw])
            O.mm(pw[:, 1, :], trif[:, 2, :], la_c, True, True, [trif, C.la_all], [pw])
            O.act(e_[:], pw[:], AF.Exp, [pw], [e_])
            O.tt("dve", dw[:], dt_c, e_[:, 0, :], ALU.mult, [C.dt_all, e_], [dw])
            O.tt("pool" if d == 0 else "dve", xw_[:].rearrange("p (h d) -> p h d", d=64), xt[:].rearrange("p (h d) -> p h d", d=64),
                 dw[:].unsqueeze(2).to_broadcast([128, 32, 64]), ALU.mult, [xt, dw], [xw_])
            ss_ = st_sb[n % 3]
            for g in range(4):
                O.mm(PSs[g][:], bt[:, g * 128:(g + 1) * 128], xw_[:, g * 512:(g + 1) * 512], True, True, [bt, xw_], [PSs[g]])
                O.copy("act", ss_[:, g * 512:(g + 1) * 512], PSs[g][:], [PSs[g]], [(ss_, g)])

        def stB(n):
            i, d = divmod(n, 2)
            c = i if d == 0 else NT_ - 1 - i
            eng = "dve" if d == 0 else "pool"
            e_ = ew[n % 4]
            hb = hbf[d][i % 2]
            Hs, Hd = H[d][i % 2], H[d][(i + 1) % 2]
            S.dma("sp", C.hin_d[d][c], hb[:], r=[hb], w=[(C.hin_d[d], c)])
            O.tt(eng, tmpH[d][:].rearrange("p (h d) -> p h d", d=64), Hs[:].rearrange("p (h d) -> p h d", d=64),
                 e_[:, 1, :].unsqueeze(2).to_broadcast([128, 32, 64]), ALU.mult, [Hs, e_], [tmpH[d]])
            O.tt(eng, Hd[:], tmpH[d][:], st_sb[n % 3][:], ALU.add, [tmpH[d], st_sb[n % 3]], [Hd])

        def stC(n):
            i, d = divmod(n, 2)
            O.copy("act", hbf[d][(i + 1) % 2][:], H[d][(i + 1) % 2][:], [H[d][(i + 1) % 2]], [hbf[d][(i + 1) % 2]])

        NN = 2 * NT_
        for n in range(NN + 2):
            if n < NN:
                stA(n)
            if 0 <= n - 1 < NN:
                stB(n - 1)
            if 0 <= n - 2 < NN:
                stC(n - 2)
        S.barrier()
        S.flush()
    with ExitStack() as st:
        def sb(name, shape, dt):
            return st.enter_context(_sbt(nc, name, list(shape), dt))
        NB = 3
        xs_t = [sb(f"p4xs{i}", [128, 2048], BF16) for i in range(NB)]
        BT_t = [sb(f"p4BT{i}", [128, 4, 128], BF16) for i in range(NB)]
        CT_t = [sb(f"p4CT{i}", [128, 4, 128], BF16) for i in range(NB)]
        sz_t = [sb(f"p4sz{i}", [128, 2048], BF16) for i in range(NB)]
        hh_t = [[sb(f"p4hh{d}_{i}", [128, 2048], BF16) for i in range(NB)] for d in range(2)]
        cbm = [[sb(f"p4cbm{d}_{i}", [128, 4, 128], BF16) for i in range(2)] for d in range(2)]
        ecum = [[sb(f"p4ecum{d}_{i}", [128, 32], F32) for i in range(2)] for d in range(2)]
        rseg = [[sb(f"p4rseg{d}_{i}", [128, 8, 128], BF16) for i in range(2)] for d in range(2)]
        xd = [[sb(f"p4xd{d}_{i}", [128, 512], BF16) for i in range(4)] for d in range(2)]
        eseg = [sb(f"p4eseg{i}", [128, 4, 128], BF16) for i in range(8)]
        MT = [sb(f"p4MT{i}", [128, 4, 128], BF16) for i in range(8)]
        tt_ = [[sb(f"p4t{d}_{i}", [128, 512], F32) for i in range(2)] for d in range(2)]
        xsD = [sb(f"p4xsD{i}", [128, 512], F32) for i in range(2)]
        yy = [sb(f"p4y{i}", [128, 512], F32) for i in range(3)]
        ynf = [sb(f"p4ynf{i}", [128, 512], BF16) for i in range(2)]
        ssg = [sb(f"p4ssg{i}", [128, 1], F32) for i in range(2)]
        sqj = sb("p4sqj", [128, 512], BF16)
        ynT_st = [sb(f"p4ynT{i}", [128, 16, 128], BF16) for i in range(3)]
        nwb = sb("p4nwb", [128, 2048], F32)
        Dbc = sb("p4Dbc", [128, 32], F32)
        PScb = st.enter_context(_pst(nc, "p4pscb", [128, 512], F32))
        PSy = [st.enter_context(_pst(nc, f"p4psy{i}", [128, 512], F32)) for i in range(2)]
        PSseg = [st.enter_context(_pst(nc, f"p4psseg{i}", [128, 512], F32)) for i in range(2)]
        PSo = [st.enter_context(_pst(nc, f"p4pso{i}", [128, 512], F32)) for i in range(2)]
        PSt = st.enter_context(_pst(nc, "p4pst", [128, 512], F32))
        S.dma("sp", nwb[:], C.snw[l].partition_broadcast(128), r=[], w=[nwb])
        S.dma("sp", Dbc[:], C.sdd[l].partition_broadcast(128), r=[], w=[Dbc])

        def loads(c):
            b = c % NB
            S.dma("sp", xs_t[b][:], C.xs_d[c * 128:(c + 1) * 128, :], r=[C.xs_d], w=[xs_t[b]])
            S.dma("sp", BT_t[b][:], C.BT_d[:, c * 128:(c + 1) * 128].rearrange("(g n) t -> n g t", n=128), r=[C.BT_d], w=[BT_t[b]])
            S.dma("sp", CT_t[b][:], C.CT_d[:, c * 128:(c + 1) * 128].rearrange("(g n) t -> n g t", n=128), r=[C.CT_d], w=[CT_t[b]])
            S.dma("sp", sz_t[b][:], C.sz_d[c * 128:(c + 1) * 128, :], r=[C.sz_d], w=[sz_t[b]])
            for d in range(2):
                S.dma("sp", hh_t[d][b][:], C.hin_d[d][c], r=[C.hin_d[d]], w=[hh_t[d][b]])

        units = [(d, q) for d in range(2) for q in range(2)]

        def s1(k):
            c, g = divmod(k, 4)
            b, cp = c % NB, c % 2
            xt, BT, CT = xs_t[b], BT_t[b], CT_t[b]
            if g == 0:
                pcb = PScb[:].rearrange("p (g t) -> p g t", g=4)
                for g2 in range(4):
                    O.mm(pcb[:, g2, :], BT[:, g2, :], CT[:, g2, :], True, True, [BT, CT], [PScb])
                for d in range(2):
                    O.tt("dve", cbm[d][cp][:], pcb, tri[:, d, :].unsqueeze(1).to_broadcast([128, 4, 128]), ALU.mult,
                         [PScb, tri], [cbm[d][cp]])
                pse = PScb[:, 0:64].rearrange("p (d h) -> p d h", d=2)
                for d in range(2):
                    la_c = C.la_all[:, c, d * 32:(d + 1) * 32]
                    O.mm(pse[:, d, :], trif[:, 3 + d, :], la_c, True, True, [trif, C.la_all], [PScb])
                for d in range(2):
                    O.act(ecum[d][cp][:], pse[:, d, :], AF.Exp, [PScb], [ecum[d][cp]])
            for d in range(2):
                la_g = C.la_all[:, c, d * 32 + g * 8:d * 32 + (g + 1) * 8]
                dt_g = C.dt_all[:, c, d * 32 + g * 8:d * 32 + (g + 1) * 8]
                O.tt("dve", rseg[d][k % 2][:], la_g.unsqueeze(2).to_broadcast([128, 8, 128]),
                     tri[:, d, :].unsqueeze(1).to_broadcast([128, 8, 128]), ALU.mult, [C.la_all, tri], [rseg[d][k % 2]])
                O.tt("pool", xd[d][k % 4][:].rearrange("p (h e) -> p h e", e=64),
                     xt[:, g * 512:(g + 1) * 512].rearrange("p (h e) -> p h e", e=64),
                     dt_g.unsqueeze(2).to_broadcast([128, 8, 64]), ALU.mult, [xt, C.dt_all], [xd[d][k % 4]])

        def s2(k):
            for u, (d, q) in enumerate(units):
                n = k * 4 + u
                pss = PSseg[n % 2]
                es = eseg[n % 8]
                O.mm(pss[:], tri[:, 3 - d, :], rseg[d][k % 2][:, q * 4:(q + 1) * 4, :].rearrange("p h t -> p (h t)"), True, True,
                     [tri, rseg[d][k % 2]], [pss])
                O.act(es[:].rearrange("p h t -> p (h t)"), pss[:], AF.Exp, [pss], [es])

        def s3(k):
            c, g = divmod(k, 4)
            cp = c % 2
            for u, (d, q) in enumerate(units):
                n = k * 4 + u
                O.tt("dve" if u % 2 else "pool", MT[n % 8][:], eseg[n % 8][:],
                     cbm[d][cp][:, g, :].unsqueeze(1).to_broadcast([128, 4, 128]), ALU.mult, [eseg[n % 8], cbm[d][cp]], [MT[n % 8]])

        def s4(k):
            for u, (d, q) in enumerate(units):
                n = k * 4 + u
                mt = MT[n % 8]
                for hh in range(4):
                    h8 = q * 4 + hh
                    O.mm(PSy[k % 2][:, h8 * 64:(h8 + 1) * 64], mt[:, hh, :], xd[d][k % 4][:, h8 * 64:(h8 + 1) * 64],
                         u == 0 and hh == 0, u == 3 and hh == 3, [mt, xd[d][k % 4]], [PSy[k % 2]])

        def s5(k):
            c, g = divmod(k, 4)
            b, cp, kp = c % NB, c % 2, k % 2
            xt, CT, sz = xs_t[b], CT_t[b], sz_t[b]
            for d in range(2):
                hd = hh_t[d][b]
                O.mm(PSo[d][:], CT[:, g, :], hd[:, g * 512:(g + 1) * 512], True, True, [CT, hd], [PSo[d]])
            O.tt("pool", xsD[kp][:].rearrange("p (h e) -> p h e", e=64),
                 xt[:, g * 512:(g + 1) * 512].rearrange("p (h e) -> p h e", e=64),
                 Dbc[:, g * 8:(g + 1) * 8].unsqueeze(2).to_broadcast([128, 8, 64]), ALU.mult, [xt, Dbc], [xsD[kp]])
            for d in range(2):
                O.tt("dve", tt_[d][kp][:].rearrange("p (h e) -> p h e", e=64), PSo[d][:].rearrange("p (h e) -> p h e", e=64),
                     ecum[d][cp][:, g * 8:(g + 1) * 8].unsqueeze(2).to_broadcast([128, 8, 64]), ALU.mult,
                     [PSo[d], ecum[d][cp]], [tt_[d][kp]])
            O.tt("pool", tt_[0][kp][:], tt_[0][kp][:], tt_[1][kp][:], ALU.add, [tt_[0][kp], tt_[1][kp]], [tt_[0][kp]])
            O.tt("dve", tt_[0][kp][:], tt_[0][kp][:], xsD[kp][:], ALU.add, [tt_[0][kp], xsD[kp]], [tt_[0][kp]])
            y_ = yy[k % 3]
            O.tt("dve", y_[:], PSy[kp][:], tt_[0][kp][:], ALU.add, [PSy[kp], tt_[0][kp]], [y_])
            O.tt("pool", y_[:], y_[:], sz[:, g * 512:(g + 1) * 512], ALU.mult, [y_, sz], [y_])

        def s6(k):
            c, g = divmod(k, 4)
            y_ = yy[k % 3]
            s_ = ssg[k % 2]
            O.act(sqj[:], y_[:], AF.Square, [y_], [sqj, s_], accum_out=s_[:])
            O.act(s_[:], s_[:], AF.Sqrt, [s_], [s_], bias=1e-6, scale=1.0 / 512)
            O.recip(s_[:], s_[:], [s_], [s_])
            O.stt("dve", ynf[k % 2][:], y_[:], s_[:], nwb[:, g * 512:(g + 1) * 512], ALU.mult, ALU.mult, [y_, s_, nwb], [ynf[k % 2]])

        def s7(k):
            c, g = divmod(k, 4)
            ptb = PSt[:, 0:256].bitcast(BF16).rearrange("p (a t) -> p a t", a=4)
            ys = ynT_st[c % 3]
            yn_ = ynf[k % 2]
            for a in range(4):
                O.tr(ptb[:, a, :], yn_[:, a * 128:(a + 1) * 128], C.ident[:], [yn_, C.ident], [PSt])
            O.copy("act", ys[:, g * 4:(g + 1) * 4, :], ptb, [PSt], [(ys, g)])
            if g == 3:
                S.dma("sp", C.ynT_d[c // 2, :, :, (c % 2) * 128:(c % 2 + 1) * 128], ys[:], r=[ys],
                      w=[(C.ynT_d, c)])

        stages = [s1, s2, s3, s4, s5, s6, s7]
        NK = NT_ * 4
        for c in range(NB):
            loads(c)
        for it in range(NK + len(stages) - 1):
            if it >= 8 and it % 4 == 0:
                cn = (it - 8) // 4 + NB
                if cn < NT_:
                    loads(cn)
            for si in range(len(stages) - 1, -1, -1):
                k = it - si
                if 0 <= k < NK:
                    stages[si](k)
        S.barrier()
        S.flush()


def phase5(C, l, x_src, x_dst):
    nc, S, O = C.nc, C.S, C.O
    with ExitStack() as st:
        def sb(name, shape, dt):
            return st.enter_context(_sbt(nc, name, list(shape), dt))
        TB = 256
        NTB = S_ // TB
        wa = sb("p5wa", [128, 8, D_], BF16)
        wb_ = sb("p5wb", [128, 8, D_], BF16)
        wc = sb("p5wc", [128, 16, D_], BF16)
        wo = sb("p5wo", [128, 8, D_], BF16)
        yA_b = [sb(f"p5yA{i}", [128, 8, TB], BF16) for i in range(2)]
        yn_b = [sb(f"p5yn{i}", [128, 16, TB], BF16) for i in range(2)]
        oT_b = [sb(f"p5oT{i}", [128, 8, TB], BF16) for i in range(2)]
        g_b = [sb(f"p5g{i}", [128, 24, TB], BF16) for i in range(2)]
        mT = [sb(f"p5mT{i}", [128, 8, TB], BF16) for i in range(1)] * 2
        ot = [[sb(f"p5ot{p}_{i}", [128, 16, 65], F32) for p in range(3)] for i in range(2)]
        rden = [sb(f"p5rden{i}", [128, 16], F32) for i in range(2)]
        ob = [sb(f"p5ob{i}", [128, 16, 64], BF16) for i in range(1)] * 2
        xt = [sb(f"p5xt{i}", [128, D_], F32) for i in range(1)] * 2
        xo = [sb(f"p5xo{i}", [128, D_], F32) for i in range(1)] * 2
        t1 = [sb(f"p5t1_{i}", [128, TB], F32) for i in range(2)]
        t2 = [sb(f"p5t2_{i}", [128, TB], F32) for i in range(2)]
        PB = [st.enter_context(_pst(nc, f"p5ps{i}", [128, 512], F32)) for i in range(6)]
        PT = [st.enter_context(_pst(nc, f"p5pt{i}", [128, 8, 128], BF16)) for i in range(2)]
        S.dma("pool", wa[:], C.w_a[l].rearrange("(k p) n -> p k n", p=128), r=[], w=[wa])
        S.dma("pool", wb_[:], C.w_b[l].rearrange("(k p) n -> p k n", p=128), r=[], w=[wb_])
        S.dma("pool", wc[:], C.w_c[l].rearrange("(k p) n -> p k n", p=128), r=[], w=[wc])
        S.dma("pool", wo[:], C.w_o[l].rearrange("(k p) n -> p k n", p=128), r=[], w=[wo])
        cnt = {"ps": 0, "o": 0}

        def PS():
            p = PB[cnt["ps"] % 6]
            cnt["ps"] += 1
            return p

        def loads(tb):
            b = tb % 2
            tsl = slice(tb * TB, (tb + 1) * TB)
            S.dma("sp", yA_b[b][:], C.yAT[tb], r=[C.yAT], w=[yA_b[b]])
            S.dma("sp", yn_b[b][:], C.ynT_d[tb], r=[C.ynT_d], w=[yn_b[b]])
            S.dma("sp", g_b[b][:], C.gT[tb], r=[C.gT], w=[g_b[b]])

        def oloads(tb):
            for tt in range(TB // 128):
                t = tb * (TB // 128) + tt
                for p in range(3):
                    S.dma("sp", ot[tt][p][:], C.o_d[p][t * 128:(t + 1) * 128], r=[C.o_d[p]], w=[ot[tt][p]])

        def combine(tb):
            b = tb % 2
            for tt in range(TB // 128):
                n = cnt["o"]
                cnt["o"] += 1
                o3 = ot[tt]
                O.tt("pool", o3[0][:], o3[0][:], o3[1][:], ALU.add, [o3[0], o3[1]], [o3[0]])
                O.tt("pool", o3[0][:], o3[0][:], o3[2][:], ALU.add, [o3[0], o3[2]], [o3[0]])
                O.recip(rden[n % 2][:].unsqueeze(2), o3[0][:, :, 64:65], [o3[0]], [rden[n % 2]])
                O.tt("pool", ob[n % 2][:], o3[0][:, :, 0:64], rden[n % 2][:].unsqueeze(2).to_broadcast([128, 16, 64]), ALU.mult,
                     [o3[0], rden[n % 2]], [ob[n % 2]])
                obf = ob[n % 2][:].rearrange("p h d -> p (h d)")
                for k in range(8):
                    O.tr(PT[n % 2][:, k, :], obf[:, k * 128:(k + 1) * 128], C.ident[:], [ob[n % 2], C.ident], [PT[n % 2]])
                O.copy("act", oT_b[b][:, :, tt * 128:(tt + 1) * 128], PT[n % 2][:], [PT[n % 2]], [(oT_b[b], tt)])

        loads(0)
        oloads(0)
        combine(0)
        nx = 0
        for tb in range(NTB):
            b = tb % 2
            if tb + 1 < NTB:
                loads(tb + 1)
                oloads(tb + 1)
            for cc in range(8):
                if cc == 4 and tb + 1 < NTB:
                    combine(tb + 1)
                pa, pb, pc = PS(), PS(), PS()
                for k in range(8):
                    O.mm(pa[:, :TB], wa[:, k, cc * 128:(cc + 1) * 128], yA_b[b][:, k, :], k == 0, k == 7, [wa, yA_b[b]], [pa])
                for k in range(8):
                    O.mm(pb[:, :TB], wb_[:, k, cc * 128:(cc + 1) * 128], oT_b[b][:, k, :], k == 0, k == 7, [wb_, oT_b[b]], [pb])
                for k in range(16):
                    O.mm(pc[:, :TB], wc[:, k, cc * 128:(cc + 1) * 128], yn_b[b][:, k, :], k == 0, k == 15, [wc, yn_b[b]], [pc])
                a1, a2 = t1[cc % 2], t2[cc % 2]
                O.tt("dve", a1[:], pa[:, :TB], g_b[b][:, cc, :], ALU.mult, [pa, g_b[b]], [a1])
                O.tt("dve", a2[:], pb[:, :TB], g_b[b][:, 8 + cc, :], ALU.mult, [pb, g_b[b]], [a2])
                O.tt("pool", a1[:], a1[:], a2[:], ALU.add, [a1, a2], [a1])
                O.tt("dve", a2[:], pc[:, :TB], g_b[b][:, 16 + cc, :], ALU.mult, [pc, g_b[b]], [a2])
                O.tt("pool", mT[b][:, cc, :], a1[:], a2[:], ALU.add, [a1, a2], [(mT[b], cc)])
            for tt in range(TB // 128):
                t = tb * (TB // 128) + tt
                xt_, xo_ = xt[nx % 2], xo[nx % 2]
                nx += 1
                S.dma("sp", xt_[:], x_src[t * 128:(t + 1) * 128, :], r=[(x_src, t)], w=[xt_])
                for hf in range(2):
                    ps = PS()
                    for k in range(8):
                        O.mm(ps[:], mT[b][:, k, tt * 128:(tt + 1) * 128], wo[:, k, hf * 512:(hf + 1) * 512], k == 0, k == 7,
                             [mT[b], wo], [ps])
                    O.tt("dve", xo_[:, hf * 512:(hf + 1) * 512], ps[:], xt_[:, hf * 512:(hf + 1) * 512], ALU.add,
                         [ps, xt_], [(xo_, hf)])
                S.dma("sp", C.xmid[t * 128:(t + 1) * 128, :], xo_[:], r=[xo_], w=[(C.xmid, t)])
        S.barrier()
        S.flush()
    if "xmid" in C.dbg and C.stop == (l, 5):
        return
    with ExitStack() as st:
        def sb(name, shape, dt):
            return st.enter_context(_sbt(nc, name, list(shape), dt))
        w2 = sb("p5w2", [128, 32, D_], BF16)
        w1b = [sb(f"p5w1_{i}", [128, 8, 512], BF16) for i in range(3)]
        xts = [sb(f"p5x{i}", [128, D_], F32) for i in range(4)]
        hT = sb("p5hT", [128, 8, 512], BF16)
        h1T = sb("p5h1T", [128, 32, 512], BF16)
        ub = [sb(f"p5ub{i}", [128, D_], BF16) for i in range(2)]
        sq = sb("p5sq", [128, D_], BF16)
        ss = [sb(f"p5ss{i}", [128, 1], F32) for i in range(2)]
        rr = [sb(f"p5r{i}", [128, 512], F32) for i in range(2)]
        xo = [sb(f"p5xo{i}", [128, D_], F32) for i in range(2)]
        nwt = sb("p5nw", [128, D_], F32)
        fnw = sb("p5fnw", [128, D_], F32)
        PB = [st.enter_context(_pst(nc, f"p5bps{i}", [128, 512], F32)) for i in range(6)]
        PT = [st.enter_context(_pst(nc, f"p5bpt{i}", [128, 8, 128], BF16)) for i in range(2)]
        w2src = C.w_2[l].rearrange("(k p) n -> p k n", p=128)
        for k8 in range(4):
            S.dma("pool", w2[:, k8 * 8:(k8 + 1) * 8, :], w2src[:, k8 * 8:(k8 + 1) * 8, :], r=[], w=[(w2, k8)])
        S.dma("sp", nwt[:], C.mlpw[l:l + 1, :].partition_broadcast(128), r=[], w=[nwt])
        S.dma("sp", fnw[:], C.finw.partition_broadcast(128), r=[], w=[fnw])
        w1src = C.wb_1[l].rearrange("(k p) n -> p k n", p=128)
        nps = 0
        nw1 = 0
        for tb in range(8):
            for tt in range(4):
                t = tb * 4 + tt
                S.dma("sp", xts[tt][:], C.xmid[t * 128:(t + 1) * 128, :], r=[(C.xmid, t)], w=[xts[tt]])
                rms_tile(C, xts[tt], ss[tt % 2], sq, nwt, ub[tt % 2], "p5")
                for k in range(8):
                    O.tr(PT[tt % 2][:, k, :], ub[tt % 2][:, k * 128:(k + 1) * 128], C.ident[:], [ub[tt % 2], C.ident], [PT[tt % 2]])
                O.copy("act", hT[:, :, tt * 128:(tt + 1) * 128], PT[tt % 2][:], [PT[tt % 2]], [(hT, tt)])
            for f4 in range(8):
                w1t = w1b[nw1 % 3]
                nw1 += 1
                S.dma("sp", w1t[:], w1src[:, :, f4 * 512:(f4 + 1) * 512], r=[C.wb_1[l]], w=[w1t])
                for fj in range(4):
                    fc = f4 * 4 + fj
                    ps = PB[nps % 6]
                    r_ = rr[nps % 2]
                    nps += 1
                    for k in range(8):
                        O.mm(ps[:], w1t[:, k, fj * 128:(fj + 1) * 128], hT[:, k, :], k == 0, k == 7, [w1t, hT], [ps])
                    O.act(r_[:], ps[:], AF.Relu, [ps], [r_])
                    O.tt("pool" if fc % 2 else "dve", h1T[:, fc, :], r_[:], r_[:], ALU.mult, [r_], [(h1T, fc)])
            for tt in range(4):
                t = tb * 4 + tt
                xo_ = xo[tt % 2]
                for hf in range(2):
                    ps = PB[nps % 6]
                    nps += 1
                    for fc in range(32):
                        O.mm(ps[:], h1T[:, fc, tt * 128:(tt + 1) * 128], w2[:, fc, hf * 512:(hf + 1) * 512], fc == 0, fc == 31,
                             [h1T, w2], [ps])
                    O.tt("dve", xo_[:, hf * 512:(hf + 1) * 512], ps[:], xts[tt][:, hf * 512:(hf + 1) * 512], ALU.add,
                         [ps, xts[tt]], [(xo_, hf)])
                if x_dst is not None:
                    S.dma("sp", x_dst[t * 128:(t + 1) * 128, :], xo_[:], r=[xo_], w=[(x_dst, t)])
                else:
                    s_ = ss[tt % 2]
                    O.act(sq[:], xo_[:], AF.Square, [xo_], [sq, s_], accum_out=s_[:])
                    O.act(s_[:], s_[:], AF.Sqrt, [s_], [s_], bias=1e-6, scale=1.0 / D_)
                    O.recip(s_[:], s_[:], [s_], [s_])
                    O.stt("dve", xo_[:], xo_[:], s_[:], fnw[:], ALU.mult, ALU.mult, [xo_, s_, fnw], [xo_])
                    S.dma("sp", C.y_out[t * 128:(t + 1) * 128, :], xo_[:], r=[xo_], w=[(C.y_out, t)])
        S.barrier()
        S.flush()


def host_consts():
    bf = ml_dtypes.bfloat16
    p = np.arange(128)[:, None]
    f = np.arange(128)[None, :]
    tri = np.stack([(p <= f), (p >= f), (p < f), (p > f)], axis=1).astype(np.float32)
    trif = np.stack([(p < f), (p > f), np.ones((128, 128), bool), (p <= f), (p >= f)], axis=1).astype(np.float32)
    invf = (500000.0 ** (-np.arange(0, 16, 2, dtype=np.float32) / 16.0)).astype(np.float32)[None, :]
    return {"c_ident": np.eye(128, dtype=np.float32).astype(bf), "c_tri": tri.astype(bf), "c_trif": trif, "c_invf": invf}


def prep_inputs(inp):
    f32 = np.float32
    sh = dict(host_consts())
    for k in ("mix_norm_w", "mlp_norm_w", "w_in", "w_a_out", "w_b_out", "w_c_out", "w_o", "w_ff1", "w_ff2"):
        sh[k] = np.ascontiguousarray(inp[k], dtype=f32)
    sh["final_norm_w"] = np.ascontiguousarray(inp["final_norm_w"], dtype=f32).reshape(1, D_)
    sh["conv_a_wT"] = np.ascontiguousarray(inp["conv_a_w"].reshape(L_, 3, 8, 128).transpose(0, 3, 2, 1), dtype=f32)
    sh["ssd_conv_wT"] = np.ascontiguousarray(inp["ssd_conv_w"].reshape(L_, 5, 24, 128).transpose(0, 3, 2, 1), dtype=f32)
    sh["ssd_conv_bT"] = np.ascontiguousarray(inp["ssd_conv_b"].reshape(L_, 24, 128).transpose(0, 2, 1), dtype=f32)
    sh["ssd_conv_bR"] = np.ascontiguousarray(inp["ssd_conv_b"].reshape(L_, 1, 3072), dtype=f32)
    sh["ssd_a_log"] = np.ascontiguousarray(inp["ssd_a_log"].reshape(L_, 1, 64), dtype=f32)
    sh["ssd_dt_bias"] = np.ascontiguousarray(inp["ssd_dt_bias"].reshape(L_, 1, 64), dtype=f32)
    sh["ssd_d"] = np.ascontiguousarray(inp["ssd_d"].reshape(L_, 1, 32), dtype=f32)
    sh["ssd_norm_w"] = np.ascontiguousarray(inp["ssd_norm_w"].reshape(L_, 1, 2048), dtype=f32)
    per = []
    for b in range(inp["x"].shape[0]):
        d = dict(sh)
        d["x"] = np.ascontiguousarray(inp["x"][b], dtype=f32)
        d["pos"] = np.ascontiguousarray(inp["positions"][b].reshape(NT_, 128).T, dtype=np.int32)
        per.append(d)
    return per


_NC_CACHE = {}


def kernel(**inputs):
    per = prep_inputs(inputs)
    if "nc" not in _NC_CACHE:
        _NC_CACHE["nc"] = build()
    nc = _NC_CACHE["nc"]
    res = run_bass_kernel_spmd(nc, per, core_ids=list(range(len(per))))
    return np.stack([np.asarray(r["y"], dtype=np.float32) for r in res.results], axis=0)
```

```python
import numpy as np
import ml_dtypes
import concourse.bass as bass
import concourse.mybir as mybir
from concourse.bass_utils import run_bass_kernel_spmd
from contextlib import ExitStack
from types import SimpleNamespace

F32 = mybir.dt.float32
BF16 = mybir.dt.bfloat16
I32 = mybir.dt.int32
AF = mybir.ActivationFunctionType
ALU = mybir.AluOpType
AX = mybir.AxisListType

ENG = ("pe", "act", "dve", "pool", "sp")
DMAQ = ("sp", "act", "pool")


class _Buf:
    __slots__ = ("w", "r")

    def __init__(self):
        self.w = None
        self.r = {}


class _TBuf:
    __slots__ = ("whole", "subs")

    def __init__(self):
        self.whole = _Buf()
        self.subs = {}


class Sched:
    NRING = 8
    SAME_ENGINE_SYNC = ("act", "dve", "pool")

    def __init__(self, nc):
        self.nc = nc
        self.sems = []
        self.esem = {}
        for e in ENG:
            self.esem[e] = self._new_sem("s_" + e)
        self.ecnt = {e: 0 for e in ENG}
        self.ring = {q: [self._new_sem(f"d_{q}{i}") for i in range(self.NRING)] for q in DMAQ}
        self.dman = {q: 0 for q in DMAQ}
        self.known = {e: {} for e in ENG}
        self.streams = {e: [] for e in ENG}
        self.tb = {}
        self.n_wait = 0
        self.n_ins = 0

    def _new_sem(self, name):
        h = self.nc.alloc_semaphore(name=name)
        self.sems.append(h)
        return len(self.sems) - 1

    def _spec(self, a):
        if isinstance(a, tuple):
            t, key = a
        else:
            t, key = a, None
        name = t if isinstance(t, str) else t.name
        tb = self.tb.get(name)
        if tb is None:
            tb = self.tb[name] = _TBuf()
        return tb, key

    def _deps(self, eng, reads, writes):
        need = {}

        def add(ev):
            if ev is not None and need.get(ev[0], 0) < ev[1]:
                need[ev[0]] = ev[1]

        def addr(b):
            for s, v in b.r.items():
                if need.get(s, 0) < v:
                    need[s] = v

        for a in reads:
            tb, key = self._spec(a)
            add(tb.whole.w)
            if key is None:
                for sb in tb.subs.values():
                    add(sb.w)
            else:
                sb = tb.subs.get(key)
                if sb is not None:
                    add(sb.w)
        for a in writes:
            tb, key = self._spec(a)
            add(tb.whole.w)
            addr(tb.whole)
            if key is None:
                for sb in tb.subs.values():
                    add(sb.w)
                    addr(sb)
            else:
                sb = tb.subs.get(key)
                if sb is not None:
                    add(sb.w)
                    addr(sb)
        kn = self.known[eng]
        own = self.esem[eng]
        waits = []
        for s, v in need.items():
            if s == own and eng not in self.SAME_ENGINE_SYNC:
                continue
            if kn.get(s, 0) < v:
                kn[s] = v
                waits.append((s, v))
        return waits

    def _record(self, ev, reads, writes):
        s, v = ev
        for a in reads:
            tb, key = self._spec(a)
            if key is None:
                b = tb.whole
            else:
                b = tb.subs.get(key)
                if b is None:
                    b = tb.subs[key] = _Buf()
            if b.r.get(s, 0) < v:
                b.r[s] = v
        for a in writes:
            tb, key = self._spec(a)
            if key is None:
                tb.whole.w = ev
                tb.whole.r = {}
                tb.subs = {}
            else:
                b = tb.subs.get(key)
                if b is None:
                    b = tb.subs[key] = _Buf()
                b.w = ev
                b.r = {}

    def op(self, eng, emit, r=(), w=()):
        waits = self._deps(eng, r, w)
        self.ecnt[eng] += 1
        ev = (self.esem[eng], self.ecnt[eng])
        self._record(ev, r, w)
        self.streams[eng].append((waits, emit, ev[0], 1))
        self.n_wait += len(waits)
        self.n_ins += 1

    def dma(self, q, out, in_, r=(), w=(), **kw):
        waits = self._deps(q, r, w)
        i = self.dman[q]
        self.dman[q] += 1
        s = self.ring[q][i % self.NRING]
        rnd = i // self.NRING
        if rnd > 0:
            kn = self.known[q]
            if kn.get(s, 0) < 16 * rnd:
                kn[s] = 16 * rnd
                waits.append((s, 16 * rnd))
        ev = (s, 16 * (rnd + 1))
        self._record(ev, r, w)
        self.streams[q].append((waits, lambda e: e.dma_start(out=out, in_=in_, **kw), s, 16))
        self.n_wait += len(waits)
        self.n_ins += 1

    def barrier(self, skip_q=()):
        evs = []
        for e in ENG:
            if self.ecnt[e] > 0:
                evs.append((self.esem[e], self.ecnt[e]))
        for q in DMAQ:
            if q in skip_q:
                continue
            n = self.dman[q]
            for j in range(min(n, self.NRING)):
                last = ((n - 1 - j) // self.NRING) * self.NRING + j
                evs.append((self.ring[q][j], 16 * (last // self.NRING + 1)))
        for e in ENG:
            kn = self.known[e]
            waits = []
            for s, v in evs:
                if kn.get(s, 0) < v:
                    kn[s] = v
                    waits.append((s, v))
            if waits:
                self.streams[e].append((waits, None, None, 0))
                self.n_wait += len(waits)

    def flush(self):
        nc = self.nc
        sems = self.sems

        def run(name):
            items = self.streams[name]
            self.streams[name] = []

            def f(e):
                for waits, emit, s, inc in items:
                    for ws, wv in waits:
                        e.wait_ge(sems[ws], wv)
                    if emit is not None:
                        ins = emit(e)
                        ins.then_inc(sems[s], inc)

            return f

        with nc.Block() as block:
            block.tensor(run("pe"))
            block.scalar(run("act"))
            block.vector(run("dve"))
            block.gpsimd(run("pool"))
            block.sync(run("sp"))


S_ = 4096
D_ = 1024
NT_ = 32
L_ = 2
DIN_ = 14400
PI = 3.141592653589793


_UID = [0]


def _sbt(nc, name, shape, dt):
    _UID[0] += 1
    return nc.sbuf_tensor(f"{name}_u{_UID[0]}", shape, dt)


def _pst(nc, name, shape, dt):
    _UID[0] += 1
    return nc.psum_tensor(f"{name}_u{_UID[0]}", shape, dt)


class Ops:
    def __init__(self, S):
        self.S = S

    def mm(self, out, lhsT, rhs, start, stop, r, w):
        self.S.op("pe", lambda e: e.matmul(out, lhsT=lhsT, rhs=rhs, start=start, stop=stop), r, w)

    def tr(self, out, in_, ident, r, w):
        self.S.op("pe", lambda e: e.transpose(out=out, in_=in_, identity=ident), r, w)

    def act(self, out, in_, func, r, w, **kw):
        self.S.op("act", lambda e: e.activation(out=out, in_=in_, func=func, **kw), r, w)

    def tt(self, eng, out, in0, in1, op, r, w):
        self.S.op(eng, lambda e: e.tensor_tensor(out=out, in0=in0, in1=in1, op=op), r, w)

    def ts(self, eng, out, in0, s1, s2, op0, op1, r, w):
        if s2 is None:
            self.S.op(eng, lambda e: e.tensor_scalar(out=out, in0=in0, scalar1=s1, scalar2=None, op0=op0), r, w)
        else:
            self.S.op(eng, lambda e: e.tensor_scalar(out=out, in0=in0, scalar1=s1, scalar2=s2, op0=op0, op1=op1), r, w)

    def stt(self, eng, out, in0, scalar, in1, op0, op1, r, w):
        self.S.op(eng, lambda e: e.scalar_tensor_tensor(out=out, in0=in0, scalar=scalar, in1=in1, op0=op0, op1=op1), r, w)

    def copy(self, eng, out, in_, r, w):
        if eng == "act":
            self.S.op(eng, lambda e: e.copy(out=out, in_=in_), r, w)
        else:
            self.S.op(eng, lambda e: e.tensor_copy(out=out, in_=in_), r, w)

    def memset(self, eng, ap, val, w):
        self.S.op(eng, lambda e: e.memset(ap, val), (), w)

    def recip(self, out, in_, r, w):
        self.S.op("dve", lambda e: e.reciprocal(out=out, in_=in_), r, w)


def build(dbg=(), stop=None, nlayers=L_):
    nc = bass.Bass("TRN2", target_bir_lowering=False)

    def din(name, shape, dt=F32):
        return nc.dram_tensor(name, list(shape), dt, kind="ExternalInput").ap()

    def dscr(name, shape, dt):
        kind = "ExternalOutput" if name in dbg else "Internal"
        return nc.dram_tensor(name, list(shape), dt, kind=kind).ap()

    x_in = din("x", [S_, D_])
    pos_in = din("pos", [128, NT_], I32)
    mixw = din("mix_norm_w", [L_, D_])
    mlpw = din("mlp_norm_w", [L_, D_])
    finw = din("final_norm_w", [1, D_])
    w_in = din("w_in", [L_, D_, DIN_])
    w_a = din("w_a_out", [L_, D_, D_])
    w_b = din("w_b_out", [L_, D_, D_])
    w_c = din("w_c_out", [L_, 2 * D_, D_])
    w_o = din("w_o", [L_, D_, D_])
    w_1 = din("w_ff1", [L_, D_, 4 * D_])
    w_2 = din("w_ff2", [L_, 4 * D_, D_])
    cawT = din("conv_a_wT", [L_, 128, 8, 3])
    scwT = din("ssd_conv_wT", [L_, 128, 24, 5])
    scbT = din("ssd_conv_bT", [L_, 128, 24])
    scbR = din("ssd_conv_bR", [L_, 1, 3072])
    alog = din("ssd_a_log", [L_, 1, 64])
    dtb = din("ssd_dt_bias", [L_, 1, 64])
    sdd = din("ssd_d", [L_, 1, 32])
    snw = din("ssd_norm_w", [L_, 1, 2048])
    c_ident = din("c_ident", [128, 128], BF16)
    c_tri = din("c_tri", [128, 4, 128], BF16)
    c_trif = din("c_trif", [128, 5, 128], F32)
    c_invf = din("c_invf", [1, 8], F32)
    y_out = nc.dram_tensor("y", [S_, D_], F32, kind="ExternalOutput").ap()

    wb_in = [dscr(f"wb_in{l}", [D_, DIN_], BF16) for l in range(L_)]
    wb_a = [dscr(f"wb_a{l}", [D_, D_], BF16) for l in range(L_)]
    wb_b = [dscr(f"wb_b{l}", [D_, D_], BF16) for l in range(L_)]
    wb_c = [dscr(f"wb_c{l}", [2 * D_, D_], BF16) for l in range(L_)]
    wb_o = [dscr(f"wb_o{l}", [D_, D_], BF16) for l in range(L_)]
    wb_1 = [dscr(f"wb_1{l}", [D_, 4 * D_], BF16) for l in range(L_)]
    wb_2 = [dscr(f"wb_2{l}", [4 * D_, D_], BF16) for l in range(L_)]
    yAT = dscr("yAT", [16, 128, 8, 256], BF16)
    gT = dscr("gT", [16, 128, 24, 256], BF16)
    qT = dscr("qT", [D_, S_], BF16)
    kT = dscr("kT", [D_, S_], BF16)
    v_d = dscr("v_d", [S_, D_], BF16)
    sz_d = dscr("sz_d", [S_, 2 * D_], BF16)
    xs_d = dscr("xs_d", [S_, 2 * D_], BF16)
    Bt_d = dscr("Bt_d", [S_, 512], BF16)
    BT_d = dscr("BT_d", [512, S_], BF16)
    CT_d = dscr("CT_d", [512, S_], BF16)
    dt_dbg = dscr("dt_dbg", [128, NT_, 64], F32)
    o_d = [dscr(f"o_d{p}", [S_, 16, 65], F32) for p in range(3)]
    hin_d = [dscr(f"hin_d{d}", [NT_, 128, 2048], BF16) for d in range(2)]
    ynT_d = dscr("ynT_d", [16, 128, 16, 256], BF16)
    xmid = dscr("xmid", [S_, D_], F32)
    xl = [dscr(f"xl{l}", [S_, D_], F32) for l in range(L_ - 1)]

    S = Sched(nc)
    O = Ops(S)

    with ExitStack() as gst:
        def gsb(name, shape, dt):
            return gst.enter_context(_sbt(nc, name, list(shape), dt))

        ident = gsb("ident", [128, 128], BF16)
        tri = gsb("tri", [128, 4, 128], BF16)
        trif = gsb("trif", [128, 5, 128], F32)
        cosT = gsb("cosT", [128, NT_, 8], F32)
        sinT = gsb("sinT", [128, NT_, 8], F32)
        dt_all = gsb("dt_all", [128, NT_, 64], F32)
        la_all = gsb("la_all", [128, NT_, 64], F32)
        S.dma("sp", ident[:], c_ident, r=[], w=[ident])
        S.dma("sp", tri[:], c_tri, r=[], w=[tri])
        S.dma("sp", trif[:], c_trif, r=[], w=[trif])

        def cast_w(src, dst, rows_per):
            R = src.shape[0]
            for r0 in range(0, R, rows_per):
                S.dma("pool", dst[r0:r0 + rows_per, :], src[r0:r0 + rows_per, :], r=[], w=[(dst, r0)])

        with ExitStack() as st:
            def sb(name, shape, dt):
                return st.enter_context(_sbt(nc, name, list(shape), dt))
            posi = sb("posi", [128, NT_], I32)
            posf = sb("posf", [128, NT_], F32)
            invf = sb("invf", [128, 8], F32)
            ang = sb("ang", [128, NT_, 8], F32)
            a1 = sb("a1", [128, NT_, 8], F32)
            S.dma("sp", posi[:], pos_in, r=[], w=[posi])
            S.dma("sp", invf[:], c_invf.partition_broadcast(128), r=[], w=[invf])
            O.copy("dve", posf[:], posi[:], [posi], [posf])
            O.tt("dve", ang[:], posf[:].unsqueeze(2).to_broadcast([128, NT_, 8]),
                 invf[:].unsqueeze(1).to_broadcast([128, NT_, 8]), ALU.mult, [posf, invf], [ang])
            ki = sb("ki", [128, NT_, 8], I32)
            kf = sb("kf", [128, NT_, 8], F32)
            mm_ = sb("mm_", [128, NT_, 8], F32)
            for shift, dstT in ((0.0, sinT), (0.5 * PI, cosT)):
                O.ts("dve", a1[:], ang[:], shift, 1.0 / (2 * PI), ALU.add, ALU.mult, [ang], [a1])
                O.copy("dve", ki[:], a1[:], [a1], [ki])
                O.copy("dve", kf[:], ki[:], [ki], [kf])
                O.ts("dve", a1[:], ang[:], shift, None, ALU.add, None, [ang], [a1])
                O.stt("dve", a1[:], kf[:], -2 * PI, a1[:], ALU.mult, ALU.add, [kf, a1], [a1])
                O.ts("dve", mm_[:], a1[:], PI, 2 * PI, ALU.is_ge, ALU.mult, [a1], [mm_])
                O.tt("dve", a1[:], a1[:], mm_[:], ALU.subtract, [a1, mm_], [a1])
                O.ts("dve", mm_[:], a1[:], -PI, 2 * PI, ALU.is_lt, ALU.mult, [a1], [mm_])
                O.tt("dve", a1[:], a1[:], mm_[:], ALU.add, [a1, mm_], [a1])
                O.ts("dve", a1[:], a1[:], -PI, PI, ALU.max, ALU.min, [a1], [a1])
                O.act(dstT[:], a1[:], AF.Sin, [a1], [dstT])
            S.barrier()
            S.flush()

        for l in range(nlayers):
            x_src = x_in if l == 0 else xl[l - 1]
            x_dst = xl[l] if l < L_ - 1 else None
            C = SimpleNamespace(**locals())
            layer(C, l, x_src, x_dst, stop)
            if stop is not None and stop[0] == l:
                break
        S.barrier(skip_q=())
        S.flush()
    return nc


def layer(C, l, x_src, x_dst, stop):
    nc, S, O = C.nc, C.S, C.O
    with ExitStack() as lst:
        uT = lst.enter_context(_sbt(nc, "uT", [128, 8, S_], BF16))
        phase1(C, l, x_src, uT)
        if stop == (l, 1):
            return
        phase2(C, l, uT)
    S.barrier()
    S.flush()
    if stop == (l, 2):
        return
    phase3(C, l)
    if stop == (l, 3):
        return
    phase4(C, l)
    if stop == (l, 4):
        return
    phase5(C, l, x_src, x_dst)


def rms_tile(C, xt, ss, sq, nwt, ub, tag):
    O = C.O
    O.act(sq[:], xt[:], AF.Square, [xt], [sq, ss], accum_out=ss[:])
    O.act(ss[:], ss[:], AF.Sqrt, [ss], [ss], bias=1e-6, scale=1.0 / D_)
    O.recip(ss[:], ss[:], [ss], [ss])
    O.stt("dve", ub[:], xt[:], ss[:], nwt[:], ALU.mult, ALU.mult, [xt, ss, nwt], [ub])


def phase1(C, l, x_src, uT):
    nc, S, O = C.nc, C.S, C.O
    with ExitStack() as st:
        def sb(name, shape, dt):
            return st.enter_context(_sbt(nc, name, list(shape), dt))
        xt = [sb(f"p1x{i}", [128, D_], F32) for i in range(2)]
        sq = sb("p1sq", [128, D_], BF16)
        ss = [sb(f"p1ss{i}", [128, 1], F32) for i in range(2)]
        ub = [sb(f"p1ub{i}", [128, D_], BF16) for i in range(2)]
        nwt = sb("p1nw", [128, D_], F32)
        pT = [st.enter_context(_pst(nc, f"p1pT{i}", [128, 8, 128], BF16)) for i in range(2)]
        S.dma("sp", nwt[:], C.mixw[l:l + 1, :].partition_broadcast(128), r=[], w=[nwt])
        for t in range(NT_):
            b = t % 2
            S.dma("sp", xt[b][:], x_src[t * 128:(t + 1) * 128, :], r=[(x_src, t)], w=[xt[b]])
            rms_tile(C, xt[b], ss[b], sq, nwt, ub[b], "p1")
            for k in range(8):
                O.tr(pT[b][:, k, :], ub[b][:, k * 128:(k + 1) * 128], C.ident[:], [ub[b], C.ident], [pT[b]])
            O.copy("act" if t % 2 else "dve", uT[:, :, t * 128:(t + 1) * 128], pT[b][:], [pT[b]], [(uT, t)])
        S.barrier()
        S.flush()


def phase2(C, l, uT):
    nc, S, O = C.nc, C.S, C.O
    wsrc = C.w_in[l].rearrange("(k p) n -> p k n", p=128)
    seq = []
    for g in range(2):
        seq += [(512 * g, 512), (1024 + 512 * g, 512), (2048 + 512 * g, 512)]
    seq += [(3072 + 512 * i, 512) for i in range(4)]
    seq += [(5120 + 512 * i, 512) for i in range(2)]
    seq += [(6144 + 512 * i, 512) for i in range(4)]
    seq += [(8192 + 512 * i, 512) for i in range(6)]
    seq += [(11264, 64)]
    seq += [(11328 + 512 * i, 512) for i in range(6)]
    with ExitStack() as st:
        wbuf = [st.enter_context(_sbt(nc, f"p2w{i}", [128, 8, 512], BF16)) for i in range(3)]
        PB = [st.enter_context(_pst(nc, f"p2ps{i}", [128, 512], F32)) for i in range(6)]
        PT = [st.enter_context(_pst(nc, f"p2pt{i}", [128, 4, 128], BF16)) for i in range(2)]
        state = {"issued": 0, "ps": 0, "done": 0}

        def prefetch():
            while state["issued"] < min(len(seq), state["done"] + 3):
                i = state["issued"]
                c0, ncw = seq[i]
                S.dma("pool", wbuf[i % 3][:, :, :ncw], wsrc[:, :, c0:c0 + ncw], r=[], w=[wbuf[i % 3]])
                state["issued"] += 1

        def get_w(idx):
            prefetch()
            assert idx < state["issued"]
            return wbuf[idx % 3]

        def release(n):
            state["done"] = n
            prefetch()

        def PS():
            p = PB[state["ps"] % len(PB)]
            state["ps"] += 1
            return p

        def mm_feat(ps, wt, jj, tb):
            for k in range(8):
                O.mm(ps[:], wt[:, k, jj * 128:(jj + 1) * 128], uT[:, k, tb * 512:(tb + 1) * 512], k == 0, k == 7,
                     [wt, uT], [ps])

        def mm_tok(ps_ap, ps, wt, t, ncw):
            for k in range(8):
                O.mm(ps_ap, uT[:, k, t * 128:(t + 1) * 128], wt[:, k, :ncw], k == 0, k == 7, [wt, uT], [ps])

        wi = 0
        with ExitStack() as s2:
            def sb(name, shape, dt):
                return s2.enter_context(_sbt(nc, name, list(shape), dt))
            bT = [sb(f"p2bT{i}", [128, S_], F32) for i in range(2)]
            cst = [sb(f"p2cst{i}", [128, 512], F32) for i in range(2)]
            cx = [sb(f"p2cx{i}", [128, S_ + 2], BF16) for i in range(2)]
            yAo = [sb(f"p2yAo{i}", [128, S_], BF16) for i in range(2)]
            dgA = [sb(f"p2dgA{i}", [128, 3, 128], BF16) for i in range(2)]
            wA = sb("p2wA", [128, 8, 3], F32)
            S.dma("sp", wA[:], C.cawT[l], r=[], w=[wA])
            for i in range(2):
                O.memset("dve", cx[i][:, 0:1], 0.0, [(cx[i], "l")])
                O.memset("dve", cx[i][:, S_ + 1:S_ + 2], 0.0, [(cx[i], "r")])

            def convA(j):
                p = j % 2
                for tb in range(8):
                    ps = PS()
                    for k3 in range(3):
                        O.mm(ps[:], dgA[p][:, k3, :], cx[p][:, tb * 512 + k3:tb * 512 + k3 + 512], k3 == 0, k3 == 2,
                             [dgA[p], cx[p]], [ps])
                    O.tt("dve", yAo[p][:, tb * 512:(tb + 1) * 512], ps[:], bT[p][:, tb * 512:(tb + 1) * 512], ALU.mult,
                         [ps, bT[p]], [(yAo[p], tb)])
                S.dma("sp", C.yAT[:, :, j, :].rearrange("tb p t -> p tb t"), yAo[p][:].rearrange("p (tb t) -> p tb t", t=256),
                      r=[yAo[p]], w=[(C.yAT, j)])

            for g in range(2):
                release(wi)
                wt_b, wt_c, wt_x = get_w(wi), get_w(wi + 1), get_w(wi + 2)
                wi += 3
                for jj in range(4):
                    j = 4 * g + jj
                    p = j % 2
                    for k3 in range(3):
                        O.ts("dve", dgA[p][:, k3, :], C.ident[:], wA[:, j, k3:k3 + 1], None, ALU.mult, None,
                             [C.ident, wA], [(dgA[p], k3)])
                    for tb in range(8):
                        pb, pc, px = PS(), PS(), PS()
                        mm_feat(pb, wt_b, jj, tb)
                        mm_feat(pc, wt_c, jj, tb)
                        mm_feat(px, wt_x, jj, tb)
                        O.copy("act", bT[p][:, tb * 512:(tb + 1) * 512], pb[:], [pb], [(bT[p], tb)])
                        O.copy("act", cst[tb % 2][:], pc[:], [pc], [cst[tb % 2]])
                        O.tt("dve", cx[p][:, 1 + tb * 512:1 + (tb + 1) * 512], px[:], cst[tb % 2][:], ALU.mult,
                             [px, cst[tb % 2]], [(cx[p], tb)])
                        if tb == 1 and j > 0:
                            convA(j - 1)
            convA(7)
            S.barrier()
            S.flush()
        with ExitStack() as s2:
            def sb(name, shape, dt):
                return s2.enter_context(_sbt(nc, name, list(shape), dt))
            qs = [sb(f"p2qs{i}", [128, 8, 64], F32) for i in range(3)]
            tmp = [sb(f"p2tmp{i}", [128, 4, 8, 8], F32) for i in range(3)]
            qr = [sb(f"p2qr{i}", [128, 512], BF16) for i in range(3)]
            stg = [sb(f"p2stg{i}", [128, 4, S_], BF16) for i in range(2)]
            for ci in range(4):
                release(wi)
                wt = get_w(wi)
                wi += 1
                sg = stg[ci % 2]
                def qk_front(t):
                    b = t % 3
                    ps = PS()
                    mm_tok(ps[:], ps, wt, t, 512)
                    O.copy("act", qs[b][:].rearrange("p h d -> p (h d)"), ps[:], [ps], [qs[b]])
                    cs = C.cosT[:, t, :].unsqueeze(1).to_broadcast([128, 8, 8])
                    sn = C.sinT[:, t, :].unsqueeze(1).to_broadcast([128, 8, 8])
                    t1 = qs[b][:, :, 0:8]
                    t2 = qs[b][:, :, 8:16]
                    tm = tmp[b]
                    O.tt("dve", tm[:, 0], t1, cs, ALU.mult, [qs[b], C.cosT], [(tm, 0)])
                    O.tt("dve", tm[:, 1], t2, sn, ALU.mult, [qs[b], C.sinT], [(tm, 1)])
                    O.tt("dve", tm[:, 2], t2, cs, ALU.mult, [qs[b], C.cosT], [(tm, 2)])
                    O.tt("dve", tm[:, 3], t1, sn, ALU.mult, [qs[b], C.sinT], [(tm, 3)])
                    O.tt("dve", t1, tm[:, 0], tm[:, 1], ALU.subtract, [(tm, 0), (tm, 1)], [(qs[b], "a")])
                    O.tt("dve", t2, tm[:, 2], tm[:, 3], ALU.add, [(tm, 2), (tm, 3)], [(qs[b], "b")])
                    O.copy("act", qr[b][:], qs[b][:].rearrange("p h d -> p (h d)"), [qs[b]], [qr[b]])

                def qk_back(t):
                    b = t % 3
                    pt = PT[t % 2]
                    for jj in range(4):
                        O.tr(pt[:, jj, :], qr[b][:, jj * 128:(jj + 1) * 128], C.ident[:], [qr[b], C.ident], [pt])
                    O.copy("act", sg[:, :, t * 128:(t + 1) * 128], pt[:], [pt], [(sg, t)])

                for t in range(NT_ + 2):
                    if t < NT_:
                        qk_front(t)
                    if 0 <= t - 2 < NT_:
                        qk_back(t - 2)
                dst = C.qT if ci < 2 else C.kT
                for jj in range(4):
                    r0 = (ci % 2) * 512 + jj * 128
                    S.dma("sp", dst[r0:r0 + 128, :], sg[:, jj, :], r=[sg], w=[(dst, r0)])
            S.barrier()
            S.flush()
        with ExitStack() as s2:
            def sb(name, shape, dt):
                return s2.enter_context(_sbt(nc, name, list(shape), dt))
            vst = [sb(f"p2vst{i}", [128, 512], BF16) for i in range(4)]
            n = 0
            for ci in range(6):
                release(wi)
                wt = get_w(wi)
                wi += 1
                for t in range(NT_):
                    ps = PS()
                    mm_tok(ps[:], ps, wt, t, 512)
                    vs = vst[n % 4]
                    n += 1
                    if ci < 2:
                        O.copy("act", vs[:], ps[:], [ps], [vs])
                        S.dma("sp", C.v_d[t * 128:(t + 1) * 128, ci * 512:(ci + 1) * 512], vs[:], r=[vs], w=[(C.v_d, (t, ci))])
                    else:
                        O.act(vs[:], ps[:], AF.Silu, [ps], [vs])
                        c2 = ci - 2
                        S.dma("sp", C.sz_d[t * 128:(t + 1) * 128, c2 * 512:(c2 + 1) * 512], vs[:], r=[vs], w=[(C.sz_d, (t, c2))])
            S.barrier()
            S.flush()
        with ExitStack() as s2:
            def sb(name, shape, dt):
                return s2.enter_context(_sbt(nc, name, list(shape), dt))
            xin = sb("p2xin", [128, 4, S_ + 4], BF16)
            diag = sb("p2diag", [128, 4, 5, 128], BF16)
            stok = [sb(f"p2stok{i}", [128, 8, 512], BF16) for i in range(2)]
            sfeat = [sb(f"p2sfeat{i}", [128, S_], BF16) for i in range(2)]
            wS = sb("p2wS", [128, 24, 5], F32)
            bS = sb("p2bS", [128, 24], F32)
            brf = sb("p2brf", [1, 3072], F32)
            brow = sb("p2brow", [1, 3072], BF16)
            ones = sb("p2ones", [1, 128], BF16)
            S.dma("sp", wS[:], C.scwT[l], r=[], w=[wS])
            S.dma("sp", bS[:], C.scbT[l], r=[], w=[bS])
            S.dma("sp", brf[:], C.scbR[l], r=[], w=[brf])
            O.copy("dve", brow[:], brf[:], [brf], [brow])
            O.memset("dve", ones[:], 1.0, [ones])
            O.memset("dve", xin[:, :, 0:2], 0.0, [(xin, "l")])
            O.memset("dve", xin[:, :, S_ + 2:S_ + 4], 0.0, [(xin, "r")])
            nf = 0
            for ci in range(6):
                release(wi)
                wt = get_w(wi)
                wi += 1
                for jj in range(4):
                    J = 4 * ci + jj
                    for tb in range(8):
                        ps = PS()
                        mm_feat(ps, wt, jj, tb)
                        O.copy("act" if tb % 2 else "dve", xin[:, jj, 2 + tb * 512:2 + (tb + 1) * 512], ps[:], [ps], [(xin, (jj, tb))])
                    for k5 in range(5):
                        O.ts("dve", diag[:, jj, k5, :], C.ident[:], wS[:, J, k5:k5 + 1], None, ALU.mult, None,
                             [C.ident, wS], [(diag, (jj, k5))])
                if ci <= 4:
                    for t in range(NT_):
                        ps = PS()
                        for jj in range(4):
                            J = 4 * ci + jj
                            o_ap = ps[:, jj * 128:(jj + 1) * 128]
                            for k5 in range(5):
                                O.mm(o_ap, xin[:, jj, t * 128 + k5:t * 128 + k5 + 128], diag[:, jj, k5, :], k5 == 0, False,
                                     [xin, diag], [ps])
                            O.mm(o_ap, ones[0:1, :], brow[0:1, J * 128:(J + 1) * 128], False, True, [ones, brow], [ps])
                        sk = stok[(t // 8) % 2]
                        O.act(sk[:, t % 8, :], ps[:], AF.Silu, [ps], [(sk, t % 8)])
                        if t % 8 == 7:
                            t0 = t - 7
                            if ci < 4:
                                dst = C.xs_d[t0 * 128:(t0 + 8) * 128, ci * 512:(ci + 1) * 512]
                                key = (C.xs_d, (t0, ci))
                            else:
                                dst = C.Bt_d[t0 * 128:(t0 + 8) * 128, :]
                                key = (C.Bt_d, t0)
                            S.dma("sp", dst.rearrange("(t p) c -> p t c", p=128), sk[:], r=[sk], w=[key])
                if ci >= 4:
                    for jj in range(4):
                        J = 4 * ci + jj
                        sf = sfeat[nf % 2]
                        nf += 1
                        for tb in range(8):
                            ps = PS()
                            for k5 in range(5):
                                O.mm(ps[:], diag[:, jj, k5, :], xin[:, jj, tb * 512 + k5:tb * 512 + k5 + 512], k5 == 0, k5 == 4,
                                     [xin, diag], [ps])
                            O.act(sf[:, tb * 512:(tb + 1) * 512], ps[:], AF.Silu, [ps, bS], [(sf, tb)], bias=bS[:, J:J + 1])
                        dst = C.BT_d if ci == 4 else C.CT_d
                        S.dma("sp", dst[jj * 128:(jj + 1) * 128, :], sf[:], r=[sf], w=[(dst, jj)])
            S.barrier()
            S.flush()
        with ExitStack() as s2:
            def sb(name, shape, dt):
                return s2.enter_context(_sbt(nc, name, list(shape), dt))
            dtbb = sb("p2dtbb", [128, 64], F32)
            abc = sb("p2abc", [128, 64], F32)
            tdt = [sb(f"p2tdt{i}", [128, 64], F32) for i in range(2)]
            S.dma("sp", dtbb[:], C.dtb[l].partition_broadcast(128), r=[], w=[dtbb])
            S.dma("sp", abc[:], C.alog[l].partition_broadcast(128), r=[], w=[abc])
            O.act(abc[:], abc[:], AF.Exp, [abc], [abc])
            O.ts("dve", abc[:], abc[:], -1.0, None, ALU.mult, None, [abc], [abc])
            release(wi)
            wt = get_w(wi)
            wi += 1
            for t in range(NT_):
                ps = PS()
                mm_tok(ps[:, 0:64], ps, wt, t, 64)
                td = tdt[t % 2]
                O.tt("dve", td[:], ps[:, 0:64], dtbb[:], ALU.add, [ps, dtbb], [td])
                O.act(td[:], td[:], AF.Exp, [td], [td])
                O.act(C.dt_all[:, t, :], td[:], AF.Ln, [td], [(C.dt_all, t)], bias=1.0)
                O.tt("dve", C.la_all[:, t, :], C.dt_all[:, t, :], abc[:], ALU.mult, [(C.dt_all, t), abc], [(C.la_all, t)])
            if "dt_dbg" in C.dbg:
                S.dma("sp", C.dt_dbg, C.dt_all[:], r=[C.dt_all], w=[C.dt_dbg])
            S.barrier()
            S.flush()
        with ExitStack() as s2:
            def sb(name, shape, dt):
                return s2.enter_context(_sbt(nc, name, list(shape), dt))
            sfeat = [sb(f"p2gfeat{i}", [128, S_], BF16) for i in range(2)]
            nf = 0
            for ci in range(6):
                release(wi)
                wt = get_w(wi)
                wi += 1
                for jj in range(4):
                    G = 4 * ci + jj
                    sf = sfeat[nf % 2]
                    nf += 1
                    for tb in range(8):
                        ps = PS()
                        mm_feat(ps, wt, jj, tb)
                        O.act(sf[:, tb * 512:(tb + 1) * 512], ps[:], AF.Sigmoid, [ps], [(sf, tb)])
                    S.dma("sp", C.gT[:, :, G, :].rearrange("tb p t -> p tb t"), sf[:].rearrange("p (tb t) -> p tb t", t=256),
                          r=[sf], w=[(C.gT, G)])
            S.barrier()
            S.flush()
        assert wi == len(seq)


def phase3(C, l):
    nc, S, O = C.nc, C.S, C.O
    C.cast_w(C.w_1[l], C.wb_1[l], 128)
    pats = [(1, 33), (4, 9), (16, 3)]
    with ExitStack() as st:
        def sb(name, shape, dt):
            return st.enter_context(_sbt(nc, name, list(shape), dt))
        qTh = [sb(f"p3q{i}", [64, S_], BF16) for i in range(2)]
        kTp = [sb(f"p3k{i}", [64, S_ + 2048], BF16) for i in range(2)]
        vP = [[sb(f"p3v{pi}_{b}", [128, nb, dil, 65], BF16) for b in range(2)] for pi, (dil, nb) in enumerate(pats)]
        mask = sb("p3mask", [128, 2, 128], BF16)
        PTs = [sb(f"p3pt{i}", [128, 2, 2, 128], BF16) for i in range(2)]
        PTm = [sb(f"p3pm{i}", [128, 2, 2, 128], BF16) for i in range(2)]
        ost = [sb(f"p3ost{i}", [128, 32, 65], F32) for i in range(2)]
        PSs = [st.enter_context(_pst(nc, f"p3ps{i}", [128, 512], F32)) for i in range(3)]
        PSo = [st.enter_context(_pst(nc, f"p3po{i}", [128, 2, 65], F32)) for i in range(3)]
        O.copy("dve", mask[:, 0, :], C.tri[:, 1, :], [C.tri], [(mask, 0)])
        O.copy("dve", mask[:, 1, :], C.tri[:, 0, :], [C.tri], [(mask, 1)])
        for b in range(2):
            O.memset("dve", kTp[b][:, 0:1024], 0.0, [(kTp[b], "l")])
            O.memset("dve", kTp[b][:, 1024 + S_:2048 + S_], 0.0, [(kTp[b], "r")])
            for pi, (dil, nb) in enumerate(pats):
                v = vP[pi][b]
                O.memset("pool", v[:], 0.0, [v])
                O.memset("pool", v[:, :, :, 64:65], 1.0, [v])
                O.memset("pool", v[0:64, 0, :, 64:65], 0.0, [v])
                O.memset("pool", v[64:128, nb - 1, :, 64:65], 0.0, [v])
        def loads(h):
            b = h % 2
            S.dma("sp", qTh[b][:], C.qT[h * 64:(h + 1) * 64, :], r=[C.qT], w=[qTh[b]])
            S.dma("sp", kTp[b][:, 1024:1024 + S_], C.kT[h * 64:(h + 1) * 64, :], r=[C.kT], w=[(kTp[b], "m")])
            vsrc = C.v_d[:, h * 64:(h + 1) * 64]
            for pi, (dil, nb) in enumerate(pats):
                v = vP[pi][b]
                nin = nb - 2
                if dil == 1:
                    for j0 in range(0, nin, 8):
                        j1 = min(nin, j0 + 8)
                        src = vsrc[64 + j0 * 128:64 + j1 * 128, :]
                        S.dma("sp", v[:, 1 + j0:1 + j1, 0, 0:64], src.rearrange("(j i) d -> i j d", i=128),
                              r=[C.v_d], w=[(v, ("in", j0))])
                else:
                    for j0 in range(nin):
                        src = vsrc[64 * dil + j0 * 128 * dil:64 * dil + (j0 + 1) * 128 * dil, :]
                        S.dma("sp", v[:, 1 + j0, :, 0:64], src.rearrange("(i r) d -> i r d", r=dil),
                              r=[C.v_d], w=[(v, ("in", j0))])
                S.dma("sp", v[64:128, 0, :, 0:64], vsrc[0:64 * dil, :].rearrange("(i r) d -> i r d", r=dil),
                      r=[C.v_d], w=[(v, "first")])
                S.dma("sp", v[0:64, nb - 1, :, 0:64], vsrc[S_ - 64 * dil:S_, :].rearrange("(i r) d -> i r d", r=dil),
                      r=[C.v_d], w=[(v, "last")])

        def store(h, pi, osb):
            dil, nb = pats[pi]
            nqb = nb - 1
            osv = osb[:].rearrange("p (q r) e -> p q r e", r=dil)
            if dil == 1:
                S.dma("sp", C.o_d[pi][:, h, :].rearrange("(q i) e -> i q e", i=128), osb[:], r=[osb],
                      w=[(C.o_d[pi], h)])
            else:
                for q in range(nqb):
                    S.dma("sp", C.o_d[pi][q * 128 * dil:(q + 1) * 128 * dil, h, :].rearrange("(i r) e -> i r e", r=dil),
                          osv[:, q, :, :], r=[osb], w=[(C.o_d[pi], (h, q))])

        pairs = []
        n_ost = 0
        for h in range(16):
            first = True
            for pi, (dil, nb) in enumerate(pats):
                osb = ost[n_ost % 2]
                n_ost += 1
                nqb = nb - 1
                lst = [(r, qp) for r in range(dil) for qp in range(nqb // 2)]
                for idx, (r, qp) in enumerate(lst):
                    pairs.append(dict(h=h, pi=pi, dil=dil, r=r, qp=qp, osb=osb, pre=None,
                                      post=(h, pi, osb) if idx == len(lst) - 1 else None))
        npairs_h = len(pairs) // 16
        for h in range(16):
            if h == 0:
                pairs[0]["pre"] = 0
            if h + 1 < 16:
                pairs[h * npairs_h + 4]["pre"] = h + 1
        NPS = 3

        def stA(i):
            p = pairs[i]
            b = p["h"] % 2
            dil, r, qp = p["dil"], p["r"], p["qp"]
            ps = PSs[i % NPS]
            psv = ps[:].rearrange("p (a b c) -> p a b c", a=2, b=2)
            for qi in range(2):
                qb = 2 * qp + qi
                q0 = 128 * dil * qb + r
                qsl = qTh[b][:, q0:q0 + 127 * dil + 1:dil]
                for ab in range(2):
                    k0 = 1024 - 64 * dil + 128 * dil * (qb + ab) + r
                    ksl = kTp[b][:, k0:k0 + 127 * dil + 1:dil]
                    O.mm(psv[:, qi, ab, :], ksl, qsl, True, True, [kTp[b], qTh[b]], [ps])

        def stB(i):
            ps = PSs[i % NPS]
            pt = PTs[i % 2]
            pm = PTm[i % 2]
            O.act(pt[:].rearrange("p a b c -> p (a b c)"), ps[:], AF.Exp, [ps], [pt], scale=0.125)
            O.tt("dve", pm[:], pt[:], mask[:].unsqueeze(1).to_broadcast([128, 2, 2, 128]), ALU.mult, [pt, mask], [pm])

        def stC(i):
            p = pairs[i]
            b = p["h"] % 2
            dil, r, qp, pi = p["dil"], p["r"], p["qp"], p["pi"]
            v = vP[pi][b]
            pm = PTm[i % 2]
            po = PSo[i % 3]
            osb = p["osb"]
            osv = osb[:].rearrange("p (q r) e -> p q r e", r=dil)
            for qi in range(2):
                qb = 2 * qp + qi
                for ab in range(2):
                    O.mm(po[:, qi, :], pm[:, qi, ab, :], v[:, qb + ab, r, :], ab == 0, ab == 1, [pm, v], [po])
            O.copy("act" if i % 2 else "dve", osv[:, 2 * qp:2 * qp + 2, r, :], po[:], [po], [(osb, (r, qp))])
            if p["post"] is not None:
                store(*p["post"])

        n = len(pairs)
        for i in range(n + 2):
            if i < n:
                if pairs[i]["pre"] is not None:
                    loads(pairs[i]["pre"])
                stA(i)
            if 0 <= i - 1 < n:
                stB(i - 1)
            if 0 <= i - 2 < n:
                stC(i - 2)
        S.barrier()
        S.flush()


def phase4(C, l):
    nc, S, O = C.nc, C.S, C.O
    tri, trif = C.tri, C.trif
    with ExitStack() as st:
        def sb(name, shape, dt):
            return st.enter_context(_sbt(nc, name, list(shape), dt))
        H = [[sb(f"p4H{d}_{i}", [128, 2048], F32) for i in range(2)] for d in range(2)]
        tmpH = [sb(f"p4tmpH{d}", [128, 2048], F32) for d in range(2)]
        hbf = [[sb(f"p4hbf{d}_{i}", [128, 2048], BF16) for i in range(2)] for d in range(2)]
        xs_t = [sb(f"p4xs{i}", [128, 2048], BF16) for i in range(4)]
        Bt_t = [sb(f"p4Bt{i}", [128, 512], BF16) for i in range(4)]
        ew = [sb(f"p4ew{i}", [128, 2, 32], F32) for i in range(4)]
        dtw = [sb(f"p4dtw{i}", [128, 32], F32) for i in range(2)]
        xw = [sb(f"p4xw{i}", [128, 2048], BF16) for i in range(2)]
        st_sb = [sb(f"p4st{i}", [128, 2048], F32) for i in range(3)]
        PSw = [st.enter_context(_pst(nc, f"p4psw{i}", [128, 2, 32], F32)) for i in range(2)]
        PSs = [st.enter_context(_pst(nc, f"p4pss{i}", [128, 512], F32)) for i in range(4)]
        for d in range(2):
            O.memset("dve", H[d][0][:], 0.0, [H[d][0]])
            O.memset("pool", hbf[d][0][:], 0.0, [hbf[d][0]])

        def stA(n):
            i, d = divmod(n, 2)
            c = i if d == 0 else NT_ - 1 - i
            xt, bt, e_, dw, xw_, pw = xs_t[n % 4], Bt_t[n % 4], ew[n % 4], dtw[n % 2], xw[n % 2], PSw[n % 2]
            S.dma("sp", xt[:], C.xs_d[c * 128:(c + 1) * 128, :], r=[C.xs_d], w=[xt])
            S.dma("sp", bt[:], C.Bt_d[c * 128:(c + 1) * 128, :], r=[C.Bt_d], w=[bt])
            la_c = C.la_all[:, c, d * 32:(d + 1) * 32]
            dt_c = C.dt_all[:, c, d * 32:(d + 1) * 32]
            O.mm(pw[:, 0, :], trif[:, 1 if d == 0 else 0, :], la_c, True, True, [trif, C.la_all], [pw])
            O.mm(pw[:, 1, :], trif[:, 2, :], la_c, True, True, [trif, C.la_all], [pw])
            O.act(e_[:], pw[:], AF.Exp, [pw], [e_])
            O.tt("dve", dw[:], dt_c, e_[:, 0, :], ALU.mult, [C.dt_all, e_], [dw])
            O.tt("pool", xw_[:].rearrange("p (h d) -> p h d", d=64), xt[:].rearrange("p (h d) -> p h d", d=64),
                 dw[:].unsqueeze(2).to_broadcast([128, 32, 64]), ALU.mult, [xt, dw], [xw_])
            ss_ = st_sb[n % 3]
            for g in range(4):
                O.mm(PSs[g][:], bt[:, g * 128:(g + 1) * 128], xw_[:, g * 512:(g + 1) * 512], True, True, [bt, xw_], [PSs[g]])
                O.copy("act", ss_[:, g * 512:(g + 1) * 512], PSs[g][:], [PSs[g]], [(ss_, g)])

        def stB(n):
            i, d = divmod(n, 2)
            c = i if d == 0 else NT_ - 1 - i
            eng = "dve"
            e_ = ew[n % 4]
            hb = hbf[d][i % 2]
            Hs, Hd = H[d][i % 2], H[d][(i + 1) % 2]
            S.dma("sp", C.hin_d[d][c], hb[:], r=[hb], w=[(C.hin_d[d], c)])
            O.tt(eng, tmpH[d][:].rearrange("p (h d) -> p h d", d=64), Hs[:].rearrange("p (h d) -> p h d", d=64),
                 e_[:, 1, :].unsqueeze(2).to_broadcast([128, 32, 64]), ALU.mult, [Hs, e_], [tmpH[d]])
            O.tt(eng, Hd[:], tmpH[d][:], st_sb[n % 3][:], ALU.add, [tmpH[d], st_sb[n % 3]], [Hd])

        def stC(n):
            i, d = divmod(n, 2)
            O.copy("act", hbf[d][(i + 1) % 2][:], H[d][(i + 1) % 2][:], [H[d][(i + 1) % 2]], [hbf[d][(i + 1) % 2]])

        NN = 2 * NT_
        for n in range(NN + 2):
            if n < NN:
                stA(n)
            if 0 <= n - 1 < NN:
                stB(n - 1)
            if 0 <= n - 2 < NN:
                stC(n - 2)
        S.barrier()
        S.flush()
    with ExitStack() as st:
        def sb(name, shape, dt):
            return st.enter_context(_sbt(nc, name, list(shape), dt))
        NB = 3
        xs_t = [sb(f"p4xs{i}", [128, 2048], BF16) for i in range(NB)]
        BT_t = [sb(f"p4BT{i}", [128, 4, 128], BF16) for i in range(NB)]
        CT_t = [sb(f"p4CT{i}", [128, 4, 128], BF16) for i in range(NB)]
        sz_t = [sb(f"p4sz{i}", [128, 2048], BF16) for i in range(NB)]
        hh_t = [[sb(f"p4hh{d}_{i}", [128, 2048], BF16) for i in range(NB)] for d in range(2)]
        cbm = [[sb(f"p4cbm{d}_{i}", [128, 4, 128], BF16) for i in range(2)] for d in range(2)]
        ecum = [[sb(f"p4ecum{d}_{i}", [128, 32], F32) for i in range(2)] for d in range(2)]
        rseg = [[sb(f"p4rseg{d}_{i}", [128, 8, 128], BF16) for i in range(2)] for d in range(2)]
        xd = [[sb(f"p4xd{d}_{i}", [128, 512], BF16) for i in range(4)] for d in range(2)]
        eseg = [sb(f"p4eseg{i}", [128, 4, 128], BF16) for i in range(8)]
        MT = [sb(f"p4MT{i}", [128, 4, 128], BF16) for i in range(8)]
        tt_ = [[sb(f"p4t{d}_{i}", [128, 512], BF16) for i in range(2)] for d in range(2)]
        xsD = [sb(f"p4xsD{i}", [128, 512], BF16) for i in range(2)]
        yy = [sb(f"p4y{i}", [128, 512], F32) for i in range(3)]
        ynf = [sb(f"p4ynf{i}", [128, 512], BF16) for i in range(2)]
        ssg = [sb(f"p4ssg{i}", [128, 1], F32) for i in range(2)]
        sqj = sb("p4sqj", [128, 512], BF16)
        ynT_st = [sb(f"p4ynT{i}", [128, 16, 128], BF16) for i in range(3)]
        nwb = sb("p4nwb", [128, 2048], F32)
        Dbc = sb("p4Dbc", [128, 32], F32)
        PScb = st.enter_context(_pst(nc, "p4pscb", [128, 512], F32))
        PSy = [st.enter_context(_pst(nc, f"p4psy{i}", [128, 512], F32)) for i in range(2)]
        PSseg = [st.enter_context(_pst(nc, f"p4psseg{i}", [128, 512], F32)) for i in range(2)]
        PSo = [st.enter_context(_pst(nc, f"p4pso{i}", [128, 512], F32)) for i in range(2)]
        PSt = st.enter_context(_pst(nc, "p4pst", [128, 512], F32))
        S.dma("sp", nwb[:], C.snw[l].partition_broadcast(128), r=[], w=[nwb])
        S.dma("sp", Dbc[:], C.sdd[l].partition_broadcast(128), r=[], w=[Dbc])

        def loads(c):
            b = c % NB
            S.dma("sp", xs_t[b][:], C.xs_d[c * 128:(c + 1) * 128, :], r=[C.xs_d], w=[xs_t[b]])
            S.dma("sp", BT_t[b][:], C.BT_d[:, c * 128:(c + 1) * 128].rearrange("(g n) t -> n g t", n=128), r=[C.BT_d], w=[BT_t[b]])
            S.dma("sp", CT_t[b][:], C.CT_d[:, c * 128:(c + 1) * 128].rearrange("(g n) t -> n g t", n=128), r=[C.CT_d], w=[CT_t[b]])
            S.dma("sp", sz_t[b][:], C.sz_d[c * 128:(c + 1) * 128, :], r=[C.sz_d], w=[sz_t[b]])
            for d in range(2):
                S.dma("sp", hh_t[d][b][:], C.hin_d[d][c], r=[C.hin_d[d]], w=[hh_t[d][b]])

        units = [(d, q) for d in range(2) for q in range(2)]

        def s1(k):
            c, g = divmod(k, 4)
            b, cp = c % NB, c % 2
            xt, BT, CT = xs_t[b], BT_t[b], CT_t[b]
            if g == 0:
                pcb = PScb[:].rearrange("p (g t) -> p g t", g=4)
                for g2 in range(4):
                    O.mm(pcb[:, g2, :], BT[:, g2, :], CT[:, g2, :], True, True, [BT, CT], [PScb])
                for d in range(2):
                    O.tt("dve", cbm[d][cp][:], pcb, tri[:, d, :].unsqueeze(1).to_broadcast([128, 4, 128]), ALU.mult,
                         [PScb, tri], [cbm[d][cp]])
                pse = PScb[:, 0:64].rearrange("p (d h) -> p d h", d=2)
                for d in range(2):
                    la_c = C.la_all[:, c, d * 32:(d + 1) * 32]
                    O.mm(pse[:, d, :], trif[:, 3 + d, :], la_c, True, True, [trif, C.la_all], [PScb])
                for d in range(2):
                    O.act(ecum[d][cp][:], pse[:, d, :], AF.Exp, [PScb], [ecum[d][cp]])
            for d in range(2):
                la_g = C.la_all[:, c, d * 32 + g * 8:d * 32 + (g + 1) * 8]
                dt_g = C.dt_all[:, c, d * 32 + g * 8:d * 32 + (g + 1) * 8]
                O.tt("dve", rseg[d][k % 2][:], la_g.unsqueeze(2).to_broadcast([128, 8, 128]),
                     tri[:, d, :].unsqueeze(1).to_broadcast([128, 8, 128]), ALU.mult, [C.la_all, tri], [rseg[d][k % 2]])
                O.tt("pool", xd[d][k % 4][:].rearrange("p (h e) -> p h e", e=64),
                     xt[:, g * 512:(g + 1) * 512].rearrange("p (h e) -> p h e", e=64),
                     dt_g.unsqueeze(2).to_broadcast([128, 8, 64]), ALU.mult, [xt, C.dt_all], [xd[d][k % 4]])

        def s2(k):
            for u, (d, q) in enumerate(units):
                n = k * 4 + u
                pss = PSseg[n % 2]
                es = eseg[n % 8]
                O.mm(pss[:], tri[:, 3 - d, :], rseg[d][k % 2][:, q * 4:(q + 1) * 4, :].rearrange("p h t -> p (h t)"), True, True,
                     [tri, rseg[d][k % 2]], [pss])
                O.act(es[:].rearrange("p h t -> p (h t)"), pss[:], AF.Exp, [pss], [es])

        def s3(k):
            c, g = divmod(k, 4)
            cp = c % 2
            for u, (d, q) in enumerate(units):
                n = k * 4 + u
                O.tt("dve" if u % 2 else "pool", MT[n % 8][:], eseg[n % 8][:],
                     cbm[d][cp][:, g, :].unsqueeze(1).to_broadcast([128, 4, 128]), ALU.mult, [eseg[n % 8], cbm[d][cp]], [MT[n % 8]])

        def s4(k):
            for u, (d, q) in enumerate(units):
                n = k * 4 + u
                mt = MT[n % 8]
                for hh in range(4):
                    h8 = q * 4 + hh
                    O.mm(PSy[k % 2][:, h8 * 64:(h8 + 1) * 64], mt[:, hh, :], xd[d][k % 4][:, h8 * 64:(h8 + 1) * 64],
                         u == 0 and hh == 0, False, [mt, xd[d][k % 4]], [PSy[k % 2]])

        def s5(k):
            c, g = divmod(k, 4)
            b, cp, kp = c % NB, c % 2, k % 2
            xt, CT, sz = xs_t[b], CT_t[b], sz_t[b]
            for d in range(2):
                hd = hh_t[d][b]
                O.mm(PSo[d][:], CT[:, g, :], hd[:, g * 512:(g + 1) * 512], True, True, [CT, hd], [PSo[d]])
            O.tt("pool", xsD[kp][:].rearrange("p (h e) -> p h e", e=64),
                 xt[:, g * 512:(g + 1) * 512].rearrange("p (h e) -> p h e", e=64),
                 Dbc[:, g * 8:(g + 1) * 8].unsqueeze(2).to_broadcast([128, 8, 64]), ALU.mult, [xt, Dbc], [xsD[kp]])
            for d in range(2):
                O.tt("dve", tt_[d][kp][:].rearrange("p (h e) -> p h e", e=64), PSo[d][:].rearrange("p (h e) -> p h e", e=64),
                     ecum[d][cp][:, g * 8:(g + 1) * 8].unsqueeze(2).to_broadcast([128, 8, 64]), ALU.mult,
                     [PSo[d], ecum[d][cp]], [tt_[d][kp]])

        def s5b(k):
            c, g = divmod(k, 4)
            b, cp, kp = c % NB, c % 2, k % 2
            sz = sz_t[b]
            O.mm(PSy[kp][:], C.ident[:], xsD[kp][:], False, False, [C.ident, xsD[kp]], [PSy[kp]])
            O.mm(PSy[kp][:], C.ident[:], tt_[0][kp][:], False, False, [C.ident, tt_[0][kp]], [PSy[kp]])
            O.mm(PSy[kp][:], C.ident[:], tt_[1][kp][:], False, True, [C.ident, tt_[1][kp]], [PSy[kp]])
            y_ = yy[k % 3]
            O.tt("dve", y_[:], PSy[kp][:], sz[:, g * 512:(g + 1) * 512], ALU.mult, [PSy[kp], sz], [y_])

        def s6(k):
            c, g = divmod(k, 4)
            y_ = yy[k % 3]
            s_ = ssg[k % 2]
            O.act(sqj[:], y_[:], AF.Square, [y_], [sqj, s_], accum_out=s_[:])
            O.act(s_[:], s_[:], AF.Sqrt, [s_], [s_], bias=1e-6, scale=1.0 / 512)
            O.recip(s_[:], s_[:], [s_], [s_])
            O.stt("dve", ynf[k % 2][:], y_[:], s_[:], nwb[:, g * 512:(g + 1) * 512], ALU.mult, ALU.mult, [y_, s_, nwb], [ynf[k % 2]])

        def s7(k):
            c, g = divmod(k, 4)
            ptb = PSt[:, 0:256].bitcast(BF16).rearrange("p (a t) -> p a t", a=4)
            ys = ynT_st[c % 3]
            yn_ = ynf[k % 2]
            for a in range(4):
                O.tr(ptb[:, a, :], yn_[:, a * 128:(a + 1) * 128], C.ident[:], [yn_, C.ident], [PSt])
            O.copy("act", ys[:, g * 4:(g + 1) * 4, :], ptb, [PSt], [(ys, g)])
            if g == 3:
                S.dma("sp", C.ynT_d[c // 2, :, :, (c % 2) * 128:(c % 2 + 1) * 128], ys[:], r=[ys],
                      w=[(C.ynT_d, c)])

        order = [(s5, 4), (s6, 5), (s7, 6), (s2, 1), (s1, 0), (s3, 2), (s4, 3), (s5b, 4)]
        NK = NT_ * 4
        for c in range(NB):
            loads(c)
        for it in range(NK + 6):
            if it >= 8 and it % 4 == 0:
                cn = (it - 8) // 4 + NB
                if cn < NT_:
                    loads(cn)
            for fn, skew in order:
                k = it - skew
                if 0 <= k < NK:
                    fn(k)
        S.barrier()
        S.flush()


def phase5(C, l, x_src, x_dst):
    nc, S, O = C.nc, C.S, C.O
    with ExitStack() as st:
        def sb(name, shape, dt):
            return st.enter_context(_sbt(nc, name, list(shape), dt))
        TB = 256
        NTB = S_ // TB
        wa = sb("p5wa", [128, 8, D_], BF16)
        wb_ = sb("p5wb", [128, 8, D_], BF16)
        wc = sb("p5wc", [128, 16, D_], BF16)
        wo = sb("p5wo", [128, 8, D_], BF16)
        yA_b = [sb(f"p5yA{i}", [128, 8, TB], BF16) for i in range(2)]
        yn_b = [sb(f"p5yn{i}", [128, 16, TB], BF16) for i in range(2)]
        oT_b = [sb(f"p5oT{i}", [128, 8, TB], BF16) for i in range(2)]
        g_b = [sb(f"p5g{i}", [128, 24, TB], BF16) for i in range(2)]
        mT = [sb(f"p5mT{i}", [128, 8, TB], BF16) for i in range(1)] * 2
        ot = [[sb(f"p5ot{p}_{i}", [128, 16, 65], F32) for p in range(3)] for i in range(2)]
        rden = [sb(f"p5rden{i}", [128, 16], F32) for i in range(2)]
        ob = [sb(f"p5ob{i}", [128, 16, 64], BF16) for i in range(1)] * 2
        xt = [sb(f"p5xt{i}", [128, D_], F32) for i in range(1)] * 2
        xo = [sb(f"p5xo{i}", [128, D_], F32) for i in range(1)] * 2
        t1 = [sb(f"p5t1_{i}", [128, TB], F32) for i in range(2)]
        t2 = [sb(f"p5t2_{i}", [128, TB], F32) for i in range(2)]
        PB = [st.enter_context(_pst(nc, f"p5ps{i}", [128, 512], F32)) for i in range(6)]
        PT = [st.enter_context(_pst(nc, f"p5pt{i}", [128, 8, 128], BF16)) for i in range(2)]
        S.dma("pool", wa[:], C.w_a[l].rearrange("(k p) n -> p k n", p=128), r=[], w=[wa])
        S.dma("pool", wb_[:], C.w_b[l].rearrange("(k p) n -> p k n", p=128), r=[], w=[wb_])
        S.dma("pool", wc[:], C.w_c[l].rearrange("(k p) n -> p k n", p=128), r=[], w=[wc])
        S.dma("pool", wo[:], C.w_o[l].rearrange("(k p) n -> p k n", p=128), r=[], w=[wo])
        cnt = {"ps": 0, "o": 0}

        def PS():
            p = PB[cnt["ps"] % 6]
            cnt["ps"] += 1
            return p

        def loads(tb):
            b = tb % 2
            tsl = slice(tb * TB, (tb + 1) * TB)
            S.dma("sp", yA_b[b][:], C.yAT[tb], r=[C.yAT], w=[yA_b[b]])
            S.dma("sp", yn_b[b][:], C.ynT_d[tb], r=[C.ynT_d], w=[yn_b[b]])
            S.dma("sp", g_b[b][:], C.gT[tb], r=[C.gT], w=[g_b[b]])

        def oloads(tb):
            for tt in range(TB // 128):
                t = tb * (TB // 128) + tt
                for p in range(3):
                    S.dma("sp", ot[tt][p][:], C.o_d[p][t * 128:(t + 1) * 128], r=[C.o_d[p]], w=[ot[tt][p]])

        def combine(tb):
            b = tb % 2
            for tt in range(TB // 128):
                n = cnt["o"]
                cnt["o"] += 1
                o3 = ot[tt]
                O.tt("pool", o3[0][:], o3[0][:], o3[1][:], ALU.add, [o3[0], o3[1]], [o3[0]])
                O.tt("pool", o3[0][:], o3[0][:], o3[2][:], ALU.add, [o3[0], o3[2]], [o3[0]])
                O.recip(rden[n % 2][:].unsqueeze(2), o3[0][:, :, 64:65], [o3[0]], [rden[n % 2]])
                O.tt("pool", ob[n % 2][:], o3[0][:, :, 0:64], rden[n % 2][:].unsqueeze(2).to_broadcast([128, 16, 64]), ALU.mult,
                     [o3[0], rden[n % 2]], [ob[n % 2]])
                obf = ob[n % 2][:].rearrange("p h d -> p (h d)")
                for k in range(8):
                    O.tr(PT[n % 2][:, k, :], obf[:, k * 128:(k + 1) * 128], C.ident[:], [ob[n % 2], C.ident], [PT[n % 2]])
                O.copy("act", oT_b[b][:, :, tt * 128:(tt + 1) * 128], PT[n % 2][:], [PT[n % 2]], [(oT_b[b], tt)])

        loads(0)
        oloads(0)
        combine(0)
        nx = 0
        for tb in range(NTB):
            b = tb % 2
            if tb + 1 < NTB:
                loads(tb + 1)
                oloads(tb + 1)
            for cc in range(8):
                if cc == 4 and tb + 1 < NTB:
                    combine(tb + 1)
                pa, pb, pc = PS(), PS(), PS()
                for k in range(8):
                    O.mm(pa[:, :TB], wa[:, k, cc * 128:(cc + 1) * 128], yA_b[b][:, k, :], k == 0, k == 7, [wa, yA_b[b]], [pa])
                for k in range(8):
                    O.mm(pb[:, :TB], wb_[:, k, cc * 128:(cc + 1) * 128], oT_b[b][:, k, :], k == 0, k == 7, [wb_, oT_b[b]], [pb])
                for k in range(16):
                    O.mm(pc[:, :TB], wc[:, k, cc * 128:(cc + 1) * 128], yn_b[b][:, k, :], k == 0, k == 15, [wc, yn_b[b]], [pc])
                a1, a2 = t1[cc % 2], t2[cc % 2]
                O.tt("dve", a1[:], pa[:, :TB], g_b[b][:, cc, :], ALU.mult, [pa, g_b[b]], [a1])
                O.tt("dve", a2[:], pb[:, :TB], g_b[b][:, 8 + cc, :], ALU.mult, [pb, g_b[b]], [a2])
                O.tt("pool", a1[:], a1[:], a2[:], ALU.add, [a1, a2], [a1])
                O.tt("dve", a2[:], pc[:, :TB], g_b[b][:, 16 + cc, :], ALU.mult, [pc, g_b[b]], [a2])
                O.tt("pool", mT[b][:, cc, :], a1[:], a2[:], ALU.add, [a1, a2], [(mT[b], cc)])
            for tt in range(TB // 128):
                t = tb * (TB // 128) + tt
                xt_, xo_ = xt[nx % 2], xo[nx % 2]
                nx += 1
                S.dma("sp", xt_[:], x_src[t * 128:(t + 1) * 128, :], r=[(x_src, t)], w=[xt_])
                for hf in range(2):
                    ps = PS()
                    for k in range(8):
                        O.mm(ps[:], mT[b][:, k, tt * 128:(tt + 1) * 128], wo[:, k, hf * 512:(hf + 1) * 512], k == 0, k == 7,
                             [mT[b], wo], [ps])
                    O.tt("dve", xo_[:, hf * 512:(hf + 1) * 512], ps[:], xt_[:, hf * 512:(hf + 1) * 512], ALU.add,
                         [ps, xt_], [(xo_, hf)])
                S.dma("sp", C.xmid[t * 128:(t + 1) * 128, :], xo_[:], r=[xo_], w=[(C.xmid, t)])
        S.barrier()
        S.flush()
    if "xmid" in C.dbg and C.stop == (l, 5):
        return
    with ExitStack() as st:
        def sb(name, shape, dt):
            return st.enter_context(_sbt(nc, name, list(shape), dt))
        w2 = sb("p5w2", [128, 32, D_], BF16)
        w1b = [sb(f"p5w1_{i}", [128, 8, 512], BF16) for i in range(3)]
        xts = [sb(f"p5x{i}", [128, D_], F32) for i in range(4)]
        hT = sb("p5hT", [128, 8, 512], BF16)
        h1T = sb("p5h1T", [128, 32, 512], BF16)
        ub = [sb(f"p5ub{i}", [128, D_], BF16) for i in range(2)]
        sq = sb("p5sq", [128, D_], BF16)
        ss = [sb(f"p5ss{i}", [128, 1], F32) for i in range(2)]
        rr = [sb(f"p5r{i}", [128, 512], F32) for i in range(2)]
        xo = [sb(f"p5xo{i}", [128, D_], F32) for i in range(2)]
        nwt = sb("p5nw", [128, D_], F32)
        fnw = sb("p5fnw", [128, D_], F32)
        PB = [st.enter_context(_pst(nc, f"p5bps{i}", [128, 512], F32)) for i in range(6)]
        PT = [st.enter_context(_pst(nc, f"p5bpt{i}", [128, 8, 128], BF16)) for i in range(2)]
        w2src = C.w_2[l].rearrange("(k p) n -> p k n", p=128)
        for k8 in range(4):
            S.dma("pool", w2[:, k8 * 8:(k8 + 1) * 8, :], w2src[:, k8 * 8:(k8 + 1) * 8, :], r=[], w=[(w2, k8)])
        S.dma("sp", nwt[:], C.mlpw[l:l + 1, :].partition_broadcast(128), r=[], w=[nwt])
        S.dma("sp", fnw[:], C.finw.partition_broadcast(128), r=[], w=[fnw])
        w1src = C.wb_1[l].rearrange("(k p) n -> p k n", p=128)
        nps = 0
        nw1 = 0
        for tb in range(8):
            for tt in range(4):
                t = tb * 4 + tt
                S.dma("sp", xts[tt][:], C.xmid[t * 128:(t + 1) * 128, :], r=[(C.xmid, t)], w=[xts[tt]])
                rms_tile(C, xts[tt], ss[tt % 2], sq, nwt, ub[tt % 2], "p5")
                for k in range(8):
                    O.tr(PT[tt % 2][:, k, :], ub[tt % 2][:, k * 128:(k + 1) * 128], C.ident[:], [ub[tt % 2], C.ident], [PT[tt % 2]])
                O.copy("act", hT[:, :, tt * 128:(tt + 1) * 128], PT[tt % 2][:], [PT[tt % 2]], [(hT, tt)])
            for f4 in range(8):
                w1t = w1b[nw1 % 3]
                nw1 += 1
                S.dma("sp", w1t[:], w1src[:, :, f4 * 512:(f4 + 1) * 512], r=[C.wb_1[l]], w=[w1t])
                for fj in range(4):
                    fc = f4 * 4 + fj
                    ps = PB[nps % 6]
                    r_ = rr[nps % 2]
                    nps += 1
                    for k in range(8):
                        O.mm(ps[:], w1t[:, k, fj * 128:(fj + 1) * 128], hT[:, k, :], k == 0, k == 7, [w1t, hT], [ps])
                    O.act(r_[:], ps[:], AF.Relu, [ps], [r_])
                    O.tt("pool" if fc % 2 else "dve", h1T[:, fc, :], r_[:], r_[:], ALU.mult, [r_], [(h1T, fc)])
            for tt in range(4):
                t = tb * 4 + tt
                xo_ = xo[tt % 2]
                for hf in range(2):
                    ps = PB[nps % 6]
                    nps += 1
                    for fc in range(32):
                        O.mm(ps[:], h1T[:, fc, tt * 128:(tt + 1) * 128], w2[:, fc, hf * 512:(hf + 1) * 512], fc == 0, fc == 31,
                             [h1T, w2], [ps])
                    O.tt("dve", xo_[:, hf * 512:(hf + 1) * 512], ps[:], xts[tt][:, hf * 512:(hf + 1) * 512], ALU.add,
                         [ps, xts[tt]], [(xo_, hf)])
                if x_dst is not None:
                    S.dma("sp", x_dst[t * 128:(t + 1) * 128, :], xo_[:], r=[xo_], w=[(x_dst, t)])
                else:
                    s_ = ss[tt % 2]
                    O.act(sq[:], xo_[:], AF.Square, [xo_], [sq, s_], accum_out=s_[:])
                    O.act(s_[:], s_[:], AF.Sqrt, [s_], [s_], bias=1e-6, scale=1.0 / D_)
                    O.recip(s_[:], s_[:], [s_], [s_])
                    O.stt("dve", xo_[:], xo_[:], s_[:], fnw[:], ALU.mult, ALU.mult, [xo_, s_, fnw], [xo_])
                    S.dma("sp", C.y_out[t * 128:(t + 1) * 128, :], xo_[:], r=[xo_], w=[(C.y_out, t)])
        S.barrier()
        S.flush()


def host_consts():
    bf = ml_dtypes.bfloat16
    p = np.arange(128)[:, None]
    f = np.arange(128)[None, :]
    tri = np.stack([(p <= f), (p >= f), (p < f), (p > f)], axis=1).astype(np.float32)
    trif = np.stack([(p < f), (p > f), np.ones((128, 128), bool), (p <= f), (p >= f)], axis=1).astype(np.float32)
    invf = (500000.0 ** (-np.arange(0, 16, 2, dtype=np.float32) / 16.0)).astype(np.float32)[None, :]
    return {"c_ident": np.eye(128, dtype=np.float32).astype(bf), "c_tri": tri.astype(bf), "c_trif": trif, "c_invf": invf}


def prep_inputs(inp):
    f32 = np.float32
    sh = dict(host_consts())
    for k in ("mix_norm_w", "mlp_norm_w", "w_in", "w_a_out", "w_b_out", "w_c_out", "w_o", "w_ff1", "w_ff2"):
        sh[k] = np.ascontiguousarray(inp[k], dtype=f32)
    sh["final_norm_w"] = np.ascontiguousarray(inp["final_norm_w"], dtype=f32).reshape(1, D_)
    sh["conv_a_wT"] = np.ascontiguousarray(inp["conv_a_w"].reshape(L_, 3, 8, 128).transpose(0, 3, 2, 1), dtype=f32)
    sh["ssd_conv_wT"] = np.ascontiguousarray(inp["ssd_conv_w"].reshape(L_, 5, 24, 128).transpose(0, 3, 2, 1), dtype=f32)
    sh["ssd_conv_bT"] = np.ascontiguousarray(inp["ssd_conv_b"].reshape(L_, 24, 128).transpose(0, 2, 1), dtype=f32)
    sh["ssd_conv_bR"] = np.ascontiguousarray(inp["ssd_conv_b"].reshape(L_, 1, 3072), dtype=f32)
    sh["ssd_a_log"] = np.ascontiguousarray(inp["ssd_a_log"].reshape(L_, 1, 64), dtype=f32)
    sh["ssd_dt_bias"] = np.ascontiguousarray(inp["ssd_dt_bias"].reshape(L_, 1, 64), dtype=f32)
    sh["ssd_d"] = np.ascontiguousarray(inp["ssd_d"].reshape(L_, 1, 32), dtype=f32)
    sh["ssd_norm_w"] = np.ascontiguousarray(inp["ssd_norm_w"].reshape(L_, 1, 2048), dtype=f32)
    per = []
    for b in range(inp["x"].shape[0]):
        d = dict(sh)
        d["x"] = np.ascontiguousarray(inp["x"][b], dtype=f32)
        d["pos"] = np.ascontiguousarray(inp["positions"][b].reshape(NT_, 128).T, dtype=np.int32)
        per.append(d)
    return per


_NC_CACHE = {}


def kernel(**inputs):
    per = prep_inputs(inputs)
    if "nc" not in _NC_CACHE:
        _NC_CACHE["nc"] = build()
    nc = _NC_CACHE["nc"]
    res = run_bass_kernel_spmd(nc, per, core_ids=list(range(len(per))))
    return np.stack([np.asarray(r["y"], dtype=np.float32) for r in res.results], axis=0)
```

```python
import numpy as np
import ml_dtypes
import concourse.bass as bass
import concourse.mybir as mybir
from concourse.bass_utils import run_bass_kernel_spmd
from contextlib import ExitStack
from types import SimpleNamespace

F32 = mybir.dt.float32
BF16 = mybir.dt.bfloat16
I32 = mybir.dt.int32
AF = mybir.ActivationFunctionType
ALU = mybir.AluOpType
AX = mybir.AxisListType

ENG = ("pe", "act", "dve", "pool", "sp")
DMAQ = ("sp", "act", "pool")


class _Buf:
    __slots__ = ("w", "r")

    def __init__(self):
        self.w = None
        self.r = {}


class _TBuf:
    __slots__ = ("whole", "subs")

    def __init__(self):
        self.whole = _Buf()
        self.subs = {}


class Sched:
    NRING = 8
    SAME_ENGINE_SYNC = ("act", "dve", "pool")

    def __init__(self, nc):
        self.nc = nc
        self.sems = []
        self.esem = {}
        for e in ENG:
            self.esem[e] = self._new_sem("s_" + e)
        self.ecnt = {e: 0 for e in ENG}
        self.ring = {q: [self._new_sem(f"d_{q}{i}") for i in range(self.NRING)] for q in DMAQ}
        self.dman = {q: 0 for q in DMAQ}
        self.known = {e: {} for e in ENG}
        self.streams = {e: [] for e in ENG}
        self.tb = {}
        self.n_wait = 0
        self.n_ins = 0

    def _new_sem(self, name):
        h = self.nc.alloc_semaphore(name=name)
        self.sems.append(h)
        return len(self.sems) - 1

    def _spec(self, a):
        if isinstance(a, tuple):
            t, key = a
        else:
            t, key = a, None
        name = t if isinstance(t, str) else t.name
        tb = self.tb.get(name)
        if tb is None:
            tb = self.tb[name] = _TBuf()
        return tb, key

    def _deps(self, eng, reads, writes):
        need = {}

        def add(ev):
            if ev is not None and need.get(ev[0], 0) < ev[1]:
                need[ev[0]] = ev[1]

        def addr(b):
            for s, v in b.r.items():
                if need.get(s, 0) < v:
                    need[s] = v

        for a in reads:
            tb, key = self._spec(a)
            add(tb.whole.w)
            if key is None:
                for sb in tb.subs.values():
                    add(sb.w)
            else:
                sb = tb.subs.get(key)
                if sb is not None:
                    add(sb.w)
        for a in writes:
            tb, key = self._spec(a)
            add(tb.whole.w)
            addr(tb.whole)
            if key is None:
                for sb in tb.subs.values():
                    add(sb.w)
                    addr(sb)
            else:
                sb = tb.subs.get(key)
                if sb is not None:
                    add(sb.w)
                    addr(sb)
        kn = self.known[eng]
        own = self.esem[eng]
        waits = []
        for s, v in need.items():
            if s == own and eng not in self.SAME_ENGINE_SYNC:
                continue
            if kn.get(s, 0) < v:
                kn[s] = v
                waits.append((s, v))
        return waits

    def _record(self, ev, reads, writes):
        s, v = ev
        for a in reads:
            tb, key = self._spec(a)
            if key is None:
                b = tb.whole
            else:
                b = tb.subs.get(key)
                if b is None:
                    b = tb.subs[key] = _Buf()
            if b.r.get(s, 0) < v:
                b.r[s] = v
        for a in writes:
            tb, key = self._spec(a)
            if key is None:
                tb.whole.w = ev
                tb.whole.r = {}
                tb.subs = {}
            else:
                b = tb.subs.get(key)
                if b is None:
                    b = tb.subs[key] = _Buf()
                b.w = ev
                b.r = {}

    def op(self, eng, emit, r=(), w=()):
        waits = self._deps(eng, r, w)
        self.ecnt[eng] += 1
        ev = (self.esem[eng], self.ecnt[eng])
        self._record(ev, r, w)
        self.streams[eng].append((waits, emit, ev[0], 1))
        self.n_wait += len(waits)
        self.n_ins += 1

    def dma(self, q, out, in_, r=(), w=(), **kw):
        waits = self._deps(q, r, w)
        i = self.dman[q]
        self.dman[q] += 1
        s = self.ring[q][i % self.NRING]
        rnd = i // self.NRING
        if rnd > 0:
            kn = self.known[q]
            if kn.get(s, 0) < 16 * rnd:
                kn[s] = 16 * rnd
                waits.append((s, 16 * rnd))
        ev = (s, 16 * (rnd + 1))
        self._record(ev, r, w)
        self.streams[q].append((waits, lambda e: e.dma_start(out=out, in_=in_, **kw), s, 16))
        self.n_wait += len(waits)
        self.n_ins += 1

    def barrier(self, skip_q=()):
        evs = []
        for e in ENG:
            if self.ecnt[e] > 0:
                evs.append((self.esem[e], self.ecnt[e]))
        for q in DMAQ:
            if q in skip_q:
                continue
            n = self.dman[q]
            for j in range(min(n, self.NRING)):
                last = ((n - 1 - j) // self.NRING) * self.NRING + j
                evs.append((self.ring[q][j], 16 * (last // self.NRING + 1)))
        for e in ENG:
            kn = self.known[e]
            waits = []
            for s, v in evs:
                if kn.get(s, 0) < v:
                    kn[s] = v
                    waits.append((s, v))
            if waits:
                self.streams[e].append((waits, None, None, 0))
                self.n_wait += len(waits)

    def flush(self):
        nc = self.nc
        sems = self.sems

        def run(name):
            items = self.streams[name]
            self.streams[name] = []

            def f(e):
                for waits, emit, s, inc in items:
                    for ws, wv in waits:
                        e.wait_ge(sems[ws], wv)
                    if emit is not None:
                        ins = emit(e)
                        ins.then_inc(sems[s], inc)

            return f

        with nc.Block() as block:
            block.tensor(run("pe"))
            block.scalar(run("act"))
            block.vector(run("dve"))
            block.gpsimd(run("pool"))
            block.sync(run("sp"))


S_ = 4096
D_ = 1024
NT_ = 32
L_ = 2
DIN_ = 14400
PI = 3.141592653589793


_UID = [0]


def _sbt(nc, name, shape, dt):
    _UID[0] += 1
    return nc.sbuf_tensor(f"{name}_u{_UID[0]}", shape, dt)


def _pst(nc, name, shape, dt):
    _UID[0] += 1
    return nc.psum_tensor(f"{name}_u{_UID[0]}", shape, dt)


class Ops:
    def __init__(self, S):
        self.S = S

    def mm(self, out, lhsT, rhs, start, stop, r, w):
        self.S.op("pe", lambda e: e.matmul(out, lhsT=lhsT, rhs=rhs, start=start, stop=stop), r, w)

    def tr(self, out, in_, ident, r, w):
        self.S.op("pe", lambda e: e.transpose(out=out, in_=in_, identity=ident), r, w)

    def act(self, out, in_, func, r, w, **kw):
        self.S.op("act", lambda e: e.activation(out=out, in_=in_, func=func, **kw), r, w)

    def tt(self, eng, out, in0, in1, op, r, w):
        self.S.op(eng, lambda e: e.tensor_tensor(out=out, in0=in0, in1=in1, op=op), r, w)

    def ts(self, eng, out, in0, s1, s2, op0, op1, r, w):
        if s2 is None:
            self.S.op(eng, lambda e: e.tensor_scalar(out=out, in0=in0, scalar1=s1, scalar2=None, op0=op0), r, w)
        else:
            self.S.op(eng, lambda e: e.tensor_scalar(out=out, in0=in0, scalar1=s1, scalar2=s2, op0=op0, op1=op1), r, w)

    def stt(self, eng, out, in0, scalar, in1, op0, op1, r, w):
        self.S.op(eng, lambda e: e.scalar_tensor_tensor(out=out, in0=in0, scalar=scalar, in1=in1, op0=op0, op1=op1), r, w)

    def copy(self, eng, out, in_, r, w):
        if eng == "act":
            self.S.op(eng, lambda e: e.copy(out=out, in_=in_), r, w)
        else:
            self.S.op(eng, lambda e: e.tensor_copy(out=out, in_=in_), r, w)

    def memset(self, eng, ap, val, w):
        self.S.op(eng, lambda e: e.memset(ap, val), (), w)

    def recip(self, out, in_, r, w):
        self.S.op("dve", lambda e: e.reciprocal(out=out, in_=in_), r, w)


def build(dbg=(), stop=None, nlayers=L_):
    nc = bass.Bass("TRN2", target_bir_lowering=False)

    def din(name, shape, dt=F32):
        return nc.dram_tensor(name, list(shape), dt, kind="ExternalInput").ap()

    def dscr(name, shape, dt):
        kind = "ExternalOutput" if name in dbg else "Internal"
        return nc.dram_tensor(name, list(shape), dt, kind=kind).ap()

    x_in = din("x", [S_, D_])
    pos_in = din("pos", [128, NT_], I32)
    mixw = din("mix_norm_w", [L_, D_])
    mlpw = din("mlp_norm_w", [L_, D_])
    finw = din("final_norm_w", [1, D_])
    w_in = din("w_in", [L_, D_, DIN_])
    w_a = din("w_a_out", [L_, D_, D_])
    w_b = din("w_b_out", [L_, D_, D_])
    w_c = din("w_c_out", [L_, 2 * D_, D_])
    w_o = din("w_o", [L_, D_, D_])
    w_1 = din("w_ff1", [L_, D_, 4 * D_])
    w_2 = din("w_ff2", [L_, 4 * D_, D_])
    cawT = din("conv_a_wT", [L_, 128, 8, 3])
    scwT = din("ssd_conv_wT", [L_, 128, 24, 5])
    scbT = din("ssd_conv_bT", [L_, 128, 24])
    scbR = din("ssd_conv_bR", [L_, 1, 3072])
    alog = din("ssd_a_log", [L_, 1, 64])
    dtb = din("ssd_dt_bias", [L_, 1, 64])
    sdd = din("ssd_d", [L_, 1, 32])
    snw = din("ssd_norm_w", [L_, 1, 2048])
    c_ident = din("c_ident", [128, 128], BF16)
    c_tri = din("c_tri", [128, 4, 128], BF16)
    c_trif = din("c_trif", [128, 5, 128], F32)
    c_invf = din("c_invf", [1, 8], F32)
    y_out = nc.dram_tensor("y", [S_, D_], F32, kind="ExternalOutput").ap()

    wb_in = [dscr(f"wb_in{l}", [D_, DIN_], BF16) for l in range(L_)]
    wb_a = [dscr(f"wb_a{l}", [D_, D_], BF16) for l in range(L_)]
    wb_b = [dscr(f"wb_b{l}", [D_, D_], BF16) for l in range(L_)]
    wb_c = [dscr(f"wb_c{l}", [2 * D_, D_], BF16) for l in range(L_)]
    wb_o = [dscr(f"wb_o{l}", [D_, D_], BF16) for l in range(L_)]
    wb_1 = [dscr(f"wb_1{l}", [D_, 4 * D_], BF16) for l in range(L_)]
    wb_2 = [dscr(f"wb_2{l}", [4 * D_, D_], BF16) for l in range(L_)]
    yAT = dscr("yAT", [16, 128, 8, 256], BF16)
    gT = dscr("gT", [16, 128, 24, 256], BF16)
    qT = dscr("qT", [D_, S_], BF16)
    kT = dscr("kT", [D_, S_], BF16)
    v_d = dscr("v_d", [S_, D_], BF16)
    sz_d = dscr("sz_d", [S_, 2 * D_], BF16)
    xs_d = dscr("xs_d", [S_, 2 * D_], BF16)
    Bt_d = dscr("Bt_d", [S_, 512], BF16)
    BT_d = dscr("BT_d", [512, S_], BF16)
    CT_d = dscr("CT_d", [512, S_], BF16)
    dt_dbg = dscr("dt_dbg", [128, NT_, 64], F32)
    o_d = [dscr(f"o_d{p}", [S_, 16, 65], F32) for p in range(3)]
    hin_d = [dscr(f"hin_d{d}", [NT_, 128, 2048], BF16) for d in range(2)]
    ynT_d = dscr("ynT_d", [16, 128, 16, 256], BF16)
    xmid = dscr("xmid", [S_, D_], F32)
    xl = [dscr(f"xl{l}", [S_, D_], F32) for l in range(L_ - 1)]

    S = Sched(nc)
    O = Ops(S)

    with ExitStack() as gst:
        def gsb(name, shape, dt):
            return gst.enter_context(_sbt(nc, name, list(shape), dt))

        ident = gsb("ident", [128, 128], BF16)
        tri = gsb("tri", [128, 4, 128], BF16)
        trif = gsb("trif", [128, 5, 128], F32)
        cosT = gsb("cosT", [128, NT_, 8], F32)
        sinT = gsb("sinT", [128, NT_, 8], F32)
        dt_all = gsb("dt_all", [128, NT_, 64], F32)
        la_all = gsb("la_all", [128, NT_, 64], F32)
        S.dma("sp", ident[:], c_ident, r=[], w=[ident])
        S.dma("sp", tri[:], c_tri, r=[], w=[tri])
        S.dma("sp", trif[:], c_trif, r=[], w=[trif])

        def cast_w(src, dst, rows_per):
            R = src.shape[0]
            for r0 in range(0, R, rows_per):
                S.dma("pool", dst[r0:r0 + rows_per, :], src[r0:r0 + rows_per, :], r=[], w=[(dst, r0)])

        with ExitStack() as st:
            def sb(name, shape, dt):
                return st.enter_context(_sbt(nc, name, list(shape), dt))
            posi = sb("posi", [128, NT_], I32)
            posf = sb("posf", [128, NT_], F32)
            invf = sb("invf", [128, 8], F32)
            ang = sb("ang", [128, NT_, 8], F32)
            a1 = sb("a1", [128, NT_, 8], F32)
            S.dma("sp", posi[:], pos_in, r=[], w=[posi])
            S.dma("sp", invf[:], c_invf.partition_broadcast(128), r=[], w=[invf])
            O.copy("dve", posf[:], posi[:], [posi], [posf])
            O.tt("dve", ang[:], posf[:].unsqueeze(2).to_broadcast([128, NT_, 8]),
                 invf[:].unsqueeze(1).to_broadcast([128, NT_, 8]), ALU.mult, [posf, invf], [ang])
            ki = sb("ki", [128, NT_, 8], I32)
            kf = sb("kf", [128, NT_, 8], F32)
            mm_ = sb("mm_", [128, NT_, 8], F32)
            for shift, dstT in ((0.0, sinT), (0.5 * PI, cosT)):
                O.ts("dve", a1[:], ang[:], shift, 1.0 / (2 * PI), ALU.add, ALU.mult, [ang], [a1])
                O.copy("dve", ki[:], a1[:], [a1], [ki])
                O.copy("dve", kf[:], ki[:], [ki], [kf])
                O.ts("dve", a1[:], ang[:], shift, None, ALU.add, None, [ang], [a1])
                O.stt("dve", a1[:], kf[:], -2 * PI, a1[:], ALU.mult, ALU.add, [kf, a1], [a1])
                O.ts("dve", mm_[:], a1[:], PI, 2 * PI, ALU.is_ge, ALU.mult, [a1], [mm_])
                O.tt("dve", a1[:], a1[:], mm_[:], ALU.subtract, [a1, mm_], [a1])
                O.ts("dve", mm_[:], a1[:], -PI, 2 * PI, ALU.is_lt, ALU.mult, [a1], [mm_])
                O.tt("dve", a1[:], a1[:], mm_[:], ALU.add, [a1, mm_], [a1])
                O.ts("dve", a1[:], a1[:], -PI, PI, ALU.max, ALU.min, [a1], [a1])
                O.act(dstT[:], a1[:], AF.Sin, [a1], [dstT])
            S.barrier()
            S.flush()

        for l in range(nlayers):
            x_src = x_in if l == 0 else xl[l - 1]
            x_dst = xl[l] if l < L_ - 1 else None
            C = SimpleNamespace(**locals())
            layer(C, l, x_src, x_dst, stop)
            if stop is not None and stop[0] == l:
                break
        S.barrier(skip_q=())
        S.flush()
    return nc


def layer(C, l, x_src, x_dst, stop):
    nc, S, O = C.nc, C.S, C.O
    with ExitStack() as lst:
        uT = lst.enter_context(_sbt(nc, "uT", [128, 8, S_], BF16))
        phase1(C, l, x_src, uT)
        if stop == (l, 1):
            return
        phase2(C, l, uT)
    S.barrier()
    S.flush()
    if stop == (l, 2):
        return
    phase3(C, l)
    if stop == (l, 3):
        return
    phase4(C, l)
    if stop == (l, 4):
        return
    phase5(C, l, x_src, x_dst)


def rms_tile(C, xt, ss, sq, nwt, ub, tag):
    O = C.O
    O.act(sq[:], xt[:], AF.Square, [xt], [sq, ss], accum_out=ss[:])
    O.act(ss[:], ss[:], AF.Sqrt, [ss], [ss], bias=1e-6, scale=1.0 / D_)
    O.recip(ss[:], ss[:], [ss], [ss])
    O.stt("dve", ub[:], xt[:], ss[:], nwt[:], ALU.mult, ALU.mult, [xt, ss, nwt], [ub])


def phase1(C, l, x_src, uT):
    nc, S, O = C.nc, C.S, C.O
    with ExitStack() as st:
        def sb(name, shape, dt):
            return st.enter_context(_sbt(nc, name, list(shape), dt))
        xt = [sb(f"p1x{i}", [128, D_], F32) for i in range(2)]
        sq = sb("p1sq", [128, D_], BF16)
        ss = [sb(f"p1ss{i}", [128, 1], F32) for i in range(2)]
        ub = [sb(f"p1ub{i}", [128, D_], BF16) for i in range(2)]
        nwt = sb("p1nw", [128, D_], F32)
        pT = [st.enter_context(_pst(nc, f"p1pT{i}", [128, 8, 128], BF16)) for i in range(2)]
        S.dma("sp", nwt[:], C.mixw[l:l + 1, :].partition_broadcast(128), r=[], w=[nwt])
        for t in range(NT_):
            b = t % 2
            S.dma("sp", xt[b][:], x_src[t * 128:(t + 1) * 128, :], r=[(x_src, t)], w=[xt[b]])
            rms_tile(C, xt[b], ss[b], sq, nwt, ub[b], "p1")
            for k in range(8):
                O.tr(pT[b][:, k, :], ub[b][:, k * 128:(k + 1) * 128], C.ident[:], [ub[b], C.ident], [pT[b]])
            O.copy("act" if t % 2 else "dve", uT[:, :, t * 128:(t + 1) * 128], pT[b][:], [pT[b]], [(uT, t)])
        S.barrier()
        S.flush()


def phase2(C, l, uT):
    nc, S, O = C.nc, C.S, C.O
    wsrc = C.w_in[l].rearrange("(k p) n -> p k n", p=128)
    seq = []
    for g in range(2):
        seq += [(512 * g, 512), (1024 + 512 * g, 512), (2048 + 512 * g, 512)]
    seq += [(3072 + 512 * i, 512) for i in range(4)]
    seq += [(5120 + 512 * i, 512) for i in range(2)]
    seq += [(6144 + 512 * i, 512) for i in range(4)]
    seq += [(8192 + 512 * i, 512) for i in range(6)]
    seq += [(11264, 64)]
    seq += [(11328 + 512 * i, 512) for i in range(6)]
    with ExitStack() as st:
        wbuf = [st.enter_context(_sbt(nc, f"p2w{i}", [128, 8, 512], BF16)) for i in range(3)]
        PB = [st.enter_context(_pst(nc, f"p2ps{i}", [128, 512], F32)) for i in range(6)]
        PT = [st.enter_context(_pst(nc, f"p2pt{i}", [128, 4, 128], BF16)) for i in range(2)]
        state = {"issued": 0, "ps": 0, "done": 0}

        def prefetch():
            while state["issued"] < min(len(seq), state["done"] + 3):
                i = state["issued"]
                c0, ncw = seq[i]
                S.dma("pool", wbuf[i % 3][:, :, :ncw], wsrc[:, :, c0:c0 + ncw], r=[], w=[wbuf[i % 3]])
                state["issued"] += 1

        def get_w(idx):
            prefetch()
            assert idx < state["issued"]
            return wbuf[idx % 3]

        def release(n):
            state["done"] = n
            prefetch()

        def PS():
            p = PB[state["ps"] % len(PB)]
            state["ps"] += 1
            return p

        def mm_feat(ps, wt, jj, tb):
            for k in range(8):
                O.mm(ps[:], wt[:, k, jj * 128:(jj + 1) * 128], uT[:, k, tb * 512:(tb + 1) * 512], k == 0, k == 7,
                     [wt, uT], [ps])

        def mm_tok(ps_ap, ps, wt, t, ncw):
            for k in range(8):
                O.mm(ps_ap, uT[:, k, t * 128:(t + 1) * 128], wt[:, k, :ncw], k == 0, k == 7, [wt, uT], [ps])

        wi = 0
        with ExitStack() as s2:
            def sb(name, shape, dt):
                return s2.enter_context(_sbt(nc, name, list(shape), dt))
            bT = [sb(f"p2bT{i}", [128, S_], F32) for i in range(2)]
            cst = [sb(f"p2cst{i}", [128, 512], F32) for i in range(2)]
            cx = [sb(f"p2cx{i}", [128, S_ + 2], BF16) for i in range(2)]
            yAo = [sb(f"p2yAo{i}", [128, S_], BF16) for i in range(2)]
            dgA = [sb(f"p2dgA{i}", [128, 3, 128], BF16) for i in range(2)]
            wA = sb("p2wA", [128, 8, 3], F32)
            S.dma("sp", wA[:], C.cawT[l], r=[], w=[wA])
            for i in range(2):
                O.memset("dve", cx[i][:, 0:1], 0.0, [(cx[i], "l")])
                O.memset("dve", cx[i][:, S_ + 1:S_ + 2], 0.0, [(cx[i], "r")])

            def convA(j):
                p = j % 2
                for tb in range(8):
                    ps = PS()
                    for k3 in range(3):
                        O.mm(ps[:], dgA[p][:, k3, :], cx[p][:, tb * 512 + k3:tb * 512 + k3 + 512], k3 == 0, k3 == 2,
                             [dgA[p], cx[p]], [ps])
                    O.tt("dve", yAo[p][:, tb * 512:(tb + 1) * 512], ps[:], bT[p][:, tb * 512:(tb + 1) * 512], ALU.mult,
                         [ps, bT[p]], [(yAo[p], tb)])
                S.dma("sp", C.yAT[:, :, j, :].rearrange("tb p t -> p tb t"), yAo[p][:].rearrange("p (tb t) -> p tb t", t=256),
                      r=[yAo[p]], w=[(C.yAT, j)])

            for g in range(2):
                release(wi)
                wt_b, wt_c, wt_x = get_w(wi), get_w(wi + 1), get_w(wi + 2)
                wi += 3
                for jj in range(4):
                    j = 4 * g + jj
                    p = j % 2
                    for k3 in range(3):
                        O.ts("dve", dgA[p][:, k3, :], C.ident[:], wA[:, j, k3:k3 + 1], None, ALU.mult, None,
                             [C.ident, wA], [(dgA[p], k3)])
                    for tb in range(8):
                        pb, pc, px = PS(), PS(), PS()
                        mm_feat(pb, wt_b, jj, tb)
                        mm_feat(pc, wt_c, jj, tb)
                        mm_feat(px, wt_x, jj, tb)
                        O.copy("act", bT[p][:, tb * 512:(tb + 1) * 512], pb[:], [pb], [(bT[p], tb)])
                        O.copy("act", cst[tb % 2][:], pc[:], [pc], [cst[tb % 2]])
                        O.tt("dve", cx[p][:, 1 + tb * 512:1 + (tb + 1) * 512], px[:], cst[tb % 2][:], ALU.mult,
                             [px, cst[tb % 2]], [(cx[p], tb)])
                        if tb == 1 and j > 0:
                            convA(j - 1)
            convA(7)
            S.barrier()
            S.flush()
        with ExitStack() as s2:
            def sb(name, shape, dt):
                return s2.enter_context(_sbt(nc, name, list(shape), dt))
            qs = [sb(f"p2qs{i}", [128, 8, 64], F32) for i in range(3)]
            tmp = [sb(f"p2tmp{i}", [128, 4, 8, 8], F32) for i in range(3)]
            qr = [sb(f"p2qr{i}", [128, 512], BF16) for i in range(3)]
            stg = [sb(f"p2stg{i}", [128, 4, S_], BF16) for i in range(2)]
            for ci in range(4):
                release(wi)
                wt = get_w(wi)
                wi += 1
                sg = stg[ci % 2]
                def qk_front(t):
                    b = t % 3
                    ps = PS()
                    mm_tok(ps[:], ps, wt, t, 512)
                    O.copy("act", qs[b][:].rearrange("p h d -> p (h d)"), ps[:], [ps], [qs[b]])
                    cs = C.cosT[:, t, :].unsqueeze(1).to_broadcast([128, 8, 8])
                    sn = C.sinT[:, t, :].unsqueeze(1).to_broadcast([128, 8, 8])
                    t1 = qs[b][:, :, 0:8]
                    t2 = qs[b][:, :, 8:16]
                    tm = tmp[b]
                    O.tt("dve", tm[:, 0], t1, cs, ALU.mult, [qs[b], C.cosT], [(tm, 0)])
                    O.tt("dve", tm[:, 1], t2, sn, ALU.mult, [qs[b], C.sinT], [(tm, 1)])
                    O.tt("dve", tm[:, 2], t2, cs, ALU.mult, [qs[b], C.cosT], [(tm, 2)])
                    O.tt("dve", tm[:, 3], t1, sn, ALU.mult, [qs[b], C.sinT], [(tm, 3)])
                    O.tt("dve", t1, tm[:, 0], tm[:, 1], ALU.subtract, [(tm, 0), (tm, 1)], [(qs[b], "a")])
                    O.tt("dve", t2, tm[:, 2], tm[:, 3], ALU.add, [(tm, 2), (tm, 3)], [(qs[b], "b")])
                    O.copy("act", qr[b][:], qs[b][:].rearrange("p h d -> p (h d)"), [qs[b]], [qr[b]])

                def qk_back(t):
                    b = t % 3
                    pt = PT[t % 2]
                    for jj in range(4):
                        O.tr(pt[:, jj, :], qr[b][:, jj * 128:(jj + 1) * 128], C.ident[:], [qr[b], C.ident], [pt])
                    O.copy("act", sg[:, :, t * 128:(t + 1) * 128], pt[:], [pt], [(sg, t)])

                for t in range(NT_ + 2):
                    if t < NT_:
                        qk_front(t)
                    if 0 <= t - 2 < NT_:
                        qk_back(t - 2)
                dst = C.qT if ci < 2 else C.kT
                for jj in range(4):
                    r0 = (ci % 2) * 512 + jj * 128
                    S.dma("sp", dst[r0:r0 + 128, :], sg[:, jj, :], r=[sg], w=[(dst, r0)])
            S.barrier()
            S.flush()
        with ExitStack() as s2:
            def sb(name, shape, dt):
                return s2.enter_context(_sbt(nc, name, list(shape), dt))
            vst = [sb(f"p2vst{i}", [128, 512], BF16) for i in range(4)]
            n = 0
            for ci in range(6):
                release(wi)
                wt = get_w(wi)
                wi += 1
                for t in range(NT_):
                    ps = PS()
                    mm_tok(ps[:], ps, wt, t, 512)
                    vs = vst[n % 4]
                    n += 1
                    if ci < 2:
                        O.copy("act", vs[:], ps[:], [ps], [vs])
                        S.dma("sp", C.v_d[t * 128:(t + 1) * 128, ci * 512:(ci + 1) * 512], vs[:], r=[vs], w=[(C.v_d, (t, ci))])
                    else:
                        O.act(vs[:], ps[:], AF.Silu, [ps], [vs])
                        c2 = ci - 2
                        S.dma("sp", C.sz_d[t * 128:(t + 1) * 128, c2 * 512:(c2 + 1) * 512], vs[:], r=[vs], w=[(C.sz_d, (t, c2))])
            S.barrier()
            S.flush()
        with ExitStack() as s2:
            def sb(name, shape, dt):
                return s2.enter_context(_sbt(nc, name, list(shape), dt))
            xin = sb("p2xin", [128, 4, S_ + 4], BF16)
            diag = sb("p2diag", [128, 4, 5, 128], BF16)
            stok = [sb(f"p2stok{i}", [128, 8, 512], BF16) for i in range(2)]
            sfeat = [sb(f"p2sfeat{i}", [128, S_], BF16) for i in range(2)]
            wS = sb("p2wS", [128, 24, 5], F32)
            bS = sb("p2bS", [128, 24], F32)
            brf = sb("p2brf", [1, 3072], F32)
            brow = sb("p2brow", [1, 3072], BF16)
            ones = sb("p2ones", [1, 128], BF16)
            S.dma("sp", wS[:], C.scwT[l], r=[], w=[wS])
            S.dma("sp", bS[:], C.scbT[l], r=[], w=[bS])
            S.dma("sp", brf[:], C.scbR[l], r=[], w=[brf])
            O.copy("dve", brow[:], brf[:], [brf], [brow])
            O.memset("dve", ones[:], 1.0, [ones])
            O.memset("dve", xin[:, :, 0:2], 0.0, [(xin, "l")])
            O.memset("dve", xin[:, :, S_ + 2:S_ + 4], 0.0, [(xin, "r")])
            nf = 0
            for ci in range(6):
                release(wi)
                wt = get_w(wi)
                wi += 1
                for jj in range(4):
                    J = 4 * ci + jj
                    for tb in range(8):
                        ps = PS()
                        mm_feat(ps, wt, jj, tb)
                        O.copy("act" if tb % 2 else "dve", xin[:, jj, 2 + tb * 512:2 + (tb + 1) * 512], ps[:], [ps], [(xin, (jj, tb))])
                    for k5 in range(5):
                        O.ts("dve", diag[:, jj, k5, :], C.ident[:], wS[:, J, k5:k5 + 1], None, ALU.mult, None,
                             [C.ident, wS], [(diag, (jj, k5))])
                if ci <= 4:
                    for t in range(NT_):
                        ps = PS()
                        for jj in range(4):
                            J = 4 * ci + jj
                            o_ap = ps[:, jj * 128:(jj + 1) * 128]
                            for k5 in range(5):
                                O.mm(o_ap, xin[:, jj, t * 128 + k5:t * 128 + k5 + 128], diag[:, jj, k5, :], k5 == 0, False,
                                     [xin, diag], [ps])
                            O.mm(o_ap, ones[0:1, :], brow[0:1, J * 128:(J + 1) * 128], False, True, [ones, brow], [ps])
                        sk = stok[(t // 8) % 2]
                        O.act(sk[:, t % 8, :], ps[:], AF.Silu, [ps], [(sk, t % 8)])
                        if t % 8 == 7:
                            t0 = t - 7
                            if ci < 4:
                                dst = C.xs_d[t0 * 128:(t0 + 8) * 128, ci * 512:(ci + 1) * 512]
                                key = (C.xs_d, (t0, ci))
                            else:
                                dst = C.Bt_d[t0 * 128:(t0 + 8) * 128, :]
                                key = (C.Bt_d, t0)
                            S.dma("sp", dst.rearrange("(t p) c -> p t c", p=128), sk[:], r=[sk], w=[key])
                if ci >= 4:
                    for jj in range(4):
                        J = 4 * ci + jj
                        sf = sfeat[nf % 2]
                        nf += 1
                        for tb in range(8):
                            ps = PS()
                            for k5 in range(5):
                                O.mm(ps[:], diag[:, jj, k5, :], xin[:, jj, tb * 512 + k5:tb * 512 + k5 + 512], k5 == 0, k5 == 4,
                                     [xin, diag], [ps])
                            O.act(sf[:, tb * 512:(tb + 1) * 512], ps[:], AF.Silu, [ps, bS], [(sf, tb)], bias=bS[:, J:J + 1])
                        dst = C.BT_d if ci == 4 else C.CT_d
                        S.dma("sp", dst[jj * 128:(jj + 1) * 128, :], sf[:], r=[sf], w=[(dst, jj)])
            S.barrier()
            S.flush()
        with ExitStack() as s2:
            def sb(name, shape, dt):
                return s2.enter_context(_sbt(nc, name, list(shape), dt))
            dtbb = sb("p2dtbb", [128, 64], F32)
            abc = sb("p2abc", [128, 64], F32)
            tdt = [sb(f"p2tdt{i}", [128, 64], F32) for i in range(2)]
            S.dma("sp", dtbb[:], C.dtb[l].partition_broadcast(128), r=[], w=[dtbb])
            S.dma("sp", abc[:], C.alog[l].partition_broadcast(128), r=[], w=[abc])
            O.act(abc[:], abc[:], AF.Exp, [abc], [abc])
            O.ts("dve", abc[:], abc[:], -1.0, None, ALU.mult, None, [abc], [abc])
            release(wi)
            wt = get_w(wi)
            wi += 1
            for t in range(NT_):
                ps = PS()
                mm_tok(ps[:, 0:64], ps, wt, t, 64)
                td = tdt[t % 2]
                O.tt("dve", td[:], ps[:, 0:64], dtbb[:], ALU.add, [ps, dtbb], [td])
                O.act(td[:], td[:], AF.Exp, [td], [td])
                O.act(C.dt_all[:, t, :], td[:], AF.Ln, [td], [(C.dt_all, t)], bias=1.0)
                O.tt("dve", C.la_all[:, t, :], C.dt_all[:, t, :], abc[:], ALU.mult, [(C.dt_all, t), abc], [(C.la_all, t)])
            if "dt_dbg" in C.dbg:
                S.dma("sp", C.dt_dbg, C.dt_all[:], r=[C.dt_all], w=[C.dt_dbg])
            S.barrier()
            S.flush()
        with ExitStack() as s2:
            def sb(name, shape, dt):
                return s2.enter_context(_sbt(nc, name, list(shape), dt))
            sfeat = [sb(f"p2gfeat{i}", [128, S_], BF16) for i in range(2)]
            nf = 0
            for ci in range(6):
                release(wi)
                wt = get_w(wi)
                wi += 1
                for jj in range(4):
                    G = 4 * ci + jj
                    sf = sfeat[nf % 2]
                    nf += 1
                    for tb in range(8):
                        ps = PS()
                        mm_feat(ps, wt, jj, tb)
                        O.act(sf[:, tb * 512:(tb + 1) * 512], ps[:], AF.Sigmoid, [ps], [(sf, tb)])
                    S.dma("sp", C.gT[:, :, G, :].rearrange("tb p t -> p tb t"), sf[:].rearrange("p (tb t) -> p tb t", t=256),
                          r=[sf], w=[(C.gT, G)])
            S.barrier()
            S.flush()
        assert wi == len(seq)


def phase3(C, l):
    nc, S, O = C.nc, C.S, C.O
    C.cast_w(C.w_1[l], C.wb_1[l], 128)
    pats = [(1, 33), (4, 9), (16, 3)]
    with ExitStack() as st:
        def sb(name, shape, dt):
            return st.enter_context(_sbt(nc, name, list(shape), dt))
        qTh = [sb(f"p3q{i}", [64, S_], BF16) for i in range(2)]
        kTp = [sb(f"p3k{i}", [64, S_ + 2048], BF16) for i in range(2)]
        vP = [[sb(f"p3v{pi}_{b}", [128, nb, dil, 65], BF16) for b in range(2)] for pi, (dil, nb) in enumerate(pats)]
        mask = sb("p3mask", [128, 2, 128], BF16)
        PTs = [sb(f"p3pt{i}", [128, 2, 2, 128], BF16) for i in range(2)]
        PTm = [sb(f"p3pm{i}", [128, 2, 2, 128], BF16) for i in range(2)]
        ost = [sb(f"p3ost{i}", [128, 32, 65], F32) for i in range(2)]
        PSs = [st.enter_context(_pst(nc, f"p3ps{i}", [128, 512], F32)) for i in range(3)]
        PSo = [st.enter_context(_pst(nc, f"p3po{i}", [128, 2, 65], F32)) for i in range(3)]
        O.copy("dve", mask[:, 0, :], C.tri[:, 1, :], [C.tri], [(mask, 0)])
        O.copy("dve", mask[:, 1, :], C.tri[:, 0, :], [C.tri], [(mask, 1)])
        for b in range(2):
            O.memset("dve", kTp[b][:, 0:1024], 0.0, [(kTp[b], "l")])
            O.memset("dve", kTp[b][:, 1024 + S_:2048 + S_], 0.0, [(kTp[b], "r")])
            for pi, (dil, nb) in enumerate(pats):
                v = vP[pi][b]
                O.memset("pool", v[:], 0.0, [v])
                O.memset("pool", v[:, :, :, 64:65], 1.0, [v])
                O.memset("pool", v[0:64, 0, :, 64:65], 0.0, [v])
                O.memset("pool", v[64:128, nb - 1, :, 64:65], 0.0, [v])
        def loads(h):
            b = h % 2
            S.dma("sp", qTh[b][:], C.qT[h * 64:(h + 1) * 64, :], r=[C.qT], w=[qTh[b]])
            S.dma("sp", kTp[b][:, 1024:1024 + S_], C.kT[h * 64:(h + 1) * 64, :], r=[C.kT], w=[(kTp[b], "m")])
            vsrc = C.v_d[:, h * 64:(h + 1) * 64]
            for pi, (dil, nb) in enumerate(pats):
                v = vP[pi][b]
                nin = nb - 2
                if dil == 1:
                    for j0 in range(0, nin, 8):
                        j1 = min(nin, j0 + 8)
                        src = vsrc[64 + j0 * 128:64 + j1 * 128, :]
                        S.dma("sp", v[:, 1 + j0:1 + j1, 0, 0:64], src.rearrange("(j i) d -> i j d", i=128),
                              r=[C.v_d], w=[(v, ("in", j0))])
                else:
                    for j0 in range(nin):
                        src = vsrc[64 * dil + j0 * 128 * dil:64 * dil + (j0 + 1) * 128 * dil, :]
                        S.dma("sp", v[:, 1 + j0, :, 0:64], src.rearrange("(i r) d -> i r d", r=dil),
                              r=[C.v_d], w=[(v, ("in", j0))])
                S.dma("sp", v[64:128, 0, :, 0:64], vsrc[0:64 * dil, :].rearrange("(i r) d -> i r d", r=dil),
                      r=[C.v_d], w=[(v, "first")])
                S.dma("sp", v[0:64, nb - 1, :, 0:64], vsrc[S_ - 64 * dil:S_, :].rearrange("(i r) d -> i r d", r=dil),
                      r=[C.v_d], w=[(v, "last")])

        def store(h, pi, osb):
            dil, nb = pats[pi]
            nqb = nb - 1
            osv = osb[:].rearrange("p (q r) e -> p q r e", r=dil)
            if dil == 1:
                S.dma("sp", C.o_d[pi][:, h, :].rearrange("(q i) e -> i q e", i=128), osb[:], r=[osb],
                      w=[(C.o_d[pi], h)])
            else:
                for q in range(nqb):
                    S.dma("sp", C.o_d[pi][q * 128 * dil:(q + 1) * 128 * dil, h, :].rearrange("(i r) e -> i r e", r=dil),
                          osv[:, q, :, :], r=[osb], w=[(C.o_d[pi], (h, q))])

        pairs = []
        n_ost = 0
        for h in range(16):
            first = True
            for pi, (dil, nb) in enumerate(pats):
                osb = ost[n_ost % 2]
                n_ost += 1
                nqb = nb - 1
                lst = [(r, qp) for r in range(dil) for qp in range(nqb // 2)]
                for idx, (r, qp) in enumerate(lst):
                    pairs.append(dict(h=h, pi=pi, dil=dil, r=r, qp=qp, osb=osb, pre=None,
                                      post=(h, pi, osb) if idx == len(lst) - 1 else None))
        npairs_h = len(pairs) // 16
        for h in range(16):
            if h == 0:
                pairs[0]["pre"] = 0
            if h + 1 < 16:
                pairs[h * npairs_h + 4]["pre"] = h + 1
        NPS = 3

        def stA(i):
            p = pairs[i]
            b = p["h"] % 2
            dil, r, qp = p["dil"], p["r"], p["qp"]
            ps = PSs[i % NPS]
            psv = ps[:].rearrange("p (a b c) -> p a b c", a=2, b=2)
            qb = 2 * qp

            def ksl(j):
                k0 = 1024 - 64 * dil + 128 * dil * j + r
                return kTp[b][:, k0:k0 + 127 * dil + 1:dil]

            def qsl(j, n):
                q0 = 128 * dil * j + r
                return qTh[b][:, q0:q0 + (128 * n - 1) * dil + 1:dil]

            O.mm(ps[:, 0:128], ksl(qb), qsl(qb, 1), True, True, [kTp[b], qTh[b]], [ps])
            O.mm(ps[:, 128:384], ksl(qb + 1), qsl(qb, 2), True, True, [kTp[b], qTh[b]], [ps])
            O.mm(ps[:, 384:512], ksl(qb + 2), qsl(qb + 1, 1), True, True, [kTp[b], qTh[b]], [ps])

        def stB(i):
            ps = PSs[i % NPS]
            pt = PTs[i % 2]
            pm = PTm[i % 2]
            O.act(pt[:].rearrange("p a b c -> p (a b c)"), ps[:], AF.Exp, [ps], [pt], scale=0.125)
            O.tt("dve", pm[:], pt[:], mask[:].unsqueeze(1).to_broadcast([128, 2, 2, 128]), ALU.mult, [pt, mask], [pm])

        def stC(i):
            p = pairs[i]
            b = p["h"] % 2
            dil, r, qp, pi = p["dil"], p["r"], p["qp"], p["pi"]
            v = vP[pi][b]
            pm = PTm[i % 2]
            po = PSo[i % 3]
            osb = p["osb"]
            osv = osb[:].rearrange("p (q r) e -> p q r e", r=dil)
            for qi in range(2):
                qb = 2 * qp + qi
                for ab in range(2):
                    O.mm(po[:, qi, :], pm[:, qi, ab, :], v[:, qb + ab, r, :], ab == 0, ab == 1, [pm, v], [po])
            O.copy("act" if i % 2 else "dve", osv[:, 2 * qp:2 * qp + 2, r, :], po[:], [po], [(osb, (r, qp))])
            if p["post"] is not None:
                store(*p["post"])

        n = len(pairs)
        for i in range(n + 2):
            if i < n:
                if pairs[i]["pre"] is not None:
                    loads(pairs[i]["pre"])
                stA(i)
            if 0 <= i - 1 < n:
                stB(i - 1)
            if 0 <= i - 2 < n:
                stC(i - 2)
        S.barrier()
        S.flush()


def phase4(C, l):
    nc, S, O = C.nc, C.S, C.O
    tri, trif = C.tri, C.trif
    with ExitStack() as st:
        def sb(name, shape, dt):
            return st.enter_context(_sbt(nc, name, list(shape), dt))
        H = [[sb(f"p4H{d}_{i}", [128, 2048], F32) for i in range(2)] for d in range(2)]
        tmpH = [sb(f"p4tmpH{d}", [128, 2048], F32) for d in range(2)]
        hbf = [[sb(f"p4hbf{d}_{i}", [128, 2048], BF16) for i in range(2)] for d in range(2)]
        xs_t = [sb(f"p4xs{i}", [128, 2048], BF16) for i in range(4)]
        Bt_t = [sb(f"p4Bt{i}", [128, 512], BF16) for i in range(4)]
        ew = [sb(f"p4ew{i}", [128, 2, 32], F32) for i in range(4)]
        dtw = [sb(f"p4dtw{i}", [128, 32], F32) for i in range(2)]
        xw = [sb(f"p4xw{i}", [128, 2048], BF16) for i in range(2)]
        st_sb = [sb(f"p4st{i}", [128, 2048], F32) for i in range(3)]
        PSw = [st.enter_context(_pst(nc, f"p4psw{i}", [128, 2, 32], F32)) for i in range(2)]
        PSs = [st.enter_context(_pst(nc, f"p4pss{i}", [128, 512], F32)) for i in range(4)]
        for d in range(2):
            O.memset("dve", H[d][0][:], 0.0, [H[d][0]])
            O.memset("pool", hbf[d][0][:], 0.0, [hbf[d][0]])

        def stA1(n):
            i, d = divmod(n, 2)
            c = i if d == 0 else NT_ - 1 - i
            xt, bt, e_, dw, pw = xs_t[n % 4], Bt_t[n % 4], ew[n % 4], dtw[n % 2], PSw[n % 2]
            S.dma("sp", xt[:], C.xs_d[c * 128:(c + 1) * 128, :], r=[C.xs_d], w=[xt])
            S.dma("sp", bt[:], C.Bt_d[c * 128:(c + 1) * 128, :], r=[C.Bt_d], w=[bt])
            la_c = C.la_all[:, c, d * 32:(d + 1) * 32]
            dt_c = C.dt_all[:, c, d * 32:(d + 1) * 32]
            O.mm(pw[:, 0, :], trif[:, 1 if d == 0 else 0, :], la_c, True, True, [trif, C.la_all], [pw])
            O.mm(pw[:, 1, :], trif[:, 2, :], la_c, True, True, [trif, C.la_all], [pw])
            O.act(e_[:], pw[:], AF.Exp, [pw], [e_])
            O.tt("dve", dw[:], dt_c, e_[:, 0, :], ALU.mult, [C.dt_all, e_], [dw])

        def stA2(n):
            xt, dw, xw_ = xs_t[n % 4], dtw[n % 2], xw[n % 2]
            O.tt("pool", xw_[:].rearrange("p (h d) -> p h d", d=64), xt[:].rearrange("p (h d) -> p h d", d=64),
                 dw[:].unsqueeze(2).to_broadcast([128, 32, 64]), ALU.mult, [xt, dw], [xw_])

        def stA3(n):
            bt, xw_ = Bt_t[n % 4], xw[n % 2]
            ss_ = st_sb[n % 3]
            for g in range(4):
                O.mm(PSs[g][:], bt[:, g * 128:(g + 1) * 128], xw_[:, g * 512:(g + 1) * 512], True, True, [bt, xw_], [PSs[g]])
                O.copy("act", ss_[:, g * 512:(g + 1) * 512], PSs[g][:], [PSs[g]], [(ss_, g)])

        def stB(n):
            i, d = divmod(n, 2)
            c = i if d == 0 else NT_ - 1 - i
            eng = "dve"
            e_ = ew[n % 4]
            hb = hbf[d][i % 2]
            Hs, Hd = H[d][i % 2], H[d][(i + 1) % 2]
            S.dma("sp", C.hin_d[d][c], hb[:], r=[hb], w=[(C.hin_d[d], c)])
            O.tt(eng, tmpH[d][:].rearrange("p (h d) -> p h d", d=64), Hs[:].rearrange("p (h d) -> p h d", d=64),
                 e_[:, 1, :].unsqueeze(2).to_broadcast([128, 32, 64]), ALU.mult, [Hs, e_], [tmpH[d]])
            O.tt(eng, Hd[:], tmpH[d][:], st_sb[n % 3][:], ALU.add, [tmpH[d], st_sb[n % 3]], [Hd])

        def stC(n):
            i, d = divmod(n, 2)
            O.copy("act", hbf[d][(i + 1) % 2][:], H[d][(i + 1) % 2][:], [H[d][(i + 1) % 2]], [hbf[d][(i + 1) % 2]])

        NN = 2 * NT_
        order1 = [(stB, 3), (stC, 4), (stA3, 2), (stA2, 1), (stA1, 0)]
        for it in range(NN + 4):
            for fn, skew in order1:
                n = it - skew
                if 0 <= n < NN:
                    fn(n)
        S.barrier()
        S.flush()
    with ExitStack() as st:
        def sb(name, shape, dt):
            return st.enter_context(_sbt(nc, name, list(shape), dt))
        NB = 3
        xs_t = [sb(f"p4xs{i}", [128, 2048], BF16) for i in range(NB)]
        BT_t = [sb(f"p4BT{i}", [128, 4, 128], BF16) for i in range(NB)]
        CT_t = [sb(f"p4CT{i}", [128, 4, 128], BF16) for i in range(NB)]
        sz_t = [sb(f"p4sz{i}", [128, 2048], BF16) for i in range(NB)]
        hh_t = [[sb(f"p4hh{d}_{i}", [128, 2048], BF16) for i in range(NB)] for d in range(2)]
        cbm = [[sb(f"p4cbm{d}_{i}", [128, 4, 128], BF16) for i in range(2)] for d in range(2)]
        ecum = [[sb(f"p4ecum{d}_{i}", [128, 32], F32) for i in range(2)] for d in range(2)]
        rseg = [[sb(f"p4rseg{d}_{i}", [128, 8, 128], BF16) for i in range(2)] for d in range(2)]
        xd = [[sb(f"p4xd{d}_{i}", [128, 512], BF16) for i in range(4)] for d in range(2)]
        eseg = [sb(f"p4eseg{i}", [128, 4, 128], BF16) for i in range(8)]
        MT = [sb(f"p4MT{i}", [128, 4, 128], BF16) for i in range(8)]
        tt_ = [[sb(f"p4t{d}_{i}", [128, 512], BF16) for i in range(2)] for d in range(2)]
        xsD = [sb(f"p4xsD{i}", [128, 512], BF16) for i in range(2)]
        yy = [sb(f"p4y{i}", [128, 512], F32) for i in range(3)]
        ynf = [sb(f"p4ynf{i}", [128, 512], BF16) for i in range(2)]
        ssg = [sb(f"p4ssg{i}", [128, 1], F32) for i in range(2)]
        sqj = sb("p4sqj", [128, 512], BF16)
        ynT_st = [sb(f"p4ynT{i}", [128, 16, 128], BF16) for i in range(3)]
        nwb = sb("p4nwb", [128, 2048], F32)
        Dbc = sb("p4Dbc", [128, 32], F32)
        PScb = st.enter_context(_pst(nc, "p4pscb", [128, 512], F32))
        PSy = [st.enter_context(_pst(nc, f"p4psy{i}", [128, 512], F32)) for i in range(2)]
        PSseg = [st.enter_context(_pst(nc, f"p4psseg{i}", [128, 512], F32)) for i in range(2)]
        PSo = [st.enter_context(_pst(nc, f"p4pso{i}", [128, 512], F32)) for i in range(2)]
        PSt = st.enter_context(_pst(nc, "p4pst", [128, 512], F32))
        S.dma("sp", nwb[:], C.snw[l].partition_broadcast(128), r=[], w=[nwb])
        S.dma("sp", Dbc[:], C.sdd[l].partition_broadcast(128), r=[], w=[Dbc])

        def loads(c):
            b = c % NB
            S.dma("sp", xs_t[b][:], C.xs_d[c * 128:(c + 1) * 128, :], r=[C.xs_d], w=[xs_t[b]])
            S.dma("sp", BT_t[b][:], C.BT_d[:, c * 128:(c + 1) * 128].rearrange("(g n) t -> n g t", n=128), r=[C.BT_d], w=[BT_t[b]])
            S.dma("sp", CT_t[b][:], C.CT_d[:, c * 128:(c + 1) * 128].rearrange("(g n) t -> n g t", n=128), r=[C.CT_d], w=[CT_t[b]])
            S.dma("sp", sz_t[b][:], C.sz_d[c * 128:(c + 1) * 128, :], r=[C.sz_d], w=[sz_t[b]])
            for d in range(2):
                S.dma("sp", hh_t[d][b][:], C.hin_d[d][c], r=[C.hin_d[d]], w=[hh_t[d][b]])

        units = [(d, q) for d in range(2) for q in range(2)]

        def s1(k):
            c, g = divmod(k, 4)
            b, cp = c % NB, c % 2
            xt, BT, CT = xs_t[b], BT_t[b], CT_t[b]
            if g == 0:
                pcb = PScb[:].rearrange("p (g t) -> p g t", g=4)
                for g2 in range(4):
                    O.mm(pcb[:, g2, :], BT[:, g2, :], CT[:, g2, :], True, True, [BT, CT], [PScb])
                for d in range(2):
                    O.tt("dve", cbm[d][cp][:], pcb, tri[:, d, :].unsqueeze(1).to_broadcast([128, 4, 128]), ALU.mult,
                         [PScb, tri], [cbm[d][cp]])
                pse = PScb[:, 0:64].rearrange("p (d h) -> p d h", d=2)
                for d in range(2):
                    la_c = C.la_all[:, c, d * 32:(d + 1) * 32]
                    O.mm(pse[:, d, :], trif[:, 3 + d, :], la_c, True, True, [trif, C.la_all], [PScb])
                for d in range(2):
                    O.act(ecum[d][cp][:], pse[:, d, :], AF.Exp, [PScb], [ecum[d][cp]])
            for d in range(2):
                la_g = C.la_all[:, c, d * 32 + g * 8:d * 32 + (g + 1) * 8]
                dt_g = C.dt_all[:, c, d * 32 + g * 8:d * 32 + (g + 1) * 8]
                O.tt("dve", rseg[d][k % 2][:], la_g.unsqueeze(2).to_broadcast([128, 8, 128]),
                     tri[:, d, :].unsqueeze(1).to_broadcast([128, 8, 128]), ALU.mult, [C.la_all, tri], [rseg[d][k % 2]])
                O.tt("pool", xd[d][k % 4][:].rearrange("p (h e) -> p h e", e=64),
                     xt[:, g * 512:(g + 1) * 512].rearrange("p (h e) -> p h e", e=64),
                     dt_g.unsqueeze(2).to_broadcast([128, 8, 64]), ALU.mult, [xt, C.dt_all], [xd[d][k % 4]])

        def s2(k):
            for u, (d, q) in enumerate(units):
                n = k * 4 + u
                pss = PSseg[n % 2]
                es = eseg[n % 8]
                O.mm(pss[:], tri[:, 3 - d, :], rseg[d][k % 2][:, q * 4:(q + 1) * 4, :].rearrange("p h t -> p (h t)"), True, True,
                     [tri, rseg[d][k % 2]], [pss])
                O.act(es[:].rearrange("p h t -> p (h t)"), pss[:], AF.Exp, [pss], [es])

        def s3(k):
            c, g = divmod(k, 4)
            cp = c % 2
            for u, (d, q) in enumerate(units):
                n = k * 4 + u
                O.tt("dve" if u % 2 else "pool", MT[n % 8][:], eseg[n % 8][:],
                     cbm[d][cp][:, g, :].unsqueeze(1).to_broadcast([128, 4, 128]), ALU.mult, [eseg[n % 8], cbm[d][cp]], [MT[n % 8]])

        def s4(k):
            for u, (d, q) in enumerate(units):
                n = k * 4 + u
                mt = MT[n % 8]
                for hh in range(4):
                    h8 = q * 4 + hh
                    O.mm(PSy[k % 2][:, h8 * 64:(h8 + 1) * 64], mt[:, hh, :], xd[d][k % 4][:, h8 * 64:(h8 + 1) * 64],
                         u == 0 and hh == 0, False, [mt, xd[d][k % 4]], [PSy[k % 2]])

        def s5(k):
            c, g = divmod(k, 4)
            b, cp, kp = c % NB, c % 2, k % 2
            xt, CT, sz = xs_t[b], CT_t[b], sz_t[b]
            for d in range(2):
                hd = hh_t[d][b]
                O.mm(PSo[d][:], CT[:, g, :], hd[:, g * 512:(g + 1) * 512], True, True, [CT, hd], [PSo[d]])
            O.tt("pool", xsD[kp][:].rearrange("p (h e) -> p h e", e=64),
                 xt[:, g * 512:(g + 1) * 512].rearrange("p (h e) -> p h e", e=64),
                 Dbc[:, g * 8:(g + 1) * 8].unsqueeze(2).to_broadcast([128, 8, 64]), ALU.mult, [xt, Dbc], [xsD[kp]])
            for d in range(2):
                O.tt("dve", tt_[d][kp][:].rearrange("p (h e) -> p h e", e=64), PSo[d][:].rearrange("p (h e) -> p h e", e=64),
                     ecum[d][cp][:, g * 8:(g + 1) * 8].unsqueeze(2).to_broadcast([128, 8, 64]), ALU.mult,
                     [PSo[d], ecum[d][cp]], [tt_[d][kp]])

        def s5b(k):
            c, g = divmod(k, 4)
            b, cp, kp = c % NB, c % 2, k % 2
            sz = sz_t[b]
            O.mm(PSy[kp][:], C.ident[:], xsD[kp][:], False, False, [C.ident, xsD[kp]], [PSy[kp]])
            O.mm(PSy[kp][:], C.ident[:], tt_[0][kp][:], False, False, [C.ident, tt_[0][kp]], [PSy[kp]])
            O.mm(PSy[kp][:], C.ident[:], tt_[1][kp][:], False, True, [C.ident, tt_[1][kp]], [PSy[kp]])
            y_ = yy[k % 3]
            O.tt("dve", y_[:], PSy[kp][:], sz[:, g * 512:(g + 1) * 512], ALU.mult, [PSy[kp], sz], [y_])

        def s6(k):
            c, g = divmod(k, 4)
            y_ = yy[k % 3]
            s_ = ssg[k % 2]
            O.act(sqj[:], y_[:], AF.Square, [y_], [sqj, s_], accum_out=s_[:])
            O.act(s_[:], s_[:], AF.Sqrt, [s_], [s_], bias=1e-6, scale=1.0 / 512)
            O.recip(s_[:], s_[:], [s_], [s_])
            O.stt("dve", ynf[k % 2][:], y_[:], s_[:], nwb[:, g * 512:(g + 1) * 512], ALU.mult, ALU.mult, [y_, s_, nwb], [ynf[k % 2]])

        def s7(k):
            c, g = divmod(k, 4)
            ptb = PSt[:, 0:256].bitcast(BF16).rearrange("p (a t) -> p a t", a=4)
            ys = ynT_st[c % 3]
            yn_ = ynf[k % 2]
            for a in range(4):
                O.tr(ptb[:, a, :], yn_[:, a * 128:(a + 1) * 128], C.ident[:], [yn_, C.ident], [PSt])
            O.copy("act", ys[:, g * 4:(g + 1) * 4, :], ptb, [PSt], [(ys, g)])
            if g == 3:
                S.dma("sp", C.ynT_d[c // 2, :, :, (c % 2) * 128:(c % 2 + 1) * 128], ys[:], r=[ys],
                      w=[(C.ynT_d, c)])

        order = [(s5, 4), (s6, 5), (s7, 6), (s2, 1), (s1, 0), (s3, 2), (s4, 3), (s5b, 4)]
        NK = NT_ * 4
        for c in range(NB):
            loads(c)
        for it in range(NK + 6):
            if it >= 8 and it % 4 == 0:
                cn = (it - 8) // 4 + NB
                if cn < NT_:
                    loads(cn)
            for fn, skew in order:
                k = it - skew
                if 0 <= k < NK:
                    fn(k)
        S.barrier()
        S.flush()


def phase5(C, l, x_src, x_dst):
    nc, S, O = C.nc, C.S, C.O
    with ExitStack() as st:
        def sb(name, shape, dt):
            return st.enter_context(_sbt(nc, name, list(shape), dt))
        TB = 256
        NTB = S_ // TB
        wa = sb("p5wa", [128, 8, D_], BF16)
        wb_ = sb("p5wb", [128, 8, D_], BF16)
        wc = sb("p5wc", [128, 16, D_], BF16)
        wo = sb("p5wo", [128, 8, D_], BF16)
        yA_b = [sb(f"p5yA{i}", [128, 8, TB], BF16) for i in range(2)]
        yn_b = [sb(f"p5yn{i}", [128, 16, TB], BF16) for i in range(2)]
        oT_b = [sb(f"p5oT{i}", [128, 8, TB], BF16) for i in range(2)]
        g_b = [sb(f"p5g{i}", [128, 24, TB], BF16) for i in range(2)]
        mT = [sb(f"p5mT{i}", [128, 8, TB], BF16) for i in range(1)] * 2
        ot = [[sb(f"p5ot{p}_{i}", [128, 16, 65], F32) for p in range(3)] for i in range(2)]
        rden = [sb(f"p5rden{i}", [128, 16], F32) for i in range(2)]
        ob = [sb(f"p5ob{i}", [128, 16, 64], BF16) for i in range(1)] * 2
        xt = [sb(f"p5xt{i}", [128, D_], F32) for i in range(1)] * 2
        xo = [sb(f"p5xo{i}", [128, D_], F32) for i in range(1)] * 2
        t1 = [sb(f"p5t1_{i}", [128, TB], F32) for i in range(2)]
        t2 = [sb(f"p5t2_{i}", [128, TB], F32) for i in range(2)]
        PB = [st.enter_context(_pst(nc, f"p5ps{i}", [128, 512], F32)) for i in range(6)]
        PT = [st.enter_context(_pst(nc, f"p5pt{i}", [128, 8, 128], BF16)) for i in range(2)]
        S.dma("pool", wa[:], C.w_a[l].rearrange("(k p) n -> p k n", p=128), r=[], w=[wa])
        S.dma("pool", wb_[:], C.w_b[l].rearrange("(k p) n -> p k n", p=128), r=[], w=[wb_])
        S.dma("pool", wc[:], C.w_c[l].rearrange("(k p) n -> p k n", p=128), r=[], w=[wc])
        S.dma("pool", wo[:], C.w_o[l].rearrange("(k p) n -> p k n", p=128), r=[], w=[wo])
        cnt = {"ps": 0, "o": 0}

        def PS():
            p = PB[cnt["ps"] % 6]
            cnt["ps"] += 1
            return p

        def loads(tb):
            b = tb % 2
            tsl = slice(tb * TB, (tb + 1) * TB)
            S.dma("sp", yA_b[b][:], C.yAT[tb], r=[C.yAT], w=[yA_b[b]])
            S.dma("sp", yn_b[b][:], C.ynT_d[tb], r=[C.ynT_d], w=[yn_b[b]])
            S.dma("sp", g_b[b][:], C.gT[tb], r=[C.gT], w=[g_b[b]])

        def oloads(tb):
            for tt in range(TB // 128):
                t = tb * (TB // 128) + tt
                for p in range(3):
                    S.dma("sp", ot[tt][p][:], C.o_d[p][t * 128:(t + 1) * 128], r=[C.o_d[p]], w=[ot[tt][p]])

        def combine(tb):
            b = tb % 2
            for tt in range(TB // 128):
                n = cnt["o"]
                cnt["o"] += 1
                o3 = ot[tt]
                O.tt("pool", o3[0][:], o3[0][:], o3[1][:], ALU.add, [o3[0], o3[1]], [o3[0]])
                O.tt("pool", o3[0][:], o3[0][:], o3[2][:], ALU.add, [o3[0], o3[2]], [o3[0]])
                O.recip(rden[n % 2][:].unsqueeze(2), o3[0][:, :, 64:65], [o3[0]], [rden[n % 2]])
                O.tt("pool", ob[n % 2][:], o3[0][:, :, 0:64], rden[n % 2][:].unsqueeze(2).to_broadcast([128, 16, 64]), ALU.mult,
                     [o3[0], rden[n % 2]], [ob[n % 2]])
                obf = ob[n % 2][:].rearrange("p h d -> p (h d)")
                for k in range(8):
                    O.tr(PT[n % 2][:, k, :], obf[:, k * 128:(k + 1) * 128], C.ident[:], [ob[n % 2], C.ident], [PT[n % 2]])
                O.copy("act", oT_b[b][:, :, tt * 128:(tt + 1) * 128], PT[n % 2][:], [PT[n % 2]], [(oT_b[b], tt)])

        loads(0)
        oloads(0)
        combine(0)
        nx = 0
        for tb in range(NTB):
            b = tb % 2
            if tb + 1 < NTB:
                loads(tb + 1)
                oloads(tb + 1)
            for cc in range(8):
                if cc == 4 and tb + 1 < NTB:
                    combine(tb + 1)
                pa, pb, pc = PS(), PS(), PS()
                for k in range(8):
                    O.mm(pa[:, :TB], wa[:, k, cc * 128:(cc + 1) * 128], yA_b[b][:, k, :], k == 0, k == 7, [wa, yA_b[b]], [pa])
                for k in range(8):
                    O.mm(pb[:, :TB], wb_[:, k, cc * 128:(cc + 1) * 128], oT_b[b][:, k, :], k == 0, k == 7, [wb_, oT_b[b]], [pb])
                for k in range(16):
                    O.mm(pc[:, :TB], wc[:, k, cc * 128:(cc + 1) * 128], yn_b[b][:, k, :], k == 0, k == 15, [wc, yn_b[b]], [pc])
                a1, a2 = t1[cc % 2], t2[cc % 2]
                O.tt("dve", a1[:], pa[:, :TB], g_b[b][:, cc, :], ALU.mult, [pa, g_b[b]], [a1])
                O.tt("dve", a2[:], pb[:, :TB], g_b[b][:, 8 + cc, :], ALU.mult, [pb, g_b[b]], [a2])
                O.tt("pool", a1[:], a1[:], a2[:], ALU.add, [a1, a2], [a1])
                O.tt("dve", a2[:], pc[:, :TB], g_b[b][:, 16 + cc, :], ALU.mult, [pc, g_b[b]], [a2])
                O.tt("pool", mT[b][:, cc, :], a1[:], a2[:], ALU.add, [a1, a2], [(mT[b], cc)])
            for tt in range(TB // 128):
                t = tb * (TB // 128) + tt
                xt_, xo_ = xt[nx % 2], xo[nx % 2]
                nx += 1
                S.dma("sp", xt_[:], x_src[t * 128:(t + 1) * 128, :], r=[(x_src, t)], w=[xt_])
                for hf in range(2):
                    ps = PS()
                    for k in range(8):
                        O.mm(ps[:], mT[b][:, k, tt * 128:(tt + 1) * 128], wo[:, k, hf * 512:(hf + 1) * 512], k == 0, k == 7,
                             [mT[b], wo], [ps])
                    O.tt("dve", xo_[:, hf * 512:(hf + 1) * 512], ps[:], xt_[:, hf * 512:(hf + 1) * 512], ALU.add,
                         [ps, xt_], [(xo_, hf)])
                S.dma("sp", C.xmid[t * 128:(t + 1) * 128, :], xo_[:], r=[xo_], w=[(C.xmid, t)])
        S.barrier()
        S.flush()
    if "xmid" in C.dbg and C.stop == (l, 5):
        return
    with ExitStack() as st:
        def sb(name, shape, dt):
            return st.enter_context(_sbt(nc, name, list(shape), dt))
        w2 = sb("p5w2", [128, 32, D_], BF16)
        w1b = [sb(f"p5w1_{i}", [128, 8, 512], BF16) for i in range(3)]
        xts = [sb(f"p5x{i}", [128, D_], F32) for i in range(4)]
        hT = sb("p5hT", [128, 8, 512], BF16)
        h1T = sb("p5h1T", [128, 32, 512], BF16)
        ub = [sb(f"p5ub{i}", [128, D_], BF16) for i in range(2)]
        sq = sb("p5sq", [128, D_], BF16)
        ss = [sb(f"p5ss{i}", [128, 1], F32) for i in range(2)]
        rr = [sb(f"p5r{i}", [128, 512], F32) for i in range(2)]
        xo = [sb(f"p5xo{i}", [128, D_], F32) for i in range(2)]
        nwt = sb("p5nw", [128, D_], F32)
        fnw = sb("p5fnw", [128, D_], F32)
        PB = [st.enter_context(_pst(nc, f"p5bps{i}", [128, 512], F32)) for i in range(6)]
        PT = [st.enter_context(_pst(nc, f"p5bpt{i}", [128, 8, 128], BF16)) for i in range(2)]
        w2src = C.w_2[l].rearrange("(k p) n -> p k n", p=128)
        for k8 in range(4):
            S.dma("pool", w2[:, k8 * 8:(k8 + 1) * 8, :], w2src[:, k8 * 8:(k8 + 1) * 8, :], r=[], w=[(w2, k8)])
        S.dma("sp", nwt[:], C.mlpw[l:l + 1, :].partition_broadcast(128), r=[], w=[nwt])
        S.dma("sp", fnw[:], C.finw.partition_broadcast(128), r=[], w=[fnw])
        w1src = C.wb_1[l].rearrange("(k p) n -> p k n", p=128)
        nps = 0
        nw1 = 0
        for tb in range(8):
            for tt in range(4):
                t = tb * 4 + tt
                S.dma("sp", xts[tt][:], C.xmid[t * 128:(t + 1) * 128, :], r=[(C.xmid, t)], w=[xts[tt]])
                rms_tile(C, xts[tt], ss[tt % 2], sq, nwt, ub[tt % 2], "p5")
                for k in range(8):
                    O.tr(PT[tt % 2][:, k, :], ub[tt % 2][:, k * 128:(k + 1) * 128], C.ident[:], [ub[tt % 2], C.ident], [PT[tt % 2]])
                O.copy("act", hT[:, :, tt * 128:(tt + 1) * 128], PT[tt % 2][:], [PT[tt % 2]], [(hT, tt)])
            for f4 in range(8):
                w1t = w1b[nw1 % 3]
                nw1 += 1
                S.dma("sp", w1t[:], w1src[:, :, f4 * 512:(f4 + 1) * 512], r=[C.wb_1[l]], w=[w1t])
                for fj in range(4):
                    fc = f4 * 4 + fj
                    ps = PB[nps % 6]
                    r_ = rr[nps % 2]
                    nps += 1
                    for k in range(8):
                        O.mm(ps[:], w1t[:, k, fj * 128:(fj + 1) * 128], hT[:, k, :], k == 0, k == 7, [w1t, hT], [ps])
                    O.act(r_[:], ps[:], AF.Relu, [ps], [r_])
                    O.tt("pool" if fc % 2 else "dve", h1T[:, fc, :], r_[:], r_[:], ALU.mult, [r_], [(h1T, fc)])
            for tt in range(4):
                t = tb * 4 + tt
                xo_ = xo[tt % 2]
                for hf in range(2):
                    ps = PB[nps % 6]
                    nps += 1
                    for fc in range(32):
                        O.mm(ps[:], h1T[:, fc, tt * 128:(tt + 1) * 128], w2[:, fc, hf * 512:(hf + 1) * 512], fc == 0, fc == 31,
                             [h1T, w2], [ps])
                    O.tt("dve", xo_[:, hf * 512:(hf + 1) * 512], ps[:], xts[tt][:, hf * 512:(hf + 1) * 512], ALU.add,
                         [ps, xts[tt]], [(xo_, hf)])
                if x_dst is not None:
                    S.dma("sp", x_dst[t * 128:(t + 1) * 128, :], xo_[:], r=[xo_], w=[(x_dst, t)])
                else:
                    s_ = ss[tt % 2]
                    O.act(sq[:], xo_[:], AF.Square, [xo_], [sq, s_], accum_out=s_[:])
                    O.act(s_[:], s_[:], AF.Sqrt, [s_], [s_], bias=1e-6, scale=1.0 / D_)
                    O.recip(s_[:], s_[:], [s_], [s_])
                    O.stt("dve", xo_[:], xo_[:], s_[:], fnw[:], ALU.mult, ALU.mult, [xo_, s_, fnw], [xo_])
                    S.dma("sp", C.y_out[t * 128:(t + 1) * 128, :], xo_[:], r=[xo_], w=[(C.y_out, t)])
        S.barrier()
        S.flush()


def host_consts():
    bf = ml_dtypes.bfloat16
    p = np.arange(128)[:, None]
    f = np.arange(128)[None, :]
    tri = np.stack([(p <= f), (p >= f), (p < f), (p > f)], axis=1).astype(np.float32)
    trif = np.stack([(p < f), (p > f), np.ones((128, 128), bool), (p <= f), (p >= f)], axis=1).astype(np.float32)
    invf = (500000.0 ** (-np.arange(0, 16, 2, dtype=np.float32) / 16.0)).astype(np.float32)[None, :]
    return {"c_ident": np.eye(128, dtype=np.float32).astype(bf), "c_tri": tri.astype(bf), "c_trif": trif, "c_invf": invf}


def prep_inputs(inp):
    f32 = np.float32
    sh = dict(host_consts())
    for k in ("mix_norm_w", "mlp_norm_w", "w_in", "w_a_out", "w_b_out", "w_c_out", "w_o", "w_ff1", "w_ff2"):
        sh[k] = np.ascontiguousarray(inp[k], dtype=f32)
    sh["final_norm_w"] = np.ascontiguousarray(inp["final_norm_w"], dtype=f32).reshape(1, D_)
    sh["conv_a_wT"] = np.ascontiguousarray(inp["conv_a_w"].reshape(L_, 3, 8, 128).transpose(0, 3, 2, 1), dtype=f32)
    sh["ssd_conv_wT"] = np.ascontiguousarray(inp["ssd_conv_w"].reshape(L_, 5, 24, 128).transpose(0, 3, 2, 1), dtype=f32)
    sh["ssd_conv_bT"] = np.ascontiguousarray(inp["ssd_conv_b"].reshape(L_, 24, 128).transpose(0, 2, 1), dtype=f32)
    sh["ssd_conv_bR"] = np.ascontiguousarray(inp["ssd_conv_b"].reshape(L_, 1, 3072), dtype=f32)
    sh["ssd_a_log"] = np.ascontiguousarray(inp["ssd_a_log"].reshape(L_, 1, 64), dtype=f32)
    sh["ssd_dt_bias"] = np.ascontiguousarray(inp["ssd_dt_bias"].reshape(L_, 1, 64), dtype=f32)
    sh["ssd_d"] = np.ascontiguousarray(inp["ssd_d"].reshape(L_, 1, 32), dtype=f32)
    sh["ssd_norm_w"] = np.ascontiguousarray(inp["ssd_norm_w"].reshape(L_, 1, 2048), dtype=f32)
    per = []
    for b in range(inp["x"].shape[0]):
        d = dict(sh)
        d["x"] = np.ascontiguousarray(inp["x"][b], dtype=f32)
        d["pos"] = np.ascontiguousarray(inp["positions"][b].reshape(NT_, 128).T, dtype=np.int32)
        per.append(d)
    return per


_NC_CACHE = {}


def kernel(**inputs):
    per = prep_inputs(inputs)
    if "nc" not in _NC_CACHE:
        _NC_CACHE["nc"] = build()
    nc = _NC_CACHE["nc"]
    res = run_bass_kernel_spmd(nc, per, core_ids=list(range(len(per))))
    return np.stack([np.asarray(r["y"], dtype=np.float32) for r in res.results], axis=0)
```

```python
import numpy as np
import ml_dtypes
import concourse.bass as bass
import concourse.mybir as mybir
from concourse.bass_utils import run_bass_kernel_spmd
from contextlib import ExitStack
from types import SimpleNamespace

F32 = mybir.dt.float32
BF16 = mybir.dt.bfloat16
I32 = mybir.dt.int32
AF = mybir.ActivationFunctionType
ALU = mybir.AluOpType
AX = mybir.AxisListType

ENG = ("pe", "act", "dve", "pool", "sp")
DMAQ = ("sp", "act", "pool")


class _Buf:
    __slots__ = ("w", "r")

    def __init__(self):
        self.w = None
        self.r = {}


class _TBuf:
    __slots__ = ("whole", "subs")

    def __init__(self):
        self.whole = _Buf()
        self.subs = {}


class Sched:
    NRING = 8
    SAME_ENGINE_SYNC = ("act", "dve", "pool")

    def __init__(self, nc):
        self.nc = nc
        self.sems = []
        self.esem = {}
        for e in ENG:
            self.esem[e] = self._new_sem("s_" + e)
        self.ecnt = {e: 0 for e in ENG}
        self.ring = {q: [self._new_sem(f"d_{q}{i}") for i in range(self.NRING)] for q in DMAQ}
        self.dman = {q: 0 for q in DMAQ}
        self.known = {e: {} for e in ENG}
        self.streams = {e: [] for e in ENG}
        self.tb = {}
        self.n_wait = 0
        self.n_ins = 0

    def _new_sem(self, name):
        h = self.nc.alloc_semaphore(name=name)
        self.sems.append(h)
        return len(self.sems) - 1

    def _spec(self, a):
        if isinstance(a, tuple):
            t, key = a
        else:
            t, key = a, None
        name = t if isinstance(t, str) else t.name
        tb = self.tb.get(name)
        if tb is None:
            tb = self.tb[name] = _TBuf()
        return tb, key

    def _deps(self, eng, reads, writes):
        need = {}

        def add(ev):
            if ev is not None and need.get(ev[0], 0) < ev[1]:
                need[ev[0]] = ev[1]

        def addr(b):
            for s, v in b.r.items():
                if need.get(s, 0) < v:
                    need[s] = v

        for a in reads:
            tb, key = self._spec(a)
            add(tb.whole.w)
            if key is None:
                for sb in tb.subs.values():
                    add(sb.w)
            else:
                sb = tb.subs.get(key)
                if sb is not None:
                    add(sb.w)
        for a in writes:
            tb, key = self._spec(a)
            add(tb.whole.w)
            addr(tb.whole)
            if key is None:
                for sb in tb.subs.values():
                    add(sb.w)
                    addr(sb)
            else:
                sb = tb.subs.get(key)
                if sb is not None:
                    add(sb.w)
                    addr(sb)
        kn = self.known[eng]
        own = self.esem[eng]
        waits = []
        for s, v in need.items():
            if s == own and eng not in self.SAME_ENGINE_SYNC:
                continue
            if kn.get(s, 0) < v:
                kn[s] = v
                waits.append((s, v))
        return waits

    def _record(self, ev, reads, writes):
        s, v = ev
        for a in reads:
            tb, key = self._spec(a)
            if key is None:
                b = tb.whole
            else:
                b = tb.subs.get(key)
                if b is None:
                    b = tb.subs[key] = _Buf()
            if b.r.get(s, 0) < v:
                b.r[s] = v
        for a in writes:
            tb, key = self._spec(a)
            if key is None:
                tb.whole.w = ev
                tb.whole.r = {}
                tb.subs = {}
            else:
                b = tb.subs.get(key)
                if b is None:
                    b = tb.subs[key] = _Buf()
                b.w = ev
                b.r = {}

    def op(self, eng, emit, r=(), w=()):
        waits = self._deps(eng, r, w)
        self.ecnt[eng] += 1
        ev = (self.esem[eng], self.ecnt[eng])
        self._record(ev, r, w)
        self.streams[eng].append((waits, emit, ev[0], 1))
        self.n_wait += len(waits)
        self.n_ins += 1

    def dma(self, q, out, in_, r=(), w=(), **kw):
        waits = self._deps(q, r, w)
        i = self.dman[q]
        self.dman[q] += 1
        s = self.ring[q][i % self.NRING]
        rnd = i // self.NRING
        if rnd > 0:
            kn = self.known[q]
            if kn.get(s, 0) < 16 * rnd:
                kn[s] = 16 * rnd
                waits.append((s, 16 * rnd))
        ev = (s, 16 * (rnd + 1))
        self._record(ev, r, w)
        self.streams[q].append((waits, lambda e: e.dma_start(out=out, in_=in_, **kw), s, 16))
        self.n_wait += len(waits)
        self.n_ins += 1

    def barrier(self, skip_q=()):
        evs = []
        for e in ENG:
            if self.ecnt[e] > 0:
                evs.append((self.esem[e], self.ecnt[e]))
        for q in DMAQ:
            if q in skip_q:
                continue
            n = self.dman[q]
            for j in range(min(n, self.NRING)):
                last = ((n - 1 - j) // self.NRING) * self.NRING + j
                evs.append((self.ring[q][j], 16 * (last // self.NRING + 1)))
        for e in ENG:
            kn = self.known[e]
            waits = []
            for s, v in evs:
                if kn.get(s, 0) < v:
                    kn[s] = v
                    waits.append((s, v))
            if waits:
                self.streams[e].append((waits, None, None, 0))
                self.n_wait += len(waits)

    def flush(self):
        nc = self.nc
        sems = self.sems

        def run(name):
            items = self.streams[name]
            self.streams[name] = []

            def f(e):
                for waits, emit, s, inc in items:
                    for ws, wv in waits:
                        e.wait_ge(sems[ws], wv)
                    if emit is not None:
                        ins = emit(e)
                        ins.then_inc(sems[s], inc)

            return f

        with nc.Block() as block:
            block.tensor(run("pe"))
            block.scalar(run("act"))
            block.vector(run("dve"))
            block.gpsimd(run("pool"))
            block.sync(run("sp"))


S_ = 4096
D_ = 1024
NT_ = 32
L_ = 2
DIN_ = 14400
PI = 3.141592653589793


_UID = [0]


def _sbt(nc, name, shape, dt):
    _UID[0] += 1
    return nc.sbuf_tensor(f"{name}_u{_UID[0]}", shape, dt)


def _pst(nc, name, shape, dt):
    _UID[0] += 1
    return nc.psum_tensor(f"{name}_u{_UID[0]}", shape, dt)


class Ops:
    def __init__(self, S):
        self.S = S

    def mm(self, out, lhsT, rhs, start, stop, r, w):
        self.S.op("pe", lambda e: e.matmul(out, lhsT=lhsT, rhs=rhs, start=start, stop=stop), r, w)

    def tr(self, out, in_, ident, r, w):
        self.S.op("pe", lambda e: e.transpose(out=out, in_=in_, identity=ident), r, w)

    def act(self, out, in_, func, r, w, **kw):
        self.S.op("act", lambda e: e.activation(out=out, in_=in_, func=func, **kw), r, w)

    def tt(self, eng, out, in0, in1, op, r, w):
        self.S.op(eng, lambda e: e.tensor_tensor(out=out, in0=in0, in1=in1, op=op), r, w)

    def ts(self, eng, out, in0, s1, s2, op0, op1, r, w):
        if s2 is None:
            self.S.op(eng, lambda e: e.tensor_scalar(out=out, in0=in0, scalar1=s1, scalar2=None, op0=op0), r, w)
        else:
            self.S.op(eng, lambda e: e.tensor_scalar(out=out, in0=in0, scalar1=s1, scalar2=s2, op0=op0, op1=op1), r, w)

    def stt(self, eng, out, in0, scalar, in1, op0, op1, r, w):
        self.S.op(eng, lambda e: e.scalar_tensor_tensor(out=out, in0=in0, scalar=scalar, in1=in1, op0=op0, op1=op1), r, w)

    def copy(self, eng, out, in_, r, w):
        if eng == "act":
            self.S.op(eng, lambda e: e.copy(out=out, in_=in_), r, w)
        else:
            self.S.op(eng, lambda e: e.tensor_copy(out=out, in_=in_), r, w)

    def memset(self, eng, ap, val, w):
        self.S.op(eng, lambda e: e.memset(ap, val), (), w)

    def recip(self, out, in_, r, w):
        self.S.op("dve", lambda e: e.reciprocal(out=out, in_=in_), r, w)


def build(dbg=(), stop=None, nlayers=L_):
    nc = bass.Bass("TRN2", target_bir_lowering=False)

    def din(name, shape, dt=F32):
        return nc.dram_tensor(name, list(shape), dt, kind="ExternalInput").ap()

    def dscr(name, shape, dt):
        kind = "ExternalOutput" if name in dbg else "Internal"
        return nc.dram_tensor(name, list(shape), dt, kind=kind).ap()

    x_in = din("x", [S_, D_])
    pos_in = din("pos", [128, NT_], I32)
    mixw = din("mix_norm_w", [L_, D_])
    mlpw = din("mlp_norm_w", [L_, D_])
    finw = din("final_norm_w", [1, D_])
    w_in = din("w_in", [L_, D_, DIN_])
    w_a = din("w_a_out", [L_, D_, D_])
    w_b = din("w_b_out", [L_, D_, D_])
    w_c = din("w_c_out", [L_, 2 * D_, D_])
    w_o = din("w_o", [L_, D_, D_])
    w_1 = din("w_ff1", [L_, D_, 4 * D_])
    w_2 = din("w_ff2", [L_, 4 * D_, D_])
    cawT = din("conv_a_wT", [L_, 128, 8, 3])
    scwT = din("ssd_conv_wT", [L_, 128, 24, 5])
    scbT = din("ssd_conv_bT", [L_, 128, 24])
    scbR = din("ssd_conv_bR", [L_, 1, 3072])
    alog = din("ssd_a_log", [L_, 1, 64])
    dtb = din("ssd_dt_bias", [L_, 1, 64])
    sdd = din("ssd_d", [L_, 1, 32])
    snw = din("ssd_norm_w", [L_, 1, 2048])
    c_ident = din("c_ident", [128, 128], BF16)
    c_tri = din("c_tri", [128, 4, 128], BF16)
    c_trif = din("c_trif", [128, 5, 128], F32)
    c_invf = din("c_invf", [1, 8], F32)
    y_out = nc.dram_tensor("y", [S_, D_], F32, kind="ExternalOutput").ap()

    wb_in = [dscr(f"wb_in{l}", [D_, DIN_], BF16) for l in range(L_)]
    wb_a = [dscr(f"wb_a{l}", [D_, D_], BF16) for l in range(L_)]
    wb_b = [dscr(f"wb_b{l}", [D_, D_], BF16) for l in range(L_)]
    wb_c = [dscr(f"wb_c{l}", [2 * D_, D_], BF16) for l in range(L_)]
    wb_o = [dscr(f"wb_o{l}", [D_, D_], BF16) for l in range(L_)]
    wb_1 = [dscr(f"wb_1{l}", [D_, 4 * D_], BF16) for l in range(L_)]
    wb_2 = [dscr(f"wb_2{l}", [4 * D_, D_], BF16) for l in range(L_)]
    yAT = dscr("yAT", [16, 128, 8, 256], BF16)
    gT = dscr("gT", [16, 128, 24, 256], BF16)
    qT = dscr("qT", [D_, S_], BF16)
    kT = dscr("kT", [D_, S_], BF16)
    v_d = dscr("v_d", [S_, D_], BF16)
    sz_d = dscr("sz_d", [S_, 2 * D_], BF16)
    xs_d = dscr("xs_d", [S_, 2 * D_], BF16)
    Bt_d = dscr("Bt_d", [S_, 512], BF16)
    BT_d = dscr("BT_d", [512, S_], BF16)
    CT_d = dscr("CT_d", [512, S_], BF16)
    dt_dbg = dscr("dt_dbg", [128, NT_, 64], F32)
    o_d = [dscr(f"o_d{p}", [S_, 16, 65], F32) for p in range(3)]
    hin_d = [dscr(f"hin_d{d}", [NT_, 128, 2048], BF16) for d in range(2)]
    ynT_d = dscr("ynT_d", [16, 128, 16, 256], BF16)
    xmid = dscr("xmid", [S_, D_], F32)
    xl = [dscr(f"xl{l}", [S_, D_], F32) for l in range(L_ - 1)]

    S = Sched(nc)
    O = Ops(S)

    with ExitStack() as gst:
        def gsb(name, shape, dt):
            return gst.enter_context(_sbt(nc, name, list(shape), dt))

        ident = gsb("ident", [128, 128], BF16)
        tri = gsb("tri", [128, 4, 128], BF16)
        trif = gsb("trif", [128, 5, 128], F32)
        cosT = gsb("cosT", [128, NT_, 8], F32)
        sinT = gsb("sinT", [128, NT_, 8], F32)
        dt_all = gsb("dt_all", [128, NT_, 64], F32)
        la_all = gsb("la_all", [128, NT_, 64], F32)
        S.dma("sp", ident[:], c_ident, r=[], w=[ident])
        S.dma("sp", tri[:], c_tri, r=[], w=[tri])
        S.dma("sp", trif[:], c_trif, r=[], w=[trif])

        def cast_w(src, dst, rows_per):
            R = src.shape[0]
            for r0 in range(0, R, rows_per):
                S.dma("pool", dst[r0:r0 + rows_per, :], src[r0:r0 + rows_per, :], r=[], w=[(dst, r0)])

        with ExitStack() as st:
            def sb(name, shape, dt):
                return st.enter_context(_sbt(nc, name, list(shape), dt))
            posi = sb("posi", [128, NT_], I32)
            posf = sb("posf", [128, NT_], F32)
            invf = sb("invf", [128, 8], F32)
            ang = sb("ang", [128, NT_, 8], F32)
            a1 = sb("a1", [128, NT_, 8], F32)
            S.dma("sp", posi[:], pos_in, r=[], w=[posi])
            S.dma("sp", invf[:], c_invf.partition_broadcast(128), r=[], w=[invf])
            O.copy("dve", posf[:], posi[:], [posi], [posf])
            O.tt("dve", ang[:], posf[:].unsqueeze(2).to_broadcast([128, NT_, 8]),
                 invf[:].unsqueeze(1).to_broadcast([128, NT_, 8]), ALU.mult, [posf, invf], [ang])
            ki = sb("ki", [128, NT_, 8], I32)
            kf = sb("kf", [128, NT_, 8], F32)
            mm_ = sb("mm_", [128, NT_, 8], F32)
            for shift, dstT in ((0.0, sinT), (0.5 * PI, cosT)):
                O.ts("dve", a1[:], ang[:], shift, 1.0 / (2 * PI), ALU.add, ALU.mult, [ang], [a1])
                O.copy("dve", ki[:], a1[:], [a1], [ki])
                O.copy("dve", kf[:], ki[:], [ki], [kf])
                O.ts("dve", a1[:], ang[:], shift, None, ALU.add, None, [ang], [a1])
                O.stt("dve", a1[:], kf[:], -2 * PI, a1[:], ALU.mult, ALU.add, [kf, a1], [a1])
                O.ts("dve", mm_[:], a1[:], PI, 2 * PI, ALU.is_ge, ALU.mult, [a1], [mm_])
                O.tt("dve", a1[:], a1[:], mm_[:], ALU.subtract, [a1, mm_], [a1])
                O.ts("dve", mm_[:], a1[:], -PI, 2 * PI, ALU.is_lt, ALU.mult, [a1], [mm_])
                O.tt("dve", a1[:], a1[:], mm_[:], ALU.add, [a1, mm_], [a1])
                O.ts("dve", a1[:], a1[:], -PI, PI, ALU.max, ALU.min, [a1], [a1])
                O.act(dstT[:], a1[:], AF.Sin, [a1], [dstT])
            S.barrier()
            S.flush()

        for l in range(nlayers):
            x_src = x_in if l == 0 else xl[l - 1]
            x_dst = xl[l] if l < L_ - 1 else None
            C = SimpleNamespace(**locals())
            layer(C, l, x_src, x_dst, stop)
            if stop is not None and stop[0] == l:
                break
        S.barrier(skip_q=())
        S.flush()
    return nc


def layer(C, l, x_src, x_dst, stop):
    nc, S, O = C.nc, C.S, C.O
    with ExitStack() as lst:
        uT = lst.enter_context(_sbt(nc, "uT", [128, 8, S_], BF16))
        phase1(C, l, x_src, uT)
        if stop == (l, 1):
            return
        phase2(C, l, uT)
    S.barrier()
    S.flush()
    if stop == (l, 2):
        return
    phase3(C, l)
    if stop == (l, 3):
        return
    phase4(C, l)
    if stop == (l, 4):
        return
    phase5(C, l, x_src, x_dst)


def rms_tile(C, xt, ss, sq, nwt, ub, tag):
    O = C.O
    O.act(sq[:], xt[:], AF.Square, [xt], [sq, ss], accum_out=ss[:])
    O.act(ss[:], ss[:], AF.Sqrt, [ss], [ss], bias=1e-6, scale=1.0 / D_)
    O.recip(ss[:], ss[:], [ss], [ss])
    O.stt("dve", ub[:], xt[:], ss[:], nwt[:], ALU.mult, ALU.mult, [xt, ss, nwt], [ub])


def phase1(C, l, x_src, uT):
    nc, S, O = C.nc, C.S, C.O
    with ExitStack() as st:
        def sb(name, shape, dt):
            return st.enter_context(_sbt(nc, name, list(shape), dt))
        xt = [sb(f"p1x{i}", [128, D_], F32) for i in range(2)]
        sq = sb("p1sq", [128, D_], BF16)
        ss = [sb(f"p1ss{i}", [128, 1], F32) for i in range(2)]
        ub = [sb(f"p1ub{i}", [128, D_], BF16) for i in range(2)]
        nwt = sb("p1nw", [128, D_], F32)
        pT = [st.enter_context(_pst(nc, f"p1pT{i}", [128, 8, 128], BF16)) for i in range(2)]
        S.dma("sp", nwt[:], C.mixw[l:l + 1, :].partition_broadcast(128), r=[], w=[nwt])
        for t in range(NT_):
            b = t % 2
            S.dma("sp", xt[b][:], x_src[t * 128:(t + 1) * 128, :], r=[(x_src, t)], w=[xt[b]])
            rms_tile(C, xt[b], ss[b], sq, nwt, ub[b], "p1")
            for k in range(8):
                O.tr(pT[b][:, k, :], ub[b][:, k * 128:(k + 1) * 128], C.ident[:], [ub[b], C.ident], [pT[b]])
            O.copy("act" if t % 2 else "dve", uT[:, :, t * 128:(t + 1) * 128], pT[b][:], [pT[b]], [(uT, t)])
        S.barrier()
        S.flush()


def phase2(C, l, uT):
    nc, S, O = C.nc, C.S, C.O
    wsrc = C.w_in[l].rearrange("(k p) n -> p k n", p=128)
    seq = []
    for g in range(2):
        seq += [(512 * g, 512), (1024 + 512 * g, 512), (2048 + 512 * g, 512)]
    seq += [(3072 + 512 * i, 512) for i in range(4)]
    seq += [(5120 + 512 * i, 512) for i in range(2)]
    seq += [(6144 + 512 * i, 512) for i in range(4)]
    seq += [(8192 + 512 * i, 512) for i in range(6)]
    seq += [(11264, 64)]
    seq += [(11328 + 512 * i, 512) for i in range(6)]
    with ExitStack() as st:
        wbuf = [st.enter_context(_sbt(nc, f"p2w{i}", [128, 8, 512], BF16)) for i in range(3)]
        PB = [st.enter_context(_pst(nc, f"p2ps{i}", [128, 512], F32)) for i in range(6)]
        PT = [st.enter_context(_pst(nc, f"p2pt{i}", [128, 4, 128], BF16)) for i in range(2)]
        state = {"issued": 0, "ps": 0, "done": 0}

        def prefetch():
            while state["issued"] < min(len(seq), state["done"] + 3):
                i = state["issued"]
                c0, ncw = seq[i]
                S.dma("pool", wbuf[i % 3][:, :, :ncw], wsrc[:, :, c0:c0 + ncw], r=[], w=[wbuf[i % 3]])
                state["issued"] += 1

        def get_w(idx):
            prefetch()
            assert idx < state["issued"]
            return wbuf[idx % 3]

        def release(n):
            state["done"] = n
            prefetch()

        def PS():
            p = PB[state["ps"] % len(PB)]
            state["ps"] += 1
            return p

        def mm_feat(ps, wt, jj, tb):
            for k in range(8):
                O.mm(ps[:], wt[:, k, jj * 128:(jj + 1) * 128], uT[:, k, tb * 512:(tb + 1) * 512], k == 0, k == 7,
                     [wt, uT], [ps])

        def mm_tok(ps_ap, ps, wt, t, ncw):
            for k in range(8):
                O.mm(ps_ap, uT[:, k, t * 128:(t + 1) * 128], wt[:, k, :ncw], k == 0, k == 7, [wt, uT], [ps])

        wi = 0
        with ExitStack() as s2:
            def sb(name, shape, dt):
                return s2.enter_context(_sbt(nc, name, list(shape), dt))
            bT = [sb(f"p2bT{i}", [128, S_], F32) for i in range(2)]
            cst = [sb(f"p2cst{i}", [128, 512], F32) for i in range(2)]
            cx = [sb(f"p2cx{i}", [128, S_ + 2], BF16) for i in range(2)]
            yAo = [sb(f"p2yAo{i}", [128, S_], BF16) for i in range(2)]
            dgA = [sb(f"p2dgA{i}", [128, 3, 128], BF16) for i in range(2)]
            wA = sb("p2wA", [128, 8, 3], F32)
            S.dma("sp", wA[:], C.cawT[l], r=[], w=[wA])
            for i in range(2):
                O.memset("dve", cx[i][:, 0:1], 0.0, [(cx[i], "l")])
                O.memset("dve", cx[i][:, S_ + 1:S_ + 2], 0.0, [(cx[i], "r")])

            def convA(j):
                p = j % 2
                for tb in range(8):
                    ps = PS()
                    for k3 in range(3):
                        O.mm(ps[:], dgA[p][:, k3, :], cx[p][:, tb * 512 + k3:tb * 512 + k3 + 512], k3 == 0, k3 == 2,
                             [dgA[p], cx[p]], [ps])
                    O.tt("dve", yAo[p][:, tb * 512:(tb + 1) * 512], ps[:], bT[p][:, tb * 512:(tb + 1) * 512], ALU.mult,
                         [ps, bT[p]], [(yAo[p], tb)])
                S.dma("sp", C.yAT[:, :, j, :].rearrange("tb p t -> p tb t"), yAo[p][:].rearrange("p (tb t) -> p tb t", t=256),
                      r=[yAo[p]], w=[(C.yAT, j)])

            for g in range(2):
                release(wi)
                wt_b, wt_c, wt_x = get_w(wi), get_w(wi + 1), get_w(wi + 2)
                wi += 3
                for jj in range(4):
                    j = 4 * g + jj
                    p = j % 2
                    for k3 in range(3):
                        O.ts("dve", dgA[p][:, k3, :], C.ident[:], wA[:, j, k3:k3 + 1], None, ALU.mult, None,
                             [C.ident, wA], [(dgA[p], k3)])
                    for tb in range(8):
                        pb, pc, px = PS(), PS(), PS()
                        mm_feat(pb, wt_b, jj, tb)
                        mm_feat(pc, wt_c, jj, tb)
                        mm_feat(px, wt_x, jj, tb)
                        O.copy("act", bT[p][:, tb * 512:(tb + 1) * 512], pb[:], [pb], [(bT[p], tb)])
                        O.copy("act", cst[tb % 2][:], pc[:], [pc], [cst[tb % 2]])
                        O.tt("dve", cx[p][:, 1 + tb * 512:1 + (tb + 1) * 512], px[:], cst[tb % 2][:], ALU.mult,
                             [px, cst[tb % 2]], [(cx[p], tb)])
                        if tb == 1 and j > 0:
                            convA(j - 1)
            convA(7)
            S.barrier()
            S.flush()
        with ExitStack() as s2:
            def sb(name, shape, dt):
                return s2.enter_context(_sbt(nc, name, list(shape), dt))
            qs = [sb(f"p2qs{i}", [128, 8, 64], F32) for i in range(3)]
            tmp = [sb(f"p2tmp{i}", [128, 4, 8, 8], F32) for i in range(3)]
            qr = [sb(f"p2qr{i}", [128, 512], BF16) for i in range(3)]
            stg = [sb(f"p2stg{i}", [128, 4, S_], BF16) for i in range(2)]
            for ci in range(4):
                release(wi)
                wt = get_w(wi)
                wi += 1
                sg = stg[ci % 2]
                def qk_front(t):
                    b = t % 3
                    ps = PS()
                    mm_tok(ps[:], ps, wt, t, 512)
                    O.copy("act", qs[b][:].rearrange("p h d -> p (h d)"), ps[:], [ps], [qs[b]])
                    cs = C.cosT[:, t, :].unsqueeze(1).to_broadcast([128, 8, 8])
                    sn = C.sinT[:, t, :].unsqueeze(1).to_broadcast([128, 8, 8])
                    t1 = qs[b][:, :, 0:8]
                    t2 = qs[b][:, :, 8:16]
                    tm = tmp[b]
                    O.tt("dve", tm[:, 0], t1, cs, ALU.mult, [qs[b], C.cosT], [(tm, 0)])
                    O.tt("dve", tm[:, 1], t2, sn, ALU.mult, [qs[b], C.sinT], [(tm, 1)])
                    O.tt("dve", tm[:, 2], t2, cs, ALU.mult, [qs[b], C.cosT], [(tm, 2)])
                    O.tt("dve", tm[:, 3], t1, sn, ALU.mult, [qs[b], C.sinT], [(tm, 3)])
                    O.tt("dve", t1, tm[:, 0], tm[:, 1], ALU.subtract, [(tm, 0), (tm, 1)], [(qs[b], "a")])
                    O.tt("dve", t2, tm[:, 2], tm[:, 3], ALU.add, [(tm, 2), (tm, 3)], [(qs[b], "b")])
                    O.copy("act", qr[b][:], qs[b][:].rearrange("p h d -> p (h d)"), [qs[b]], [qr[b]])

                def qk_back(t):
                    b = t % 3
                    pt = PT[t % 2]
                    for jj in range(4):
                        O.tr(pt[:, jj, :], qr[b][:, jj * 128:(jj + 1) * 128], C.ident[:], [qr[b], C.ident], [pt])
                    O.copy("act", sg[:, :, t * 128:(t + 1) * 128], pt[:], [pt], [(sg, t)])

                for t in range(NT_ + 2):
                    if t < NT_:
                        qk_front(t)
                    if 0 <= t - 2 < NT_:
                        qk_back(t - 2)
                dst = C.qT if ci < 2 else C.kT
                for jj in range(4):
                    r0 = (ci % 2) * 512 + jj * 128
                    S.dma("sp", dst[r0:r0 + 128, :], sg[:, jj, :], r=[sg], w=[(dst, r0)])
            S.barrier()
            S.flush()
        with ExitStack() as s2:
            def sb(name, shape, dt):
                return s2.enter_context(_sbt(nc, name, list(shape), dt))
            vst = [sb(f"p2vst{i}", [128, 512], BF16) for i in range(4)]
            n = 0
            for ci in range(6):
                release(wi)
                wt = get_w(wi)
                wi += 1
                for t in range(NT_):
                    ps = PS()
                    mm_tok(ps[:], ps, wt, t, 512)
                    vs = vst[n % 4]
                    n += 1
                    if ci < 2:
                        O.copy("act", vs[:], ps[:], [ps], [vs])
                        S.dma("sp", C.v_d[t * 128:(t + 1) * 128, ci * 512:(ci + 1) * 512], vs[:], r=[vs], w=[(C.v_d, (t, ci))])
                    else:
                        O.act(vs[:], ps[:], AF.Silu, [ps], [vs])
                        c2 = ci - 2
                        S.dma("sp", C.sz_d[t * 128:(t + 1) * 128, c2 * 512:(c2 + 1) * 512], vs[:], r=[vs], w=[(C.sz_d, (t, c2))])
            S.barrier()
            S.flush()
        with ExitStack() as s2:
            def sb(name, shape, dt):
                return s2.enter_context(_sbt(nc, name, list(shape), dt))
            xin = sb("p2xin", [128, 4, S_ + 4], BF16)
            diag = sb("p2diag", [128, 4, 5, 128], BF16)
            stok = [sb(f"p2stok{i}", [128, 8, 512], BF16) for i in range(2)]
            sfeat = [sb(f"p2sfeat{i}", [128, S_], BF16) for i in range(2)]
            wS = sb("p2wS", [128, 24, 5], F32)
            bS = sb("p2bS", [128, 24], F32)
            brf = sb("p2brf", [1, 3072], F32)
            brow = sb("p2brow", [1, 3072], BF16)
            ones = sb("p2ones", [1, 128], BF16)
            S.dma("sp", wS[:], C.scwT[l], r=[], w=[wS])
            S.dma("sp", bS[:], C.scbT[l], r=[], w=[bS])
            S.dma("sp", brf[:], C.scbR[l], r=[], w=[brf])
            O.copy("dve", brow[:], brf[:], [brf], [brow])
            O.memset("dve", ones[:], 1.0, [ones])
            O.memset("dve", xin[:, :, 0:2], 0.0, [(xin, "l")])
            O.memset("dve", xin[:, :, S_ + 2:S_ + 4], 0.0, [(xin, "r")])
            nf = 0
            for ci in range(6):
                release(wi)
                wt = get_w(wi)
                wi += 1
                for jj in range(4):
                    J = 4 * ci + jj
                    for tb in range(8):
                        ps = PS()
                        mm_feat(ps, wt, jj, tb)
                        O.copy("act" if tb % 2 else "dve", xin[:, jj, 2 + tb * 512:2 + (tb + 1) * 512], ps[:], [ps], [(xin, (jj, tb))])
                    for k5 in range(5):
                        O.ts("dve", diag[:, jj, k5, :], C.ident[:], wS[:, J, k5:k5 + 1], None, ALU.mult, None,
                             [C.ident, wS], [(diag, (jj, k5))])
                if ci <= 4:
                    for t in range(NT_):
                        ps = PS()
                        for jj in range(4):
                            J = 4 * ci + jj
                            o_ap = ps[:, jj * 128:(jj + 1) * 128]
                            for k5 in range(5):
                                O.mm(o_ap, xin[:, jj, t * 128 + k5:t * 128 + k5 + 128], diag[:, jj, k5, :], k5 == 0, False,
                                     [xin, diag], [ps])
                            O.mm(o_ap, ones[0:1, :], brow[0:1, J * 128:(J + 1) * 128], False, True, [ones, brow], [ps])
                        sk = stok[(t // 8) % 2]
                        O.act(sk[:, t % 8, :], ps[:], AF.Silu, [ps], [(sk, t % 8)])
                        if t % 8 == 7:
                            t0 = t - 7
                            if ci < 4:
                                dst = C.xs_d[t0 * 128:(t0 + 8) * 128, ci * 512:(ci + 1) * 512]
                                key = (C.xs_d, (t0, ci))
                            else:
                                dst = C.Bt_d[t0 * 128:(t0 + 8) * 128, :]
                                key = (C.Bt_d, t0)
                            S.dma("sp", dst.rearrange("(t p) c -> p t c", p=128), sk[:], r=[sk], w=[key])
                if ci >= 4:
                    for jj in range(4):
                        J = 4 * ci + jj
                        sf = sfeat[nf % 2]
                        nf += 1
                        for tb in range(8):
                            ps = PS()
                            for k5 in range(5):
                                O.mm(ps[:], diag[:, jj, k5, :], xin[:, jj, tb * 512 + k5:tb * 512 + k5 + 512], k5 == 0, k5 == 4,
                                     [xin, diag], [ps])
                            O.act(sf[:, tb * 512:(tb + 1) * 512], ps[:], AF.Silu, [ps, bS], [(sf, tb)], bias=bS[:, J:J + 1])
                        dst = C.BT_d if ci == 4 else C.CT_d
                        S.dma("sp", dst[jj * 128:(jj + 1) * 128, :], sf[:], r=[sf], w=[(dst, jj)])
            S.barrier()
            S.flush()
        with ExitStack() as s2:
            def sb(name, shape, dt):
                return s2.enter_context(_sbt(nc, name, list(shape), dt))
            dtbb = sb("p2dtbb", [128, 64], F32)
            abc = sb("p2abc", [128, 64], F32)
            tdt = [sb(f"p2tdt{i}", [128, 64], F32) for i in range(2)]
            S.dma("sp", dtbb[:], C.dtb[l].partition_broadcast(128), r=[], w=[dtbb])
            S.dma("sp", abc[:], C.alog[l].partition_broadcast(128), r=[], w=[abc])
            O.act(abc[:], abc[:], AF.Exp, [abc], [abc])
            O.ts("dve", abc[:], abc[:], -1.0, None, ALU.mult, None, [abc], [abc])
            release(wi)
            wt = get_w(wi)
            wi += 1
            for t in range(NT_):
                ps = PS()
                mm_tok(ps[:, 0:64], ps, wt, t, 64)
                td = tdt[t % 2]
                O.tt("dve", td[:], ps[:, 0:64], dtbb[:], ALU.add, [ps, dtbb], [td])
                O.act(td[:], td[:], AF.Exp, [td], [td])
                O.act(C.dt_all[:, t, :], td[:], AF.Ln, [td], [(C.dt_all, t)], bias=1.0)
                O.tt("dve", C.la_all[:, t, :], C.dt_all[:, t, :], abc[:], ALU.mult, [(C.dt_all, t), abc], [(C.la_all, t)])
            if "dt_dbg" in C.dbg:
                S.dma("sp", C.dt_dbg, C.dt_all[:], r=[C.dt_all], w=[C.dt_dbg])
            S.barrier()
            S.flush()
        with ExitStack() as s2:
            def sb(name, shape, dt):
                return s2.enter_context(_sbt(nc, name, list(shape), dt))
            sfeat = [sb(f"p2gfeat{i}", [128, S_], BF16) for i in range(2)]
            nf = 0
            for ci in range(6):
                release(wi)
                wt = get_w(wi)
                wi += 1
                for jj in range(4):
                    G = 4 * ci + jj
                    sf = sfeat[nf % 2]
                    nf += 1
                    for tb in range(8):
                        ps = PS()
                        mm_feat(ps, wt, jj, tb)
                        O.act(sf[:, tb * 512:(tb + 1) * 512], ps[:], AF.Sigmoid, [ps], [(sf, tb)])
                    S.dma("sp", C.gT[:, :, G, :].rearrange("tb p t -> p tb t"), sf[:].rearrange("p (tb t) -> p tb t", t=256),
                          r=[sf], w=[(C.gT, G)])
            S.barrier()
            S.flush()
        assert wi == len(seq)


def phase3(C, l):
    nc, S, O = C.nc, C.S, C.O
    C.cast_w(C.w_1[l], C.wb_1[l], 128)
    pats = [(1, 33), (4, 9), (16, 3)]
    with ExitStack() as st:
        def sb(name, shape, dt):
            return st.enter_context(_sbt(nc, name, list(shape), dt))
        qTh = [sb(f"p3q{i}", [64, S_], BF16) for i in range(2)]
        kTp = [sb(f"p3k{i}", [64, S_ + 2048], BF16) for i in range(2)]
        vP = [[sb(f"p3v{pi}_{b}", [128, nb, dil, 65], BF16) for b in range(2)] for pi, (dil, nb) in enumerate(pats)]
        mask = sb("p3mask", [128, 2, 128], BF16)
        PTs = [sb(f"p3pt{i}", [128, 2, 2, 128], BF16) for i in range(4)]
        PTm = [sb(f"p3pm{i}", [128, 2, 2, 128], BF16) for i in range(4)]
        ost = [sb(f"p3ost{i}", [128, 32, 65], F32) for i in range(2)]
        PSs = [st.enter_context(_pst(nc, f"p3ps{i}", [128, 512], F32)) for i in range(5)]
        PSo = [st.enter_context(_pst(nc, f"p3po{i}", [128, 2, 65], F32)) for i in range(3)]
        O.copy("dve", mask[:, 0, :], C.tri[:, 1, :], [C.tri], [(mask, 0)])
        O.copy("dve", mask[:, 1, :], C.tri[:, 0, :], [C.tri], [(mask, 1)])
        for b in range(2):
            O.memset("dve", kTp[b][:, 0:1024], 0.0, [(kTp[b], "l")])
            O.memset("dve", kTp[b][:, 1024 + S_:2048 + S_], 0.0, [(kTp[b], "r")])
            for pi, (dil, nb) in enumerate(pats):
                v = vP[pi][b]
                O.memset("pool", v[:], 0.0, [v])
                O.memset("pool", v[:, :, :, 64:65], 1.0, [v])
                O.memset("pool", v[0:64, 0, :, 64:65], 0.0, [v])
                O.memset("pool", v[64:128, nb - 1, :, 64:65], 0.0, [v])
        def loads(h):
            b = h % 2
            S.dma("sp", qTh[b][:], C.qT[h * 64:(h + 1) * 64, :], r=[C.qT], w=[qTh[b]])
            S.dma("sp", kTp[b][:, 1024:1024 + S_], C.kT[h * 64:(h + 1) * 64, :], r=[C.kT], w=[(kTp[b], "m")])
            vsrc = C.v_d[:, h * 64:(h + 1) * 64]
            for pi, (dil, nb) in enumerate(pats):
                v = vP[pi][b]
                nin = nb - 2
                if dil == 1:
                    for j0 in range(0, nin, 8):
                        j1 = min(nin, j0 + 8)
                        src = vsrc[64 + j0 * 128:64 + j1 * 128, :]
                        S.dma("sp", v[:, 1 + j0:1 + j1, 0, 0:64], src.rearrange("(j i) d -> i j d", i=128),
                              r=[C.v_d], w=[(v, ("in", j0))])
                else:
                    for j0 in range(nin):
                        src = vsrc[64 * dil + j0 * 128 * dil:64 * dil + (j0 + 1) * 128 * dil, :]
                        S.dma("sp", v[:, 1 + j0, :, 0:64], src.rearrange("(i r) d -> i r d", r=dil),
                              r=[C.v_d], w=[(v, ("in", j0))])
                S.dma("sp", v[64:128, 0, :, 0:64], vsrc[0:64 * dil, :].rearrange("(i r) d -> i r d", r=dil),
                      r=[C.v_d], w=[(v, "first")])
                S.dma("sp", v[0:64, nb - 1, :, 0:64], vsrc[S_ - 64 * dil:S_, :].rearrange("(i r) d -> i r d", r=dil),
                      r=[C.v_d], w=[(v, "last")])

        def store(h, pi, osb):
            dil, nb = pats[pi]
            nqb = nb - 1
            osv = osb[:].rearrange("p (q r) e -> p q r e", r=dil)
            if dil == 1:
                S.dma("sp", C.o_d[pi][:, h, :].rearrange("(q i) e -> i q e", i=128), osb[:], r=[osb],
                      w=[(C.o_d[pi], h)])
            else:
                for q in range(nqb):
                    S.dma("sp", C.o_d[pi][q * 128 * dil:(q + 1) * 128 * dil, h, :].rearrange("(i r) e -> i r e", r=dil),
                          osv[:, q, :, :], r=[osb], w=[(C.o_d[pi], (h, q))])

        pairs = []
        n_ost = 0
        for h in range(16):
            first = True
            for pi, (dil, nb) in enumerate(pats):
                osb = ost[n_ost % 2]
                n_ost += 1
                nqb = nb - 1
                lst = [(r, qp) for r in range(dil) for qp in range(nqb // 2)]
                for idx, (r, qp) in enumerate(lst):
                    pairs.append(dict(h=h, pi=pi, dil=dil, r=r, qp=qp, osb=osb, pre=None,
                                      post=(h, pi, osb) if idx == len(lst) - 1 else None))
        npairs_h = len(pairs) // 16
        for h in range(16):
            if h == 0:
                pairs[0]["pre"] = 0
            if h + 1 < 16:
                pairs[h * npairs_h + 8]["pre"] = h + 1
        NPS = 5

        def stA(i):
            p = pairs[i]
            b = p["h"] % 2
            dil, r, qp = p["dil"], p["r"], p["qp"]
            ps = PSs[i % NPS]
            psv = ps[:].rearrange("p (a b c) -> p a b c", a=2, b=2)
            qb = 2 * qp

            def ksl(j):
                k0 = 1024 - 64 * dil + 128 * dil * j + r
                return kTp[b][:, k0:k0 + 127 * dil + 1:dil]

            def qsl(j, n):
                q0 = 128 * dil * j + r
                return qTh[b][:, q0:q0 + (128 * n - 1) * dil + 1:dil]

            O.mm(ps[:, 0:128], ksl(qb), qsl(qb, 1), True, True, [kTp[b], qTh[b]], [ps])
            O.mm(ps[:, 128:384], ksl(qb + 1), qsl(qb, 2), True, True, [kTp[b], qTh[b]], [ps])
            O.mm(ps[:, 384:512], ksl(qb + 2), qsl(qb + 1, 1), True, True, [kTp[b], qTh[b]], [ps])

        def stB(i):
            ps = PSs[i % NPS]
            pt = PTs[i % 4]
            pm = PTm[i % 4]
            O.act(pt[:].rearrange("p a b c -> p (a b c)"), ps[:], AF.Exp, [ps], [pt], scale=0.125)
            O.tt("dve", pm[:], pt[:], mask[:].unsqueeze(1).to_broadcast([128, 2, 2, 128]), ALU.mult, [pt, mask], [pm])

        def stC(i):
            p = pairs[i]
            b = p["h"] % 2
            dil, r, qp, pi = p["dil"], p["r"], p["qp"], p["pi"]
            v = vP[pi][b]
            pm = PTm[i % 4]
            po = PSo[i % 3]
            osb = p["osb"]
            osv = osb[:].rearrange("p (q r) e -> p q r e", r=dil)
            for qi in range(2):
                qb = 2 * qp + qi
                for ab in range(2):
                    O.mm(po[:, qi, :], pm[:, qi, ab, :], v[:, qb + ab, r, :], ab == 0, ab == 1, [pm, v], [po])
            O.copy("act" if i % 2 else "dve", osv[:, 2 * qp:2 * qp + 2, r, :], po[:], [po], [(osb, (r, qp))])
            if p["post"] is not None:
                store(*p["post"])

        n = len(pairs)
        for i in range(n + 4):
            if 0 <= i - 4 < n:
                stC(i - 4)
            if 0 <= i - 2 < n:
                stB(i - 2)
            if i < n:
                if pairs[i]["pre"] is not None:
                    loads(pairs[i]["pre"])
                stA(i)
        S.barrier()
        S.flush()


def phase4(C, l):
    nc, S, O = C.nc, C.S, C.O
    tri, trif = C.tri, C.trif
    with ExitStack() as st:
        def sb(name, shape, dt):
            return st.enter_context(_sbt(nc, name, list(shape), dt))
        H = [[sb(f"p4H{d}_{i}", [128, 2048], F32) for i in range(2)] for d in range(2)]
        tmpH = [sb(f"p4tmpH{d}", [128, 2048], F32) for d in range(2)]
        hbf = [[sb(f"p4hbf{d}_{i}", [128, 2048], BF16) for i in range(2)] for d in range(2)]
        xs_t = [sb(f"p4xs{i}", [128, 2048], BF16) for i in range(4)]
        Bt_t = [sb(f"p4Bt{i}", [128, 512], BF16) for i in range(4)]
        ew = [sb(f"p4ew{i}", [128, 2, 32], F32) for i in range(4)]
        dtw = [sb(f"p4dtw{i}", [128, 32], F32) for i in range(2)]
        xw = [sb(f"p4xw{i}", [128, 2048], BF16) for i in range(2)]
        st_sb = [sb(f"p4st{i}", [128, 2048], F32) for i in range(3)]
        PSw = [st.enter_context(_pst(nc, f"p4psw{i}", [128, 2, 32], F32)) for i in range(2)]
        PSs = [st.enter_context(_pst(nc, f"p4pss{i}", [128, 512], F32)) for i in range(4)]
        for d in range(2):
            O.memset("dve", H[d][0][:], 0.0, [H[d][0]])
            O.memset("pool", hbf[d][0][:], 0.0, [hbf[d][0]])

        def stA1(n):
            i, d = divmod(n, 2)
            c = i if d == 0 else NT_ - 1 - i
            xt, bt, e_, dw, pw = xs_t[n % 4], Bt_t[n % 4], ew[n % 4], dtw[n % 2], PSw[n % 2]
            S.dma("sp", xt[:], C.xs_d[c * 128:(c + 1) * 128, :], r=[C.xs_d], w=[xt])
            S.dma("sp", bt[:], C.Bt_d[c * 128:(c + 1) * 128, :], r=[C.Bt_d], w=[bt])
            la_c = C.la_all[:, c, d * 32:(d + 1) * 32]
            dt_c = C.dt_all[:, c, d * 32:(d + 1) * 32]
            O.mm(pw[:, 0, :], trif[:, 1 if d == 0 else 0, :], la_c, True, True, [trif, C.la_all], [pw])
            O.mm(pw[:, 1, :], trif[:, 2, :], la_c, True, True, [trif, C.la_all], [pw])
            O.act(e_[:], pw[:], AF.Exp, [pw], [e_])
            O.tt("dve", dw[:], dt_c, e_[:, 0, :], ALU.mult, [C.dt_all, e_], [dw])

        def stA2(n):
            xt, dw, xw_ = xs_t[n % 4], dtw[n % 2], xw[n % 2]
            O.tt("pool", xw_[:].rearrange("p (h d) -> p h d", d=64), xt[:].rearrange("p (h d) -> p h d", d=64),
                 dw[:].unsqueeze(2).to_broadcast([128, 32, 64]), ALU.mult, [xt, dw], [xw_])

        def stA3(n):
            bt, xw_ = Bt_t[n % 4], xw[n % 2]
            ss_ = st_sb[n % 3]
            for g in range(4):
                O.mm(PSs[g][:], bt[:, g * 128:(g + 1) * 128], xw_[:, g * 512:(g + 1) * 512], True, True, [bt, xw_], [PSs[g]])
                O.copy("act", ss_[:, g * 512:(g + 1) * 512], PSs[g][:], [PSs[g]], [(ss_, g)])

        def stB(n):
            i, d = divmod(n, 2)
            c = i if d == 0 else NT_ - 1 - i
            eng = "dve"
            e_ = ew[n % 4]
            hb = hbf[d][i % 2]
            Hs, Hd = H[d][i % 2], H[d][(i + 1) % 2]
            S.dma("sp", C.hin_d[d][c], hb[:], r=[hb], w=[(C.hin_d[d], c)])
            O.tt(eng, tmpH[d][:].rearrange("p (h d) -> p h d", d=64), Hs[:].rearrange("p (h d) -> p h d", d=64),
                 e_[:, 1, :].unsqueeze(2).to_broadcast([128, 32, 64]), ALU.mult, [Hs, e_], [tmpH[d]])
            O.tt(eng, Hd[:], tmpH[d][:], st_sb[n % 3][:], ALU.add, [tmpH[d], st_sb[n % 3]], [Hd])

        def stC(n):
            i, d = divmod(n, 2)
            O.copy("act", hbf[d][(i + 1) % 2][:], H[d][(i + 1) % 2][:], [H[d][(i + 1) % 2]], [hbf[d][(i + 1) % 2]])

        NN = 2 * NT_
        order1 = [(stB, 3), (stC, 4), (stA3, 2), (stA2, 1), (stA1, 0)]
        for it in range(NN + 4):
            for fn, skew in order1:
                n = it - skew
                if 0 <= n < NN:
                    fn(n)
        S.barrier()
        S.flush()
    with ExitStack() as st:
        def sb(name, shape, dt):
            return st.enter_context(_sbt(nc, name, list(shape), dt))
        NB = 3
        xs_t = [sb(f"p4xs{i}", [128, 2048], BF16) for i in range(NB)]
        BT_t = [sb(f"p4BT{i}", [128, 4, 128], BF16) for i in range(NB)]
        CT_t = [sb(f"p4CT{i}", [128, 4, 128], BF16) for i in range(NB)]
        sz_t = [sb(f"p4sz{i}", [128, 2048], BF16) for i in range(NB)]
        hh_t = [[sb(f"p4hh{d}_{i}", [128, 2048], BF16) for i in range(NB)] for d in range(2)]
        cbm = [[sb(f"p4cbm{d}_{i}", [128, 4, 128], BF16) for i in range(2)] for d in range(2)]
        ecum = [[sb(f"p4ecum{d}_{i}", [128, 32], F32) for i in range(2)] for d in range(2)]
        rseg = [[sb(f"p4rseg{d}_{i}", [128, 8, 128], BF16) for i in range(2)] for d in range(2)]
        xd = [[sb(f"p4xd{d}_{i}", [128, 512], BF16) for i in range(4)] for d in range(2)]
        eseg = [sb(f"p4eseg{i}", [128, 4, 128], BF16) for i in range(8)]
        MT = [sb(f"p4MT{i}", [128, 4, 128], BF16) for i in range(8)]
        tt_ = [[sb(f"p4t{d}_{i}", [128, 512], BF16) for i in range(2)] for d in range(2)]
        xsD = [sb(f"p4xsD{i}", [128, 512], BF16) for i in range(2)]
        yy = [sb(f"p4y{i}", [128, 512], F32) for i in range(3)]
        ynf = [sb(f"p4ynf{i}", [128, 512], BF16) for i in range(2)]
        ssg = [sb(f"p4ssg{i}", [128, 1], F32) for i in range(2)]
        sqj = sb("p4sqj", [128, 512], BF16)
        ynT_st = [sb(f"p4ynT{i}", [128, 16, 128], BF16) for i in range(3)]
        nwb = sb("p4nwb", [128, 2048], F32)
        Dbc = sb("p4Dbc", [128, 32], F32)
        PScb = st.enter_context(_pst(nc, "p4pscb", [128, 512], F32))
        PSy = [st.enter_context(_pst(nc, f"p4psy{i}", [128, 512], F32)) for i in range(2)]
        PSseg = [st.enter_context(_pst(nc, f"p4psseg{i}", [128, 512], F32)) for i in range(2)]
        PSo = [st.enter_context(_pst(nc, f"p4pso{i}", [128, 512], F32)) for i in range(2)]
        PSt = st.enter_context(_pst(nc, "p4pst", [128, 512], F32))
        S.dma("sp", nwb[:], C.snw[l].partition_broadcast(128), r=[], w=[nwb])
        S.dma("sp", Dbc[:], C.sdd[l].partition_broadcast(128), r=[], w=[Dbc])

        def loads(c):
            b = c % NB
            S.dma("sp", xs_t[b][:], C.xs_d[c * 128:(c + 1) * 128, :], r=[C.xs_d], w=[xs_t[b]])
            S.dma("sp", BT_t[b][:], C.BT_d[:, c * 128:(c + 1) * 128].rearrange("(g n) t -> n g t", n=128), r=[C.BT_d], w=[BT_t[b]])
            S.dma("sp", CT_t[b][:], C.CT_d[:, c * 128:(c + 1) * 128].rearrange("(g n) t -> n g t", n=128), r=[C.CT_d], w=[CT_t[b]])
            S.dma("sp", sz_t[b][:], C.sz_d[c * 128:(c + 1) * 128, :], r=[C.sz_d], w=[sz_t[b]])
            for d in range(2):
                S.dma("sp", hh_t[d][b][:], C.hin_d[d][c], r=[C.hin_d[d]], w=[hh_t[d][b]])

        units = [(d, q) for d in range(2) for q in range(2)]

        def s1(k):
            c, g = divmod(k, 4)
            b, cp = c % NB, c % 2
            xt, BT, CT = xs_t[b], BT_t[b], CT_t[b]
            if g == 0:
                pcb = PScb[:].rearrange("p (g t) -> p g t", g=4)
                for g2 in range(4):
                    O.mm(pcb[:, g2, :], BT[:, g2, :], CT[:, g2, :], True, True, [BT, CT], [PScb])
                for d in range(2):
                    O.tt("dve", cbm[d][cp][:], pcb, tri[:, d, :].unsqueeze(1).to_broadcast([128, 4, 128]), ALU.mult,
                         [PScb, tri], [cbm[d][cp]])
                pse = PScb[:, 0:64].rearrange("p (d h) -> p d h", d=2)
                for d in range(2):
                    la_c = C.la_all[:, c, d * 32:(d + 1) * 32]
                    O.mm(pse[:, d, :], trif[:, 3 + d, :], la_c, True, True, [trif, C.la_all], [PScb])
                for d in range(2):
                    O.act(ecum[d][cp][:], pse[:, d, :], AF.Exp, [PScb], [ecum[d][cp]])
            for d in range(2):
                la_g = C.la_all[:, c, d * 32 + g * 8:d * 32 + (g + 1) * 8]
                dt_g = C.dt_all[:, c, d * 32 + g * 8:d * 32 + (g + 1) * 8]
                O.tt("dve", rseg[d][k % 2][:], la_g.unsqueeze(2).to_broadcast([128, 8, 128]),
                     tri[:, d, :].unsqueeze(1).to_broadcast([128, 8, 128]), ALU.mult, [C.la_all, tri], [rseg[d][k % 2]])
                O.tt("pool", xd[d][k % 4][:].rearrange("p (h e) -> p h e", e=64),
                     xt[:, g * 512:(g + 1) * 512].rearrange("p (h e) -> p h e", e=64),
                     dt_g.unsqueeze(2).to_broadcast([128, 8, 64]), ALU.mult, [xt, C.dt_all], [xd[d][k % 4]])

        def s2(k):
            for u, (d, q) in enumerate(units):
                n = k * 4 + u
                pss = PSseg[n % 2]
                es = eseg[n % 8]
                O.mm(pss[:], tri[:, 3 - d, :], rseg[d][k % 2][:, q * 4:(q + 1) * 4, :].rearrange("p h t -> p (h t)"), True, True,
                     [tri, rseg[d][k % 2]], [pss])
                O.act(es[:].rearrange("p h t -> p (h t)"), pss[:], AF.Exp, [pss], [es])

        def s3(k):
            c, g = divmod(k, 4)
            cp = c % 2
            for u, (d, q) in enumerate(units):
                n = k * 4 + u
                O.tt("dve" if u % 2 else "pool", MT[n % 8][:], eseg[n % 8][:],
                     cbm[d][cp][:, g, :].unsqueeze(1).to_broadcast([128, 4, 128]), ALU.mult, [eseg[n % 8], cbm[d][cp]], [MT[n % 8]])

        def s4(k):
            for u, (d, q) in enumerate(units):
                n = k * 4 + u
                mt = MT[n % 8]
                for hh in range(4):
                    h8 = q * 4 + hh
                    O.mm(PSy[k % 2][:, h8 * 64:(h8 + 1) * 64], mt[:, hh, :], xd[d][k % 4][:, h8 * 64:(h8 + 1) * 64],
                         u == 0 and hh == 0, False, [mt, xd[d][k % 4]], [PSy[k % 2]])

        def s5(k):
            c, g = divmod(k, 4)
            b, cp, kp = c % NB, c % 2, k % 2
            xt, CT, sz = xs_t[b], CT_t[b], sz_t[b]
            for d in range(2):
                hd = hh_t[d][b]
                O.mm(PSo[d][:], CT[:, g, :], hd[:, g * 512:(g + 1) * 512], True, True, [CT, hd], [PSo[d]])
            O.tt("pool", xsD[kp][:].rearrange("p (h e) -> p h e", e=64),
                 xt[:, g * 512:(g + 1) * 512].rearrange("p (h e) -> p h e", e=64),
                 Dbc[:, g * 8:(g + 1) * 8].unsqueeze(2).to_broadcast([128, 8, 64]), ALU.mult, [xt, Dbc], [xsD[kp]])
            for d in range(2):
                O.tt("dve", tt_[d][kp][:].rearrange("p (h e) -> p h e", e=64), PSo[d][:].rearrange("p (h e) -> p h e", e=64),
                     ecum[d][cp][:, g * 8:(g + 1) * 8].unsqueeze(2).to_broadcast([128, 8, 64]), ALU.mult,
                     [PSo[d], ecum[d][cp]], [tt_[d][kp]])

        def s5b(k):
            c, g = divmod(k, 4)
            b, cp, kp = c % NB, c % 2, k % 2
            sz = sz_t[b]
            O.mm(PSy[kp][:], C.ident[:], xsD[kp][:], False, False, [C.ident, xsD[kp]], [PSy[kp]])
            O.mm(PSy[kp][:], C.ident[:], tt_[0][kp][:], False, False, [C.ident, tt_[0][kp]], [PSy[kp]])
            O.mm(PSy[kp][:], C.ident[:], tt_[1][kp][:], False, True, [C.ident, tt_[1][kp]], [PSy[kp]])
            y_ = yy[k % 3]
            O.tt("dve", y_[:], PSy[kp][:], sz[:, g * 512:(g + 1) * 512], ALU.mult, [PSy[kp], sz], [y_])

        def s6(k):
            c, g = divmod(k, 4)
            y_ = yy[k % 3]
            s_ = ssg[k % 2]
            O.act(sqj[:], y_[:], AF.Square, [y_], [sqj, s_], accum_out=s_[:])
            O.act(s_[:], s_[:], AF.Sqrt, [s_], [s_], bias=1e-6, scale=1.0 / 512)
            O.recip(s_[:], s_[:], [s_], [s_])
            O.stt("dve", ynf[k % 2][:], y_[:], s_[:], nwb[:, g * 512:(g + 1) * 512], ALU.mult, ALU.mult, [y_, s_, nwb], [ynf[k % 2]])

        def s7(k):
            c, g = divmod(k, 4)
            ptb = PSt[:, 0:256].bitcast(BF16).rearrange("p (a t) -> p a t", a=4)
            ys = ynT_st[c % 3]
            yn_ = ynf[k % 2]
            for a in range(4):
                O.tr(ptb[:, a, :], yn_[:, a * 128:(a + 1) * 128], C.ident[:], [yn_, C.ident], [PSt])
            O.copy("act", ys[:, g * 4:(g + 1) * 4, :], ptb, [PSt], [(ys, g)])
            if g == 3:
                S.dma("sp", C.ynT_d[c // 2, :, :, (c % 2) * 128:(c % 2 + 1) * 128], ys[:], r=[ys],
                      w=[(C.ynT_d, c)])

        order = [(s5, 4), (s7, 6), (s2, 1), (s1, 0), (s3, 2), (s4, 3), (s6, 5), (s5b, 4)]
        NK = NT_ * 4
        for c in range(NB):
            loads(c)
        for it in range(NK + 6):
            if it >= 8 and it % 4 == 0:
                cn = (it - 8) // 4 + NB
                if cn < NT_:
                    loads(cn)
            for fn, skew in order:
                k = it - skew
                if 0 <= k < NK:
                    fn(k)
        S.barrier()
        S.flush()


def phase5(C, l, x_src, x_dst):
    nc, S, O = C.nc, C.S, C.O
    with ExitStack() as st:
        def sb(name, shape, dt):
            return st.enter_context(_sbt(nc, name, list(shape), dt))
        TB = 256
        NTB = S_ // TB
        wa = sb("p5wa", [128, 8, D_], BF16)
        wb_ = sb("p5wb", [128, 8, D_], BF16)
        wc = sb("p5wc", [128, 16, D_], BF16)
        wo = sb("p5wo", [128, 8, D_], BF16)
        yA_b = [sb(f"p5yA{i}", [128, 8, TB], BF16) for i in range(2)]
        yn_b = [sb(f"p5yn{i}", [128, 16, TB], BF16) for i in range(2)]
        oT_b = [sb(f"p5oT{i}", [128, 8, TB], BF16) for i in range(2)]
        g_b = [sb(f"p5g{i}", [128, 24, TB], BF16) for i in range(2)]
        mT = [sb(f"p5mT{i}", [128, 8, TB], BF16) for i in range(1)] * 2
        ot = [[sb(f"p5ot{p}_{i}", [128, 16, 65], F32) for p in range(3)] for i in range(2)]
        rden = [sb(f"p5rden{i}", [128, 16], F32) for i in range(2)]
        ob2 = [sb(f"p5ob{i}", [128, 16, 64], BF16) for i in range(2)]
        xt = [sb(f"p5xt{i}", [128, D_], F32) for i in range(1)] * 2
        xo = [sb(f"p5xo{i}", [128, D_], F32) for i in range(1)] * 2
        t1 = [sb(f"p5t1_{i}", [128, TB], F32) for i in range(2)]
        t2 = [sb(f"p5t2_{i}", [128, TB], F32) for i in range(2)]
        PB = [st.enter_context(_pst(nc, f"p5ps{i}", [128, 512], F32)) for i in range(6)]
        PT = [st.enter_context(_pst(nc, f"p5pt{i}", [128, 8, 128], BF16)) for i in range(2)]
        S.dma("pool", wa[:], C.w_a[l].rearrange("(k p) n -> p k n", p=128), r=[], w=[wa])
        S.dma("pool", wb_[:], C.w_b[l].rearrange("(k p) n -> p k n", p=128), r=[], w=[wb_])
        S.dma("pool", wc[:], C.w_c[l].rearrange("(k p) n -> p k n", p=128), r=[], w=[wc])
        S.dma("pool", wo[:], C.w_o[l].rearrange("(k p) n -> p k n", p=128), r=[], w=[wo])
        cnt = {"ps": 0, "o": 0}

        def PS():
            p = PB[cnt["ps"] % 6]
            cnt["ps"] += 1
            return p

        def loads(tb):
            b = tb % 2
            tsl = slice(tb * TB, (tb + 1) * TB)
            S.dma("sp", yA_b[b][:], C.yAT[tb], r=[C.yAT], w=[yA_b[b]])
            S.dma("sp", yn_b[b][:], C.ynT_d[tb], r=[C.ynT_d], w=[yn_b[b]])
            S.dma("sp", g_b[b][:], C.gT[tb], r=[C.gT], w=[g_b[b]])

        def oloads(tb):
            for tt in range(TB // 128):
                t = tb * (TB // 128) + tt
                for p in range(3):
                    S.dma("sp", ot[tt][p][:], C.o_d[p][t * 128:(t + 1) * 128], r=[C.o_d[p]], w=[ot[tt][p]])

        def combine_pool(tb):
            for tt in range(TB // 128):
                o3 = ot[tt]
                O.tt("pool", o3[0][:], o3[0][:], o3[1][:], ALU.add, [o3[0], o3[1]], [o3[0]])
                O.tt("pool", o3[0][:], o3[0][:], o3[2][:], ALU.add, [o3[0], o3[2]], [o3[0]])
                O.recip(rden[tt][:].unsqueeze(2), o3[0][:, :, 64:65], [o3[0]], [rden[tt]])
                O.tt("pool", ob2[tt][:], o3[0][:, :, 0:64], rden[tt][:].unsqueeze(2).to_broadcast([128, 16, 64]), ALU.mult,
                     [o3[0], rden[tt]], [ob2[tt]])

        def combine_pe(tb):
            b = tb % 2
            for tt in range(TB // 128):
                obf = ob2[tt][:].rearrange("p h d -> p (h d)")
                for k in range(8):
                    O.tr(PT[tt][:, k, :], obf[:, k * 128:(k + 1) * 128], C.ident[:], [ob2[tt], C.ident], [PT[tt]])
                O.copy("act", oT_b[b][:, :, tt * 128:(tt + 1) * 128], PT[tt][:], [PT[tt]], [(oT_b[b], tt)])

        loads(0)
        oloads(0)
        combine_pool(0)
        combine_pe(0)
        nx = 0
        for tb in range(NTB):
            b = tb % 2
            if tb + 1 < NTB:
                loads(tb + 1)
                oloads(tb + 1)
            if tb + 1 < NTB:
                combine_pool(tb + 1)
            for cc in range(8):
                pa, pb, pc = PS(), PS(), PS()
                for k in range(8):
                    O.mm(pa[:, :TB], wa[:, k, cc * 128:(cc + 1) * 128], yA_b[b][:, k, :], k == 0, k == 7, [wa, yA_b[b]], [pa])
                for k in range(8):
                    O.mm(pb[:, :TB], wb_[:, k, cc * 128:(cc + 1) * 128], oT_b[b][:, k, :], k == 0, k == 7, [wb_, oT_b[b]], [pb])
                for k in range(16):
                    O.mm(pc[:, :TB], wc[:, k, cc * 128:(cc + 1) * 128], yn_b[b][:, k, :], k == 0, k == 15, [wc, yn_b[b]], [pc])
                a1, a2 = t1[cc % 2], t2[cc % 2]
                O.tt("dve", a1[:], pa[:, :TB], g_b[b][:, cc, :], ALU.mult, [pa, g_b[b]], [a1])
                O.tt("dve", a2[:], pb[:, :TB], g_b[b][:, 8 + cc, :], ALU.mult, [pb, g_b[b]], [a2])
                O.tt("dve", a1[:], a1[:], a2[:], ALU.add, [a1, a2], [a1])
                O.tt("dve", a2[:], pc[:, :TB], g_b[b][:, 16 + cc, :], ALU.mult, [pc, g_b[b]], [a2])
                O.tt("dve", mT[b][:, cc, :], a1[:], a2[:], ALU.add, [a1, a2], [(mT[b], cc)])
            if tb + 1 < NTB:
                combine_pe(tb + 1)
            for tt in range(TB // 128):
                t = tb * (TB // 128) + tt
                xt_, xo_ = xt[nx % 2], xo[nx % 2]
                nx += 1
                S.dma("sp", xt_[:], x_src[t * 128:(t + 1) * 128, :], r=[(x_src, t)], w=[xt_])
                for hf in range(2):
                    ps = PS()
                    for k in range(8):
                        O.mm(ps[:], mT[b][:, k, tt * 128:(tt + 1) * 128], wo[:, k, hf * 512:(hf + 1) * 512], k == 0, k == 7,
                             [mT[b], wo], [ps])
                    O.tt("dve", xo_[:, hf * 512:(hf + 1) * 512], ps[:], xt_[:, hf * 512:(hf + 1) * 512], ALU.add,
                         [ps, xt_], [(xo_, hf)])
                S.dma("sp", C.xmid[t * 128:(t + 1) * 128, :], xo_[:], r=[xo_], w=[(C.xmid, t)])
        S.barrier()
        S.flush()
    if "xmid" in C.dbg and C.stop == (l, 5):
        return
    with ExitStack() as st:
        def sb(name, shape, dt):
            return st.enter_context(_sbt(nc, name, list(shape), dt))
        w2 = sb("p5w2", [128, 32, D_], BF16)
        w1b = [sb(f"p5w1_{i}", [128, 8, 512], BF16) for i in range(3)]
        xts = [sb(f"p5x{i}", [128, D_], F32) for i in range(4)]
        hT = sb("p5hT", [128, 8, 512], BF16)
        h1T = sb("p5h1T", [128, 32, 512], BF16)
        ub = [sb(f"p5ub{i}", [128, D_], BF16) for i in range(2)]
        sq = sb("p5sq", [128, D_], BF16)
        ss = [sb(f"p5ss{i}", [128, 1], F32) for i in range(2)]
        rr = [sb(f"p5r{i}", [128, 512], F32) for i in range(2)]
        xo = [sb(f"p5xo{i}", [128, D_], F32) for i in range(2)]
        nwt = sb("p5nw", [128, D_], F32)
        fnw = sb("p5fnw", [128, D_], F32)
        PB = [st.enter_context(_pst(nc, f"p5bps{i}", [128, 512], F32)) for i in range(6)]
        PT = [st.enter_context(_pst(nc, f"p5bpt{i}", [128, 8, 128], BF16)) for i in range(2)]
        w2src = C.w_2[l].rearrange("(k p) n -> p k n", p=128)
        for k8 in range(4):
            S.dma("pool", w2[:, k8 * 8:(k8 + 1) * 8, :], w2src[:, k8 * 8:(k8 + 1) * 8, :], r=[], w=[(w2, k8)])
        S.dma("sp", nwt[:], C.mlpw[l:l + 1, :].partition_broadcast(128), r=[], w=[nwt])
        S.dma("sp", fnw[:], C.finw.partition_broadcast(128), r=[], w=[fnw])
        w1src = C.wb_1[l].rearrange("(k p) n -> p k n", p=128)
        nps = 0
        nw1 = 0
        for tb in range(8):
            for tt in range(4):
                t = tb * 4 + tt
                S.dma("sp", xts[tt][:], C.xmid[t * 128:(t + 1) * 128, :], r=[(C.xmid, t)], w=[xts[tt]])
                rms_tile(C, xts[tt], ss[tt % 2], sq, nwt, ub[tt % 2], "p5")
                for k in range(8):
                    O.tr(PT[tt % 2][:, k, :], ub[tt % 2][:, k * 128:(k + 1) * 128], C.ident[:], [ub[tt % 2], C.ident], [PT[tt % 2]])
                O.copy("act", hT[:, :, tt * 128:(tt + 1) * 128], PT[tt % 2][:], [PT[tt % 2]], [(hT, tt)])
            for f4 in range(8):
                w1t = w1b[nw1 % 3]
                nw1 += 1
                S.dma("sp", w1t[:], w1src[:, :, f4 * 512:(f4 + 1) * 512], r=[C.wb_1[l]], w=[w1t])
                for fj in range(4):
                    fc = f4 * 4 + fj
                    ps = PB[nps % 6]
                    r_ = rr[nps % 2]
                    nps += 1
                    for k in range(8):
                        O.mm(ps[:], w1t[:, k, fj * 128:(fj + 1) * 128], hT[:, k, :], k == 0, k == 7, [w1t, hT], [ps])
                    O.act(r_[:], ps[:], AF.Relu, [ps], [r_])
                    O.tt("pool" if fc % 2 else "dve", h1T[:, fc, :], r_[:], r_[:], ALU.mult, [r_], [(h1T, fc)])
            for tt in range(4):
                t = tb * 4 + tt
                xo_ = xo[tt % 2]
                for hf in range(2):
                    ps = PB[nps % 6]
                    nps += 1
                    for fc in range(32):
                        O.mm(ps[:], h1T[:, fc, tt * 128:(tt + 1) * 128], w2[:, fc, hf * 512:(hf + 1) * 512], fc == 0, fc == 31,
                             [h1T, w2], [ps])
                    O.tt("dve", xo_[:, hf * 512:(hf + 1) * 512], ps[:], xts[tt][:, hf * 512:(hf + 1) * 512], ALU.add,
                         [ps, xts[tt]], [(xo_, hf)])
                if x_dst is not None:
                    S.dma("sp", x_dst[t * 128:(t + 1) * 128, :], xo_[:], r=[xo_], w=[(x_dst, t)])
                else:
                    s_ = ss[tt % 2]
                    O.act(sq[:], xo_[:], AF.Square, [xo_], [sq, s_], accum_out=s_[:])
                    O.act(s_[:], s_[:], AF.Sqrt, [s_], [s_], bias=1e-6, scale=1.0 / D_)
                    O.recip(s_[:], s_[:], [s_], [s_])
                    O.stt("dve", xo_[:], xo_[:], s_[:], fnw[:], ALU.mult, ALU.mult, [xo_, s_, fnw], [xo_])
                    S.dma("sp", C.y_out[t * 128:(t + 1) * 128, :], xo_[:], r=[xo_], w=[(C.y_out, t)])
        S.barrier()
        S.flush()


def host_consts():
    bf = ml_dtypes.bfloat16
    p = np.arange(128)[:, None]
    f = np.arange(128)[None, :]
    tri = np.stack([(p <= f), (p >= f), (p < f), (p > f)], axis=1).astype(np.float32)
    trif = np.stack([(p < f), (p > f), np.ones((128, 128), bool), (p <= f), (p >= f)], axis=1).astype(np.float32)
    invf = (500000.0 ** (-np.arange(0, 16, 2, dtype=np.float32) / 16.0)).astype(np.float32)[None, :]
    return {"c_ident": np.eye(128, dtype=np.float32).astype(bf), "c_tri": tri.astype(bf), "c_trif": trif, "c_invf": invf}


def prep_inputs(inp):
    f32 = np.float32
    sh = dict(host_consts())
    for k in ("mix_norm_w", "mlp_norm_w", "w_in", "w_a_out", "w_b_out", "w_c_out", "w_o", "w_ff1", "w_ff2"):
        sh[k] = np.ascontiguousarray(inp[k], dtype=f32)
    sh["final_norm_w"] = np.ascontiguousarray(inp["final_norm_w"], dtype=f32).reshape(1, D_)
    sh["conv_a_wT"] = np.ascontiguousarray(inp["conv_a_w"].reshape(L_, 3, 8, 128).transpose(0, 3, 2, 1), dtype=f32)
    sh["ssd_conv_wT"] = np.ascontiguousarray(inp["ssd_conv_w"].reshape(L_, 5, 24, 128).transpose(0, 3, 2, 1), dtype=f32)
    sh["ssd_conv_bT"] = np.ascontiguousarray(inp["ssd_conv_b"].reshape(L_, 24, 128).transpose(0, 2, 1), dtype=f32)
    sh["ssd_conv_bR"] = np.ascontiguousarray(inp["ssd_conv_b"].reshape(L_, 1, 3072), dtype=f32)
    sh["ssd_a_log"] = np.ascontiguousarray(inp["ssd_a_log"].reshape(L_, 1, 64), dtype=f32)
    sh["ssd_dt_bias"] = np.ascontiguousarray(inp["ssd_dt_bias"].reshape(L_, 1, 64), dtype=f32)
    sh["ssd_d"] = np.ascontiguousarray(inp["ssd_d"].reshape(L_, 1, 32), dtype=f32)
    sh["ssd_norm_w"] = np.ascontiguousarray(inp["ssd_norm_w"].reshape(L_, 1, 2048), dtype=f32)
    per = []
    for b in range(inp["x"].shape[0]):
        d = dict(sh)
        d["x"] = np.ascontiguousarray(inp["x"][b], dtype=f32)
        d["pos"] = np.ascontiguousarray(inp["positions"][b].reshape(NT_, 128).T, dtype=np.int32)
        per.append(d)
    return per


_NC_CACHE = {}


def kernel(**inputs):
    per = prep_inputs(inputs)
    if "nc" not in _NC_CACHE:
        _NC_CACHE["nc"] = build()
    nc = _NC_CACHE["nc"]
    res = run_bass_kernel_spmd(nc, per, core_ids=list(range(len(per))))
    return np.stack([np.asarray(r["y"], dtype=np.float32) for r in res.results], axis=0)
```

```python
import numpy as np
import ml_dtypes
import concourse.bass as bass
import concourse.mybir as mybir
from concourse.bass_utils import run_bass_kernel_spmd
from contextlib import ExitStack
from types import SimpleNamespace

F32 = mybir.dt.float32
BF16 = mybir.dt.bfloat16
I32 = mybir.dt.int32
AF = mybir.ActivationFunctionType
ALU = mybir.AluOpType
AX = mybir.AxisListType

ENG = ("pe", "act", "dve", "pool", "sp")
DMAQ = ("sp", "act", "pool")


class _Buf:
    __slots__ = ("w", "r")

    def __init__(self):
        self.w = None
        self.r = {}


class _TBuf:
    __slots__ = ("whole", "subs")

    def __init__(self):
        self.whole = _Buf()
        self.subs = {}


class Sched:
    NRING = 8
    SAME_ENGINE_SYNC = ("act", "dve", "pool")

    def __init__(self, nc):
        self.nc = nc
        self.sems = []
        self.esem = {}
        for e in ENG:
            self.esem[e] = self._new_sem("s_" + e)
        self.ecnt = {e: 0 for e in ENG}
        self.ring = {q: [self._new_sem(f"d_{q}{i}") for i in range(self.NRING)] for q in DMAQ}
        self.dman = {q: 0 for q in DMAQ}
        self.known = {e: {} for e in ENG}
        self.streams = {e: [] for e in ENG}
        self.tb = {}
        self.n_wait = 0
        self.n_ins = 0

    def _new_sem(self, name):
        h = self.nc.alloc_semaphore(name=name)
        self.sems.append(h)
        return len(self.sems) - 1

    def _spec(self, a):
        if isinstance(a, tuple):
            t, key = a
        else:
            t, key = a, None
        name = t if isinstance(t, str) else t.name
        tb = self.tb.get(name)
        if tb is None:
            tb = self.tb[name] = _TBuf()
        return tb, key

    def _deps(self, eng, reads, writes):
        need = {}

        def add(ev):
            if ev is not None and need.get(ev[0], 0) < ev[1]:
                need[ev[0]] = ev[1]

        def addr(b):
            for s, v in b.r.items():
                if need.get(s, 0) < v:
                    need[s] = v

        for a in reads:
            tb, key = self._spec(a)
            add(tb.whole.w)
            if key is None:
                for sb in tb.subs.values():
                    add(sb.w)
            else:
                sb = tb.subs.get(key)
                if sb is not None:
                    add(sb.w)
        for a in writes:
            tb, key = self._spec(a)
            add(tb.whole.w)
            addr(tb.whole)
            if key is None:
                for sb in tb.subs.values():
                    add(sb.w)
                    addr(sb)
            else:
                sb = tb.subs.get(key)
                if sb is not None:
                    add(sb.w)
                    addr(sb)
        kn = self.known[eng]
        own = self.esem[eng]
        waits = []
        for s, v in need.items():
            if s == own and eng not in self.SAME_ENGINE_SYNC:
                continue
            if kn.get(s, 0) < v:
                kn[s] = v
                waits.append((s, v))
        return waits

    def _record(self, ev, reads, writes):
        s, v = ev
        for a in reads:
            tb, key = self._spec(a)
            if key is None:
                b = tb.whole
            else:
                b = tb.subs.get(key)
                if b is None:
                    b = tb.subs[key] = _Buf()
            if b.r.get(s, 0) < v:
                b.r[s] = v
        for a in writes:
            tb, key = self._spec(a)
            if key is None:
                tb.whole.w = ev
                tb.whole.r = {}
                tb.subs = {}
            else:
                b = tb.subs.get(key)
                if b is None:
                    b = tb.subs[key] = _Buf()
                b.w = ev
                b.r = {}

    def op(self, eng, emit, r=(), w=()):
        waits = self._deps(eng, r, w)
        self.ecnt[eng] += 1
        ev = (self.esem[eng], self.ecnt[eng])
        self._record(ev, r, w)
        self.streams[eng].append((waits, emit, ev[0], 1))
        self.n_wait += len(waits)
        self.n_ins += 1

    def dma(self, q, out, in_, r=(), w=(), **kw):
        waits = self._deps(q, r, w)
        i = self.dman[q]
        self.dman[q] += 1
        s = self.ring[q][i % self.NRING]
        rnd = i // self.NRING
        if rnd > 0:
            kn = self.known[q]
            if kn.get(s, 0) < 16 * rnd:
                kn[s] = 16 * rnd
                waits.append((s, 16 * rnd))
        ev = (s, 16 * (rnd + 1))
        self._record(ev, r, w)
        self.streams[q].append((waits, lambda e: e.dma_start(out=out, in_=in_, **kw), s, 16))
        self.n_wait += len(waits)
        self.n_ins += 1

    def barrier(self, skip_q=()):
        evs = []
        for e in ENG:
            if self.ecnt[e] > 0:
                evs.append((self.esem[e], self.ecnt[e]))
        for q in DMAQ:
            if q in skip_q:
                continue
            n = self.dman[q]
            for j in range(min(n, self.NRING)):
                last = ((n - 1 - j) // self.NRING) * self.NRING + j
                evs.append((self.ring[q][j], 16 * (last // self.NRING + 1)))
        for e in ENG:
            kn = self.known[e]
            waits = []
            for s, v in evs:
                if kn.get(s, 0) < v:
                    kn[s] = v
                    waits.append((s, v))
            if waits:
                self.streams[e].append((waits, None, None, 0))
                self.n_wait += len(waits)

    def flush(self):
        nc = self.nc
        sems = self.sems

        def run(name):
            items = self.streams[name]
            self.streams[name] = []

            def f(e):
                for waits, emit, s, inc in items:
                    for ws, wv in waits:
                        e.wait_ge(sems[ws], wv)
                    if emit is not None:
                        ins = emit(e)
                        ins.then_inc(sems[s], inc)

            return f

        with nc.Block() as block:
            block.tensor(run("pe"))
            block.scalar(run("act"))
            block.vector(run("dve"))
            block.gpsimd(run("pool"))
            block.sync(run("sp"))


S_ = 4096
D_ = 1024
NT_ = 32
L_ = 2
DIN_ = 14400
PI = 3.141592653589793


_UID = [0]


def _sbt(nc, name, shape, dt):
    _UID[0] += 1
    return nc.sbuf_tensor(f"{name}_u{_UID[0]}", shape, dt)


def _pst(nc, name, shape, dt):
    _UID[0] += 1
    return nc.psum_tensor(f"{name}_u{_UID[0]}", shape, dt)


class Ops:
    def __init__(self, S):
        self.S = S

    def mm(self, out, lhsT, rhs, start, stop, r, w):
        self.S.op("pe", lambda e: e.matmul(out, lhsT=lhsT, rhs=rhs, start=start, stop=stop), r, w)

    def tr(self, out, in_, ident, r, w):
        self.S.op("pe", lambda e: e.transpose(out=out, in_=in_, identity=ident), r, w)

    def act(self, out, in_, func, r, w, **kw):
        self.S.op("act", lambda e: e.activation(out=out, in_=in_, func=func, **kw), r, w)

    def tt(self, eng, out, in0, in1, op, r, w):
        self.S.op(eng, lambda e: e.tensor_tensor(out=out, in0=in0, in1=in1, op=op), r, w)

    def ts(self, eng, out, in0, s1, s2, op0, op1, r, w):
        if s2 is None:
            self.S.op(eng, lambda e: e.tensor_scalar(out=out, in0=in0, scalar1=s1, scalar2=None, op0=op0), r, w)
        else:
            self.S.op(eng, lambda e: e.tensor_scalar(out=out, in0=in0, scalar1=s1, scalar2=s2, op0=op0, op1=op1), r, w)

    def stt(self, eng, out, in0, scalar, in1, op0, op1, r, w):
        self.S.op(eng, lambda e: e.scalar_tensor_tensor(out=out, in0=in0, scalar=scalar, in1=in1, op0=op0, op1=op1), r, w)

    def copy(self, eng, out, in_, r, w):
        if eng == "act":
            self.S.op(eng, lambda e: e.copy(out=out, in_=in_), r, w)
        else:
            self.S.op(eng, lambda e: e.tensor_copy(out=out, in_=in_), r, w)

    def memset(self, eng, ap, val, w):
        self.S.op(eng, lambda e: e.memset(ap, val), (), w)

    def recip(self, out, in_, r, w):
        self.S.op("dve", lambda e: e.reciprocal(out=out, in_=in_), r, w)


def build(dbg=(), stop=None, nlayers=L_):
    nc = bass.Bass("TRN2", target_bir_lowering=False)

    def din(name, shape, dt=F32):
        return nc.dram_tensor(name, list(shape), dt, kind="ExternalInput").ap()

    def dscr(name, shape, dt):
        kind = "ExternalOutput" if name in dbg else "Internal"
        return nc.dram_tensor(name, list(shape), dt, kind=kind).ap()

    x_in = din("x", [S_, D_])
    pos_in = din("pos", [128, NT_], I32)
    mixw = din("mix_norm_w", [L_, D_])
    mlpw = din("mlp_norm_w", [L_, D_])
    finw = din("final_norm_w", [1, D_])
    w_in = din("w_in", [L_, D_, DIN_])
    w_a = din("w_a_out", [L_, D_, D_])
    w_b = din("w_b_out", [L_, D_, D_])
    w_c = din("w_c_out", [L_, 2 * D_, D_])
    w_o = din("w_o", [L_, D_, D_])
    w_1 = din("w_ff1", [L_, D_, 4 * D_])
    w_2 = din("w_ff2", [L_, 4 * D_, D_])
    cawT = din("conv_a_wT", [L_, 128, 8, 3])
    scwT = din("ssd_conv_wT", [L_, 128, 24, 5])
    scbT = din("ssd_conv_bT", [L_, 128, 24])
    scbR = din("ssd_conv_bR", [L_, 1, 3072])
    alog = din("ssd_a_log", [L_, 1, 64])
    dtb = din("ssd_dt_bias", [L_, 1, 64])
    sdd = din("ssd_d", [L_, 1, 32])
    snw = din("ssd_norm_w", [L_, 1, 2048])
    c_ident = din("c_ident", [128, 128], BF16)
    c_tri = din("c_tri", [128, 4, 128], BF16)
    c_trif = din("c_trif", [128, 5, 128], F32)
    c_invf = din("c_invf", [1, 8], F32)
    y_out = nc.dram_tensor("y", [S_, D_], F32, kind="ExternalOutput").ap()

    wb_in = [dscr(f"wb_in{l}", [D_, DIN_], BF16) for l in range(L_)]
    wb_a = [dscr(f"wb_a{l}", [D_, D_], BF16) for l in range(L_)]
    wb_b = [dscr(f"wb_b{l}", [D_, D_], BF16) for l in range(L_)]
    wb_c = [dscr(f"wb_c{l}", [2 * D_, D_], BF16) for l in range(L_)]
    wb_o = [dscr(f"wb_o{l}", [D_, D_], BF16) for l in range(L_)]
    wb_1 = [dscr(f"wb_1{l}", [D_, 4 * D_], BF16) for l in range(L_)]
    wb_2 = [dscr(f"wb_2{l}", [4 * D_, D_], BF16) for l in range(L_)]
    yAT = dscr("yAT", [16, 128, 8, 256], BF16)
    gT = dscr("gT", [16, 128, 24, 256], BF16)
    qT = dscr("qT", [D_, S_], BF16)
    kT = dscr("kT", [D_, S_], BF16)
    v_d = dscr("v_d", [S_, D_], BF16)
    sz_d = dscr("sz_d", [S_, 2 * D_], BF16)
    xs_d = dscr("xs_d", [S_, 2 * D_], BF16)
    Bt_d = dscr("Bt_d", [S_, 512], BF16)
    BT_d = dscr("BT_d", [512, S_], BF16)
    CT_d = dscr("CT_d", [512, S_], BF16)
    dt_dbg = dscr("dt_dbg", [128, NT_, 64], F32)
    o_d = [dscr(f"o_d{p}", [S_, 16, 65], F32) for p in range(3)]
    hin_d = [dscr(f"hin_d{d}", [NT_, 128, 2048], BF16) for d in range(2)]
    ynT_d = dscr("ynT_d", [16, 128, 16, 256], BF16)
    xmid = dscr("xmid", [S_, D_], F32)
    xl = [dscr(f"xl{l}", [S_, D_], F32) for l in range(L_ - 1)]

    S = Sched(nc)
    O = Ops(S)

    with ExitStack() as gst:
        def gsb(name, shape, dt):
            return gst.enter_context(_sbt(nc, name, list(shape), dt))

        ident = gsb("ident", [128, 128], BF16)
        tri = gsb("tri", [128, 4, 128], BF16)
        trif = gsb("trif", [128, 5, 128], F32)
        cosT = gsb("cosT", [128, NT_, 8], F32)
        sinT = gsb("sinT", [128, NT_, 8], F32)
        dt_all = gsb("dt_all", [128, NT_, 64], F32)
        la_all = gsb("la_all", [128, NT_, 64], F32)
        S.dma("sp", ident[:], c_ident, r=[], w=[ident])
        S.dma("sp", tri[:], c_tri, r=[], w=[tri])
        S.dma("sp", trif[:], c_trif, r=[], w=[trif])

        def cast_w(src, dst, rows_per):
            R = src.shape[0]
            for r0 in range(0, R, rows_per):
                S.dma("pool", dst[r0:r0 + rows_per, :], src[r0:r0 + rows_per, :], r=[], w=[(dst, r0)])

        with ExitStack() as st:
            def sb(name, shape, dt):
                return st.enter_context(_sbt(nc, name, list(shape), dt))
            posi = sb("posi", [128, NT_], I32)
            posf = sb("posf", [128, NT_], F32)
            invf = sb("invf", [128, 8], F32)
            ang = sb("ang", [128, NT_, 8], F32)
            a1 = sb("a1", [128, NT_, 8], F32)
            S.dma("sp", posi[:], pos_in, r=[], w=[posi])
            S.dma("sp", invf[:], c_invf.partition_broadcast(128), r=[], w=[invf])
            O.copy("dve", posf[:], posi[:], [posi], [posf])
            O.tt("dve", ang[:], posf[:].unsqueeze(2).to_broadcast([128, NT_, 8]),
                 invf[:].unsqueeze(1).to_broadcast([128, NT_, 8]), ALU.mult, [posf, invf], [ang])
            ki = sb("ki", [128, NT_, 8], I32)
            kf = sb("kf", [128, NT_, 8], F32)
            mm_ = sb("mm_", [128, NT_, 8], F32)
            for shift, dstT in ((0.0, sinT), (0.5 * PI, cosT)):
                O.ts("dve", a1[:], ang[:], shift, 1.0 / (2 * PI), ALU.add, ALU.mult, [ang], [a1])
                O.copy("dve", ki[:], a1[:], [a1], [ki])
                O.copy("dve", kf[:], ki[:], [ki], [kf])
                O.ts("dve", a1[:], ang[:], shift, None, ALU.add, None, [ang], [a1])
                O.stt("dve", a1[:], kf[:], -2 * PI, a1[:], ALU.mult, ALU.add, [kf, a1], [a1])
                O.ts("dve", mm_[:], a1[:], PI, 2 * PI, ALU.is_ge, ALU.mult, [a1], [mm_])
                O.tt("dve", a1[:], a1[:], mm_[:], ALU.subtract, [a1, mm_], [a1])
                O.ts("dve", mm_[:], a1[:], -PI, 2 * PI, ALU.is_lt, ALU.mult, [a1], [mm_])
                O.tt("dve", a1[:], a1[:], mm_[:], ALU.add, [a1, mm_], [a1])
                O.ts("dve", a1[:], a1[:], -PI, PI, ALU.max, ALU.min, [a1], [a1])
                O.act(dstT[:], a1[:], AF.Sin, [a1], [dstT])
            S.barrier()
            S.flush()

        for l in range(nlayers):
            x_src = x_in if l == 0 else xl[l - 1]
            x_dst = xl[l] if l < L_ - 1 else None
            C = SimpleNamespace(**locals())
            layer(C, l, x_src, x_dst, stop)
            if stop is not None and stop[0] == l:
                break
        S.barrier(skip_q=())
        S.flush()
    return nc


def layer(C, l, x_src, x_dst, stop):
    nc, S, O = C.nc, C.S, C.O
    with ExitStack() as lst:
        uT = lst.enter_context(_sbt(nc, "uT", [128, 8, S_], BF16))
        phase1(C, l, x_src, uT)
        if stop == (l, 1):
            return
        phase2(C, l, uT)
    S.barrier()
    S.flush()
    if stop == (l, 2):
        return
    phase3(C, l)
    if stop == (l, 3):
        return
    phase4(C, l)
    if stop == (l, 4):
        return
    phase5(C, l, x_src, x_dst)


def rms_tile(C, xt, ss, sq, nwt, ub, tag):
    O = C.O
    O.act(sq[:], xt[:], AF.Square, [xt], [sq, ss], accum_out=ss[:])
    O.act(ss[:], ss[:], AF.Sqrt, [ss], [ss], bias=1e-6, scale=1.0 / D_)
    O.recip(ss[:], ss[:], [ss], [ss])
    O.stt("dve", ub[:], xt[:], ss[:], nwt[:], ALU.mult, ALU.mult, [xt, ss, nwt], [ub])


def phase1(C, l, x_src, uT):
    nc, S, O = C.nc, C.S, C.O
    with ExitStack() as st:
        def sb(name, shape, dt):
            return st.enter_context(_sbt(nc, name, list(shape), dt))
        xt = [sb(f"p1x{i}", [128, D_], F32) for i in range(2)]
        sq = sb("p1sq", [128, D_], BF16)
        ss = [sb(f"p1ss{i}", [128, 1], F32) for i in range(2)]
        ub = [sb(f"p1ub{i}", [128, D_], BF16) for i in range(2)]
        nwt = sb("p1nw", [128, D_], F32)
        pT = [st.enter_context(_pst(nc, f"p1pT{i}", [128, 8, 128], BF16)) for i in range(2)]
        S.dma("sp", nwt[:], C.mixw[l:l + 1, :].partition_broadcast(128), r=[], w=[nwt])
        for t in range(NT_):
            b = t % 2
            S.dma("sp", xt[b][:], x_src[t * 128:(t + 1) * 128, :], r=[(x_src, t)], w=[xt[b]])
            rms_tile(C, xt[b], ss[b], sq, nwt, ub[b], "p1")
            for k in range(8):
                O.tr(pT[b][:, k, :], ub[b][:, k * 128:(k + 1) * 128], C.ident[:], [ub[b], C.ident], [pT[b]])
            O.copy("act" if t % 2 else "dve", uT[:, :, t * 128:(t + 1) * 128], pT[b][:], [pT[b]], [(uT, t)])
        S.barrier()
        S.flush()


def phase2(C, l, uT):
    nc, S, O = C.nc, C.S, C.O
    wsrc = C.w_in[l].rearrange("(k p) n -> p k n", p=128)
    seq = []
    for g in range(2):
        seq += [(512 * g, 512), (1024 + 512 * g, 512), (2048 + 512 * g, 512)]
    seq += [(3072 + 512 * i, 512) for i in range(4)]
    seq += [(5120 + 512 * i, 512) for i in range(2)]
    seq += [(6144 + 512 * i, 512) for i in range(4)]
    seq += [(8192 + 512 * i, 512) for i in range(6)]
    seq += [(11264, 64)]
    seq += [(11328 + 512 * i, 512) for i in range(6)]
    with ExitStack() as st:
        wbuf = [st.enter_context(_sbt(nc, f"p2w{i}", [128, 8, 512], BF16)) for i in range(3)]
        PB = [st.enter_context(_pst(nc, f"p2ps{i}", [128, 512], F32)) for i in range(6)]
        PT = [st.enter_context(_pst(nc, f"p2pt{i}", [128, 4, 128], BF16)) for i in range(2)]
        state = {"issued": 0, "ps": 0, "done": 0}

        def prefetch():
            while state["issued"] < min(len(seq), state["done"] + 3):
                i = state["issued"]
                c0, ncw = seq[i]
                S.dma("pool", wbuf[i % 3][:, :, :ncw], wsrc[:, :, c0:c0 + ncw], r=[], w=[wbuf[i % 3]])
                state["issued"] += 1

        def get_w(idx):
            prefetch()
            assert idx < state["issued"]
            return wbuf[idx % 3]

        def release(n):
            state["done"] = n
            prefetch()

        def PS():
            p = PB[state["ps"] % len(PB)]
            state["ps"] += 1
            return p

        def mm_feat(ps, wt, jj, tb):
            for k in range(8):
                O.mm(ps[:], wt[:, k, jj * 128:(jj + 1) * 128], uT[:, k, tb * 512:(tb + 1) * 512], k == 0, k == 7,
                     [wt, uT], [ps])

        def mm_tok(ps_ap, ps, wt, t, ncw):
            for k in range(8):
                O.mm(ps_ap, uT[:, k, t * 128:(t + 1) * 128], wt[:, k, :ncw], k == 0, k == 7, [wt, uT], [ps])

        wi = 0
        with ExitStack() as s2:
            def sb(name, shape, dt):
                return s2.enter_context(_sbt(nc, name, list(shape), dt))
            bT = [sb(f"p2bT{i}", [128, S_], F32) for i in range(2)]
            cst = [sb(f"p2cst{i}", [128, 512], F32) for i in range(2)]
            cx = [sb(f"p2cx{i}", [128, S_ + 2], BF16) for i in range(2)]
            yAo = [sb(f"p2yAo{i}", [128, S_], BF16) for i in range(2)]
            dgA = [sb(f"p2dgA{i}", [128, 3, 128], BF16) for i in range(2)]
            wA = sb("p2wA", [128, 8, 3], F32)
            S.dma("sp", wA[:], C.cawT[l], r=[], w=[wA])
            for i in range(2):
                O.memset("dve", cx[i][:, 0:1], 0.0, [(cx[i], "l")])
                O.memset("dve", cx[i][:, S_ + 1:S_ + 2], 0.0, [(cx[i], "r")])

            def convA(j):
                p = j % 2
                for tb in range(8):
                    ps = PS()
                    for k3 in range(3):
                        O.mm(ps[:], dgA[p][:, k3, :], cx[p][:, tb * 512 + k3:tb * 512 + k3 + 512], k3 == 0, k3 == 2,
                             [dgA[p], cx[p]], [ps])
                    O.tt("dve", yAo[p][:, tb * 512:(tb + 1) * 512], ps[:], bT[p][:, tb * 512:(tb + 1) * 512], ALU.mult,
                         [ps, bT[p]], [(yAo[p], tb)])
                S.dma("sp", C.yAT[:, :, j, :].rearrange("tb p t -> p tb t"), yAo[p][:].rearrange("p (tb t) -> p tb t", t=256),
                      r=[yAo[p]], w=[(C.yAT, j)])

            for g in range(2):
                release(wi)
                wt_b, wt_c, wt_x = get_w(wi), get_w(wi + 1), get_w(wi + 2)
                wi += 3
                for jj in range(4):
                    j = 4 * g + jj
                    p = j % 2
                    for k3 in range(3):
                        O.ts("dve", dgA[p][:, k3, :], C.ident[:], wA[:, j, k3:k3 + 1], None, ALU.mult, None,
                             [C.ident, wA], [(dgA[p], k3)])
                    for tb in range(8):
                        pb, pc, px = PS(), PS(), PS()
                        mm_feat(pb, wt_b, jj, tb)
                        mm_feat(pc, wt_c, jj, tb)
                        mm_feat(px, wt_x, jj, tb)
                        O.copy("act", bT[p][:, tb * 512:(tb + 1) * 512], pb[:], [pb], [(bT[p], tb)])
                        O.copy("act", cst[tb % 2][:], pc[:], [pc], [cst[tb % 2]])
                        O.tt("dve", cx[p][:, 1 + tb * 512:1 + (tb + 1) * 512], px[:], cst[tb % 2][:], ALU.mult,
                             [px, cst[tb % 2]], [(cx[p], tb)])
                        if tb == 1 and j > 0:
                            convA(j - 1)
            convA(7)
            S.barrier()
            S.flush()
        with ExitStack() as s2:
            def sb(name, shape, dt):
                return s2.enter_context(_sbt(nc, name, list(shape), dt))
            qs = [sb(f"p2qs{i}", [128, 8, 64], F32) for i in range(3)]
            tmp = [sb(f"p2tmp{i}", [128, 4, 8, 8], F32) for i in range(3)]
            qr = [sb(f"p2qr{i}", [128, 512], BF16) for i in range(3)]
            stg = [sb(f"p2stg{i}", [128, 4, S_], BF16) for i in range(2)]
            for ci in range(4):
                release(wi)
                wt = get_w(wi)
                wi += 1
                sg = stg[ci % 2]
                def qk_front(t):
                    b = t % 3
                    ps = PS()
                    mm_tok(ps[:], ps, wt, t, 512)
                    O.copy("act", qs[b][:].rearrange("p h d -> p (h d)"), ps[:], [ps], [qs[b]])
                    cs = C.cosT[:, t, :].unsqueeze(1).to_broadcast([128, 8, 8])
                    sn = C.sinT[:, t, :].unsqueeze(1).to_broadcast([128, 8, 8])
                    t1 = qs[b][:, :, 0:8]
                    t2 = qs[b][:, :, 8:16]
                    tm = tmp[b]
                    O.tt("dve", tm[:, 0], t1, cs, ALU.mult, [qs[b], C.cosT], [(tm, 0)])
                    O.tt("dve", tm[:, 1], t2, sn, ALU.mult, [qs[b], C.sinT], [(tm, 1)])
                    O.tt("dve", tm[:, 2], t2, cs, ALU.mult, [qs[b], C.cosT], [(tm, 2)])
                    O.tt("dve", tm[:, 3], t1, sn, ALU.mult, [qs[b], C.sinT], [(tm, 3)])
                    O.tt("dve", t1, tm[:, 0], tm[:, 1], ALU.subtract, [(tm, 0), (tm, 1)], [(qs[b], "a")])
                    O.tt("dve", t2, tm[:, 2], tm[:, 3], ALU.add, [(tm, 2), (tm, 3)], [(qs[b], "b")])
                    O.copy("act", qr[b][:], qs[b][:].rearrange("p h d -> p (h d)"), [qs[b]], [qr[b]])

                def qk_back(t):
                    b = t % 3
                    pt = PT[t % 2]
                    for jj in range(4):
                        O.tr(pt[:, jj, :], qr[b][:, jj * 128:(jj + 1) * 128], C.ident[:], [qr[b], C.ident], [pt])
                    O.copy("act", sg[:, :, t * 128:(t + 1) * 128], pt[:], [pt], [(sg, t)])

                for t in range(NT_ + 2):
                    if t < NT_:
                        qk_front(t)
                    if 0 <= t - 2 < NT_:
                        qk_back(t - 2)
                dst = C.qT if ci < 2 else C.kT
                for jj in range(4):
                    r0 = (ci % 2) * 512 + jj * 128
                    S.dma("sp", dst[r0:r0 + 128, :], sg[:, jj, :], r=[sg], w=[(dst, r0)])
            S.barrier()
            S.flush()
        with ExitStack() as s2:
            def sb(name, shape, dt):
                return s2.enter_context(_sbt(nc, name, list(shape), dt))
            vst = [sb(f"p2vst{i}", [128, 512], BF16) for i in range(4)]
            n = 0
            for ci in range(6):
                release(wi)
                wt = get_w(wi)
                wi += 1
                for t in range(NT_):
                    ps = PS()
                    mm_tok(ps[:], ps, wt, t, 512)
                    vs = vst[n % 4]
                    n += 1
                    if ci < 2:
                        O.copy("act", vs[:], ps[:], [ps], [vs])
                        S.dma("sp", C.v_d[t * 128:(t + 1) * 128, ci * 512:(ci + 1) * 512], vs[:], r=[vs], w=[(C.v_d, (t, ci))])
                    else:
                        O.act(vs[:], ps[:], AF.Silu, [ps], [vs])
                        c2 = ci - 2
                        S.dma("sp", C.sz_d[t * 128:(t + 1) * 128, c2 * 512:(c2 + 1) * 512], vs[:], r=[vs], w=[(C.sz_d, (t, c2))])
            S.barrier()
            S.flush()
        with ExitStack() as s2:
            def sb(name, shape, dt):
                return s2.enter_context(_sbt(nc, name, list(shape), dt))
            xin = sb("p2xin", [128, 4, S_ + 4], BF16)
            diag = sb("p2diag", [128, 4, 5, 128], BF16)
            stok = [sb(f"p2stok{i}", [128, 8, 512], BF16) for i in range(2)]
            sfeat = [sb(f"p2sfeat{i}", [128, S_], BF16) for i in range(2)]
            wS = sb("p2wS", [128, 24, 5], F32)
            bS = sb("p2bS", [128, 24], F32)
            brf = sb("p2brf", [1, 3072], F32)
            brow = sb("p2brow", [1, 3072], BF16)
            ones = sb("p2ones", [1, 128], BF16)
            S.dma("sp", wS[:], C.scwT[l], r=[], w=[wS])
            S.dma("sp", bS[:], C.scbT[l], r=[], w=[bS])
            S.dma("sp", brf[:], C.scbR[l], r=[], w=[brf])
            O.copy("dve", brow[:], brf[:], [brf], [brow])
            O.memset("dve", ones[:], 1.0, [ones])
            O.memset("dve", xin[:, :, 0:2], 0.0, [(xin, "l")])
            O.memset("dve", xin[:, :, S_ + 2:S_ + 4], 0.0, [(xin, "r")])
            nf = 0
            for ci in range(6):
                release(wi)
                wt = get_w(wi)
                wi += 1
                for jj in range(4):
                    J = 4 * ci + jj
                    for tb in range(8):
                        ps = PS()
                        mm_feat(ps, wt, jj, tb)
                        O.copy("act" if tb % 2 else "dve", xin[:, jj, 2 + tb * 512:2 + (tb + 1) * 512], ps[:], [ps], [(xin, (jj, tb))])
                    for k5 in range(5):
                        O.ts("dve", diag[:, jj, k5, :], C.ident[:], wS[:, J, k5:k5 + 1], None, ALU.mult, None,
                             [C.ident, wS], [(diag, (jj, k5))])
                if ci <= 4:
                    for t in range(NT_):
                        ps = PS()
                        for jj in range(4):
                            J = 4 * ci + jj
                            o_ap = ps[:, jj * 128:(jj + 1) * 128]
                            for k5 in range(5):
                                O.mm(o_ap, xin[:, jj, t * 128 + k5:t * 128 + k5 + 128], diag[:, jj, k5, :], k5 == 0, False,
                                     [xin, diag], [ps])
                            O.mm(o_ap, ones[0:1, :], brow[0:1, J * 128:(J + 1) * 128], False, True, [ones, brow], [ps])
                        sk = stok[(t // 8) % 2]
                        O.act(sk[:, t % 8, :], ps[:], AF.Silu, [ps], [(sk, t % 8)])
                        if t % 8 == 7:
                            t0 = t - 7
                            if ci < 4:
                                dst = C.xs_d[t0 * 128:(t0 + 8) * 128, ci * 512:(ci + 1) * 512]
                                key = (C.xs_d, (t0, ci))
                            else:
                                dst = C.Bt_d[t0 * 128:(t0 + 8) * 128, :]
                                key = (C.Bt_d, t0)
                            S.dma("sp", dst.rearrange("(t p) c -> p t c", p=128), sk[:], r=[sk], w=[key])
                if ci >= 4:
                    for jj in range(4):
                        J = 4 * ci + jj
                        sf = sfeat[nf % 2]
                        nf += 1
                        for tb in range(8):
                            ps = PS()
                            for k5 in range(5):
                                O.mm(ps[:], diag[:, jj, k5, :], xin[:, jj, tb * 512 + k5:tb * 512 + k5 + 512], k5 == 0, k5 == 4,
                                     [xin, diag], [ps])
                            O.act(sf[:, tb * 512:(tb + 1) * 512], ps[:], AF.Silu, [ps, bS], [(sf, tb)], bias=bS[:, J:J + 1])
                        dst = C.BT_d if ci == 4 else C.CT_d
                        S.dma("sp", dst[jj * 128:(jj + 1) * 128, :], sf[:], r=[sf], w=[(dst, jj)])
            S.barrier()
            S.flush()
        with ExitStack() as s2:
            def sb(name, shape, dt):
                return s2.enter_context(_sbt(nc, name, list(shape), dt))
            dtbb = sb("p2dtbb", [128, 64], F32)
            abc = sb("p2abc", [128, 64], F32)
            tdt = [sb(f"p2tdt{i}", [128, 64], F32) for i in range(2)]
            S.dma("sp", dtbb[:], C.dtb[l].partition_broadcast(128), r=[], w=[dtbb])
            S.dma("sp", abc[:], C.alog[l].partition_broadcast(128), r=[], w=[abc])
            O.act(abc[:], abc[:], AF.Exp, [abc], [abc])
            O.ts("dve", abc[:], abc[:], -1.0, None, ALU.mult, None, [abc], [abc])
            release(wi)
            wt = get_w(wi)
            wi += 1
            for t in range(NT_):
                ps = PS()
                mm_tok(ps[:, 0:64], ps, wt, t, 64)
                td = tdt[t % 2]
                O.tt("dve", td[:], ps[:, 0:64], dtbb[:], ALU.add, [ps, dtbb], [td])
                O.act(td[:], td[:], AF.Exp, [td], [td])
                O.act(C.dt_all[:, t, :], td[:], AF.Ln, [td], [(C.dt_all, t)], bias=1.0)
                O.tt("dve", C.la_all[:, t, :], C.dt_all[:, t, :], abc[:], ALU.mult, [(C.dt_all, t), abc], [(C.la_all, t)])
            if "dt_dbg" in C.dbg:
                S.dma("sp", C.dt_dbg, C.dt_all[:], r=[C.dt_all], w=[C.dt_dbg])
            S.barrier()
            S.flush()
        with ExitStack() as s2:
            def sb(name, shape, dt):
                return s2.enter_context(_sbt(nc, name, list(shape), dt))
            sfeat = [sb(f"p2gfeat{i}", [128, S_], BF16) for i in range(2)]
            nf = 0
            for ci in range(6):
                release(wi)
                wt = get_w(wi)
                wi += 1
                for jj in range(4):
                    G = 4 * ci + jj
                    sf = sfeat[nf % 2]
                    nf += 1
                    for tb in range(8):
                        ps = PS()
                        mm_feat(ps, wt, jj, tb)
                        O.act(sf[:, tb * 512:(tb + 1) * 512], ps[:], AF.Sigmoid, [ps], [(sf, tb)])
                    S.dma("sp", C.gT[:, :, G, :].rearrange("tb p t -> p tb t"), sf[:].rearrange("p (tb t) -> p tb t", t=256),
                          r=[sf], w=[(C.gT, G)])
            S.barrier()
            S.flush()
        assert wi == len(seq)


def phase3(C, l):
    nc, S, O = C.nc, C.S, C.O
    C.cast_w(C.w_1[l], C.wb_1[l], 128)
    pats = [(1, 33), (4, 9), (16, 3)]
    with ExitStack() as st:
        def sb(name, shape, dt):
            return st.enter_context(_sbt(nc, name, list(shape), dt))
        qTh = [sb(f"p3q{i}", [64, S_], BF16) for i in range(2)]
        kTp = [sb(f"p3k{i}", [64, S_ + 2048], BF16) for i in range(2)]
        vP = [[sb(f"p3v{pi}_{b}", [128, nb, dil, 65], BF16) for b in range(2)] for pi, (dil, nb) in enumerate(pats)]
        mask = sb("p3mask", [128, 2, 128], BF16)
        PTs = [sb(f"p3pt{i}", [128, 2, 2, 128], BF16) for i in range(4)]
        PTm = [sb(f"p3pm{i}", [128, 2, 2, 128], BF16) for i in range(4)]
        ost = [sb(f"p3ost{i}", [128, 32, 65], F32) for i in range(2)]
        PSs = [st.enter_context(_pst(nc, f"p3ps{i}", [128, 512], F32)) for i in range(5)]
        PSo = [st.enter_context(_pst(nc, f"p3po{i}", [128, 2, 65], F32)) for i in range(3)]
        O.copy("dve", mask[:, 0, :], C.tri[:, 1, :], [C.tri], [(mask, 0)])
        O.copy("dve", mask[:, 1, :], C.tri[:, 0, :], [C.tri], [(mask, 1)])
        for b in range(2):
            O.memset("dve", kTp[b][:, 0:1024], 0.0, [(kTp[b], "l")])
            O.memset("dve", kTp[b][:, 1024 + S_:2048 + S_], 0.0, [(kTp[b], "r")])
            for pi, (dil, nb) in enumerate(pats):
                v = vP[pi][b]
                O.memset("pool", v[:], 0.0, [v])
                O.memset("pool", v[:, :, :, 64:65], 1.0, [v])
                O.memset("pool", v[0:64, 0, :, 64:65], 0.0, [v])
                O.memset("pool", v[64:128, nb - 1, :, 64:65], 0.0, [v])
        def loads(h):
            b = h % 2
            S.dma("sp", qTh[b][:], C.qT[h * 64:(h + 1) * 64, :], r=[C.qT], w=[qTh[b]])
            S.dma("sp", kTp[b][:, 1024:1024 + S_], C.kT[h * 64:(h + 1) * 64, :], r=[C.kT], w=[(kTp[b], "m")])
            vsrc = C.v_d[:, h * 64:(h + 1) * 64]
            for pi, (dil, nb) in enumerate(pats):
                v = vP[pi][b]
                nin = nb - 2
                if dil == 1:
                    for j0 in range(0, nin, 8):
                        j1 = min(nin, j0 + 8)
                        src = vsrc[64 + j0 * 128:64 + j1 * 128, :]
                        S.dma("sp", v[:, 1 + j0:1 + j1, 0, 0:64], src.rearrange("(j i) d -> i j d", i=128),
                              r=[C.v_d], w=[(v, ("in", j0))])
                else:
                    for j0 in range(nin):
                        src = vsrc[64 * dil + j0 * 128 * dil:64 * dil + (j0 + 1) * 128 * dil, :]
                        S.dma("sp", v[:, 1 + j0, :, 0:64], src.rearrange("(i r) d -> i r d", r=dil),
                              r=[C.v_d], w=[(v, ("in", j0))])
                S.dma("sp", v[64:128, 0, :, 0:64], vsrc[0:64 * dil, :].rearrange("(i r) d -> i r d", r=dil),
                      r=[C.v_d], w=[(v, "first")])
                S.dma("sp", v[0:64, nb - 1, :, 0:64], vsrc[S_ - 64 * dil:S_, :].rearrange("(i r) d -> i r d", r=dil),
                      r=[C.v_d], w=[(v, "last")])

        def store(h, pi, osb):
            dil, nb = pats[pi]
            nqb = nb - 1
            osv = osb[:].rearrange("p (q r) e -> p q r e", r=dil)
            if dil == 1:
                S.dma("sp", C.o_d[pi][:, h, :].rearrange("(q i) e -> i q e", i=128), osb[:], r=[osb],
                      w=[(C.o_d[pi], h)])
            else:
                for q in range(nqb):
                    S.dma("sp", C.o_d[pi][q * 128 * dil:(q + 1) * 128 * dil, h, :].rearrange("(i r) e -> i r e", r=dil),
                          osv[:, q, :, :], r=[osb], w=[(C.o_d[pi], (h, q))])

        pairs = []
        n_ost = 0
        for h in range(16):
            first = True
            for pi, (dil, nb) in enumerate(pats):
                osb = ost[n_ost % 2]
                n_ost += 1
                nqb = nb - 1
                lst = [(r, qp) for r in range(dil) for qp in range(nqb // 2)]
                for idx, (r, qp) in enumerate(lst):
                    pairs.append(dict(h=h, pi=pi, dil=dil, r=r, qp=qp, osb=osb, pre=None,
                                      post=(h, pi, osb) if idx == len(lst) - 1 else None))
        npairs_h = len(pairs) // 16
        for h in range(16):
            if h == 0:
                pairs[0]["pre"] = 0
            if h + 1 < 16:
                pairs[h * npairs_h + 8]["pre"] = h + 1
        NPS = 5

        def stA(i):
            p = pairs[i]
            b = p["h"] % 2
            dil, r, qp = p["dil"], p["r"], p["qp"]
            ps = PSs[i % NPS]
            psv = ps[:].rearrange("p (a b c) -> p a b c", a=2, b=2)
            qb = 2 * qp

            def ksl(j):
                k0 = 1024 - 64 * dil + 128 * dil * j + r
                return kTp[b][:, k0:k0 + 127 * dil + 1:dil]

            def qsl(j, n):
                q0 = 128 * dil * j + r
                return qTh[b][:, q0:q0 + (128 * n - 1) * dil + 1:dil]

            O.mm(ps[:, 0:128], ksl(qb), qsl(qb, 1), True, True, [kTp[b], qTh[b]], [ps])
            O.mm(ps[:, 128:384], ksl(qb + 1), qsl(qb, 2), True, True, [kTp[b], qTh[b]], [ps])
            O.mm(ps[:, 384:512], ksl(qb + 2), qsl(qb + 1, 1), True, True, [kTp[b], qTh[b]], [ps])

        def stB(i):
            ps = PSs[i % NPS]
            pt = PTs[i % 4]
            pm = PTm[i % 4]
            O.act(pt[:].rearrange("p a b c -> p (a b c)"), ps[:], AF.Exp, [ps], [pt], scale=0.125)
            O.tt("dve", pm[:], pt[:], mask[:].unsqueeze(1).to_broadcast([128, 2, 2, 128]), ALU.mult, [pt, mask], [pm])

        def stC(i):
            p = pairs[i]
            b = p["h"] % 2
            dil, r, qp, pi = p["dil"], p["r"], p["qp"], p["pi"]
            v = vP[pi][b]
            pm = PTm[i % 4]
            po = PSo[i % 3]
            osb = p["osb"]
            osv = osb[:].rearrange("p (q r) e -> p q r e", r=dil)
            for qi in range(2):
                qb = 2 * qp + qi
                for ab in range(2):
                    O.mm(po[:, qi, :], pm[:, qi, ab, :], v[:, qb + ab, r, :], ab == 0, ab == 1, [pm, v], [po])
            O.copy("act" if i % 2 else "dve", osv[:, 2 * qp:2 * qp + 2, r, :], po[:], [po], [(osb, (r, qp))])
            if p["post"] is not None:
                store(*p["post"])

        n = len(pairs)
        for i in range(n + 4):
            if 0 <= i - 4 < n:
                stC(i - 4)
            if 0 <= i - 2 < n:
                stB(i - 2)
            if i < n:
                if pairs[i]["pre"] is not None:
                    loads(pairs[i]["pre"])
                stA(i)
        S.barrier()
        S.flush()


def phase4(C, l):
    nc, S, O = C.nc, C.S, C.O
    tri, trif = C.tri, C.trif
    with ExitStack() as st:
        def sb(name, shape, dt):
            return st.enter_context(_sbt(nc, name, list(shape), dt))
        H = [[sb(f"p4H{d}_{i}", [128, 2048], F32) for i in range(2)] for d in range(2)]
        tmpH = [sb(f"p4tmpH{d}", [128, 2048], F32) for d in range(2)]
        hbf = [[sb(f"p4hbf{d}_{i}", [128, 2048], BF16) for i in range(2)] for d in range(2)]
        xs_t = [sb(f"p4xs{i}", [128, 2048], BF16) for i in range(4)]
        Bt_t = [sb(f"p4Bt{i}", [128, 512], BF16) for i in range(4)]
        ew = [sb(f"p4ew{i}", [128, 2, 32], F32) for i in range(4)]
        dtw = [sb(f"p4dtw{i}", [128, 32], F32) for i in range(2)]
        xw = [sb(f"p4xw{i}", [128, 2048], BF16) for i in range(2)]
        st_sb = [sb(f"p4st{i}", [128, 2048], F32) for i in range(3)]
        PSw = [st.enter_context(_pst(nc, f"p4psw{i}", [128, 2, 32], F32)) for i in range(2)]
        PSs = [st.enter_context(_pst(nc, f"p4pss{i}", [128, 512], F32)) for i in range(4)]
        for d in range(2):
            O.memset("dve", H[d][0][:], 0.0, [H[d][0]])
            O.memset("pool", hbf[d][0][:], 0.0, [hbf[d][0]])

        def stA1(n):
            i, d = divmod(n, 2)
            c = i if d == 0 else NT_ - 1 - i
            xt, bt, e_, dw, pw = xs_t[n % 4], Bt_t[n % 4], ew[n % 4], dtw[n % 2], PSw[n % 2]
            S.dma("sp", xt[:], C.xs_d[c * 128:(c + 1) * 128, :], r=[C.xs_d], w=[xt])
            S.dma("sp", bt[:], C.Bt_d[c * 128:(c + 1) * 128, :], r=[C.Bt_d], w=[bt])
            la_c = C.la_all[:, c, d * 32:(d + 1) * 32]
            dt_c = C.dt_all[:, c, d * 32:(d + 1) * 32]
            O.mm(pw[:, 0, :], trif[:, 1 if d == 0 else 0, :], la_c, True, True, [trif, C.la_all], [pw])
            O.mm(pw[:, 1, :], trif[:, 2, :], la_c, True, True, [trif, C.la_all], [pw])
            O.act(e_[:], pw[:], AF.Exp, [pw], [e_])
            O.tt("dve", dw[:], dt_c, e_[:, 0, :], ALU.mult, [C.dt_all, e_], [dw])

        def stA2(n):
            xt, dw, xw_ = xs_t[n % 4], dtw[n % 2], xw[n % 2]
            O.tt("pool", xw_[:].rearrange("p (h d) -> p h d", d=64), xt[:].rearrange("p (h d) -> p h d", d=64),
                 dw[:].unsqueeze(2).to_broadcast([128, 32, 64]), ALU.mult, [xt, dw], [xw_])

        def stA3(n):
            bt, xw_ = Bt_t[n % 4], xw[n % 2]
            ss_ = st_sb[n % 3]
            for g in range(4):
                O.mm(PSs[g][:], bt[:, g * 128:(g + 1) * 128], xw_[:, g * 512:(g + 1) * 512], True, True, [bt, xw_], [PSs[g]])
                O.copy("act", ss_[:, g * 512:(g + 1) * 512], PSs[g][:], [PSs[g]], [(ss_, g)])

        def stB(n):
            i, d = divmod(n, 2)
            c = i if d == 0 else NT_ - 1 - i
            eng = "dve"
            e_ = ew[n % 4]
            hb = hbf[d][i % 2]
            Hs, Hd = H[d][i % 2], H[d][(i + 1) % 2]
            S.dma("sp", C.hin_d[d][c], hb[:], r=[hb], w=[(C.hin_d[d], c)])
            O.tt(eng, tmpH[d][:].rearrange("p (h d) -> p h d", d=64), Hs[:].rearrange("p (h d) -> p h d", d=64),
                 e_[:, 1, :].unsqueeze(2).to_broadcast([128, 32, 64]), ALU.mult, [Hs, e_], [tmpH[d]])
            O.tt(eng, Hd[:], tmpH[d][:], st_sb[n % 3][:], ALU.add, [tmpH[d], st_sb[n % 3]], [Hd])

        def stC(n):
            i, d = divmod(n, 2)
            O.copy("act", hbf[d][(i + 1) % 2][:], H[d][(i + 1) % 2][:], [H[d][(i + 1) % 2]], [hbf[d][(i + 1) % 2]])

        NN = 2 * NT_
        order1 = [(stB, 3), (stC, 4), (stA3, 2), (stA2, 1), (stA1, 0)]
        for it in range(NN + 4):
            for fn, skew in order1:
                n = it - skew
                if 0 <= n < NN:
                    fn(n)
        S.barrier()
        S.flush()
    with ExitStack() as st:
        def sb(name, shape, dt):
            return st.enter_context(_sbt(nc, name, list(shape), dt))
        NB = 3
        xs_t = [sb(f"p4xs{i}", [128, 2048], BF16) for i in range(NB)]
        BT_t = [sb(f"p4BT{i}", [128, 4, 128], BF16) for i in range(NB)]
        CT_t = [sb(f"p4CT{i}", [128, 4, 128], BF16) for i in range(NB)]
        sz_t = [sb(f"p4sz{i}", [128, 2048], BF16) for i in range(NB)]
        hh_t = [[sb(f"p4hh{d}_{i}", [128, 2048], BF16) for i in range(NB)] for d in range(2)]
        cbm = [[sb(f"p4cbm{d}_{i}", [128, 4, 128], BF16) for i in range(2)] for d in range(2)]
        ecum = [[sb(f"p4ecum{d}_{i}", [128, 32], F32) for i in range(2)] for d in range(2)]
        rseg = [[sb(f"p4rseg{d}_{i}", [128, 8, 128], BF16) for i in range(2)] for d in range(2)]
        xd = [[sb(f"p4xd{d}_{i}", [128, 512], BF16) for i in range(4)] for d in range(2)]
        eseg = [sb(f"p4eseg{i}", [128, 4, 128], BF16) for i in range(8)]
        MT = [sb(f"p4MT{i}", [128, 4, 128], BF16) for i in range(8)]
        tt_ = [[sb(f"p4t{d}_{i}", [128, 512], BF16) for i in range(2)] for d in range(2)]
        xsD = [sb(f"p4xsD{i}", [128, 512], BF16) for i in range(2)]
        yy = [sb(f"p4y{i}", [128, 512], F32) for i in range(3)]
        ynf = [sb(f"p4ynf{i}", [128, 512], BF16) for i in range(2)]
        ssg = [sb(f"p4ssg{i}", [128, 1], F32) for i in range(2)]
        sqj = sb("p4sqj", [128, 512], BF16)
        ynT_st = [sb(f"p4ynT{i}", [128, 16, 128], BF16) for i in range(3)]
        nwb = sb("p4nwb", [128, 2048], F32)
        Dbc = sb("p4Dbc", [128, 32], F32)
        PScb = st.enter_context(_pst(nc, "p4pscb", [128, 512], F32))
        PSy = [st.enter_context(_pst(nc, f"p4psy{i}", [128, 512], F32)) for i in range(2)]
        PSseg = [st.enter_context(_pst(nc, f"p4psseg{i}", [128, 512], F32)) for i in range(2)]
        PSo = [st.enter_context(_pst(nc, f"p4pso{i}", [128, 512], F32)) for i in range(2)]
        PSt = st.enter_context(_pst(nc, "p4pst", [128, 512], F32))
        S.dma("sp", nwb[:], C.snw[l].partition_broadcast(128), r=[], w=[nwb])
        S.dma("sp", Dbc[:], C.sdd[l].partition_broadcast(128), r=[], w=[Dbc])

        def loads(c):
            b = c % NB
            S.dma("sp", xs_t[b][:], C.xs_d[c * 128:(c + 1) * 128, :], r=[C.xs_d], w=[xs_t[b]])
            S.dma("sp", BT_t[b][:], C.BT_d[:, c * 128:(c + 1) * 128].rearrange("(g n) t -> n g t", n=128), r=[C.BT_d], w=[BT_t[b]])
            S.dma("sp", CT_t[b][:], C.CT_d[:, c * 128:(c + 1) * 128].rearrange("(g n) t -> n g t", n=128), r=[C.CT_d], w=[CT_t[b]])
            S.dma("sp", sz_t[b][:], C.sz_d[c * 128:(c + 1) * 128, :], r=[C.sz_d], w=[sz_t[b]])
            for d in range(2):
                S.dma("sp", hh_t[d][b][:], C.hin_d[d][c], r=[C.hin_d[d]], w=[hh_t[d][b]])

        units = [(d, q) for d in range(2) for q in range(2)]

        def s1(k):
            c, g = divmod(k, 4)
            b, cp = c % NB, c % 2
            xt, BT, CT = xs_t[b], BT_t[b], CT_t[b]
            if g == 0:
                pcb = PScb[:].rearrange("p (g t) -> p g t", g=4)
                for g2 in range(4):
                    O.mm(pcb[:, g2, :], BT[:, g2, :], CT[:, g2, :], True, True, [BT, CT], [PScb])
                for d in range(2):
                    O.tt("dve", cbm[d][cp][:], pcb, tri[:, d, :].unsqueeze(1).to_broadcast([128, 4, 128]), ALU.mult,
                         [PScb, tri], [cbm[d][cp]])
                pse = PScb[:, 0:64].rearrange("p (d h) -> p d h", d=2)
                for d in range(2):
                    la_c = C.la_all[:, c, d * 32:(d + 1) * 32]
                    O.mm(pse[:, d, :], trif[:, 3 + d, :], la_c, True, True, [trif, C.la_all], [PScb])
                for d in range(2):
                    O.act(ecum[d][cp][:], pse[:, d, :], AF.Exp, [PScb], [ecum[d][cp]])
            for d in range(2):
                la_g = C.la_all[:, c, d * 32 + g * 8:d * 32 + (g + 1) * 8]
                dt_g = C.dt_all[:, c, d * 32 + g * 8:d * 32 + (g + 1) * 8]
                O.tt("dve", rseg[d][k % 2][:], la_g.unsqueeze(2).to_broadcast([128, 8, 128]),
                     tri[:, d, :].unsqueeze(1).to_broadcast([128, 8, 128]), ALU.mult, [C.la_all, tri], [rseg[d][k % 2]])
                O.tt("pool", xd[d][k % 4][:].rearrange("p (h e) -> p h e", e=64),
                     xt[:, g * 512:(g + 1) * 512].rearrange("p (h e) -> p h e", e=64),
                     dt_g.unsqueeze(2).to_broadcast([128, 8, 64]), ALU.mult, [xt, C.dt_all], [xd[d][k % 4]])

        def s2(k):
            for u, (d, q) in enumerate(units):
                n = k * 4 + u
                pss = PSseg[n % 2]
                es = eseg[n % 8]
                O.mm(pss[:], tri[:, 3 - d, :], rseg[d][k % 2][:, q * 4:(q + 1) * 4, :].rearrange("p h t -> p (h t)"), True, True,
                     [tri, rseg[d][k % 2]], [pss])
                O.act(es[:].rearrange("p h t -> p (h t)"), pss[:], AF.Exp, [pss], [es])

        def s3(k):
            c, g = divmod(k, 4)
            cp = c % 2
            for u, (d, q) in enumerate(units):
                n = k * 4 + u
                O.tt("dve" if u % 2 else "pool", MT[n % 8][:], eseg[n % 8][:],
                     cbm[d][cp][:, g, :].unsqueeze(1).to_broadcast([128, 4, 128]), ALU.mult, [eseg[n % 8], cbm[d][cp]], [MT[n % 8]])

        def s4(k):
            for u, (d, q) in enumerate(units):
                n = k * 4 + u
                mt = MT[n % 8]
                for hh in range(4):
                    h8 = q * 4 + hh
                    O.mm(PSy[k % 2][:, h8 * 64:(h8 + 1) * 64], mt[:, hh, :], xd[d][k % 4][:, h8 * 64:(h8 + 1) * 64],
                         u == 0 and hh == 0, False, [mt, xd[d][k % 4]], [PSy[k % 2]])

        def s5(k):
            c, g = divmod(k, 4)
            b, cp, kp = c % NB, c % 2, k % 2
            xt, CT, sz = xs_t[b], CT_t[b], sz_t[b]
            for d in range(2):
                hd = hh_t[d][b]
                O.mm(PSo[d][:], CT[:, g, :], hd[:, g * 512:(g + 1) * 512], True, True, [CT, hd], [PSo[d]])
            O.tt("pool", xsD[kp][:].rearrange("p (h e) -> p h e", e=64),
                 xt[:, g * 512:(g + 1) * 512].rearrange("p (h e) -> p h e", e=64),
                 Dbc[:, g * 8:(g + 1) * 8].unsqueeze(2).to_broadcast([128, 8, 64]), ALU.mult, [xt, Dbc], [xsD[kp]])
            for d in range(2):
                O.tt("dve", tt_[d][kp][:].rearrange("p (h e) -> p h e", e=64), PSo[d][:].rearrange("p (h e) -> p h e", e=64),
                     ecum[d][cp][:, g * 8:(g + 1) * 8].unsqueeze(2).to_broadcast([128, 8, 64]), ALU.mult,
                     [PSo[d], ecum[d][cp]], [tt_[d][kp]])

        def s5b(k):
            c, g = divmod(k, 4)
            b, cp, kp = c % NB, c % 2, k % 2
            sz = sz_t[b]
            O.mm(PSy[kp][:], C.ident[:], xsD[kp][:], False, False, [C.ident, xsD[kp]], [PSy[kp]])
            O.mm(PSy[kp][:], C.ident[:], tt_[0][kp][:], False, False, [C.ident, tt_[0][kp]], [PSy[kp]])
            O.mm(PSy[kp][:], C.ident[:], tt_[1][kp][:], False, True, [C.ident, tt_[1][kp]], [PSy[kp]])
            y_ = yy[k % 3]
            O.tt("dve", y_[:], PSy[kp][:], sz[:, g * 512:(g + 1) * 512], ALU.mult, [PSy[kp], sz], [y_])

        def s6(k):
            c, g = divmod(k, 4)
            y_ = yy[k % 3]
            s_ = ssg[k % 2]
            O.act(sqj[:], y_[:], AF.Square, [y_], [sqj, s_], accum_out=s_[:])
            O.act(s_[:], s_[:], AF.Sqrt, [s_], [s_], bias=1e-6, scale=1.0 / 512)
            O.recip(s_[:], s_[:], [s_], [s_])
            O.stt("dve", ynf[k % 2][:], y_[:], s_[:], nwb[:, g * 512:(g + 1) * 512], ALU.mult, ALU.mult, [y_, s_, nwb], [ynf[k % 2]])

        def s7(k):
            c, g = divmod(k, 4)
            ptb = PSt[:, 0:256].bitcast(BF16).rearrange("p (a t) -> p a t", a=4)
            ys = ynT_st[c % 3]
            yn_ = ynf[k % 2]
            for a in range(4):
                O.tr(ptb[:, a, :], yn_[:, a * 128:(a + 1) * 128], C.ident[:], [yn_, C.ident], [PSt])
            O.copy("act", ys[:, g * 4:(g + 1) * 4, :], ptb, [PSt], [(ys, g)])
            if g == 3:
                S.dma("sp", C.ynT_d[c // 2, :, :, (c % 2) * 128:(c % 2 + 1) * 128], ys[:], r=[ys],
                      w=[(C.ynT_d, c)])

        order = [(s5, 4), (s7, 6), (s2, 1), (s1, 0), (s3, 2), (s4, 3), (s6, 5), (s5b, 4)]
        NK = NT_ * 4
        for c in range(NB):
            loads(c)
        for it in range(NK + 6):
            if it >= 8 and it % 4 == 0:
                cn = (it - 8) // 4 + NB
                if cn < NT_:
                    loads(cn)
            for fn, skew in order:
                k = it - skew
                if 0 <= k < NK:
                    fn(k)
        S.barrier()
        S.flush()


def phase5(C, l, x_src, x_dst):
    nc, S, O = C.nc, C.S, C.O
    with ExitStack() as st:
        def sb(name, shape, dt):
            return st.enter_context(_sbt(nc, name, list(shape), dt))
        TB = 256
        NTB = S_ // TB
        wa = sb("p5wa", [128, 8, D_], BF16)
        wb_ = sb("p5wb", [128, 8, D_], BF16)
        wc = sb("p5wc", [128, 16, D_], BF16)
        wo = sb("p5wo", [128, 8, D_], BF16)
        yA_b = [sb(f"p5yA{i}", [128, 8, TB], BF16) for i in range(2)]
        yn_b = [sb(f"p5yn{i}", [128, 16, TB], BF16) for i in range(2)]
        oT_b = [sb(f"p5oT{i}", [128, 8, TB], BF16) for i in range(2)]
        g_b = [sb(f"p5g{i}", [128, 24, TB], BF16) for i in range(2)]
        mT = [sb(f"p5mT{i}", [128, 8, TB], BF16) for i in range(1)] * 2
        ot = [[sb(f"p5ot{p}_{i}", [128, 16, 65], F32) for p in range(3)] for i in range(2)]
        rden = [sb(f"p5rden{i}", [128, 16], F32) for i in range(2)]
        ob2 = [sb(f"p5ob{i}", [128, 16, 64], BF16) for i in range(2)]
        xt = [sb(f"p5xt{i}", [128, D_], F32) for i in range(1)] * 2
        xo = [sb(f"p5xo{i}", [128, D_], F32) for i in range(1)] * 2
        t1 = [sb(f"p5t1_{i}", [128, TB], F32) for i in range(2)]
        t2 = [sb(f"p5t2_{i}", [128, TB], F32) for i in range(2)]
        PB = [st.enter_context(_pst(nc, f"p5ps{i}", [128, 512], F32)) for i in range(6)]
        PT = [st.enter_context(_pst(nc, f"p5pt{i}", [128, 8, 128], BF16)) for i in range(2)]
        S.dma("pool", wa[:], C.w_a[l].rearrange("(k p) n -> p k n", p=128), r=[], w=[wa])
        S.dma("pool", wb_[:], C.w_b[l].rearrange("(k p) n -> p k n", p=128), r=[], w=[wb_])
        S.dma("pool", wc[:], C.w_c[l].rearrange("(k p) n -> p k n", p=128), r=[], w=[wc])
        S.dma("pool", wo[:], C.w_o[l].rearrange("(k p) n -> p k n", p=128), r=[], w=[wo])
        cnt = {"ps": 0, "o": 0}

        def PS():
            p = PB[cnt["ps"] % 6]
            cnt["ps"] += 1
            return p

        def loads(tb):
            b = tb % 2
            tsl = slice(tb * TB, (tb + 1) * TB)
            S.dma("sp", yA_b[b][:], C.yAT[tb], r=[C.yAT], w=[yA_b[b]])
            S.dma("sp", yn_b[b][:], C.ynT_d[tb], r=[C.ynT_d], w=[yn_b[b]])
            S.dma("sp", g_b[b][:], C.gT[tb], r=[C.gT], w=[g_b[b]])

        def oloads(tb):
            for tt in range(TB // 128):
                t = tb * (TB // 128) + tt
                for p in range(3):
                    S.dma("sp", ot[tt][p][:], C.o_d[p][t * 128:(t + 1) * 128], r=[C.o_d[p]], w=[ot[tt][p]])

        def combine_pool(tb, tts=(0, 1)):
            for tt in tts:
                o3 = ot[tt]
                O.tt("pool", o3[0][:], o3[0][:], o3[1][:], ALU.add, [o3[0], o3[1]], [o3[0]])
                O.tt("pool", o3[0][:], o3[0][:], o3[2][:], ALU.add, [o3[0], o3[2]], [o3[0]])
                O.recip(rden[tt][:].unsqueeze(2), o3[0][:, :, 64:65], [o3[0]], [rden[tt]])
                O.tt("pool", ob2[tt][:], o3[0][:, :, 0:64], rden[tt][:].unsqueeze(2).to_broadcast([128, 16, 64]), ALU.mult,
                     [o3[0], rden[tt]], [ob2[tt]])

        def combine_pe(tb):
            b = tb % 2
            for tt in range(TB // 128):
                obf = ob2[tt][:].rearrange("p h d -> p (h d)")
                for k in range(8):
                    O.tr(PT[tt][:, k, :], obf[:, k * 128:(k + 1) * 128], C.ident[:], [ob2[tt], C.ident], [PT[tt]])
                O.copy("act", oT_b[b][:, :, tt * 128:(tt + 1) * 128], PT[tt][:], [PT[tt]], [(oT_b[b], tt)])

        loads(0)
        oloads(0)
        combine_pool(0)
        combine_pe(0)
        nx = 0
        for tb in range(NTB):
            b = tb % 2
            if tb + 1 < NTB:
                loads(tb + 1)
                oloads(tb + 1)
            for cc in range(8):
                if tb + 1 < NTB and cc in (1, 4):
                    combine_pool(tb + 1, (0,) if cc == 1 else (1,))
                pa, pb, pc = PS(), PS(), PS()
                for k in range(8):
                    O.mm(pa[:, :TB], wa[:, k, cc * 128:(cc + 1) * 128], yA_b[b][:, k, :], k == 0, k == 7, [wa, yA_b[b]], [pa])
                for k in range(8):
                    O.mm(pb[:, :TB], wb_[:, k, cc * 128:(cc + 1) * 128], oT_b[b][:, k, :], k == 0, k == 7, [wb_, oT_b[b]], [pb])
                for k in range(16):
                    O.mm(pc[:, :TB], wc[:, k, cc * 128:(cc + 1) * 128], yn_b[b][:, k, :], k == 0, k == 15, [wc, yn_b[b]], [pc])
                a1, a2 = t1[cc % 2], t2[cc % 2]
                O.tt("dve", a1[:], pa[:, :TB], g_b[b][:, cc, :], ALU.mult, [pa, g_b[b]], [a1])
                O.tt("dve", a2[:], pb[:, :TB], g_b[b][:, 8 + cc, :], ALU.mult, [pb, g_b[b]], [a2])
                O.tt("pool", a1[:], a1[:], a2[:], ALU.add, [a1, a2], [a1])
                O.tt("dve", a2[:], pc[:, :TB], g_b[b][:, 16 + cc, :], ALU.mult, [pc, g_b[b]], [a2])
                O.tt("pool", mT[b][:, cc, :], a1[:], a2[:], ALU.add, [a1, a2], [(mT[b], cc)])
            if tb + 1 < NTB:
                combine_pe(tb + 1)
            for tt in range(TB // 128):
                t = tb * (TB // 128) + tt
                xt_, xo_ = xt[nx % 2], xo[nx % 2]
                nx += 1
                S.dma("sp", xt_[:], x_src[t * 128:(t + 1) * 128, :], r=[(x_src, t)], w=[xt_])
                for hf in range(2):
                    ps = PS()
                    for k in range(8):
                        O.mm(ps[:], mT[b][:, k, tt * 128:(tt + 1) * 128], wo[:, k, hf * 512:(hf + 1) * 512], k == 0, k == 7,
                             [mT[b], wo], [ps])
                    O.tt("dve", xo_[:, hf * 512:(hf + 1) * 512], ps[:], xt_[:, hf * 512:(hf + 1) * 512], ALU.add,
                         [ps, xt_], [(xo_, hf)])
                S.dma("sp", C.xmid[t * 128:(t + 1) * 128, :], xo_[:], r=[xo_], w=[(C.xmid, t)])
        S.barrier()
        S.flush()
    if "xmid" in C.dbg and C.stop == (l, 5):
        return
    with ExitStack() as st:
        def sb(name, shape, dt):
            return st.enter_context(_sbt(nc, name, list(shape), dt))
        w2 = sb("p5w2", [128, 32, D_], BF16)
        w1b = [sb(f"p5w1_{i}", [128, 8, 512], BF16) for i in range(3)]
        xts = [sb(f"p5x{i}", [128, D_], F32) for i in range(4)]
        hT = sb("p5hT", [128, 8, 512], BF16)
        h1T = sb("p5h1T", [128, 32, 512], BF16)
        ub = [sb(f"p5ub{i}", [128, D_], BF16) for i in range(2)]
        sq = sb("p5sq", [128, D_], BF16)
        ss = [sb(f"p5ss{i}", [128, 1], F32) for i in range(2)]
        rr = [sb(f"p5r{i}", [128, 512], F32) for i in range(2)]
        xo = [sb(f"p5xo{i}", [128, D_], F32) for i in range(2)]
        nwt = sb("p5nw", [128, D_], F32)
        fnw = sb("p5fnw", [128, D_], F32)
        PB = [st.enter_context(_pst(nc, f"p5bps{i}", [128, 512], F32)) for i in range(6)]
        PT = [st.enter_context(_pst(nc, f"p5bpt{i}", [128, 8, 128], BF16)) for i in range(2)]
        w2src = C.w_2[l].rearrange("(k p) n -> p k n", p=128)
        for k8 in range(4):
            S.dma("pool", w2[:, k8 * 8:(k8 + 1) * 8, :], w2src[:, k8 * 8:(k8 + 1) * 8, :], r=[], w=[(w2, k8)])
        S.dma("sp", nwt[:], C.mlpw[l:l + 1, :].partition_broadcast(128), r=[], w=[nwt])
        S.dma("sp", fnw[:], C.finw.partition_broadcast(128), r=[], w=[fnw])
        w1src = C.wb_1[l].rearrange("(k p) n -> p k n", p=128)
        nps = 0
        nw1 = 0
        for tb in range(8):
            for tt in range(4):
                t = tb * 4 + tt
                S.dma("sp", xts[tt][:], C.xmid[t * 128:(t + 1) * 128, :], r=[(C.xmid, t)], w=[xts[tt]])
                rms_tile(C, xts[tt], ss[tt % 2], sq, nwt, ub[tt % 2], "p5")
                for k in range(8):
                    O.tr(PT[tt % 2][:, k, :], ub[tt % 2][:, k * 128:(k + 1) * 128], C.ident[:], [ub[tt % 2], C.ident], [PT[tt % 2]])
                O.copy("act", hT[:, :, tt * 128:(tt + 1) * 128], PT[tt % 2][:], [PT[tt % 2]], [(hT, tt)])
            for f4 in range(8):
                w1t = w1b[nw1 % 3]
                nw1 += 1
                S.dma("sp", w1t[:], w1src[:, :, f4 * 512:(f4 + 1) * 512], r=[C.wb_1[l]], w=[w1t])
                for fj in range(4):
                    fc = f4 * 4 + fj
                    ps = PB[nps % 6]
                    r_ = rr[nps % 2]
                    nps += 1
                    for k in range(8):
                        O.mm(ps[:], w1t[:, k, fj * 128:(fj + 1) * 128], hT[:, k, :], k == 0, k == 7, [w1t, hT], [ps])
                    O.act(r_[:], ps[:], AF.Relu, [ps], [r_])
                    O.tt("pool" if fc % 2 else "dve", h1T[:, fc, :], r_[:], r_[:], ALU.mult, [r_], [(h1T, fc)])
            for tt in range(4):
                t = tb * 4 + tt
                xo_ = xo[tt % 2]
                for hf in range(2):
                    ps = PB[nps % 6]
                    nps += 1
                    for fc in range(32):
                        O.mm(ps[:], h1T[:, fc, tt * 128:(tt + 1) * 128], w2[:, fc, hf * 512:(hf + 1) * 512], fc == 0, fc == 31,
                             [h1T, w2], [ps])
                    O.tt("dve", xo_[:, hf * 512:(hf + 1) * 512], ps[:], xts[tt][:, hf * 512:(hf + 1) * 512], ALU.add,
                         [ps, xts[tt]], [(xo_, hf)])
                if x_dst is not None:
                    S.dma("sp", x_dst[t * 128:(t + 1) * 128, :], xo_[:], r=[xo_], w=[(x_dst, t)])
                else:
                    s_ = ss[tt % 2]
                    O.act(sq[:], xo_[:], AF.Square, [xo_], [sq, s_], accum_out=s_[:])
                    O.act(s_[:], s_[:], AF.Sqrt, [s_], [s_], bias=1e-6, scale=1.0 / D_)
                    O.recip(s_[:], s_[:], [s_], [s_])
                    O.stt("dve", xo_[:], xo_[:], s_[:], fnw[:], ALU.mult, ALU.mult, [xo_, s_, fnw], [xo_])
                    S.dma("sp", C.y_out[t * 128:(t + 1) * 128, :], xo_[:], r=[xo_], w=[(C.y_out, t)])
        S.barrier()
        S.flush()


def host_consts():
    bf = ml_dtypes.bfloat16
    p = np.arange(128)[:, None]
    f = np.arange(128)[None, :]
    tri = np.stack([(p <= f), (p >= f), (p < f), (p > f)], axis=1).astype(np.float32)
    trif = np.stack([(p < f), (p > f), np.ones((128, 128), bool), (p <= f), (p >= f)], axis=1).astype(np.float32)
    invf = (500000.0 ** (-np.arange(0, 16, 2, dtype=np.float32) / 16.0)).astype(np.float32)[None, :]
    return {"c_ident": np.eye(128, dtype=np.float32).astype(bf), "c_tri": tri.astype(bf), "c_trif": trif, "c_invf": invf}


def prep_inputs(inp):
    f32 = np.float32
    sh = dict(host_consts())
    for k in ("mix_norm_w", "mlp_norm_w", "w_in", "w_a_out", "w_b_out", "w_c_out", "w_o", "w_ff1", "w_ff2"):
        sh[k] = np.ascontiguousarray(inp[k], dtype=f32)
    sh["final_norm_w"] = np.ascontiguousarray(inp["final_norm_w"], dtype=f32).reshape(1, D_)
    sh["conv_a_wT"] = np.ascontiguousarray(inp["conv_a_w"].reshape(L_, 3, 8, 128).transpose(0, 3, 2, 1), dtype=f32)
    sh["ssd_conv_wT"] = np.ascontiguousarray(inp["ssd_conv_w"].reshape(L_, 5, 24, 128).transpose(0, 3, 2, 1), dtype=f32)
    sh["ssd_conv_bT"] = np.ascontiguousarray(inp["ssd_conv_b"].reshape(L_, 24, 128).transpose(0, 2, 1), dtype=f32)
    sh["ssd_conv_bR"] = np.ascontiguousarray(inp["ssd_conv_b"].reshape(L_, 1, 3072), dtype=f32)
    sh["ssd_a_log"] = np.ascontiguousarray(inp["ssd_a_log"].reshape(L_, 1, 64), dtype=f32)
    sh["ssd_dt_bias"] = np.ascontiguousarray(inp["ssd_dt_bias"].reshape(L_, 1, 64), dtype=f32)
    sh["ssd_d"] = np.ascontiguousarray(inp["ssd_d"].reshape(L_, 1, 32), dtype=f32)
    sh["ssd_norm_w"] = np.ascontiguousarray(inp["ssd_norm_w"].reshape(L_, 1, 2048), dtype=f32)
    per = []
    for b in range(inp["x"].shape[0]):
        d = dict(sh)
        d["x"] = np.ascontiguousarray(inp["x"][b], dtype=f32)
        d["pos"] = np.ascontiguousarray(inp["positions"][b].reshape(NT_, 128).T, dtype=np.int32)
        per.append(d)
    return per


_NC_CACHE = {}


def kernel(**inputs):
    per = prep_inputs(inputs)
    if "nc" not in _NC_CACHE:
        _NC_CACHE["nc"] = build()
    nc = _NC_CACHE["nc"]
    res = run_bass_kernel_spmd(nc, per, core_ids=list(range(len(per))))
    return np.stack([np.asarray(r["y"], dtype=np.float32) for r in res.results], axis=0)
```
